# Optimizing a Trainium2 kernel written in Bass

```python
import jax, jax.numpy as jnp
from jax import lax
import numpy as np

D_MODEL = 1024
BATCH = 16
SEQ = 256
DEPTH = 2
DEC_BATCH = 8
DEC_SEQ = 1024
PAST_LEN = 512

GRID_W = 64
N_EVEN = (DEPTH + 1) // 2
N_ODD = DEPTH // 2
D_MIX = D_MODEL
D_FF = 4 * D_MODEL
EPS = 1e-6

GLA_HEADS = 4
GLA_DK_HEAD = (D_MODEL // 2) // GLA_HEADS
GLA_DV_HEAD = (3 * D_MODEL // 4) // GLA_HEADS
GLA_DK = GLA_HEADS * GLA_DK_HEAD
GLA_DV = GLA_HEADS * GLA_DV_HEAD
GLA_LOWRANK = 16
GLA_TAU = 16.0
GLA_CHUNK = 64
FNET_WIDTH = D_MIX - GLA_DV
FNET_GROUPS = 4
FNET_GROUP_DIM = FNET_WIDTH // FNET_GROUPS
EV_SIZES = (GLA_DK, GLA_DK, GLA_DV, GLA_DV, GLA_LOWRANK, GLA_LOWRANK, FNET_WIDTH)
EV_IN = sum(EV_SIZES)
GMLP_WIDTH = D_MIX // 2
GMLP_GROUPS = 4
GMLP_GROUP_DIM = GMLP_WIDTH // GMLP_GROUPS
GMLP_CHUNK = 128
CONV_WIDTH = D_MIX - GMLP_WIDTH
CONV_K = 3
OD_SIZES = (2 * GMLP_WIDTH, CONV_WIDTH, CONV_WIDTH, CONV_WIDTH)
OD_IN = sum(OD_SIZES)
N_MOD = 6

kernel_name = "hybrid_diffusion_gla_fnet_gmlp_shortconv_step"


def _split(t, sizes):
    cuts = [int(s) for s in np.cumsum(sizes)[:-1]]
    return jnp.split(t, cuts, axis=-1)


def rmsnorm(x, g):
    xf = x.astype(jnp.float32)
    y = xf * lax.rsqrt(jnp.mean(xf * xf, axis=-1, keepdims=True) + EPS)
    return (y * g.astype(jnp.float32)).astype(x.dtype)


def gla_chunked(q, k, v, log_a, s0):
    bsz, L, H, _ = q.shape
    dv = v.shape[-1]
    n = L // GLA_CHUNK
    rs = lambda t: t.reshape(bsz, n, GLA_CHUNK, H, t.shape[-1])
    q, k, v, log_a = rs(q), rs(k), rs(v), rs(log_a)
    b = jnp.cumsum(log_a, axis=2)
    b_last = b[:, :, -1:]
    qe = q * jnp.exp(b)
    ke = k * jnp.exp(-b)
    kd = k * jnp.exp(b_last - b)
    lower = jnp.tril(jnp.ones((GLA_CHUNK, GLA_CHUNK), dtype=bool))
    att = jnp.where(lower, jnp.einsum('bnihd,bnjhd->bnhij', qe, ke), 0.0)
    o_intra = jnp.einsum('bnhij,bnjhe->bnihe', att, v)
    u = jnp.einsum('bnjhd,bnjhe->bnhde', kd, v)
    decay = jnp.exp(b_last[:, :, 0])

    def step(s, inp):
        d, du = inp
        return d[..., None] * s + du, s

    s_final, s_prev = lax.scan(step, s0, (jnp.moveaxis(decay, 1, 0), jnp.moveaxis(u, 1, 0)))
    s_prev = jnp.moveaxis(s_prev, 0, 1)
    o_inter = jnp.einsum('bnihd,bnhde->bnihe', qe, s_prev)
    return (o_intra + o_inter).reshape(bsz, L, H, dv), s_final


def gla_bidir(q, k, v, la_f, la_b, s0_f, s0_b):
    o_f, s_f = gla_chunked(q, k, v, la_f, s0_f)
    flip = lambda t: jnp.flip(t, axis=1)
    o_b, s_b = gla_chunked(flip(q), flip(k), flip(v), flip(la_b), s0_b)
    return o_f + flip(o_b), s_f, s_b


def even_mixer(h, w_in, w_out, w2_f, b2_f, w2_b, b2_b, norm_g, s0_f, s0_b):
    bsz, L, _ = h.shape
    f32 = jnp.float32
    q, k, v, g, lr_f, lr_b, fin = _split(h @ w_in, EV_SIZES)
    heads = lambda t, d: t.reshape(bsz, L, GLA_HEADS, d).astype(f32)
    la_f = jax.nn.log_sigmoid((lr_f @ w2_f + b2_f).astype(f32)) / GLA_TAU
    la_b = jax.nn.log_sigmoid((lr_b @ w2_b + b2_b).astype(f32)) / GLA_TAU
    o, s_f, s_b = gla_bidir(heads(q, GLA_DK_HEAD) * (GLA_DK_HEAD ** -0.5),
                            heads(k, GLA_DK_HEAD), heads(v, GLA_DV_HEAD),
                            heads(la_f, GLA_DK_HEAD), heads(la_b, GLA_DK_HEAD),
                            s0_f.astype(f32), s0_b.astype(f32))
    o = o * lax.rsqrt(jnp.mean(o * o, axis=-1, keepdims=True) + EPS) * norm_g.astype(f32)
    o = o.reshape(bsz, L, GLA_DV).astype(h.dtype) * jax.nn.silu(g)
    fg = fin.reshape(bsz, L, FNET_GROUPS, FNET_GROUP_DIM).astype(f32)
    fo = jnp.fft.fftn(fg, axes=(1, 3), norm='ortho').real.reshape(bsz, L, FNET_WIDTH).astype(h.dtype)
    return jnp.concatenate([o, fo], axis=-1) @ w_out, s_f, s_b


def conv3(z, w):
    zp = jnp.pad(z, [(0, 0)] * (z.ndim - 2) + [(1, 1), (0, 0)])
    return zp[..., :-2, :] * w[0] + zp[..., 1:-1, :] * w[1] + zp[..., 2:, :] * w[2]


def odd_mixer(h, w_in, w_out, ws, bs, conv_w, rows):
    bsz, L, _ = h.shape
    uv, gb, gc, hin = _split(h @ w_in, OD_SIZES)
    u, v = jnp.split(jax.nn.gelu(uv), 2, axis=-1)
    n = L // GMLP_CHUNK
    vc = v.reshape(bsz, n, GMLP_CHUNK, GMLP_GROUPS, GMLP_GROUP_DIM)
    sp = jnp.einsum('gpq,bnqgc->bnpgc', ws, vc) + bs.T[:, :, None]
    out_c = u * sp.reshape(bsz, L, GMLP_WIDTH)
    z = gc * hin
    if rows is not None:
        z = z.reshape(bsz, rows, GRID_W, CONV_WIDTH)
    out_d = gb * conv3(z, conv_w).reshape(bsz, L, CONV_WIDTH)
    return jnp.concatenate([out_c, out_d], axis=-1) @ w_out


def setup_inputs(seed: int = 0) -> dict:
    key = jax.random.key(seed)
    ks = iter(jax.random.split(key, 40))
    nrm = lambda shape, s: jax.random.normal(next(ks), shape, jnp.float32) * s
    D = D_MODEL
    st_shape = (DEC_BATCH, N_EVEN, GLA_HEADS, GLA_DK_HEAD, GLA_DV_HEAD)
    return {
        "x_prompt": nrm((BATCH, SEQ, D), 1.0),
        "x_sample": nrm((DEC_BATCH, DEC_SEQ, D), 1.0),
        "state_gla_fwd": nrm(st_shape, 0.5),
        "state_gla_bwd": nrm(st_shape, 0.5),
        "c": nrm((DEC_BATCH, D), 1.0),
        "c_ctx": nrm((D,), 1.0),
        "ada_w": nrm((DEPTH, D, N_MOD * D), 0.5 * D ** -0.5),
        "ada_b": nrm((DEPTH, N_MOD * D), 0.02),
        "norm_mix_g": 1.0 + nrm((DEPTH, D), 0.02),
        "norm_ffn_g": 1.0 + nrm((DEPTH, D), 0.02),
        "ffn_w1": nrm((DEPTH, D, D_FF), D ** -0.5),
        "ffn_w2": nrm((DEPTH, D_FF, D), D_FF ** -0.5),
        "ev_w_in": nrm((N_EVEN, D, EV_IN), D ** -0.5),
        "ev_w_out": nrm((N_EVEN, D_MIX, D), D_MIX ** -0.5),
        "gla_w2_f": nrm((N_EVEN, GLA_LOWRANK, GLA_DK), GLA_LOWRANK ** -0.5),
        "gla_b2_f": nrm((N_EVEN, GLA_DK), 0.5),
        "gla_w2_b": nrm((N_EVEN, GLA_LOWRANK, GLA_DK), GLA_LOWRANK ** -0.5),
        "gla_b2_b": nrm((N_EVEN, GLA_DK), 0.5),
        "gla_norm_g": 1.0 + nrm((N_EVEN, GLA_DV_HEAD), 0.02),
        "od_w_in": nrm((N_ODD, D, OD_IN), D ** -0.5),
        "od_w_out": nrm((N_ODD, D_MIX, D), D_MIX ** -0.5),
        "gmlp_ws": nrm((N_ODD, GMLP_GROUPS, GMLP_CHUNK, GMLP_CHUNK), GMLP_CHUNK ** -0.5),
        "gmlp_b": 1.0 + nrm((N_ODD, GMLP_GROUPS, GMLP_CHUNK), 0.02),
        "conv_w": nrm((N_ODD, CONV_K, CONV_WIDTH), CONV_K ** -0.5),
        "final_norm_g": 1.0 + nrm((D,), 0.02),
    }


def reference(x_prompt, x_sample, state_gla_fwd, state_gla_bwd, c, c_ctx,
              ada_w, ada_b, norm_mix_g, norm_ffn_g, ffn_w1, ffn_w2,
              ev_w_in, ev_w_out, gla_w2_f, gla_b2_f, gla_w2_b, gla_b2_b, gla_norm_g,
              od_w_in, od_w_out, gmlp_ws, gmlp_b, conv_w, final_norm_g):

    def trunk(x, cond, s_in_f, s_in_b, rows):
        outs_f, outs_b = [], []
        for l in range(DEPTH):
            mod = (jax.nn.silu(cond) @ ada_w[l] + ada_b[l])[:, None, :]
            sh_m, sc_m, g_m, sh_f, sc_f, g_f = jnp.split(mod, N_MOD, axis=-1)
            h = rmsnorm(x, norm_mix_g[l]) * (1.0 + sc_m) + sh_m
            if l % 2 == 0:
                i = l // 2
                y, s_f, s_b = even_mixer(h, ev_w_in[i], ev_w_out[i], gla_w2_f[i], gla_b2_f[i],
                                         gla_w2_b[i], gla_b2_b[i], gla_norm_g[i],
                                         s_in_f[:, i], s_in_b[:, i])
                outs_f.append(s_f.astype(x.dtype))
                outs_b.append(s_b.astype(x.dtype))
            else:
                j = l // 2
                y = odd_mixer(h, od_w_in[j], od_w_out[j], gmlp_ws[j], gmlp_b[j], conv_w[j], rows)
            x = x + g_m * y
            h = rmsnorm(x, norm_ffn_g[l]) * (1.0 + sc_f) + sh_f
            x = x + g_f * (jnp.square(jax.nn.relu(h @ ffn_w1[l])) @ ffn_w2[l])
        return rmsnorm(x, final_norm_g), outs_f, outs_b

    b_ctx = x_prompt.shape[0]
    zero_state = jnp.zeros((b_ctx, N_EVEN, GLA_HEADS, GLA_DK_HEAD, GLA_DV_HEAD), jnp.float32)
    y_prompt, sf_list, sb_list = trunk(x_prompt, c_ctx[None, :], zero_state, zero_state, None)
    new_state_gla_fwd = jnp.stack(sf_list, axis=1)
    new_state_gla_bwd = jnp.stack(sb_list, axis=1)

    rows = x_sample.shape[1] // GRID_W
    y_sample, _, _ = trunk(x_sample, c, state_gla_fwd, state_gla_bwd, rows)

    return (y_prompt, y_sample, new_state_gla_fwd, new_state_gla_bwd)
```

```python
import contextlib
import numpy as np
import concourse.bass as bass
import concourse.mybir as mybir
from concourse.bass_utils import run_bass_kernel_spmd

F32 = mybir.dt.float32
BF16 = mybir.dt.bfloat16
AF = mybir.ActivationFunctionType
ALU = mybir.AluOpType

NCORES = 8
D = 1024
NSLOT = 3
EPS = 1e-6


class Buf:
    _n = 0

    def __init__(self, name, excl=False):
        Buf._n += 1
        self.id = Buf._n
        self.name = name
        self.excl = excl
        self.w = {}
        self.r = {}
        self.dsem = None
        self.dcnt = 0


class Tracker:
    def __init__(self, nc, es):
        self.nc = nc
        self.es = es
        self.eng = {}
        for k, h in (("pe", nc.tensor), ("act", nc.scalar), ("dve", nc.vector),
                     ("pool", nc.gpsimd), ("sp", nc.sync)):
            sem = es.enter_context(nc.semaphore("sem_" + k))
            self.eng[k] = dict(h=h, sem=sem, cnt=0, seen={}, key=k)
        self.owners = []
        self.dead = False

    def _collect(self, e, reads, writes):
        need = {}

        def add(key, sem, val):
            if key == "pe" and e["key"] == "pe":
                return
            if e["seen"].get(key, 0) >= val:
                return
            if key not in need or need[key][1] < val:
                need[key] = (sem, val)

        for b in reads:
            for key, (sem, val) in b.w.items():
                add(key, sem, val)
            if b.excl:
                for key, (sem, val) in b.r.items():
                    if key != e["key"]:
                        add(key, sem, val)
        for b in writes:
            for key, (sem, val) in b.w.items():
                add(key, sem, val)
            for key, (sem, val) in b.r.items():
                add(key, sem, val)
        return need

    def _emit(self, e, need, fn):
        items = list(need.items())
        for key, (sem, val) in items[:-1]:
            e["h"].wait_ge(sem, val)
            e["seen"][key] = val
        ins = fn()
        if items:
            key, (sem, val) = items[-1]
            ins._wait_ge(sem, val)
            e["seen"][key] = val
        return ins

    @staticmethod
    def _rec(d, key, sem, val):
        old = d.get(key)
        if old is None or old[1] < val:
            d[key] = (sem, val)

    def op(self, ek, fn, reads=(), writes=(), sig=True):
        if self.dead:
            return None
        e = self.eng[ek]
        need = self._collect(e, reads, writes)
        ins = self._emit(e, need, fn)
        if sig:
            e["cnt"] += 1
            ins.then_inc(e["sem"], 1)
            val = e["cnt"]
        else:
            val = e["cnt"] + 1
        for b in reads:
            self._rec(b.r, ek, e["sem"], val)
        for b in writes:
            self._rec(b.w, ek, e["sem"], val)
        return ins

    def dma(self, qk, out, in_, owner, reads=(), writes=()):
        if self.dead:
            return None
        e = self.eng[qk]
        if owner.dsem is None:
            owner.dsem = self.es.enter_context(self.nc.semaphore("dsem_%d" % owner.id))
            self.owners.append(owner)
        need = self._collect(e, reads, writes)
        ins = self._emit(e, need, lambda: e["h"].dma_start(out=out, in_=in_))
        owner.dcnt += 16
        ins.then_inc(owner.dsem, 16)
        key = "dma%d" % owner.id
        for b in reads:
            self._rec(b.r, key, owner.dsem, owner.dcnt)
        for b in writes:
            self._rec(b.w, key, owner.dsem, owner.dcnt)
        return ins

    def barrier(self, engines=("pe", "act", "dve", "sp")):
        if self.dead:
            return
        targets = [(k, e["sem"], e["cnt"]) for k, e in self.eng.items() if e["cnt"] > 0]
        targets += [("dma%d" % b.id, b.dsem, b.dcnt) for b in self.owners if b.dcnt > 0]
        for ek in engines:
            e = self.eng[ek]
            for key, sem, val in targets:
                if key == ek and ek == "pe":
                    continue
                if e["seen"].get(key, 0) >= val:
                    continue
                e["h"].wait_ge(sem, val)
                e["seen"][key] = val

    def wait_all(self, ek, bufs):
        e = self.eng[ek]
        need = self._collect(e, [], bufs)
        for key, (sem, val) in need.items():
            e["h"].wait_ge(sem, val)
            e["seen"][key] = val


def _consts():
    c = {}
    c["ident"] = np.eye(128, dtype=np.float32)
    s = np.arange(128)[:, None]
    t = np.arange(128)[None, :]
    tri = np.zeros((128, 4, 128), np.float32)
    tri[:, 0, :] = (s <= t) / 16.0
    tri[:, 1, :] = (s >= t) / 16.0
    tri[:, 2, :] = (s > t) / 16.0
    tri[:, 3, :] = (s < t) / 16.0
    c["tri"] = tri
    m2 = np.zeros((128, 256), np.float32)
    m2[:, 0:128] = (s <= t)
    m2[:, 128:256] = (s >= t)
    c["mask2"] = m2
    cs = np.zeros((128, 256), np.float32)
    a = np.arange(64)
    ang = 2 * np.pi * ((a[:, None] * a[None, :]) % 64) / 64.0
    for g in range(2):
        cs[g * 64:(g + 1) * 64, g * 64:(g + 1) * 64] = np.cos(ang)
        cs[g * 64:(g + 1) * 64, 128 + g * 64:128 + (g + 1) * 64] = np.sin(ang)
    c["cs64"] = cs
    for L in (256, 1024):
        tt = np.arange(L, dtype=np.int64)
        ang = 2 * np.pi * ((tt[:, None] * tt[None, :]) % L).astype(np.float64) / L
        sc = 1.0 / np.sqrt(64.0 * L)
        c["dftc%d" % L] = (np.cos(ang) * sc).astype(np.float32)
        c["dfts%d" % L] = (-np.sin(ang) * sc).astype(np.float32)
    return c


PP_GMIX, PP_GFFN, PP_ADAB, PP_CONV, PP_NG, PP_N = 0, 16, 32, 128, 140, 146


def build(debug_dump=None, stop_at=None):
    nc = bass.Bass("TRN2", target_bir_lowering=False)
    dbg = {}

    def din(name, shape):
        return nc.dram_tensor(name, list(shape), F32, kind="ExternalInput").ap()

    def dout(name, shape):
        return nc.dram_tensor(name, list(shape), F32, kind="ExternalOutput").ap()

    xin_d = [din("xp", [512, D]), din("xs", [1024, D])]
    s0_d = [din("s0f", [4, 128, 192]), din("s0b", [4, 128, 192])]
    condT_d = din("condT", [128, 8, 2])
    pp_d = din("pp", [128, PP_N])
    w2aug_d = din("w2aug", [2, 64, 512])
    gfin_d = din("gfin", [1, D])
    gws_d = din("gws", [4, 128, 128])
    gbs_d = din("gbs", [1, 512])
    ada_w = din("ada_w", [2, D, 6 * D])
    ffn_w1 = din("ffn_w1", [2, D, 4 * D])
    ffn_w2 = din("ffn_w2", [2, 4 * D, D])
    ev_w_in = din("ev_w_in", [D, 2848])
    ev_w_out = din("ev_w_out", [D, D])
    od_w_in = din("od_w_in", [D, 2560])
    od_w_out = din("od_w_out", [D, D])
    ident_d = din("ident", [128, 128])
    tri_d = din("tri", [128, 4, 128])
    mask2_d = din("mask2", [128, 256])
    cs64_d = din("cs64", [128, 256])
    dft256_d = [din("dftc256", [256, 256]), din("dfts256", [256, 256])]
    dft1024_d = [din("dftc1024", [1024, 1024]), din("dfts1024", [1024, 1024])]

    y_d = [dout("yp", [512, D]), dout("ys", [1024, D])]
    ns_d = [dout("nsf", [2, 4, 128, 192]), dout("nsb", [2, 4, 128, 192])]

    with contextlib.ExitStack() as es:
        T = Tracker(nc, es)

        def stage(name):
            if stop_at is not None and name == stop_at and not T.dead:
                T.barrier(engines=("pe", "act", "dve", "sp", "pool"))
                T.dead = True

        def sb(name, shape, dt=F32):
            return es.enter_context(nc.sbuf_tensor("sb_" + name, list(shape), dt))

        TMAX = 1024
        xT = sb("xT", [128, 8, TMAX])
        hT = sb("hT", [128, 8, TMAX], BF16)
        slots = [sb("slot%d" % i, [128, 8192], BF16) for i in range(NSLOT)]
        slot_b = [Buf("slot%d" % i) for i in range(NSLOT)]
        U = sb("U", [128, 32768], BF16)
        H2 = sb("H2", [128, 6144], BF16)
        enbx = [sb("enb%d" % i, [128, 256]) for i in range(2)]
        enbx_b = [Buf("enb0"), Buf("enb1")]
        hs1_b = dict(q=Buf("q1"), k=Buf("k1"), kt=Buf("kt1"), vt=Buf("vt1"), sg=Buf("sg1"))
        ident = sb("ident", [128, 128]); identb = sb("identb", [128, 128], BF16)
        tri = sb("tri", [128, 4, 128], BF16)
        mask2 = sb("mask2", [128, 256], BF16)
        cs64 = sb("cs64", [128, 256], BF16)
        onesb = sb("onesb", [128, 128], BF16)
        onesrow = sb("onesrow", [1, 128])
        cst = sb("cst", [128, 4])
        pp = sb("pp", [128, PP_N])
        w2aug = sb("w2aug", [64, 2, 512], BF16)
        gfinb = sb("gfinb", [128, D])
        gws = sb("gws", [128, 4, 128])
        wsT = sb("wsT", [128, 4, 128], BF16)
        gbs = sb("gbs", [1, 512])
        condT = sb("condT", [128, 8, 2])
        scT = sb("scT", [128, 8, 2], BF16)
        modT = [sb("modT%d" % l, [128, 48, 2]) for l in range(2)]
        gscT = [sb("gscT%d" % l, [128, 2, 8, 2]) for l in range(2)]
        rstd = sb("rstd", [128, 512]); lnr = sb("lnr", [128, 512])
        sqc = [sb("sqc%d" % i, [128, 512], BF16) for i in range(2)]
        ntm = [sb("ntm%d" % i, [128, 512]) for i in range(2)]
        sqc_b = [Buf("sqc0"), Buf("sqc1")]; ntm_b = [Buf("ntm0"), Buf("ntm1")]
        relu_t = [sb("relu%d" % i, [128, 512], BF16) for i in range(2)]
        relu_b = [Buf("relu0"), Buf("relu1")]
        rtok = sb("rtok", [128, 8]);
        CB = Buf("consts")
        U_all = []

        def ub(name):
            b = Buf(name)
            for o in U_all:
                for dct in (o.w, o.r):
                    for key, (sem, val) in dct.items():
                        Tracker._rec(b.r, key, sem, val)
            U_all.append(b)
            return b
        xT_b = [[Buf("xT%d_%d" % (c, m)) for m in range(2)] for c in range(8)]
        hT_b = [[Buf("hT%d_%d" % (c, m)) for m in range(2)] for c in range(8)]
        rstd_b = Buf("rstd"); lnr_b = Buf("lnr"); rtok_b = Buf("rtok")
        mod_b = Buf("mod")

        psf = [es.enter_context(nc.psum_tensor("psf%d" % i, [128, 512], F32)) for i in range(7)]
        psf_b = [Buf("psf%d" % i, excl=True) for i in range(7)]
        psb = es.enter_context(nc.psum_tensor("psb", [128, 1024], BF16))
        psb_b = Buf("psb", excl=True)
        ps_rr = [0]

        def PS():
            i = ps_rr[0] % 7
            ps_rr[0] += 1
            return psf[i], psf_b[i]

        def mmg(out, pairs, R, W):
            n = len(pairs)
            for i, (l, r) in enumerate(pairs):
                T.op("pe", lambda l=l, r=r, i=i: nc.tensor.matmul(out, l, r, start=(i == 0), stop=(i == n - 1)),
                     reads=R, writes=W, sig=(i == n - 1))

        def act(out, in_, func, R, W, **kw):
            T.op("act", lambda: nc.scalar.activation(out=out, in_=in_, func=func, **kw), reads=R, writes=W)

        def tt(out, a, b, op, R, W, eng="dve"):
            h = nc.vector if eng == "dve" else nc.gpsimd
            T.op(eng, lambda: h.tensor_tensor(out=out, in0=a, in1=b, op=op), reads=R, writes=W)

        def stt(out, a, scalar, b, op0, op1, R, W):
            T.op("dve", lambda: nc.vector.scalar_tensor_tensor(out=out, in0=a, scalar=scalar, in1=b, op0=op0, op1=op1),
                 reads=R, writes=W)

        def ts(out, a, s1, s2, op0, op1, R, W):
            if s2 is None:
                T.op("dve", lambda: nc.vector.tensor_scalar(out=out, in0=a, scalar1=s1, scalar2=None, op0=op0), reads=R, writes=W)
            else:
                T.op("dve", lambda: nc.vector.tensor_scalar(out=out, in0=a, scalar1=s1, scalar2=s2, op0=op0, op1=op1), reads=R, writes=W)

        def cp(out, in_, R, W, eng="dve"):
            if eng == "act":
                T.op("act", lambda: nc.scalar.copy(out=out, in_=in_), reads=R, writes=W)
            else:
                T.op("dve", lambda: nc.vector.tensor_copy(out, in_), reads=R, writes=W)

        def dump(name, ap, R):
            if debug_dump is None or name not in debug_dump:
                return
            d = dout("dbg_" + name, list(ap.shape))
            b = Buf("dbg")
            T.dma("sp" if ap.dtype == F32 else "pool", d, ap, b, reads=R)
            dbg[name] = b

        def cload(dst, src):
            T.dma("sp", dst, src, CB, writes=[CB])

        cload(ident[:], ident_d)
        cload(pp[:], pp_d)
        cload(gfinb[:], gfin_d.to_broadcast([128, D]))
        cload(gws[:], gws_d.rearrange("g p q -> p g q"))
        cload(gbs[:], gbs_d)
        cload(condT[:], condT_d)
        CBP = Buf("consts_pool")
        T.dma("pool", mask2[:], mask2_d, CBP, writes=[CB])
        T.dma("pool", cs64[:], cs64_d, CBP, writes=[CB])
        T.dma("pool", tri[:], tri_d, CBP, writes=[CB])
        T.dma("pool", w2aug[:], w2aug_d.rearrange("d k n -> k d n"), CBP, writes=[CB])
        T.op("dve", lambda: nc.vector.memset(onesb[:], 1.0 / 1024), writes=[CB])
        T.op("dve", lambda: nc.vector.memset(onesrow[:], 1.0), writes=[CB])
        T.op("dve", lambda: nc.vector.memset(cst[:, 0:1], EPS), writes=[CB])
        T.op("dve", lambda: nc.vector.memset(cst[:, 1:2], 1.0), writes=[CB])
        cp(identb[:], ident[:], [CB], [CB])
        act(scT[:], condT[:], AF.Silu, [CB], [CB])
        for g in range(4):
            ps, pb = PS()
            T.op("pe", lambda g=g, ps=ps: nc.tensor.transpose(ps[:, 0:128], gws[:, g, :], ident[:]), reads=[CB], writes=[pb])
            cp(wsT[:, g, :], ps[:, 0:128], [pb], [CB])

        stage("consts")
        def v3(sl, k, n):
            return sl[:, 0:k * n].rearrange("p (k n) -> p k n", k=k)

        def rows(w2d):
            return w2d.rearrange("(k p) n -> p k n", p=128)

        WSEQ = []

        def wadd(name, fn):
            WSEQ.append((name, fn))

        def wada(l, b):
            wadd("ada%d_%d" % (l, b), lambda sl, l=l, b=b: [(v3(sl, 8, 1024), rows(ada_w[l, :, b * 1024:(b + 1) * 1024]))])

        PH_ORDER = (1, 0)
        PH_ADA = PH_ORDER[0]
        for ph in PH_ORDER:
            for l in range(2):
                if ph == PH_ADA and l == 0:
                    wada(0, 0); wada(0, 1)
                if l == 0:
                    wadd("evC", lambda sl: [(v3(sl, 8, 288), rows(ev_w_in[:, 2560:2848]))])
                    for h in range(4):
                        def f(sl, h=h):
                            v = v3(sl, 8, 640)
                            return [(v[:, :, 0:128], rows(ev_w_in[:, h * 128:(h + 1) * 128])),
                                    (v[:, :, 128:256], rows(ev_w_in[:, 512 + h * 128:512 + (h + 1) * 128])),
                                    (v[:, :, 256:448], rows(ev_w_in[:, 1024 + h * 192:1024 + (h + 1) * 192])),
                                    (v[:, :, 448:640], rows(ev_w_in[:, 1792 + h * 192:1792 + (h + 1) * 192]))]
                        wadd("evH%d" % h, f)
                    if ph == 0:
                        def f(sl):
                            v = sl[:, 0:1024].rearrange("p (j k n) -> p j k n", j=2, k=2)
                            return [(v[:, :, 0, :], rows(dft256_d[0])), (v[:, :, 1, :], rows(dft256_d[1]))]
                        wadd("dft", f)
                    else:
                        wadd("dftc", lambda sl: [(v3(sl, 8, 1024), rows(dft1024_d[0]))])
                        wadd("dfts", lambda sl: [(v3(sl, 8, 1024), rows(dft1024_d[1]))])
                    if ph == PH_ADA:
                        wada(l, 2); wada(l, 3); wada(l, 4)
                    wadd("evO", lambda sl: [(v3(sl, 8, 1024), rows(ev_w_out))])
                else:
                    wadd("odG", lambda sl: [(v3(sl, 8, 1024), rows(od_w_in[:, 1024:2048]))])
                    wadd("odH", lambda sl: [(v3(sl, 8, 512), rows(od_w_in[:, 2048:2560]))])
                    wadd("odUV", lambda sl: [(v3(sl, 8, 1024), rows(od_w_in[:, 0:1024]))])
                    if ph == PH_ADA:
                        wada(l, 2); wada(l, 3); wada(l, 4)
                    wadd("odO", lambda sl: [(v3(sl, 8, 1024), rows(od_w_out))])
                if ph == PH_ADA:
                    wada(l, 5)
                for b in range(4):
                    wadd("w1_%d" % b, lambda sl, l=l, b=b: [(v3(sl, 8, 1024), rows(ffn_w1[l, :, b * 1024:(b + 1) * 1024]))])
                if ph == PH_ADA and l == 0:
                    wada(1, 0); wada(1, 1)
                for b in range(4):
                    wadd("w2_%d" % b, lambda sl, l=l, b=b: [(v3(sl, 32, 256), rows(ffn_w2[l, :, b * 256:(b + 1) * 256]))])

        wstate = dict(issued=0, nxt=0)

        def wget(name, ahead=NSLOT - 1):
            i = wstate["nxt"]
            assert WSEQ[i][0] == name, (WSEQ[i][0], name)
            while wstate["issued"] < min(i + ahead + 1, len(WSEQ)):
                j = wstate["issued"]
                s = j % NSLOT
                for (o, src) in WSEQ[j][1](slots[s]):
                    T.dma("pool", o, src, slot_b[s], writes=[slot_b[s]])
                wstate["issued"] += 1
            wstate["nxt"] += 1
            return slots[i % NSLOT], slot_b[i % NSLOT]

        def ada_block(l, b):
            sl, slb = wget("ada%d_%d" % (l, b))
            w = v3(sl, 8, 1024)
            ps, pb = PS()
            for oc in range(8):
                mmg(ps[:, oc * 2:oc * 2 + 2],
                    [(w[:, kc, oc * 128:(oc + 1) * 128], scT[:, kc, :]) for kc in range(8)],
                    [slb, CB], [pb])
            a0 = PP_ADAB + l * 48 + b * 8
            tt(modT[l][:, b * 8:(b + 1) * 8, :], ps[:, 0:16].rearrange("p (c k) -> p c k", k=2),
               pp[:, a0:a0 + 8].unsqueeze(2).to_broadcast([128, 8, 2]), ALU.add, [pb, CB], [mod_b])
            if b in (1, 4):
                wh = 0 if b == 1 else 1
                g0 = (PP_GMIX if wh == 0 else PP_GFFN) + l * 8
                stt(gscT[l][:, wh, :, :], modT[l][:, b * 8:(b + 1) * 8, :], 1.0,
                    pp[:, g0:g0 + 8].unsqueeze(2).to_broadcast([128, 8, 2]), ALU.add, ALU.mult, [mod_b, CB], [mod_b])

        stage("ada")

        def modcol(l, ch, ci):
            return modT[l][:, ch, ci:ci + 1]

        for ph in PH_ORDER:
            ci = ph
            TT = 512 if ph == 0 else 1024
            NT = TT // 128
            NM = TT // 512
            seqs = [(0, 2), (2, 2)] if ph == 0 else [(0, 8)]
            msl = lambda m: slice(m * 512, (m + 1) * 512)
            tsl = lambda j: slice(j * 128, (j + 1) * 128)

            xin = [U[:, 0:2048].bitcast(F32), U[:, 2048:4096].bitcast(F32)]
            xin_b = [ub("xin0"), ub("xin1")]
            for j in range(NT):
                xi, xb = xin[j % 2], xin_b[j % 2]
                T.dma("sp", xi, xin_d[ph][j * 128:(j + 1) * 128, :], xb, writes=[xb])
                for hf in range(2):
                    ps, pb = PS()
                    for k in range(4):
                        c = hf * 4 + k
                        T.op("pe", lambda ps=ps, k=k, c=c, xi=xi: nc.tensor.transpose(ps[:, k * 128:(k + 1) * 128], xi[:, c * 128:(c + 1) * 128], ident[:]),
                             reads=[xb, CB], writes=[pb], sig=(k == 3))
                    cp(xT[:, hf * 4:hf * 4 + 4, tsl(j)], ps[:].rearrange("p (k t) -> p k t", k=4), [pb],
                       [xT_b[c][j // 4] for c in range(hf * 4, hf * 4 + 4)], eng=("act" if hf == 0 else "dve"))

            def norm(l, wh):
                shc = 0 if wh == 0 else 24
                if ph == PH_ADA and wh == 0 and l == 0:
                    ada_block(0, 0); ada_block(0, 1)
                for m in range(NM):
                    ps, pb = PS()
                    for c in range(8):
                        if c % 2 == 0:
                            act(sqc[0][:], xT[:, c, msl(m)], AF.Square, [xT_b[c][m]], [sqc_b[0]])
                        else:
                            tt(sqc[1][:], xT[:, c, msl(m)], xT[:, c, msl(m)], ALU.mult, [xT_b[c][m]], [sqc_b[1]])
                        T.op("pe", lambda c=c, ps=ps: nc.tensor.matmul(ps[:], onesb[:], sqc[c % 2][:], start=(c == 0), stop=(c == 7)),
                             reads=[sqc_b[c % 2], CB], writes=[pb], sig=True)
                    act(lnr[:], ps[:], AF.Ln, [pb, CB], [lnr_b], bias=cst[:, 0:1])
                    act(rstd[:], lnr[:], AF.Exp, [lnr_b], [rstd_b], scale=-0.5)
                    for c in range(8):
                        stt(ntm[c % 2][:], xT[:, c, msl(m)], gscT[l][:, wh, c, ci:ci + 1], rstd[:], ALU.mult, ALU.mult,
                            [xT_b[c][m], rstd_b, mod_b], [ntm_b[c % 2]])
                        act(hT[:, c, msl(m)], ntm[c % 2][:], AF.Identity, [ntm_b[c % 2], mod_b], [hT_b[c][m]],
                            bias=modcol(l, shc + c, ci))

            nb = {}

            def outproj_resid(name, l, gch):
                if ph == PH_ADA:
                    ada_block(l, 2); ada_block(l, 3); ada_block(l, 4)
                sl, slb = wget(name)
                w = v3(sl, 8, 1024)
                for oc in range(8):
                    for m in range(NM):
                        ps, pb = PS()
                        mmg(ps[:], [(w[:, kc, oc * 128:(oc + 1) * 128], hT[:, kc, msl(m)]) for kc in range(8)],
                            [slb] + [hT_b[kc][m] for kc in range(8)], [pb])
                        stt(xT[:, oc, msl(m)], ps[:], modcol(l, gch + oc, ci), xT[:, oc, msl(m)], ALU.mult, ALU.add,
                            [pb, mod_b], [xT_b[oc][m]])

            def ffn(l):
                aT = U[:, 0:32 * TT].rearrange("p (c t) -> p c t", c=32)
                aT_b = [[ub("aT") for m in range(2)] for c in range(32)]
                if ph == PH_ADA:
                    ada_block(l, 5)
                for b in range(4):
                    sl, slb = wget("w1_%d" % b)
                    w = v3(sl, 8, 1024)
                    order = [(oc, m) for m in range(NM) for oc in range(8)] if b == 0 else [(oc, m) for oc in range(8) for m in range(NM)]
                    for (oc, m) in order:
                        ch = b * 8 + oc
                        if True:
                            ps, pb = PS()
                            mmg(ps[:], [(w[:, kc, oc * 128:(oc + 1) * 128], hT[:, kc, msl(m)]) for kc in range(8)],
                                [slb] + [hT_b[kc][m] for kc in range(8)], [pb])
                            r = relu_t[(ch * NM + m) % 2]; r_b = relu_b[(ch * NM + m) % 2]
                            act(r[:], ps[:], AF.Relu, [pb], [r_b])
                            tt(aT[:, ch, msl(m)], r[:], r[:], ALU.mult, [r_b], [aT_b[ch][m]])
                if ph == PH_ADA and l == 0:
                    ada_block(1, 0); ada_block(1, 1)
                for b in range(4):
                    sl, slb = wget("w2_%d" % b)
                    w = v3(sl, 32, 256)
                    for o2 in range(2):
                        oc = b * 2 + o2
                        for m in range(NM):
                            ps, pb = PS()
                            mmg(ps[:], [(w[:, kc, o2 * 128:(o2 + 1) * 128], aT[:, kc, msl(m)]) for kc in range(32)],
                                [slb] + [aT_b[kc][m] for kc in range(32)], [pb])
                            stt(xT[:, oc, msl(m)], ps[:], modcol(l, 40 + oc, ci), xT[:, oc, msl(m)], ALU.mult, ALU.add,
                                [pb, mod_b], [xT_b[oc][m]])


            stage("load%d" % ph)
            dump("x_in_%d" % ph, xT[:, :, 0:TT], [xT_b[c][m] for c in range(8) for m in range(NM)])
            norm(0, 0)
            dump("h_in_%d" % ph, hT[:, :, 0:TT], [hT_b[c][m] for c in range(8) for m in range(NM)])
            stage("norm00_%d" % ph)
            o = [0]

            def ualloc(nelem_bf16):
                a = o[0]
                o[0] += nelem_bf16
                assert o[0] <= 32768, o[0]
                return a

            a_lr = ualloc(TT); lrT = U[:, a_lr:a_lr + TT]
            a_fin = ualloc(2 * TT); finT = U[:, a_fin:a_fin + 2 * TT].rearrange("p (c t) -> p c t", c=2)
            a_xcs = ualloc(NT * 512); xcs = U[:, a_xcs:a_xcs + NT * 512].rearrange("p (j c n) -> p j c n", j=NT, c=2)
            a_og = ualloc(NT * 768); og = U[:, a_og:a_og + NT * 768].rearrange("p (j n) -> p j n", j=NT)
            a_q = ualloc(TT); qT = U[:, a_q:a_q + TT]
            a_k = ualloc(TT); kT = U[:, a_k:a_k + TT]
            a_kt = ualloc(NT * 128); ktok = U[:, a_kt:a_kt + NT * 128].rearrange("p (j n) -> p j n", j=NT)
            a_vt = ualloc(NT * 192); vtok = U[:, a_vt:a_vt + NT * 192].rearrange("p (j n) -> p j n", j=NT)
            a_sg = ualloc(NT * 192); sgtok = U[:, a_sg:a_sg + NT * 192].rearrange("p (j n) -> p j n", j=NT)
            qe = []; ke = []; sprev = []; Sst = []
            for d in range(2):
                a = ualloc(TT); qe.append(U[:, a:a + TT])
                a = ualloc(TT); ke.append(U[:, a:a + TT])
                a = ualloc(NT * 192); sprev.append(U[:, a:a + NT * 192].rearrange("p (j n) -> p j n", j=NT))
                a = ualloc(384); Sst.append(U[:, a:a + 384].bitcast(F32))
            tr = []
            for par in range(2):
                dct = {}
                for nm, ne, dtp in (("e1", 512, F32), ("sp", 256, BF16), ("ekd", 512, F32), ("kd", 256, BF16),
                                    ("eb", 512, F32), ("att", 256, BF16), ("junk", 192, BF16)):
                    a = ualloc(ne)
                    v = U[:, a:a + ne]
                    dct[nm] = v.bitcast(F32) if dtp == F32 else v
                    dct[nm + "_b"] = ub(nm)
                tr.append(dct)
            a_ss = ualloc(NT * 8); ssqa = U[:, a_ss:a_ss + NT * 8].bitcast(F32)
            a_ls = ualloc(NT * 8); lnsa = U[:, a_ls:a_ls + NT * 8].bitcast(F32)
            a_rs = ualloc(NT * 8); rsa = U[:, a_rs:a_rs + NT * 8].bitcast(F32)
            ssqa_b = ub("ssqa"); lnsa_b = ub("lnsa"); rsa_b = ub("rsa")
            lr_b = ub("lr"); fin_b = ub("fin"); xcs_b = [ub("xcs") for _ in range(NT)]; og_b = [ub("og") for _ in range(NT)]
            q_b = ub("q"); k_b = ub("k"); kt_b = ub("kt"); vt_b = ub("vt"); sg_b = ub("sg")
            qe_b = [ub("qef"), ub("qeb")]; ke_b = [ub("kef"), ub("keb")]
            sprev_b = [ub("spf"), ub("spb")]; S_b = [ub("Sf"), ub("Sb")]
            hall = lambda m: [hT_b[kc][m] for kc in range(8)]

            T.op("dve", lambda: nc.vector.memset(lrT[32:64, :], 0.0), writes=[lr_b])
            T.op("dve", lambda: nc.vector.memset(lrT[32:33, :], 1.0), writes=[lr_b])

            sl, slb = wget("evC")
            w = v3(sl, 8, 288)
            for m in range(NM):
                ps, pb = PS()
                mmg(ps[0:32, :], [(w[:, kc, 0:32], hT[:, kc, msl(m)]) for kc in range(8)], [slb] + hall(m), [pb])
                cp(lrT[0:32, msl(m)], ps[0:32, :], [pb], [lr_b], eng="act")
                for c2 in range(2):
                    ps, pb = PS()
                    mmg(ps[:], [(w[:, kc, 32 + c2 * 128:32 + (c2 + 1) * 128], hT[:, kc, msl(m)]) for kc in range(8)],
                        [slb] + hall(m), [pb])
                    cp(finT[:, c2, msl(m)], ps[:], [pb], [fin_b], eng="dve")
            for j in range(NT):
                for c2 in range(2):
                    ps, pb = PS()
                    mmg(ps[:, 0:256], [(finT[:, c2, tsl(j)], cs64[:])], [fin_b, CB], [pb])
                    cp(xcs[:, j, c2, :], ps[:, 0:256], [pb], [xcs_b[j]], eng=("act" if c2 == 0 else "dve"))

            stage("evC_%d" % ph)
            o2 = [0]

            def h2alloc(n):
                a = o2[0]; o2[0] += n
                assert o2[0] <= 6144
                return H2[:, a:a + n]

            hs = [dict(qT=qT, kT=kT, ktok=ktok, vtok=vtok, sgtok=sgtok, q=q_b, k=k_b, kt=kt_b, vt=vt_b, sg=sg_b),
                  dict(qT=h2alloc(TT), kT=h2alloc(TT),
                       ktok=h2alloc(NT * 128).rearrange("p (j n) -> p j n", j=NT),
                       vtok=h2alloc(NT * 192).rearrange("p (j n) -> p j n", j=NT),
                       sgtok=h2alloc(NT * 192).rearrange("p (j n) -> p j n", j=NT), **hs1_b)]

            def proj_groups(h):
                H = hs[h % 2]
                st = {}

                def getw():
                    if "w" not in st:
                        sl, slb = wget("evH%d" % h)
                        st["w"] = v3(sl, 8, 640); st["b"] = slb
                    return st["w"], st["b"]

                def gq(m):
                    w, slb = getw()
                    ps, pb = PS()
                    mmg(ps[:], [(w[:, kc, 0:128], hT[:, kc, msl(m)]) for kc in range(8)], [slb] + hall(m), [pb])
                    cp(H["qT"][:, msl(m)], ps[:], [pb], [H["q"]], eng="act")

                def gk(m):
                    w, slb = getw()
                    ps, pb = PS()
                    mmg(ps[:], [(w[:, kc, 128:256], hT[:, kc, msl(m)]) for kc in range(8)], [slb] + hall(m), [pb])
                    cp(H["kT"][:, msl(m)], ps[:], [pb], [H["k"]], eng="dve")

                def gt(j):
                    w, slb = getw()
                    ps, pb = PS()
                    mmg(ps[:], [(hT[:, kc, tsl(j)], w[:, kc, 128:640]) for kc in range(8)], [slb] + hall(j // 4), [pb])
                    cp(H["ktok"][:, j, :], ps[:, 0:128], [pb], [H["kt"]], eng="dve")
                    cp(H["vtok"][:, j, :], ps[:, 128:320], [pb], [H["vt"]], eng="dve")
                    cp(H["sgtok"][:, j, :], ps[:, 320:512], [pb], [H["sg"]], eng="act")

                def gsilu():
                    act(H["sgtok"], H["sgtok"], AF.Silu, [H["sg"]], [H["sg"]])

                gl = []
                for m in range(NM):
                    gl.append(lambda m=m: gq(m)); gl.append(lambda m=m: gk(m))
                for j in range(NT):
                    gl.append(lambda j=j: gt(j))
                gl.append(gsilu)
                return gl

            for g in proj_groups(0):
                g()
            for h in range(4):
                H = hs[h % 2]
                pend = proj_groups(h + 1) if h < 3 else []
                steps = []
                for (t0, n) in seqs:
                    fw = list(range(t0, t0 + n)); bw = fw[::-1]
                    for a, b in zip(fw, bw):
                        steps.append((t0 // 2, 0, a, a == fw[0], a == fw[-1]))
                        steps.append((t0 // 2, 1, b, b == bw[0], b == bw[-1]))

                def stA(p):
                    X = tr[p % 2]
                    ps, pb = PS()
                    for i in range(2):
                        si, d, j, first, lastt = steps[2 * p + i]
                        mmg(ps[:, i * 128:(i + 1) * 128], [(lrT[0:64, tsl(j)], w2aug[0:64, d, h * 128:(h + 1) * 128])], [lr_b, CB], [pb])
                    act(X["e1"], ps[:, 0:256], AF.Exp, [pb], [X["e1_b"]], scale=-1.0)
                    act(X["sp"], X["e1"], AF.Ln, [X["e1_b"], CB], [X["sp_b"]], bias=cst[:, 1:2])

                def stB(p):
                    X = tr[p % 2]
                    ps, pb = PS()
                    for i in range(2):
                        si, d, j, first, lastt = steps[2 * p + i]
                        spi = X["sp"][:, i * 128:(i + 1) * 128]
                        mmg(ps[:, i * 128:(i + 1) * 128], [(tri[:, 2 + d, :], spi)], [X["sp_b"], CB], [pb])
                        mmg(ps[:, 256 + i * 128:256 + (i + 1) * 128], [(spi, tri[:, d, :])], [X["sp_b"], CB], [pb])
                    act(X["ekd"], ps[:, 0:256], AF.Exp, [pb], [X["ekd_b"]], scale=-1.0)
                    act(X["eb"], ps[:, 256:512], AF.Exp, [pb], [X["eb_b"]], scale=-1.0)
                    act(enbx[p % 2][:], ps[:, 256:512], AF.Exp, [pb], [enbx_b[p % 2]])

                def stC1(k):
                    si, d, j, first, lastt = steps[k]
                    X = tr[(k // 2) % 2]
                    i = k % 2
                    cs_ = slice(i * 128, (i + 1) * 128)
                    tt(X["kd"][:, cs_], H["ktok"][:, j, :], X["ekd"][:, cs_], ALU.mult, [H["kt"], X["ekd_b"]], [X["kd_b"]])
                    stt(qe[d][:, tsl(j)], H["qT"][:, tsl(j)], 128.0 ** -0.5, X["eb"][:, cs_], ALU.mult, ALU.mult,
                        [H["q"], X["eb_b"]], [qe_b[d]])
                    tt(ke[d][:, tsl(j)], H["kT"][:, tsl(j)], enbx[(k // 2) % 2][:, cs_], ALU.mult, [H["k"], enbx_b[(k // 2) % 2]], [ke_b[d]])

                def stC2(k):
                    si, d, j, first, lastt = steps[k]
                    X = tr[(k // 2) % 2]
                    i = k % 2
                    cs_ = slice(i * 128, (i + 1) * 128)
                    last = i * 128 + (127 if d == 0 else 0)
                    if first:
                        if ph == 0:
                            T.op("dve", lambda d=d: nc.vector.memset(Sst[d], 0.0), writes=[S_b[d]])
                        else:
                            T.dma("sp", Sst[d], s0_d[d][h], S_b[d], writes=[S_b[d]])
                    ps, pb = PS()
                    mmg(ps[:, 0:192], [(X["kd"][:, cs_], H["vtok"][:, j, :])], [X["kd_b"], H["vt"]], [pb])
                    cp(sprev[d][:, j, :], Sst[d], [S_b[d]], [sprev_b[d]], eng="act")
                    stt(Sst[d], Sst[d], X["eb"][:, last:last + 1], ps[:, 0:192], ALU.mult, ALU.add,
                        [S_b[d], X["eb_b"], pb], [S_b[d]])
                    if lastt and ph == 0:
                        T.dma("sp", ns_d[d][si, h], Sst[d], S_b[d], reads=[S_b[d]])

                def popg():
                    if len(pend) > 1:
                        pend.pop(0)()

                npair = len(steps) // 2
                for i in range(npair + 2):
                    if 0 <= i - 2 < npair:
                        stC1(2 * (i - 2)); stC1(2 * (i - 2) + 1)
                    if i < npair:
                        stA(i)
                    popg()
                    if 0 <= i - 1 < npair:
                        stB(i - 1)
                    popg()
                    if 0 <= i - 2 < npair:
                        stC2(2 * (i - 2)); stC2(2 * (i - 2) + 1)
                stage("h%d_p1_%d" % (h, ph))

                def p2a(j):
                    X = tr[j % 2]
                    ps, pb = PS()
                    mmg(ps[:, 0:128], [(ke[0][:, tsl(j)], qe[0][:, tsl(j)])], [ke_b[0], qe_b[0]], [pb])
                    mmg(ps[:, 128:256], [(ke[1][:, tsl(j)], qe[1][:, tsl(j)])], [ke_b[1], qe_b[1]], [pb])
                    tt(X["att"], ps[:, 0:256], mask2[:], ALU.mult, [pb, CB], [X["att_b"]])

                def p2b(j):
                    X = tr[j % 2]
                    ps, pb = PS()
                    mmg(ps[:, 0:192], [(X["att"][:, 0:128], H["vtok"][:, j, :]), (X["att"][:, 128:256], H["vtok"][:, j, :]),
                                       (qe[0][:, tsl(j)], sprev[0][:, j, :]), (qe[1][:, tsl(j)], sprev[1][:, j, :])],
                        [X["att_b"], H["vt"], qe_b[0], qe_b[1], sprev_b[0], sprev_b[1]], [pb])
                    act(X["junk"], ps[:, 0:192], AF.Square, [pb], [X["junk_b"], ssqa_b], scale=192.0 ** -0.5,
                        accum_out=ssqa[:, j * 4 + h:j * 4 + h + 1])
                    tt(og[:, j, h * 192:(h + 1) * 192], ps[:, 0:192], H["sgtok"][:, j, :], ALU.mult, [pb, H["sg"]], [og_b[j]])

                p2a(0)
                for j in range(NT):
                    if j + 1 < NT:
                        p2a(j + 1)
                    p2b(j)
                    if len(pend) > 1:
                        pend.pop(0)()
                while pend:
                    pend.pop(0)()

            act(lnsa, ssqa, AF.Ln, [ssqa_b, CB], [lnsa_b], bias=cst[:, 0:1])
            act(rsa, lnsa, AF.Exp, [lnsa_b], [rsa_b], scale=-0.5)
            og3 = U[:, a_og:a_og + NT * 768].rearrange("p (g e) -> p g e", e=192)
            tt(og3, og3, rsa.unsqueeze(2).to_broadcast([128, NT * 4, 192]), ALU.mult, [rsa_b] + og_b, og_b)
            stage("gla_%d" % ph)
            for j in range(NT):
                for c in range(6):
                    T.op("pe", lambda j=j, c=c: nc.tensor.transpose(psb[:, c * 128:(c + 1) * 128], og[:, j, c * 128:(c + 1) * 128], identb[:]),
                         reads=[og_b[j], CB], writes=[psb_b], sig=(c == 5))
                tt(hT[:, 0:6, tsl(j)], psb[:, 0:768].rearrange("p (c t) -> p c t", c=6),
                   pp[:, PP_NG:PP_NG + 6].unsqueeze(2).to_broadcast([128, 6, 128]), ALU.mult,
                   [psb_b, CB], [hT_b[c][j // 4] for c in range(6)])
            if ph == 0:
                sl, slb = wget("dft")
                dv = sl[:, 0:1024].rearrange("p (j k n) -> p j k n", j=2, k=2)
                for (t0, n) in seqs:
                    for c2 in range(2):
                        ps, pb = PS()
                        pairs = []
                        for jj in range(2):
                            pairs.append((xcs[:, t0 + jj, c2, 0:128], dv[:, jj, 0, :]))
                            pairs.append((xcs[:, t0 + jj, c2, 128:256], dv[:, jj, 1, :]))
                        mmg(ps[:, 0:256], pairs, [slb, xcs_b[t0], xcs_b[t0 + 1]], [pb])
                        cp(hT[:, 6 + c2, t0 * 128:t0 * 128 + 256], ps[:, 0:256], [pb], [hT_b[6 + c2][0]],
                           eng=("act" if c2 == 0 else "dve"))
            else:
                slc, slcb = wget("dftc")
                sls, slsb = wget("dfts", ahead=NSLOT - 2)
                dc = v3(slc, 8, 1024); ds = v3(sls, 8, 1024)
                for c2 in range(2):
                    for m in range(2):
                        ps, pb = PS()
                        pairs = []
                        for jj in range(8):
                            pairs.append((xcs[:, jj, c2, 0:128], dc[:, jj, msl(m)]))
                            pairs.append((xcs[:, jj, c2, 128:256], ds[:, jj, msl(m)]))
                        mmg(ps[:], pairs, [slcb, slsb] + xcs_b, [pb])
                        cp(hT[:, 6 + c2, msl(m)], ps[:], [pb], [hT_b[6 + c2][m]], eng=("act" if c2 == 0 else "dve"))
            stage("mix0_%d" % ph)
            dump("mix0_%d" % ph, hT[:, :, 0:TT], [hT_b[c][m] for c in range(8) for m in range(NM)])
            outproj_resid("evO", 0, 16)
            dump("x_l0mix_%d" % ph, xT[:, :, 0:TT], [xT_b[c][m] for c in range(8) for m in range(NM)])
            stage("out0_%d" % ph)
            norm(0, 1)
            ffn(0)
            stage("ffn0_%d" % ph)
            dump("x_l0_%d" % ph, xT[:, :, 0:TT], [xT_b[c][m] for c in range(8) for m in range(NM)])

            norm(1, 0)
            uT = U[:, 0:4 * TT].rearrange("p (c t) -> p c t", c=4)
            v1 = U[:, 4096:4096 + NT * 512].rearrange("p (j n) -> p j n", j=NT)
            gbT = U[:, 8192:8192 + 4 * TT].rearrange("p (c t) -> p c t", c=4)
            gcT = U[:, 12288:12288 + 4 * TT].rearrange("p (c t) -> p c t", c=4)
            zT = U[:, 24576:32768].bitcast(F32).rearrange("p (c t) -> p c t", c=4)
            u_b = ub("uT"); v1_b = ub("v1"); gb_b = ub("gb"); gc_b = ub("gc"); z_b = ub("z")
            accs = [U[:, 16384 + i * 2048:16384 + (i + 1) * 2048].bitcast(F32) for i in range(4)]
            acc_b = [ub("acc%d" % i) for i in range(4)]
            sl, slb = wget("odG")
            w = v3(sl, 8, 1024)
            for m in range(NM):
                for oc in range(8):
                    ps, pb = PS()
                    mmg(ps[:], [(w[:, kc, oc * 128:(oc + 1) * 128], hT[:, kc, msl(m)]) for kc in range(8)], [slb] + hall(m), [pb])
                    if oc < 4:
                        cp(gbT[:, oc, msl(m)], ps[:], [pb], [gb_b], eng="act")
                    else:
                        cp(gcT[:, oc - 4, msl(m)], ps[:], [pb], [gc_b], eng="dve")
            sl, slb = wget("odH")
            w = v3(sl, 8, 512)
            for oc in range(4):
                for m in range(NM):
                    ps, pb = PS()
                    mmg(ps[:], [(w[:, kc, oc * 128:(oc + 1) * 128], hT[:, kc, msl(m)]) for kc in range(8)], [slb] + hall(m), [pb])
                    tt(zT[:, oc, msl(m)], ps[:], gcT[:, oc, msl(m)], ALU.mult, [pb, gc_b], [z_b])
            RW = 256 if ph == 0 else 64
            for oc in range(4):
                acc = accs[oc][:, 0:TT]; ab = acc_b[oc]
                z = zT[:, oc, 0:TT]
                cw = lambda k: pp[:, PP_CONV + oc * 3 + k:PP_CONV + oc * 3 + k + 1]
                ts(acc, z, cw(1), None, ALU.mult, None, [z_b, CB], [ab])
                a3 = acc.rearrange("p (r w) -> p r w", w=RW)
                z3 = z.rearrange("p (r w) -> p r w", w=RW)
                stt(a3[:, :, 1:RW], z3[:, :, 0:RW - 1], cw(0), a3[:, :, 1:RW], ALU.mult, ALU.add, [z_b, CB, ab], [ab])
                stt(a3[:, :, 0:RW - 1], z3[:, :, 1:RW], cw(2), a3[:, :, 0:RW - 1], ALU.mult, ALU.add, [z_b, CB, ab], [ab])
            sl, slb = wget("odUV")
            w = v3(sl, 8, 1024)
            for oc in range(4):
                for m in range(NM):
                    ps, pb = PS()
                    mmg(ps[:], [(w[:, kc, oc * 128:(oc + 1) * 128], hT[:, kc, msl(m)]) for kc in range(8)], [slb] + hall(m), [pb])
                    act(uT[:, oc, msl(m)], ps[:], AF.Gelu_apprx_tanh, [pb], [u_b])
            for j in range(NT):
                ps, pb = PS()
                mmg(ps[:], [(hT[:, kc, tsl(j)], w[:, kc, 512:1024]) for kc in range(8)], [slb] + hall(j // 4), [pb])
                act(v1[:, j, :], ps[:], AF.Gelu_apprx_tanh, [pb], [v1_b])
            for j in range(NT):
                ps, pb = PS()
                for g in range(4):
                    T.op("pe", lambda g=g, ps=ps, j=j: nc.tensor.matmul(ps[:, g * 128:(g + 1) * 128], v1[:, j, g * 128:(g + 1) * 128], wsT[:, g, :], start=True, stop=False),
                         reads=[v1_b, CB], writes=[pb], sig=False)
                    T.op("pe", lambda g=g, ps=ps: nc.tensor.matmul(ps[:, g * 128:(g + 1) * 128], onesrow[0:1, :], gbs[0:1, g * 128:(g + 1) * 128], start=False, stop=True),
                         reads=[CB], writes=[pb], sig=(g == 3))
                tt(hT[:, 0:4, tsl(j)], ps[:].rearrange("p (g t) -> p g t", g=4), uT[:, :, tsl(j)], ALU.mult,
                   [pb, u_b], [hT_b[c][j // 4] for c in range(4)])
            for oc in range(4):
                tt(hT[:, 4 + oc, 0:TT], accs[oc][:, 0:TT], gbT[:, oc, 0:TT], ALU.mult, [acc_b[oc], gb_b], [hT_b[4 + oc][m] for m in range(NM)])
            stage("mix1_%d" % ph)
            outproj_resid("odO", 1, 16)
            dump("x_l1mix_%d" % ph, xT[:, :, 0:TT], [xT_b[c][m] for c in range(8) for m in range(NM)])
            norm(1, 1)
            ffn(1)
            stage("ffn1_%d" % ph)

            yst = [U[:, 0:2048].bitcast(F32), U[:, 2048:4096].bitcast(F32)]
            yst_b = [ub("yst0"), ub("yst1")]; nb["sq"] = ub("sq")
            for m in range(NM):
                sq = U[:, 20480:24576].rearrange("p (c t) -> p c t", c=8)
                xr = [xT_b[c][m] for c in range(8)]
                act(sq, xT[:, :, msl(m)], AF.Square, xr, [nb["sq"]])
                psr, pbr = PS()
                for jj in range(4):
                    mmg(psr[:, jj:jj + 1], [(sq[:, c, jj * 128:(jj + 1) * 128], onesb[:, 0:1]) for c in range(8)], [nb["sq"], CB], [pbr])
                act(rtok[:, 0:4], psr[:, 0:4], AF.Ln, [pbr, CB], [rtok_b], bias=cst[:, 0:1])
                act(rtok[:, 4:8], rtok[:, 0:4], AF.Exp, [rtok_b], [rtok_b], scale=-0.5)
                for jj in range(4):
                    j = m * 4 + jj
                    ys, yb = yst[j % 2], yst_b[j % 2]
                    for hf in range(2):
                        ps, pb = PS()
                        for k in range(4):
                            c = hf * 4 + k
                            T.op("pe", lambda ps=ps, k=k, c=c, j=j: nc.tensor.transpose(ps[:, k * 128:(k + 1) * 128], xT[:, c, j * 128:(j + 1) * 128], ident[:]),
                                 reads=[xT_b[c][m], CB], writes=[pb], sig=(k == 3))
                        stt(ys[:, hf * 512:(hf + 1) * 512], ps[:], rtok[:, 4 + jj:5 + jj], gfinb[:, hf * 512:(hf + 1) * 512], ALU.mult, ALU.mult,
                            [pb, rtok_b, CB], [yb])
                    T.dma("sp", y_d[ph][j * 128:(j + 1) * 128, :], ys, yb, reads=[yb])

        T.barrier(engines=("pe", "act", "dve", "sp", "pool"))
    return nc


_CACHE = {}


def _get_nc():
    if "nc" not in _CACHE:
        _CACHE["nc"] = build()
        _CACHE["consts"] = _consts()
    return _CACHE["nc"], _CACHE["consts"]


def make_in_maps(inputs, consts):
    f = lambda a: np.ascontiguousarray(np.asarray(a, dtype=np.float32))
    x_prompt = f(inputs["x_prompt"]); x_sample = f(inputs["x_sample"])
    sf = f(inputs["state_gla_fwd"]); sbw = f(inputs["state_gla_bwd"])
    c = f(inputs["c"]); c_ctx = f(inputs["c_ctx"])
    pp = np.zeros((128, PP_N), np.float32)
    for l in range(2):
        pp[:, PP_GMIX + l * 8:PP_GMIX + (l + 1) * 8] = f(inputs["norm_mix_g"])[l].reshape(8, 128).T
        pp[:, PP_GFFN + l * 8:PP_GFFN + (l + 1) * 8] = f(inputs["norm_ffn_g"])[l].reshape(8, 128).T
        pp[:, PP_ADAB + l * 48:PP_ADAB + (l + 1) * 48] = f(inputs["ada_b"])[l].reshape(48, 128).T
    pp[:, PP_CONV:PP_CONV + 12] = f(inputs["conv_w"])[0].T.reshape(4, 128, 3).transpose(1, 0, 2).reshape(128, 12)
    pp[:, PP_NG:PP_NG + 6] = np.tile(f(inputs["gla_norm_g"])[0], 4).reshape(6, 128).T
    w2aug = np.zeros((2, 64, 512), np.float32)
    w2aug[0, 0:16] = f(inputs["gla_w2_f"])[0]
    w2aug[0, 32] = f(inputs["gla_b2_f"])[0]
    w2aug[1, 16:32] = f(inputs["gla_w2_b"])[0]
    w2aug[1, 32] = f(inputs["gla_b2_b"])[0]
    shared = dict(
        pp=pp, w2aug=w2aug, gfin=f(inputs["final_norm_g"]).reshape(1, D),
        gws=f(inputs["gmlp_ws"])[0], gbs=f(inputs["gmlp_b"])[0].reshape(1, 512),
        ada_w=f(inputs["ada_w"]), ffn_w1=f(inputs["ffn_w1"]), ffn_w2=f(inputs["ffn_w2"]),
        ev_w_in=f(inputs["ev_w_in"])[0], ev_w_out=f(inputs["ev_w_out"])[0],
        od_w_in=f(inputs["od_w_in"])[0], od_w_out=f(inputs["od_w_out"])[0],
        ident=consts["ident"], tri=consts["tri"], mask2=consts["mask2"], cs64=consts["cs64"],
        dftc256=consts["dftc256"], dfts256=consts["dfts256"],
        dftc1024=consts["dftc1024"], dfts1024=consts["dfts1024"],
    )
    maps = []
    for i in range(NCORES):
        cond = np.stack([c_ctx, c[i]], axis=0)
        m = dict(shared)
        m["xp"] = x_prompt[2 * i:2 * i + 2].reshape(512, D)
        m["xs"] = x_sample[i]
        m["s0f"] = sf[i, 0]
        m["s0b"] = sbw[i, 0]
        m["condT"] = np.ascontiguousarray(cond.reshape(2, 8, 128).transpose(2, 1, 0))
        maps.append(m)
    return maps


def kernel(**inputs):
    nc, consts = _get_nc()
    maps = make_in_maps(inputs, consts)
    res = run_bass_kernel_spmd(nc, maps, core_ids=list(range(NCORES)))
    r = res.results
    y_prompt = np.concatenate([r[i]["yp"].reshape(2, 256, D) for i in range(NCORES)], axis=0)
    y_sample = np.stack([r[i]["ys"] for i in range(NCORES)], axis=0)
    nsf = np.concatenate([r[i]["nsf"].reshape(2, 1, 4, 128, 192) for i in range(NCORES)], axis=0)
    nsb = np.concatenate([r[i]["nsb"].reshape(2, 1, 4, 128, 192) for i in range(NCORES)], axis=0)
    return (y_prompt.astype(np.float32), y_sample.astype(np.float32),
            nsf.astype(np.float32), nsb.astype(np.float32))
```

```python
import contextlib
import numpy as np
import concourse.bass as bass
import concourse.mybir as mybir
from concourse.bass_utils import run_bass_kernel_spmd

F32 = mybir.dt.float32
BF16 = mybir.dt.bfloat16
AF = mybir.ActivationFunctionType
ALU = mybir.AluOpType

NCORES = 8
D = 1024
NSLOT = 3
EPS = 1e-6


class Buf:
    _n = 0

    def __init__(self, name, excl=False):
        Buf._n += 1
        self.id = Buf._n
        self.name = name
        self.excl = excl
        self.w = {}
        self.r = {}
        self.dsem = None
        self.dcnt = 0


class Tracker:
    def __init__(self, nc, es):
        self.nc = nc
        self.es = es
        self.eng = {}
        for k, h in (("pe", nc.tensor), ("act", nc.scalar), ("dve", nc.vector),
                     ("pool", nc.gpsimd), ("sp", nc.sync)):
            sem = es.enter_context(nc.semaphore("sem_" + k))
            self.eng[k] = dict(h=h, sem=sem, cnt=0, seen={}, key=k)
        self.owners = []
        self.dead = False

    def _collect(self, e, reads, writes):
        need = {}

        def add(key, sem, val):
            if key == "pe" and e["key"] == "pe":
                return
            if e["seen"].get(key, 0) >= val:
                return
            if key not in need or need[key][1] < val:
                need[key] = (sem, val)

        for b in reads:
            for key, (sem, val) in b.w.items():
                add(key, sem, val)
            if b.excl:
                for key, (sem, val) in b.r.items():
                    if key != e["key"]:
                        add(key, sem, val)
        for b in writes:
            for key, (sem, val) in b.w.items():
                add(key, sem, val)
            for key, (sem, val) in b.r.items():
                add(key, sem, val)
        return need

    def _emit(self, e, need, fn):
        items = list(need.items())
        for key, (sem, val) in items[:-1]:
            e["h"].wait_ge(sem, val)
            e["seen"][key] = val
        ins = fn()
        if items:
            key, (sem, val) = items[-1]
            ins._wait_ge(sem, val)
            e["seen"][key] = val
        return ins

    @staticmethod
    def _rec(d, key, sem, val):
        old = d.get(key)
        if old is None or old[1] < val:
            d[key] = (sem, val)

    def op(self, ek, fn, reads=(), writes=(), sig=True):
        if self.dead:
            return None
        e = self.eng[ek]
        need = self._collect(e, reads, writes)
        ins = self._emit(e, need, fn)
        if sig:
            e["cnt"] += 1
            ins.then_inc(e["sem"], 1)
            val = e["cnt"]
        else:
            val = e["cnt"] + 1
        for b in reads:
            self._rec(b.r, ek, e["sem"], val)
        for b in writes:
            self._rec(b.w, ek, e["sem"], val)
        return ins

    def dma(self, qk, out, in_, owner, reads=(), writes=()):
        if self.dead:
            return None
        e = self.eng[qk]
        if owner.dsem is None:
            owner.dsem = self.es.enter_context(self.nc.semaphore("dsem_%d" % owner.id))
            self.owners.append(owner)
        need = self._collect(e, reads, writes)
        ins = self._emit(e, need, lambda: e["h"].dma_start(out=out, in_=in_))
        owner.dcnt += 16
        ins.then_inc(owner.dsem, 16)
        key = "dma%d" % owner.id
        for b in reads:
            self._rec(b.r, key, owner.dsem, owner.dcnt)
        for b in writes:
            self._rec(b.w, key, owner.dsem, owner.dcnt)
        return ins

    def barrier(self, engines=("pe", "act", "dve", "sp")):
        if self.dead:
            return
        targets = [(k, e["sem"], e["cnt"]) for k, e in self.eng.items() if e["cnt"] > 0]
        targets += [("dma%d" % b.id, b.dsem, b.dcnt) for b in self.owners if b.dcnt > 0]
        for ek in engines:
            e = self.eng[ek]
            for key, sem, val in targets:
                if key == ek and ek == "pe":
                    continue
                if e["seen"].get(key, 0) >= val:
                    continue
                e["h"].wait_ge(sem, val)
                e["seen"][key] = val

    def wait_all(self, ek, bufs):
        e = self.eng[ek]
        need = self._collect(e, [], bufs)
        for key, (sem, val) in need.items():
            e["h"].wait_ge(sem, val)
            e["seen"][key] = val


def _consts():
    c = {}
    c["ident"] = np.eye(128, dtype=np.float32)
    s = np.arange(128)[:, None]
    t = np.arange(128)[None, :]
    tri = np.zeros((128, 4, 128), np.float32)
    tri[:, 0, :] = (s <= t) / 16.0
    tri[:, 1, :] = (s >= t) / 16.0
    tri[:, 2, :] = (s > t) / 16.0
    tri[:, 3, :] = (s < t) / 16.0
    c["tri"] = tri
    m2 = np.zeros((128, 256), np.float32)
    m2[:, 0:128] = (s <= t)
    m2[:, 128:256] = (s >= t)
    c["mask2"] = m2
    cs = np.zeros((128, 256), np.float32)
    a = np.arange(64)
    ang = 2 * np.pi * ((a[:, None] * a[None, :]) % 64) / 64.0
    for g in range(2):
        cs[g * 64:(g + 1) * 64, g * 64:(g + 1) * 64] = np.cos(ang)
        cs[g * 64:(g + 1) * 64, 128 + g * 64:128 + (g + 1) * 64] = np.sin(ang)
    c["cs64"] = cs
    for L in (256, 1024):
        tt = np.arange(L, dtype=np.int64)
        ang = 2 * np.pi * ((tt[:, None] * tt[None, :]) % L).astype(np.float64) / L
        sc = 1.0 / np.sqrt(64.0 * L)
        c["dftc%d" % L] = (np.cos(ang) * sc).astype(np.float32)
        c["dfts%d" % L] = (-np.sin(ang) * sc).astype(np.float32)
    return c


PP_GMIX, PP_GFFN, PP_ADAB, PP_CONV, PP_NG, PP_N = 0, 16, 32, 128, 140, 146


def build(debug_dump=None, stop_at=None):
    nc = bass.Bass("TRN2", target_bir_lowering=False)
    dbg = {}

    def din(name, shape):
        return nc.dram_tensor(name, list(shape), F32, kind="ExternalInput").ap()

    def dout(name, shape):
        return nc.dram_tensor(name, list(shape), F32, kind="ExternalOutput").ap()

    xin_d = [din("xp", [512, D]), din("xs", [1024, D])]
    s0_d = [din("s0f", [4, 128, 192]), din("s0b", [4, 128, 192])]
    condT_d = din("condT", [128, 8, 2])
    pp_d = din("pp", [128, PP_N])
    w2aug_d = din("w2aug", [2, 64, 512])
    gfin_d = din("gfin", [1, D])
    gws_d = din("gws", [4, 128, 128])
    gbs_d = din("gbs", [1, 512])
    ada_w = din("ada_w", [2, D, 6 * D])
    ffn_w1 = din("ffn_w1", [2, D, 4 * D])
    ffn_w2 = din("ffn_w2", [2, 4 * D, D])
    ev_w_in = din("ev_w_in", [D, 2848])
    ev_w_out = din("ev_w_out", [D, D])
    od_w_in = din("od_w_in", [D, 2560])
    od_w_out = din("od_w_out", [D, D])
    ident_d = din("ident", [128, 128])
    tri_d = din("tri", [128, 4, 128])
    mask2_d = din("mask2", [128, 256])
    cs64_d = din("cs64", [128, 256])
    dft256_d = [din("dftc256", [256, 256]), din("dfts256", [256, 256])]
    dft1024_d = [din("dftc1024", [1024, 1024]), din("dfts1024", [1024, 1024])]

    y_d = [dout("yp", [512, D]), dout("ys", [1024, D])]
    ns_d = [dout("nsf", [2, 4, 128, 192]), dout("nsb", [2, 4, 128, 192])]

    with contextlib.ExitStack() as es:
        T = Tracker(nc, es)

        def stage(name):
            if stop_at is not None and name == stop_at and not T.dead:
                T.barrier(engines=("pe", "act", "dve", "sp", "pool"))
                T.dead = True

        def sb(name, shape, dt=F32):
            return es.enter_context(nc.sbuf_tensor("sb_" + name, list(shape), dt))

        TMAX = 1024
        xT = sb("xT", [128, 8, TMAX])
        hT = sb("hT", [128, 8, TMAX], BF16)
        slots = [sb("slot%d" % i, [128, 8192], BF16) for i in range(NSLOT)]
        slot_b = [Buf("slot%d" % i) for i in range(NSLOT)]
        U = sb("U", [128, 32768], BF16)
        H2 = sb("H2", [128, 6144], BF16)
        enbx = [sb("enb%d" % i, [128, 256]) for i in range(2)]
        enbx_b = [Buf("enb0"), Buf("enb1")]
        hs1_b = dict(q=Buf("q1"), k=Buf("k1"), kt=Buf("kt1"), vt=Buf("vt1"), sg=Buf("sg1"))
        ident = sb("ident", [128, 128]); identb = sb("identb", [128, 128], BF16)
        tri = sb("tri", [128, 4, 128], BF16)
        mask2 = sb("mask2", [128, 256], BF16)
        cs64 = sb("cs64", [128, 256], BF16)
        onesb = sb("onesb", [128, 128], BF16)
        onesrow = sb("onesrow", [1, 128])
        cst = sb("cst", [128, 4])
        pp = sb("pp", [128, PP_N])
        w2aug = sb("w2aug", [64, 2, 512], BF16)
        gfinb = sb("gfinb", [128, D])
        gws = sb("gws", [128, 4, 128])
        wsT = sb("wsT", [128, 4, 128], BF16)
        gbs = sb("gbs", [1, 512])
        condT = sb("condT", [128, 8, 2])
        scT = sb("scT", [128, 8, 2], BF16)
        modT = [sb("modT%d" % l, [128, 48, 2]) for l in range(2)]
        gscT = [sb("gscT%d" % l, [128, 2, 8, 2]) for l in range(2)]
        rstd = sb("rstd", [128, 512]); lnr = sb("lnr", [128, 512])
        sqc = [sb("sqc%d" % i, [128, 512], BF16) for i in range(2)]
        ntm = [sb("ntm%d" % i, [128, 512]) for i in range(2)]
        sqc_b = [Buf("sqc0"), Buf("sqc1")]; ntm_b = [Buf("ntm0"), Buf("ntm1")]
        relu_t = [sb("relu%d" % i, [128, 512], BF16) for i in range(2)]
        relu_b = [Buf("relu0"), Buf("relu1")]
        rtok = sb("rtok", [128, 8]);
        CB = Buf("consts")
        U_all = []

        def ub(name):
            b = Buf(name)
            for o in U_all:
                for dct in (o.w, o.r):
                    for key, (sem, val) in dct.items():
                        Tracker._rec(b.r, key, sem, val)
            U_all.append(b)
            return b
        xT_b = [[Buf("xT%d_%d" % (c, m)) for m in range(2)] for c in range(8)]
        hT_b = [[Buf("hT%d_%d" % (c, m)) for m in range(2)] for c in range(8)]
        rstd_b = Buf("rstd"); lnr_b = Buf("lnr"); rtok_b = Buf("rtok")
        mod_b = Buf("mod")

        psf = [es.enter_context(nc.psum_tensor("psf%d" % i, [128, 512], F32)) for i in range(7)]
        psf_b = [Buf("psf%d" % i, excl=True) for i in range(7)]
        psb = es.enter_context(nc.psum_tensor("psb", [128, 1024], BF16))
        psb_b = Buf("psb", excl=True)
        ps_rr = [0]

        def PS():
            i = ps_rr[0] % 7
            ps_rr[0] += 1
            return psf[i], psf_b[i]

        def mmg(out, pairs, R, W):
            n = len(pairs)
            for i, (l, r) in enumerate(pairs):
                T.op("pe", lambda l=l, r=r, i=i: nc.tensor.matmul(out, l, r, start=(i == 0), stop=(i == n - 1)),
                     reads=R, writes=W, sig=(i == n - 1))

        def act(out, in_, func, R, W, **kw):
            T.op("act", lambda: nc.scalar.activation(out=out, in_=in_, func=func, **kw), reads=R, writes=W)

        def tt(out, a, b, op, R, W, eng="dve"):
            h = nc.vector if eng == "dve" else nc.gpsimd
            T.op(eng, lambda: h.tensor_tensor(out=out, in0=a, in1=b, op=op), reads=R, writes=W)

        def stt(out, a, scalar, b, op0, op1, R, W):
            T.op("dve", lambda: nc.vector.scalar_tensor_tensor(out=out, in0=a, scalar=scalar, in1=b, op0=op0, op1=op1),
                 reads=R, writes=W)

        def ts(out, a, s1, s2, op0, op1, R, W):
            if s2 is None:
                T.op("dve", lambda: nc.vector.tensor_scalar(out=out, in0=a, scalar1=s1, scalar2=None, op0=op0), reads=R, writes=W)
            else:
                T.op("dve", lambda: nc.vector.tensor_scalar(out=out, in0=a, scalar1=s1, scalar2=s2, op0=op0, op1=op1), reads=R, writes=W)

        def cp(out, in_, R, W, eng="dve"):
            if eng == "act":
                T.op("act", lambda: nc.scalar.copy(out=out, in_=in_), reads=R, writes=W)
            else:
                T.op("dve", lambda: nc.vector.tensor_copy(out, in_), reads=R, writes=W)

        def dump(name, ap, R):
            if debug_dump is None or name not in debug_dump:
                return
            d = dout("dbg_" + name, list(ap.shape))
            b = Buf("dbg")
            T.dma("sp" if ap.dtype == F32 else "pool", d, ap, b, reads=R)
            dbg[name] = b

        def cload(dst, src):
            T.dma("sp", dst, src, CB, writes=[CB])

        cload(ident[:], ident_d)
        cload(pp[:], pp_d)
        cload(gfinb[:], gfin_d.to_broadcast([128, D]))
        cload(gws[:], gws_d.rearrange("g p q -> p g q"))
        cload(gbs[:], gbs_d)
        cload(condT[:], condT_d)
        CBP = Buf("consts_pool")
        T.dma("pool", mask2[:], mask2_d, CBP, writes=[CB])
        T.dma("pool", cs64[:], cs64_d, CBP, writes=[CB])
        T.dma("pool", tri[:], tri_d, CBP, writes=[CB])
        T.dma("pool", w2aug[:], w2aug_d.rearrange("d k n -> k d n"), CBP, writes=[CB])
        T.op("dve", lambda: nc.vector.memset(onesb[:], 1.0 / 1024), writes=[CB])
        T.op("dve", lambda: nc.vector.memset(onesrow[:], 1.0), writes=[CB])
        T.op("dve", lambda: nc.vector.memset(cst[:, 0:1], EPS), writes=[CB])
        T.op("dve", lambda: nc.vector.memset(cst[:, 1:2], 1.0), writes=[CB])
        cp(identb[:], ident[:], [CB], [CB])
        act(scT[:], condT[:], AF.Silu, [CB], [CB])
        for g in range(4):
            ps, pb = PS()
            T.op("pe", lambda g=g, ps=ps: nc.tensor.transpose(ps[:, 0:128], gws[:, g, :], ident[:]), reads=[CB], writes=[pb])
            cp(wsT[:, g, :], ps[:, 0:128], [pb], [CB])

        stage("consts")
        def v3(sl, k, n):
            return sl[:, 0:k * n].rearrange("p (k n) -> p k n", k=k)

        def rows(w2d):
            return w2d.rearrange("(k p) n -> p k n", p=128)

        WSEQ = []

        def wadd(name, fn):
            WSEQ.append((name, fn))

        def wada(l, b):
            wadd("ada%d_%d" % (l, b), lambda sl, l=l, b=b: [(v3(sl, 8, 1024), rows(ada_w[l, :, b * 1024:(b + 1) * 1024]))])

        PH_ORDER = (1, 0)
        PH_ADA = PH_ORDER[0]
        for ph in PH_ORDER:
            for l in range(2):
                if ph == PH_ADA:
                    wada(l, 0); wada(l, 1)
                if l == 0:
                    wadd("evC", lambda sl: [(v3(sl, 8, 288), rows(ev_w_in[:, 2560:2848]))])
                    for h in range(4):
                        def f(sl, h=h):
                            v = v3(sl, 8, 640)
                            return [(v[:, :, 0:128], rows(ev_w_in[:, h * 128:(h + 1) * 128])),
                                    (v[:, :, 128:256], rows(ev_w_in[:, 512 + h * 128:512 + (h + 1) * 128])),
                                    (v[:, :, 256:448], rows(ev_w_in[:, 1024 + h * 192:1024 + (h + 1) * 192])),
                                    (v[:, :, 448:640], rows(ev_w_in[:, 1792 + h * 192:1792 + (h + 1) * 192]))]
                        wadd("evH%d" % h, f)
                    if ph == 0:
                        def f(sl):
                            v = sl[:, 0:1024].rearrange("p (j k n) -> p j k n", j=2, k=2)
                            return [(v[:, :, 0, :], rows(dft256_d[0])), (v[:, :, 1, :], rows(dft256_d[1]))]
                        wadd("dft", f)
                    else:
                        wadd("dftc", lambda sl: [(v3(sl, 8, 1024), rows(dft1024_d[0]))])
                        wadd("dfts", lambda sl: [(v3(sl, 8, 1024), rows(dft1024_d[1]))])
                    if ph == PH_ADA:
                        wada(l, 2)
                    wadd("evO", lambda sl: [(v3(sl, 8, 1024), rows(ev_w_out))])
                else:
                    wadd("odG", lambda sl: [(v3(sl, 8, 1024), rows(od_w_in[:, 1024:2048]))])
                    wadd("odH", lambda sl: [(v3(sl, 8, 512), rows(od_w_in[:, 2048:2560]))])
                    wadd("odUV", lambda sl: [(v3(sl, 8, 1024), rows(od_w_in[:, 0:1024]))])
                    if ph == PH_ADA:
                        wada(l, 2)
                    wadd("odO", lambda sl: [(v3(sl, 8, 1024), rows(od_w_out))])
                if ph == PH_ADA:
                    wada(l, 3); wada(l, 4)
                for b in range(4):
                    wadd("w1_%d" % b, lambda sl, l=l, b=b: [(v3(sl, 8, 1024), rows(ffn_w1[l, :, b * 1024:(b + 1) * 1024]))])
                if ph == PH_ADA:
                    wada(l, 5)
                for b in range(4):
                    wadd("w2_%d" % b, lambda sl, l=l, b=b: [(v3(sl, 32, 256), rows(ffn_w2[l, :, b * 256:(b + 1) * 256]))])

        wstate = dict(issued=0, nxt=0)

        def wget(name, ahead=NSLOT - 1):
            i = wstate["nxt"]
            assert WSEQ[i][0] == name, (WSEQ[i][0], name)
            while wstate["issued"] < min(i + ahead + 1, len(WSEQ)):
                j = wstate["issued"]
                s = j % NSLOT
                for (o, src) in WSEQ[j][1](slots[s]):
                    T.dma("pool", o, src, slot_b[s], writes=[slot_b[s]])
                wstate["issued"] += 1
            wstate["nxt"] += 1
            return slots[i % NSLOT], slot_b[i % NSLOT]

        def ada_block(l, b):
            sl, slb = wget("ada%d_%d" % (l, b))
            w = v3(sl, 8, 1024)
            ps, pb = PS()
            for oc in range(8):
                mmg(ps[:, oc * 2:oc * 2 + 2],
                    [(w[:, kc, oc * 128:(oc + 1) * 128], scT[:, kc, :]) for kc in range(8)],
                    [slb, CB], [pb])
            a0 = PP_ADAB + l * 48 + b * 8
            tt(modT[l][:, b * 8:(b + 1) * 8, :], ps[:, 0:16].rearrange("p (c k) -> p c k", k=2),
               pp[:, a0:a0 + 8].unsqueeze(2).to_broadcast([128, 8, 2]), ALU.add, [pb, CB], [mod_b])
            if b in (1, 4):
                wh = 0 if b == 1 else 1
                g0 = (PP_GMIX if wh == 0 else PP_GFFN) + l * 8
                stt(gscT[l][:, wh, :, :], modT[l][:, b * 8:(b + 1) * 8, :], 1.0,
                    pp[:, g0:g0 + 8].unsqueeze(2).to_broadcast([128, 8, 2]), ALU.add, ALU.mult, [mod_b, CB], [mod_b])

        stage("ada")

        def modcol(l, ch, ci):
            return modT[l][:, ch, ci:ci + 1]

        for ph in PH_ORDER:
            ci = ph
            TT = 512 if ph == 0 else 1024
            NT = TT // 128
            NM = TT // 512
            seqs = [(0, 2), (2, 2)] if ph == 0 else [(0, 8)]
            msl = lambda m: slice(m * 512, (m + 1) * 512)
            tsl = lambda j: slice(j * 128, (j + 1) * 128)

            xin = [U[:, 0:2048].bitcast(F32), U[:, 2048:4096].bitcast(F32)]
            xin_b = [ub("xin0"), ub("xin1")]
            for j in range(NT):
                xi, xb = xin[j % 2], xin_b[j % 2]
                T.dma("sp", xi, xin_d[ph][j * 128:(j + 1) * 128, :], xb, writes=[xb])
                for hf in range(2):
                    ps, pb = PS()
                    for k in range(4):
                        c = hf * 4 + k
                        T.op("pe", lambda ps=ps, k=k, c=c, xi=xi: nc.tensor.transpose(ps[:, k * 128:(k + 1) * 128], xi[:, c * 128:(c + 1) * 128], ident[:]),
                             reads=[xb, CB], writes=[pb], sig=(k == 3))
                    cp(xT[:, hf * 4:hf * 4 + 4, tsl(j)], ps[:].rearrange("p (k t) -> p k t", k=4), [pb],
                       [xT_b[c][j // 4] for c in range(hf * 4, hf * 4 + 4)], eng=("act" if hf == 0 else "dve"))

            def norm(l, wh):
                shc = 0 if wh == 0 else 24
                if ph == PH_ADA:
                    if wh == 0:
                        ada_block(l, 0); ada_block(l, 1)
                    else:
                        ada_block(l, 3); ada_block(l, 4)
                for m in range(NM):
                    ps, pb = PS()
                    for c in range(8):
                        if c % 2 == 0:
                            act(sqc[0][:], xT[:, c, msl(m)], AF.Square, [xT_b[c][m]], [sqc_b[0]])
                        else:
                            tt(sqc[1][:], xT[:, c, msl(m)], xT[:, c, msl(m)], ALU.mult, [xT_b[c][m]], [sqc_b[1]])
                        T.op("pe", lambda c=c, ps=ps: nc.tensor.matmul(ps[:], onesb[:], sqc[c % 2][:], start=(c == 0), stop=(c == 7)),
                             reads=[sqc_b[c % 2], CB], writes=[pb], sig=True)
                    act(lnr[:], ps[:], AF.Ln, [pb, CB], [lnr_b], bias=cst[:, 0:1])
                    act(rstd[:], lnr[:], AF.Exp, [lnr_b], [rstd_b], scale=-0.5)
                    for c in range(8):
                        stt(ntm[c % 2][:], xT[:, c, msl(m)], gscT[l][:, wh, c, ci:ci + 1], rstd[:], ALU.mult, ALU.mult,
                            [xT_b[c][m], rstd_b, mod_b], [ntm_b[c % 2]])
                        act(hT[:, c, msl(m)], ntm[c % 2][:], AF.Identity, [ntm_b[c % 2], mod_b], [hT_b[c][m]],
                            bias=modcol(l, shc + c, ci))

            nb = {}

            def outproj_resid(name, l, gch):
                if ph == PH_ADA:
                    ada_block(l, 2)
                sl, slb = wget(name)
                w = v3(sl, 8, 1024)
                for oc in range(8):
                    for m in range(NM):
                        ps, pb = PS()
                        mmg(ps[:], [(w[:, kc, oc * 128:(oc + 1) * 128], hT[:, kc, msl(m)]) for kc in range(8)],
                            [slb] + [hT_b[kc][m] for kc in range(8)], [pb])
                        stt(xT[:, oc, msl(m)], ps[:], modcol(l, gch + oc, ci), xT[:, oc, msl(m)], ALU.mult, ALU.add,
                            [pb, mod_b], [xT_b[oc][m]])

            def ffn(l):
                aT = U[:, 0:32 * TT].rearrange("p (c t) -> p c t", c=32)
                aT_b = [[ub("aT") for m in range(2)] for c in range(32)]
                for b in range(4):
                    sl, slb = wget("w1_%d" % b)
                    w = v3(sl, 8, 1024)
                    order = [(oc, m) for m in range(NM) for oc in range(8)] if b == 0 else [(oc, m) for oc in range(8) for m in range(NM)]
                    for (oc, m) in order:
                        ch = b * 8 + oc
                        if True:
                            ps, pb = PS()
                            mmg(ps[:], [(w[:, kc, oc * 128:(oc + 1) * 128], hT[:, kc, msl(m)]) for kc in range(8)],
                                [slb] + [hT_b[kc][m] for kc in range(8)], [pb])
                            r = relu_t[(ch * NM + m) % 2]; r_b = relu_b[(ch * NM + m) % 2]
                            act(r[:], ps[:], AF.Relu, [pb], [r_b])
                            tt(aT[:, ch, msl(m)], r[:], r[:], ALU.mult, [r_b], [aT_b[ch][m]])
                if ph == PH_ADA:
                    ada_block(l, 5)
                for b in range(4):
                    sl, slb = wget("w2_%d" % b)
                    w = v3(sl, 32, 256)
                    for o2 in range(2):
                        oc = b * 2 + o2
                        for m in range(NM):
                            ps, pb = PS()
                            mmg(ps[:], [(w[:, kc, o2 * 128:(o2 + 1) * 128], aT[:, kc, msl(m)]) for kc in range(32)],
                                [slb] + [aT_b[kc][m] for kc in range(32)], [pb])
                            stt(xT[:, oc, msl(m)], ps[:], modcol(l, 40 + oc, ci), xT[:, oc, msl(m)], ALU.mult, ALU.add,
                                [pb, mod_b], [xT_b[oc][m]])


            stage("load%d" % ph)
            dump("x_in_%d" % ph, xT[:, :, 0:TT], [xT_b[c][m] for c in range(8) for m in range(NM)])
            norm(0, 0)
            dump("h_in_%d" % ph, hT[:, :, 0:TT], [hT_b[c][m] for c in range(8) for m in range(NM)])
            stage("norm00_%d" % ph)
            o = [0]

            def ualloc(nelem_bf16):
                a = o[0]
                o[0] += nelem_bf16
                assert o[0] <= 32768, o[0]
                return a

            a_lr = ualloc(TT); lrT = U[:, a_lr:a_lr + TT]
            a_fin = ualloc(2 * TT); finT = U[:, a_fin:a_fin + 2 * TT].rearrange("p (c t) -> p c t", c=2)
            a_xcs = ualloc(NT * 512); xcs = U[:, a_xcs:a_xcs + NT * 512].rearrange("p (j c n) -> p j c n", j=NT, c=2)
            a_og = ualloc(NT * 768); og = U[:, a_og:a_og + NT * 768].rearrange("p (j n) -> p j n", j=NT)
            a_q = ualloc(TT); qT = U[:, a_q:a_q + TT]
            a_k = ualloc(TT); kT = U[:, a_k:a_k + TT]
            a_kt = ualloc(NT * 128); ktok = U[:, a_kt:a_kt + NT * 128].rearrange("p (j n) -> p j n", j=NT)
            a_vt = ualloc(NT * 192); vtok = U[:, a_vt:a_vt + NT * 192].rearrange("p (j n) -> p j n", j=NT)
            a_sg = ualloc(NT * 192); sgtok = U[:, a_sg:a_sg + NT * 192].rearrange("p (j n) -> p j n", j=NT)
            qe = []; ke = []; sprev = []; Sst = []
            for d in range(2):
                a = ualloc(TT); qe.append(U[:, a:a + TT])
                a = ualloc(TT); ke.append(U[:, a:a + TT])
                a = ualloc(NT * 192); sprev.append(U[:, a:a + NT * 192].rearrange("p (j n) -> p j n", j=NT))
                a = ualloc(384); Sst.append(U[:, a:a + 384].bitcast(F32))
            tr = []
            for par in range(2):
                dct = {}
                for nm, ne, dtp in (("e1", 512, F32), ("sp", 256, BF16), ("ekd", 512, F32), ("kd", 256, BF16),
                                    ("eb", 512, F32), ("att", 256, BF16), ("junk", 192, BF16)):
                    a = ualloc(ne)
                    v = U[:, a:a + ne]
                    dct[nm] = v.bitcast(F32) if dtp == F32 else v
                    dct[nm + "_b"] = ub(nm)
                tr.append(dct)
            a_ss = ualloc(NT * 8); ssqa = U[:, a_ss:a_ss + NT * 8].bitcast(F32)
            a_ls = ualloc(NT * 8); lnsa = U[:, a_ls:a_ls + NT * 8].bitcast(F32)
            a_rs = ualloc(NT * 8); rsa = U[:, a_rs:a_rs + NT * 8].bitcast(F32)
            ssqa_b = ub("ssqa"); lnsa_b = ub("lnsa"); rsa_b = ub("rsa")
            lr_b = ub("lr"); fin_b = ub("fin"); xcs_b = [ub("xcs") for _ in range(NT)]; og_b = [ub("og") for _ in range(NT)]
            q_b = ub("q"); k_b = ub("k"); kt_b = ub("kt"); vt_b = ub("vt"); sg_b = ub("sg")
            qe_b = [ub("qef"), ub("qeb")]; ke_b = [ub("kef"), ub("keb")]
            sprev_b = [ub("spf"), ub("spb")]; S_b = [ub("Sf"), ub("Sb")]
            hall = lambda m: [hT_b[kc][m] for kc in range(8)]

            T.op("dve", lambda: nc.vector.memset(lrT[32:64, :], 0.0), writes=[lr_b])
            T.op("dve", lambda: nc.vector.memset(lrT[32:33, :], 1.0), writes=[lr_b])

            sl, slb = wget("evC")
            w = v3(sl, 8, 288)
            for m in range(NM):
                ps, pb = PS()
                mmg(ps[0:32, :], [(w[:, kc, 0:32], hT[:, kc, msl(m)]) for kc in range(8)], [slb] + hall(m), [pb])
                cp(lrT[0:32, msl(m)], ps[0:32, :], [pb], [lr_b], eng="act")
                for c2 in range(2):
                    ps, pb = PS()
                    mmg(ps[:], [(w[:, kc, 32 + c2 * 128:32 + (c2 + 1) * 128], hT[:, kc, msl(m)]) for kc in range(8)],
                        [slb] + hall(m), [pb])
                    cp(finT[:, c2, msl(m)], ps[:], [pb], [fin_b], eng="dve")
            for j in range(NT):
                for c2 in range(2):
                    ps, pb = PS()
                    mmg(ps[:, 0:256], [(finT[:, c2, tsl(j)], cs64[:])], [fin_b, CB], [pb])
                    cp(xcs[:, j, c2, :], ps[:, 0:256], [pb], [xcs_b[j]], eng=("act" if c2 == 0 else "dve"))

            stage("evC_%d" % ph)
            o2 = [0]

            def h2alloc(n):
                a = o2[0]; o2[0] += n
                assert o2[0] <= 6144
                return H2[:, a:a + n]

            hs = [dict(qT=qT, kT=kT, ktok=ktok, vtok=vtok, sgtok=sgtok, q=q_b, k=k_b, kt=kt_b, vt=vt_b, sg=sg_b),
                  dict(qT=h2alloc(TT), kT=h2alloc(TT),
                       ktok=h2alloc(NT * 128).rearrange("p (j n) -> p j n", j=NT),
                       vtok=h2alloc(NT * 192).rearrange("p (j n) -> p j n", j=NT),
                       sgtok=h2alloc(NT * 192).rearrange("p (j n) -> p j n", j=NT), **hs1_b)]

            def proj_groups(h):
                H = hs[h % 2]
                st = {}

                def getw():
                    if "w" not in st:
                        sl, slb = wget("evH%d" % h)
                        st["w"] = v3(sl, 8, 640); st["b"] = slb
                    return st["w"], st["b"]

                def gq(m):
                    w, slb = getw()
                    ps, pb = PS()
                    mmg(ps[:], [(w[:, kc, 0:128], hT[:, kc, msl(m)]) for kc in range(8)], [slb] + hall(m), [pb])
                    cp(H["qT"][:, msl(m)], ps[:], [pb], [H["q"]], eng="act")

                def gk(m):
                    w, slb = getw()
                    ps, pb = PS()
                    mmg(ps[:], [(w[:, kc, 128:256], hT[:, kc, msl(m)]) for kc in range(8)], [slb] + hall(m), [pb])
                    cp(H["kT"][:, msl(m)], ps[:], [pb], [H["k"]], eng="dve")

                def gt(j):
                    w, slb = getw()
                    ps, pb = PS()
                    mmg(ps[:], [(hT[:, kc, tsl(j)], w[:, kc, 128:640]) for kc in range(8)], [slb] + hall(j // 4), [pb])
                    cp(H["ktok"][:, j, :], ps[:, 0:128], [pb], [H["kt"]], eng="dve")
                    cp(H["vtok"][:, j, :], ps[:, 128:320], [pb], [H["vt"]], eng="dve")
                    cp(H["sgtok"][:, j, :], ps[:, 320:512], [pb], [H["sg"]], eng="act")

                def gsilu():
                    act(H["sgtok"], H["sgtok"], AF.Silu, [H["sg"]], [H["sg"]])

                gl = []
                for m in range(NM):
                    gl.append(lambda m=m: gq(m)); gl.append(lambda m=m: gk(m))
                for j in range(NT):
                    gl.append(lambda j=j: gt(j))
                gl.append(gsilu)
                return gl

            for g in proj_groups(0):
                g()
            for h in range(4):
                H = hs[h % 2]
                pend = proj_groups(h + 1) if h < 3 else []
                steps = []
                for (t0, n) in seqs:
                    fw = list(range(t0, t0 + n)); bw = fw[::-1]
                    for a, b in zip(fw, bw):
                        steps.append((t0 // 2, 0, a, a == fw[0], a == fw[-1]))
                        steps.append((t0 // 2, 1, b, b == bw[0], b == bw[-1]))

                def stA(p):
                    X = tr[p % 2]
                    ps, pb = PS()
                    for i in range(2):
                        si, d, j, first, lastt = steps[2 * p + i]
                        mmg(ps[:, i * 128:(i + 1) * 128], [(lrT[0:64, tsl(j)], w2aug[0:64, d, h * 128:(h + 1) * 128])], [lr_b, CB], [pb])
                    act(X["e1"], ps[:, 0:256], AF.Exp, [pb], [X["e1_b"]], scale=-1.0)
                    act(X["sp"], X["e1"], AF.Ln, [X["e1_b"], CB], [X["sp_b"]], bias=cst[:, 1:2])

                def stB(p):
                    X = tr[p % 2]
                    ps, pb = PS()
                    for i in range(2):
                        si, d, j, first, lastt = steps[2 * p + i]
                        spi = X["sp"][:, i * 128:(i + 1) * 128]
                        mmg(ps[:, i * 128:(i + 1) * 128], [(tri[:, 2 + d, :], spi)], [X["sp_b"], CB], [pb])
                        mmg(ps[:, 256 + i * 128:256 + (i + 1) * 128], [(spi, tri[:, d, :])], [X["sp_b"], CB], [pb])
                    act(X["ekd"], ps[:, 0:256], AF.Exp, [pb], [X["ekd_b"]], scale=-1.0)
                    act(X["eb"], ps[:, 256:512], AF.Exp, [pb], [X["eb_b"]], scale=-1.0)
                    act(enbx[p % 2][:], ps[:, 256:512], AF.Exp, [pb], [enbx_b[p % 2]])

                def stC1(k):
                    si, d, j, first, lastt = steps[k]
                    X = tr[(k // 2) % 2]
                    i = k % 2
                    cs_ = slice(i * 128, (i + 1) * 128)
                    tt(X["kd"][:, cs_], H["ktok"][:, j, :], X["ekd"][:, cs_], ALU.mult, [H["kt"], X["ekd_b"]], [X["kd_b"]])
                    stt(qe[d][:, tsl(j)], H["qT"][:, tsl(j)], 128.0 ** -0.5, X["eb"][:, cs_], ALU.mult, ALU.mult,
                        [H["q"], X["eb_b"]], [qe_b[d]])
                    tt(ke[d][:, tsl(j)], H["kT"][:, tsl(j)], enbx[(k // 2) % 2][:, cs_], ALU.mult, [H["k"], enbx_b[(k // 2) % 2]], [ke_b[d]])

                def stC2(k):
                    si, d, j, first, lastt = steps[k]
                    X = tr[(k // 2) % 2]
                    i = k % 2
                    cs_ = slice(i * 128, (i + 1) * 128)
                    last = i * 128 + (127 if d == 0 else 0)
                    if first:
                        if ph == 0:
                            T.op("dve", lambda d=d: nc.vector.memset(Sst[d], 0.0), writes=[S_b[d]])
                        else:
                            T.dma("sp", Sst[d], s0_d[d][h], S_b[d], writes=[S_b[d]])
                    ps, pb = PS()
                    mmg(ps[:, 0:192], [(X["kd"][:, cs_], H["vtok"][:, j, :])], [X["kd_b"], H["vt"]], [pb])
                    cp(sprev[d][:, j, :], Sst[d], [S_b[d]], [sprev_b[d]], eng="act")
                    stt(Sst[d], Sst[d], X["eb"][:, last:last + 1], ps[:, 0:192], ALU.mult, ALU.add,
                        [S_b[d], X["eb_b"], pb], [S_b[d]])
                    if lastt and ph == 0:
                        T.dma("sp", ns_d[d][si, h], Sst[d], S_b[d], reads=[S_b[d]])

                def popg():
                    if len(pend) > 1:
                        pend.pop(0)()

                npair = len(steps) // 2
                for i in range(npair + 2):
                    if 0 <= i - 2 < npair:
                        stC1(2 * (i - 2)); stC1(2 * (i - 2) + 1)
                    if i < npair:
                        stA(i)
                    popg()
                    if 0 <= i - 1 < npair:
                        stB(i - 1)
                    popg()
                    if 0 <= i - 2 < npair:
                        stC2(2 * (i - 2)); stC2(2 * (i - 2) + 1)
                stage("h%d_p1_%d" % (h, ph))

                def p2a(j):
                    X = tr[j % 2]
                    ps, pb = PS()
                    mmg(ps[:, 0:128], [(ke[0][:, tsl(j)], qe[0][:, tsl(j)])], [ke_b[0], qe_b[0]], [pb])
                    mmg(ps[:, 128:256], [(ke[1][:, tsl(j)], qe[1][:, tsl(j)])], [ke_b[1], qe_b[1]], [pb])
                    tt(X["att"], ps[:, 0:256], mask2[:], ALU.mult, [pb, CB], [X["att_b"]])

                def p2b(j):
                    X = tr[j % 2]
                    ps, pb = PS()
                    mmg(ps[:, 0:192], [(X["att"][:, 0:128], H["vtok"][:, j, :]), (X["att"][:, 128:256], H["vtok"][:, j, :]),
                                       (qe[0][:, tsl(j)], sprev[0][:, j, :]), (qe[1][:, tsl(j)], sprev[1][:, j, :])],
                        [X["att_b"], H["vt"], qe_b[0], qe_b[1], sprev_b[0], sprev_b[1]], [pb])
                    act(X["junk"], ps[:, 0:192], AF.Square, [pb], [X["junk_b"], ssqa_b], scale=192.0 ** -0.5,
                        accum_out=ssqa[:, j * 4 + h:j * 4 + h + 1])
                    tt(og[:, j, h * 192:(h + 1) * 192], ps[:, 0:192], H["sgtok"][:, j, :], ALU.mult, [pb, H["sg"]], [og_b[j]])

                p2a(0)
                for j in range(NT):
                    if j + 1 < NT:
                        p2a(j + 1)
                    p2b(j)
                    if len(pend) > 1:
                        pend.pop(0)()
                while pend:
                    pend.pop(0)()

            act(lnsa, ssqa, AF.Ln, [ssqa_b, CB], [lnsa_b], bias=cst[:, 0:1])
            act(rsa, lnsa, AF.Exp, [lnsa_b], [rsa_b], scale=-0.5)
            og3 = U[:, a_og:a_og + NT * 768].rearrange("p (g e) -> p g e", e=192)
            tt(og3, og3, rsa.unsqueeze(2).to_broadcast([128, NT * 4, 192]), ALU.mult, [rsa_b] + og_b, og_b)
            stage("gla_%d" % ph)
            for j in range(NT):
                for c in range(6):
                    T.op("pe", lambda j=j, c=c: nc.tensor.transpose(psb[:, c * 128:(c + 1) * 128], og[:, j, c * 128:(c + 1) * 128], identb[:]),
                         reads=[og_b[j], CB], writes=[psb_b], sig=(c == 5))
                tt(hT[:, 0:6, tsl(j)], psb[:, 0:768].rearrange("p (c t) -> p c t", c=6),
                   pp[:, PP_NG:PP_NG + 6].unsqueeze(2).to_broadcast([128, 6, 128]), ALU.mult,
                   [psb_b, CB], [hT_b[c][j // 4] for c in range(6)])
            if ph == 0:
                sl, slb = wget("dft")
                dv = sl[:, 0:1024].rearrange("p (j k n) -> p j k n", j=2, k=2)
                for (t0, n) in seqs:
                    for c2 in range(2):
                        ps, pb = PS()
                        pairs = []
                        for jj in range(2):
                            pairs.append((xcs[:, t0 + jj, c2, 0:128], dv[:, jj, 0, :]))
                            pairs.append((xcs[:, t0 + jj, c2, 128:256], dv[:, jj, 1, :]))
                        mmg(ps[:, 0:256], pairs, [slb, xcs_b[t0], xcs_b[t0 + 1]], [pb])
                        cp(hT[:, 6 + c2, t0 * 128:t0 * 128 + 256], ps[:, 0:256], [pb], [hT_b[6 + c2][0]],
                           eng=("act" if c2 == 0 else "dve"))
            else:
                slc, slcb = wget("dftc")
                sls, slsb = wget("dfts", ahead=NSLOT - 2)
                dc = v3(slc, 8, 1024); ds = v3(sls, 8, 1024)
                for c2 in range(2):
                    for m in range(2):
                        ps, pb = PS()
                        pairs = []
                        for jj in range(8):
                            pairs.append((xcs[:, jj, c2, 0:128], dc[:, jj, msl(m)]))
                            pairs.append((xcs[:, jj, c2, 128:256], ds[:, jj, msl(m)]))
                        mmg(ps[:], pairs, [slcb, slsb] + xcs_b, [pb])
                        cp(hT[:, 6 + c2, msl(m)], ps[:], [pb], [hT_b[6 + c2][m]], eng=("act" if c2 == 0 else "dve"))
            stage("mix0_%d" % ph)
            dump("mix0_%d" % ph, hT[:, :, 0:TT], [hT_b[c][m] for c in range(8) for m in range(NM)])
            outproj_resid("evO", 0, 16)
            dump("x_l0mix_%d" % ph, xT[:, :, 0:TT], [xT_b[c][m] for c in range(8) for m in range(NM)])
            stage("out0_%d" % ph)
            norm(0, 1)
            ffn(0)
            stage("ffn0_%d" % ph)
            dump("x_l0_%d" % ph, xT[:, :, 0:TT], [xT_b[c][m] for c in range(8) for m in range(NM)])

            norm(1, 0)
            uT = U[:, 0:4 * TT].rearrange("p (c t) -> p c t", c=4)
            v1 = U[:, 4096:4096 + NT * 512].rearrange("p (j n) -> p j n", j=NT)
            gbT = U[:, 8192:8192 + 4 * TT].rearrange("p (c t) -> p c t", c=4)
            gcT = U[:, 12288:12288 + 4 * TT].rearrange("p (c t) -> p c t", c=4)
            zT = U[:, 24576:32768].bitcast(F32).rearrange("p (c t) -> p c t", c=4)
            u_b = ub("uT"); v1_b = ub("v1"); gb_b = ub("gb"); gc_b = ub("gc"); z_b = ub("z")
            accs = [U[:, 16384 + i * 2048:16384 + (i + 1) * 2048].bitcast(F32) for i in range(4)]
            acc_b = [ub("acc%d" % i) for i in range(4)]
            sl, slb = wget("odG")
            w = v3(sl, 8, 1024)
            for m in range(NM):
                for oc in range(8):
                    ps, pb = PS()
                    mmg(ps[:], [(w[:, kc, oc * 128:(oc + 1) * 128], hT[:, kc, msl(m)]) for kc in range(8)], [slb] + hall(m), [pb])
                    if oc < 4:
                        cp(gbT[:, oc, msl(m)], ps[:], [pb], [gb_b], eng="act")
                    else:
                        cp(gcT[:, oc - 4, msl(m)], ps[:], [pb], [gc_b], eng="dve")
            sl, slb = wget("odH")
            w = v3(sl, 8, 512)
            for oc in range(4):
                for m in range(NM):
                    ps, pb = PS()
                    mmg(ps[:], [(w[:, kc, oc * 128:(oc + 1) * 128], hT[:, kc, msl(m)]) for kc in range(8)], [slb] + hall(m), [pb])
                    tt(zT[:, oc, msl(m)], ps[:], gcT[:, oc, msl(m)], ALU.mult, [pb, gc_b], [z_b])
            RW = 256 if ph == 0 else 64
            for oc in range(4):
                acc = accs[oc][:, 0:TT]; ab = acc_b[oc]
                z = zT[:, oc, 0:TT]
                cw = lambda k: pp[:, PP_CONV + oc * 3 + k:PP_CONV + oc * 3 + k + 1]
                ts(acc, z, cw(1), None, ALU.mult, None, [z_b, CB], [ab])
                a3 = acc.rearrange("p (r w) -> p r w", w=RW)
                z3 = z.rearrange("p (r w) -> p r w", w=RW)
                stt(a3[:, :, 1:RW], z3[:, :, 0:RW - 1], cw(0), a3[:, :, 1:RW], ALU.mult, ALU.add, [z_b, CB, ab], [ab])
                stt(a3[:, :, 0:RW - 1], z3[:, :, 1:RW], cw(2), a3[:, :, 0:RW - 1], ALU.mult, ALU.add, [z_b, CB, ab], [ab])
            sl, slb = wget("odUV")
            w = v3(sl, 8, 1024)
            for oc in range(4):
                for m in range(NM):
                    ps, pb = PS()
                    mmg(ps[:], [(w[:, kc, oc * 128:(oc + 1) * 128], hT[:, kc, msl(m)]) for kc in range(8)], [slb] + hall(m), [pb])
                    act(uT[:, oc, msl(m)], ps[:], AF.Gelu_apprx_tanh, [pb], [u_b])
            for j in range(NT):
                ps, pb = PS()
                mmg(ps[:], [(hT[:, kc, tsl(j)], w[:, kc, 512:1024]) for kc in range(8)], [slb] + hall(j // 4), [pb])
                act(v1[:, j, :], ps[:], AF.Gelu_apprx_tanh, [pb], [v1_b])
            for j in range(NT):
                ps, pb = PS()
                for g in range(4):
                    T.op("pe", lambda g=g, ps=ps, j=j: nc.tensor.matmul(ps[:, g * 128:(g + 1) * 128], v1[:, j, g * 128:(g + 1) * 128], wsT[:, g, :], start=True, stop=False),
                         reads=[v1_b, CB], writes=[pb], sig=False)
                    T.op("pe", lambda g=g, ps=ps: nc.tensor.matmul(ps[:, g * 128:(g + 1) * 128], onesrow[0:1, :], gbs[0:1, g * 128:(g + 1) * 128], start=False, stop=True),
                         reads=[CB], writes=[pb], sig=(g == 3))
                tt(hT[:, 0:4, tsl(j)], ps[:].rearrange("p (g t) -> p g t", g=4), uT[:, :, tsl(j)], ALU.mult,
                   [pb, u_b], [hT_b[c][j // 4] for c in range(4)])
            for oc in range(4):
                tt(hT[:, 4 + oc, 0:TT], accs[oc][:, 0:TT], gbT[:, oc, 0:TT], ALU.mult, [acc_b[oc], gb_b], [hT_b[4 + oc][m] for m in range(NM)])
            stage("mix1_%d" % ph)
            outproj_resid("odO", 1, 16)
            dump("x_l1mix_%d" % ph, xT[:, :, 0:TT], [xT_b[c][m] for c in range(8) for m in range(NM)])
            norm(1, 1)
            ffn(1)
            stage("ffn1_%d" % ph)

            yst = [U[:, 0:2048].bitcast(F32), U[:, 2048:4096].bitcast(F32)]
            yst_b = [ub("yst0"), ub("yst1")]; nb["sq"] = ub("sq")
            for m in range(NM):
                sq = U[:, 20480:24576].rearrange("p (c t) -> p c t", c=8)
                xr = [xT_b[c][m] for c in range(8)]
                act(sq, xT[:, :, msl(m)], AF.Square, xr, [nb["sq"]])
                psr, pbr = PS()
                for jj in range(4):
                    mmg(psr[:, jj:jj + 1], [(sq[:, c, jj * 128:(jj + 1) * 128], onesb[:, 0:1]) for c in range(8)], [nb["sq"], CB], [pbr])
                act(rtok[:, 0:4], psr[:, 0:4], AF.Ln, [pbr, CB], [rtok_b], bias=cst[:, 0:1])
                act(rtok[:, 4:8], rtok[:, 0:4], AF.Exp, [rtok_b], [rtok_b], scale=-0.5)
                for jj in range(4):
                    j = m * 4 + jj
                    ys, yb = yst[j % 2], yst_b[j % 2]
                    for hf in range(2):
                        ps, pb = PS()
                        for k in range(4):
                            c = hf * 4 + k
                            T.op("pe", lambda ps=ps, k=k, c=c, j=j: nc.tensor.transpose(ps[:, k * 128:(k + 1) * 128], xT[:, c, j * 128:(j + 1) * 128], ident[:]),
                                 reads=[xT_b[c][m], CB], writes=[pb], sig=(k == 3))
                        stt(ys[:, hf * 512:(hf + 1) * 512], ps[:], rtok[:, 4 + jj:5 + jj], gfinb[:, hf * 512:(hf + 1) * 512], ALU.mult, ALU.mult,
                            [pb, rtok_b, CB], [yb])
                    T.dma("sp", y_d[ph][j * 128:(j + 1) * 128, :], ys, yb, reads=[yb])

        T.barrier(engines=("pe", "act", "dve", "sp", "pool"))
    return nc


_CACHE = {}


def _get_nc():
    if "nc" not in _CACHE:
        _CACHE["nc"] = build()
        _CACHE["consts"] = _consts()
    return _CACHE["nc"], _CACHE["consts"]


def make_in_maps(inputs, consts):
    f = lambda a: np.ascontiguousarray(np.asarray(a, dtype=np.float32))
    x_prompt = f(inputs["x_prompt"]); x_sample = f(inputs["x_sample"])
    sf = f(inputs["state_gla_fwd"]); sbw = f(inputs["state_gla_bwd"])
    c = f(inputs["c"]); c_ctx = f(inputs["c_ctx"])
    pp = np.zeros((128, PP_N), np.float32)
    for l in range(2):
        pp[:, PP_GMIX + l * 8:PP_GMIX + (l + 1) * 8] = f(inputs["norm_mix_g"])[l].reshape(8, 128).T
        pp[:, PP_GFFN + l * 8:PP_GFFN + (l + 1) * 8] = f(inputs["norm_ffn_g"])[l].reshape(8, 128).T
        pp[:, PP_ADAB + l * 48:PP_ADAB + (l + 1) * 48] = f(inputs["ada_b"])[l].reshape(48, 128).T
    pp[:, PP_CONV:PP_CONV + 12] = f(inputs["conv_w"])[0].T.reshape(4, 128, 3).transpose(1, 0, 2).reshape(128, 12)
    pp[:, PP_NG:PP_NG + 6] = np.tile(f(inputs["gla_norm_g"])[0], 4).reshape(6, 128).T
    w2aug = np.zeros((2, 64, 512), np.float32)
    w2aug[0, 0:16] = f(inputs["gla_w2_f"])[0]
    w2aug[0, 32] = f(inputs["gla_b2_f"])[0]
    w2aug[1, 16:32] = f(inputs["gla_w2_b"])[0]
    w2aug[1, 32] = f(inputs["gla_b2_b"])[0]
    shared = dict(
        pp=pp, w2aug=w2aug, gfin=f(inputs["final_norm_g"]).reshape(1, D),
        gws=f(inputs["gmlp_ws"])[0], gbs=f(inputs["gmlp_b"])[0].reshape(1, 512),
        ada_w=f(inputs["ada_w"]), ffn_w1=f(inputs["ffn_w1"]), ffn_w2=f(inputs["ffn_w2"]),
        ev_w_in=f(inputs["ev_w_in"])[0], ev_w_out=f(inputs["ev_w_out"])[0],
        od_w_in=f(inputs["od_w_in"])[0], od_w_out=f(inputs["od_w_out"])[0],
        ident=consts["ident"], tri=consts["tri"], mask2=consts["mask2"], cs64=consts["cs64"],
        dftc256=consts["dftc256"], dfts256=consts["dfts256"],
        dftc1024=consts["dftc1024"], dfts1024=consts["dfts1024"],
    )
    maps = []
    for i in range(NCORES):
        cond = np.stack([c_ctx, c[i]], axis=0)
        m = dict(shared)
        m["xp"] = x_prompt[2 * i:2 * i + 2].reshape(512, D)
        m["xs"] = x_sample[i]
        m["s0f"] = sf[i, 0]
        m["s0b"] = sbw[i, 0]
        m["condT"] = np.ascontiguousarray(cond.reshape(2, 8, 128).transpose(2, 1, 0))
        maps.append(m)
    return maps


def kernel(**inputs):
    nc, consts = _get_nc()
    maps = make_in_maps(inputs, consts)
    res = run_bass_kernel_spmd(nc, maps, core_ids=list(range(NCORES)))
    r = res.results
    y_prompt = np.concatenate([r[i]["yp"].reshape(2, 256, D) for i in range(NCORES)], axis=0)
    y_sample = np.stack([r[i]["ys"] for i in range(NCORES)], axis=0)
    nsf = np.concatenate([r[i]["nsf"].reshape(2, 1, 4, 128, 192) for i in range(NCORES)], axis=0)
    nsb = np.concatenate([r[i]["nsb"].reshape(2, 1, 4, 128, 192) for i in range(NCORES)], axis=0)
    return (y_prompt.astype(np.float32), y_sample.astype(np.float32),
            nsf.astype(np.float32), nsb.astype(np.float32))
```

```python
import contextlib
import numpy as np
import concourse.bass as bass
import concourse.mybir as mybir
from concourse.bass_utils import run_bass_kernel_spmd

F32 = mybir.dt.float32
BF16 = mybir.dt.bfloat16
AF = mybir.ActivationFunctionType
ALU = mybir.AluOpType

NCORES = 8
D = 1024
NSLOT = 3
EPS = 1e-6


class Buf:
    _n = 0

    def __init__(self, name, excl=False):
        Buf._n += 1
        self.id = Buf._n
        self.name = name
        self.excl = excl
        self.w = {}
        self.r = {}
        self.dsem = None
        self.dcnt = 0


class Tracker:
    def __init__(self, nc, es):
        self.nc = nc
        self.es = es
        self.eng = {}
        for k, h in (("pe", nc.tensor), ("act", nc.scalar), ("dve", nc.vector),
                     ("pool", nc.gpsimd), ("sp", nc.sync)):
            sem = es.enter_context(nc.semaphore("sem_" + k))
            self.eng[k] = dict(h=h, sem=sem, cnt=0, seen={}, key=k)
        self.owners = []
        self.dead = False

    def _collect(self, e, reads, writes):
        need = {}

        def add(key, sem, val):
            if key == "pe" and e["key"] == "pe":
                return
            if e["seen"].get(key, 0) >= val:
                return
            if key not in need or need[key][1] < val:
                need[key] = (sem, val)

        for b in reads:
            for key, (sem, val) in b.w.items():
                add(key, sem, val)
            if b.excl:
                for key, (sem, val) in b.r.items():
                    if key != e["key"]:
                        add(key, sem, val)
        for b in writes:
            for key, (sem, val) in b.w.items():
                add(key, sem, val)
            for key, (sem, val) in b.r.items():
                add(key, sem, val)
        return need

    def _emit(self, e, need, fn):
        items = list(need.items())
        for key, (sem, val) in items[:-1]:
            e["h"].wait_ge(sem, val)
            e["seen"][key] = val
        ins = fn()
        if items:
            key, (sem, val) = items[-1]
            ins._wait_ge(sem, val)
            e["seen"][key] = val
        return ins

    @staticmethod
    def _rec(d, key, sem, val):
        old = d.get(key)
        if old is None or old[1] < val:
            d[key] = (sem, val)

    def op(self, ek, fn, reads=(), writes=(), sig=True):
        if self.dead:
            return None
        e = self.eng[ek]
        need = self._collect(e, reads, writes)
        ins = self._emit(e, need, fn)
        if sig:
            e["cnt"] += 1
            ins.then_inc(e["sem"], 1)
            val = e["cnt"]
        else:
            val = e["cnt"] + 1
        for b in reads:
            self._rec(b.r, ek, e["sem"], val)
        for b in writes:
            self._rec(b.w, ek, e["sem"], val)
        return ins

    def dma(self, qk, out, in_, owner, reads=(), writes=()):
        if self.dead:
            return None
        e = self.eng[qk]
        if owner.dsem is None:
            owner.dsem = self.es.enter_context(self.nc.semaphore("dsem_%d" % owner.id))
            self.owners.append(owner)
        need = self._collect(e, reads, writes)
        ins = self._emit(e, need, lambda: e["h"].dma_start(out=out, in_=in_))
        owner.dcnt += 16
        ins.then_inc(owner.dsem, 16)
        key = "dma%d" % owner.id
        for b in reads:
            self._rec(b.r, key, owner.dsem, owner.dcnt)
        for b in writes:
            self._rec(b.w, key, owner.dsem, owner.dcnt)
        return ins

    def barrier(self, engines=("pe", "act", "dve", "sp")):
        if self.dead:
            return
        targets = [(k, e["sem"], e["cnt"]) for k, e in self.eng.items() if e["cnt"] > 0]
        targets += [("dma%d" % b.id, b.dsem, b.dcnt) for b in self.owners if b.dcnt > 0]
        for ek in engines:
            e = self.eng[ek]
            for key, sem, val in targets:
                if key == ek and ek == "pe":
                    continue
                if e["seen"].get(key, 0) >= val:
                    continue
                e["h"].wait_ge(sem, val)
                e["seen"][key] = val

    def wait_all(self, ek, bufs):
        e = self.eng[ek]
        need = self._collect(e, [], bufs)
        for key, (sem, val) in need.items():
            e["h"].wait_ge(sem, val)
            e["seen"][key] = val


def _consts():
    c = {}
    c["ident"] = np.eye(128, dtype=np.float32)
    s = np.arange(128)[:, None]
    t = np.arange(128)[None, :]
    tri = np.zeros((128, 4, 128), np.float32)
    tri[:, 0, :] = (s <= t) / 16.0
    tri[:, 1, :] = (s >= t) / 16.0
    tri[:, 2, :] = (s > t) / 16.0
    tri[:, 3, :] = (s < t) / 16.0
    c["tri"] = tri
    m2 = np.zeros((128, 256), np.float32)
    m2[:, 0:128] = (s <= t)
    m2[:, 128:256] = (s >= t)
    c["mask2"] = m2
    cs = np.zeros((128, 256), np.float32)
    a = np.arange(64)
    ang = 2 * np.pi * ((a[:, None] * a[None, :]) % 64) / 64.0
    for g in range(2):
        cs[g * 64:(g + 1) * 64, g * 64:(g + 1) * 64] = np.cos(ang)
        cs[g * 64:(g + 1) * 64, 128 + g * 64:128 + (g + 1) * 64] = np.sin(ang)
    c["cs64"] = cs
    for L in (256, 1024):
        tt = np.arange(L, dtype=np.int64)
        ang = 2 * np.pi * ((tt[:, None] * tt[None, :]) % L).astype(np.float64) / L
        sc = 1.0 / np.sqrt(64.0 * L)
        c["dftc%d" % L] = (np.cos(ang) * sc).astype(np.float32)
        c["dfts%d" % L] = (-np.sin(ang) * sc).astype(np.float32)
    return c


PP_GMIX, PP_GFFN, PP_ADAB, PP_CONV, PP_NG, PP_N = 0, 16, 32, 128, 140, 146


def build(debug_dump=None, stop_at=None):
    nc = bass.Bass("TRN2", target_bir_lowering=False)
    dbg = {}

    def din(name, shape):
        return nc.dram_tensor(name, list(shape), F32, kind="ExternalInput").ap()

    def dout(name, shape):
        return nc.dram_tensor(name, list(shape), F32, kind="ExternalOutput").ap()

    xin_d = [din("xp", [512, D]), din("xs", [1024, D])]
    s0_d = [din("s0f", [4, 128, 192]), din("s0b", [4, 128, 192])]
    condT_d = din("condT", [128, 8, 2])
    pp_d = din("pp", [128, PP_N])
    w2aug_d = din("w2aug", [2, 64, 512])
    gfin_d = din("gfin", [1, D])
    gws_d = din("gws", [4, 128, 128])
    gbs_d = din("gbs", [1, 512])
    ada_w = din("ada_w", [2, D, 6 * D])
    ffn_w1 = din("ffn_w1", [2, D, 4 * D])
    ffn_w2 = din("ffn_w2", [2, 4 * D, D])
    ev_w_in = din("ev_w_in", [D, 2848])
    ev_w_out = din("ev_w_out", [D, D])
    od_w_in = din("od_w_in", [D, 2560])
    od_w_out = din("od_w_out", [D, D])
    ident_d = din("ident", [128, 128])
    tri_d = din("tri", [128, 4, 128])
    mask2_d = din("mask2", [128, 256])
    cs64_d = din("cs64", [128, 256])
    dft256_d = [din("dftc256", [256, 256]), din("dfts256", [256, 256])]
    dft1024_d = [din("dftc1024", [1024, 1024]), din("dfts1024", [1024, 1024])]

    y_d = [dout("yp", [512, D]), dout("ys", [1024, D])]
    ns_d = [dout("nsf", [2, 4, 128, 192]), dout("nsb", [2, 4, 128, 192])]

    with contextlib.ExitStack() as es:
        T = Tracker(nc, es)

        def stage(name):
            if stop_at is not None and name == stop_at and not T.dead:
                T.barrier(engines=("pe", "act", "dve", "sp", "pool"))
                T.dead = True

        def sb(name, shape, dt=F32):
            return es.enter_context(nc.sbuf_tensor("sb_" + name, list(shape), dt))

        TMAX = 1024
        xT = sb("xT", [128, 8, TMAX])
        hT = sb("hT", [128, 8, TMAX], BF16)
        slots = [sb("slot%d" % i, [128, 8192], BF16) for i in range(NSLOT)]
        slot_b = [Buf("slot%d" % i) for i in range(NSLOT)]
        U = sb("U", [128, 32768], BF16)
        H2 = sb("H2", [128, 6144], BF16)
        enbx = [sb("enb%d" % i, [128, 256]) for i in range(2)]
        enbx_b = [Buf("enb0"), Buf("enb1")]
        hs1_b = dict(q=Buf("q1"), k=Buf("k1"), kt=Buf("kt1"), vt=Buf("vt1"), sg=Buf("sg1"))
        ident = sb("ident", [128, 128]); identb = sb("identb", [128, 128], BF16)
        tri = sb("tri", [128, 4, 128], BF16)
        mask2 = sb("mask2", [128, 256], BF16)
        cs64 = sb("cs64", [128, 256], BF16)
        onesb = sb("onesb", [128, 128], BF16)
        onesrow = sb("onesrow", [1, 128])
        cst = sb("cst", [128, 4])
        pp = sb("pp", [128, PP_N])
        w2aug = sb("w2aug", [64, 2, 512], BF16)
        gfinb = sb("gfinb", [128, D])
        gws = sb("gws", [128, 4, 128])
        wsT = sb("wsT", [128, 4, 128], BF16)
        gbs = sb("gbs", [1, 512])
        condT = sb("condT", [128, 8, 2])
        scT = sb("scT", [128, 8, 2], BF16)
        modT = [sb("modT%d" % l, [128, 48, 2]) for l in range(2)]
        gscT = [sb("gscT%d" % l, [128, 2, 8, 2]) for l in range(2)]
        rstd = sb("rstd", [128, 512]); lnr = sb("lnr", [128, 512])
        sqc = [sb("sqc%d" % i, [128, 512], BF16) for i in range(2)]
        ntm = [sb("ntm%d" % i, [128, 512]) for i in range(2)]
        sqc_b = [Buf("sqc0"), Buf("sqc1")]; ntm_b = [Buf("ntm0"), Buf("ntm1")]
        relu_t = [sb("relu%d" % i, [128, 512], BF16) for i in range(2)]
        relu_b = [Buf("relu0"), Buf("relu1")]
        rtok = sb("rtok", [128, 8]);
        CB = Buf("consts")
        U_all = []

        def ub(name):
            b = Buf(name)
            for o in U_all:
                for dct in (o.w, o.r):
                    for key, (sem, val) in dct.items():
                        Tracker._rec(b.r, key, sem, val)
            U_all.append(b)
            return b
        xT_b = [[Buf("xT%d_%d" % (c, m)) for m in range(2)] for c in range(8)]
        hT_b = [[Buf("hT%d_%d" % (c, m)) for m in range(2)] for c in range(8)]
        rstd_b = Buf("rstd"); lnr_b = Buf("lnr"); rtok_b = Buf("rtok")
        mod_b = Buf("mod")

        psf = [es.enter_context(nc.psum_tensor("psf%d" % i, [128, 512], F32)) for i in range(7)]
        psf_b = [Buf("psf%d" % i, excl=True) for i in range(7)]
        psb = es.enter_context(nc.psum_tensor("psb", [128, 1024], BF16))
        psb_b = Buf("psb", excl=True)
        ps_rr = [0]

        def PS():
            i = ps_rr[0] % 7
            ps_rr[0] += 1
            return psf[i], psf_b[i]

        def mmg(out, pairs, R, W):
            n = len(pairs)
            for i, (l, r) in enumerate(pairs):
                T.op("pe", lambda l=l, r=r, i=i: nc.tensor.matmul(out, l, r, start=(i == 0), stop=(i == n - 1)),
                     reads=R, writes=W, sig=(i == n - 1))

        def act(out, in_, func, R, W, **kw):
            T.op("act", lambda: nc.scalar.activation(out=out, in_=in_, func=func, **kw), reads=R, writes=W)

        def tt(out, a, b, op, R, W, eng="dve"):
            h = nc.vector if eng == "dve" else nc.gpsimd
            T.op(eng, lambda: h.tensor_tensor(out=out, in0=a, in1=b, op=op), reads=R, writes=W)

        def stt(out, a, scalar, b, op0, op1, R, W):
            T.op("dve", lambda: nc.vector.scalar_tensor_tensor(out=out, in0=a, scalar=scalar, in1=b, op0=op0, op1=op1),
                 reads=R, writes=W)

        def ts(out, a, s1, s2, op0, op1, R, W):
            if s2 is None:
                T.op("dve", lambda: nc.vector.tensor_scalar(out=out, in0=a, scalar1=s1, scalar2=None, op0=op0), reads=R, writes=W)
            else:
                T.op("dve", lambda: nc.vector.tensor_scalar(out=out, in0=a, scalar1=s1, scalar2=s2, op0=op0, op1=op1), reads=R, writes=W)

        def cp(out, in_, R, W, eng="dve"):
            if eng == "act":
                T.op("act", lambda: nc.scalar.copy(out=out, in_=in_), reads=R, writes=W)
            else:
                T.op("dve", lambda: nc.vector.tensor_copy(out, in_), reads=R, writes=W)

        def dump(name, ap, R):
            if debug_dump is None or name not in debug_dump:
                return
            d = dout("dbg_" + name, list(ap.shape))
            b = Buf("dbg")
            T.dma("sp" if ap.dtype == F32 else "pool", d, ap, b, reads=R)
            dbg[name] = b

        def cload(dst, src):
            T.dma("sp", dst, src, CB, writes=[CB])

        cload(ident[:], ident_d)
        cload(pp[:], pp_d)
        cload(gfinb[:], gfin_d.to_broadcast([128, D]))
        cload(gws[:], gws_d.rearrange("g p q -> p g q"))
        cload(gbs[:], gbs_d)
        cload(condT[:], condT_d)
        CBP = Buf("consts_pool")
        T.dma("pool", mask2[:], mask2_d, CBP, writes=[CB])
        T.dma("pool", cs64[:], cs64_d, CBP, writes=[CB])
        T.dma("pool", tri[:], tri_d, CBP, writes=[CB])
        T.dma("pool", w2aug[:], w2aug_d.rearrange("d k n -> k d n"), CBP, writes=[CB])
        T.op("dve", lambda: nc.vector.memset(onesb[:], 1.0 / 1024), writes=[CB])
        T.op("dve", lambda: nc.vector.memset(onesrow[:], 1.0), writes=[CB])
        T.op("dve", lambda: nc.vector.memset(cst[:, 0:1], EPS), writes=[CB])
        T.op("dve", lambda: nc.vector.memset(cst[:, 1:2], 1.0), writes=[CB])
        cp(identb[:], ident[:], [CB], [CB])
        act(scT[:], condT[:], AF.Silu, [CB], [CB])
        for g in range(4):
            ps, pb = PS()
            T.op("pe", lambda g=g, ps=ps: nc.tensor.transpose(ps[:, 0:128], gws[:, g, :], ident[:]), reads=[CB], writes=[pb])
            cp(wsT[:, g, :], ps[:, 0:128], [pb], [CB])

        stage("consts")
        def v3(sl, k, n):
            return sl[:, 0:k * n].rearrange("p (k n) -> p k n", k=k)

        def rows(w2d):
            return w2d.rearrange("(k p) n -> p k n", p=128)

        WSEQ = []

        def wadd(name, fn):
            WSEQ.append((name, fn))

        def wada(l, b):
            wadd("ada%d_%d" % (l, b), lambda sl, l=l, b=b: [(v3(sl, 8, 1024), rows(ada_w[l, :, b * 1024:(b + 1) * 1024]))])

        PH_ORDER = (1, 0)
        PH_ADA = PH_ORDER[0]
        for ph in PH_ORDER:
            A = (ph == PH_ADA)
            for l in range(2):
                if l == 0:
                    if A:
                        wada(0, 0); wada(0, 1)
                    wadd("evC", lambda sl: [(v3(sl, 8, 288), rows(ev_w_in[:, 2560:2848]))])
                    for h in range(4):
                        def f(sl, h=h):
                            v = v3(sl, 8, 640)
                            return [(v[:, :, 0:128], rows(ev_w_in[:, h * 128:(h + 1) * 128])),
                                    (v[:, :, 128:256], rows(ev_w_in[:, 512 + h * 128:512 + (h + 1) * 128])),
                                    (v[:, :, 256:448], rows(ev_w_in[:, 1024 + h * 192:1024 + (h + 1) * 192])),
                                    (v[:, :, 448:640], rows(ev_w_in[:, 1792 + h * 192:1792 + (h + 1) * 192]))]
                        wadd("evH%d" % h, f)
                        if A:
                            wada(0, 2 + h)
                    if ph == 0:
                        def f(sl):
                            v = sl[:, 0:1024].rearrange("p (j k n) -> p j k n", j=2, k=2)
                            return [(v[:, :, 0, :], rows(dft256_d[0])), (v[:, :, 1, :], rows(dft256_d[1]))]
                        wadd("dft", f)
                    else:
                        wadd("dftc", lambda sl: [(v3(sl, 8, 1024), rows(dft1024_d[0]))])
                        wadd("dfts", lambda sl: [(v3(sl, 8, 1024), rows(dft1024_d[1]))])
                    wadd("evO", lambda sl: [(v3(sl, 8, 1024), rows(ev_w_out))])
                else:
                    wadd("odG", lambda sl: [(v3(sl, 8, 1024), rows(od_w_in[:, 1024:2048]))])
                    wadd("odH", lambda sl: [(v3(sl, 8, 512), rows(od_w_in[:, 2048:2560]))])
                    wadd("odUV", lambda sl: [(v3(sl, 8, 1024), rows(od_w_in[:, 0:1024]))])
                    wadd("odO", lambda sl: [(v3(sl, 8, 1024), rows(od_w_out))])
                for b in range(4):
                    wadd("w1_%d" % b, lambda sl, l=l, b=b: [(v3(sl, 8, 1024), rows(ffn_w1[l, :, b * 1024:(b + 1) * 1024]))])
                    if A and l == 0 and b < 2:
                        wada(1, b)
                    if A and l == 1 and b == 0:
                        wada(1, 5)
                for b in range(4):
                    wadd("w2_%d" % b, lambda sl, l=l, b=b: [(v3(sl, 32, 256), rows(ffn_w2[l, :, b * 256:(b + 1) * 256]))])
                    if A and l == 0 and b < 3:
                        wada(1, 2 + b)

        wstate = dict(issued=0, nxt=0)

        def wget(name, ahead=NSLOT - 1):
            i = wstate["nxt"]
            assert WSEQ[i][0] == name, (WSEQ[i][0], name)
            while wstate["issued"] < min(i + ahead + 1, len(WSEQ)):
                j = wstate["issued"]
                s = j % NSLOT
                for (o, src) in WSEQ[j][1](slots[s]):
                    T.dma("pool", o, src, slot_b[s], writes=[slot_b[s]])
                wstate["issued"] += 1
            wstate["nxt"] += 1
            return slots[i % NSLOT], slot_b[i % NSLOT]

        def ada_block(l, b):
            sl, slb = wget("ada%d_%d" % (l, b))
            w = v3(sl, 8, 1024)
            ps, pb = PS()
            for oc in range(8):
                mmg(ps[:, oc * 2:oc * 2 + 2],
                    [(w[:, kc, oc * 128:(oc + 1) * 128], scT[:, kc, :]) for kc in range(8)],
                    [slb, CB], [pb])
            a0 = PP_ADAB + l * 48 + b * 8
            tt(modT[l][:, b * 8:(b + 1) * 8, :], ps[:, 0:16].rearrange("p (c k) -> p c k", k=2),
               pp[:, a0:a0 + 8].unsqueeze(2).to_broadcast([128, 8, 2]), ALU.add, [pb, CB], [mod_b])
            if b in (1, 4):
                wh = 0 if b == 1 else 1
                g0 = (PP_GMIX if wh == 0 else PP_GFFN) + l * 8
                stt(gscT[l][:, wh, :, :], modT[l][:, b * 8:(b + 1) * 8, :], 1.0,
                    pp[:, g0:g0 + 8].unsqueeze(2).to_broadcast([128, 8, 2]), ALU.add, ALU.mult, [mod_b, CB], [mod_b])

        stage("ada")

        def modcol(l, ch, ci):
            return modT[l][:, ch, ci:ci + 1]

        for ph in PH_ORDER:
            ci = ph
            TT = 512 if ph == 0 else 1024
            NT = TT // 128
            NM = TT // 512
            seqs = [(0, 2), (2, 2)] if ph == 0 else [(0, 8)]
            msl = lambda m: slice(m * 512, (m + 1) * 512)
            tsl = lambda j: slice(j * 128, (j + 1) * 128)

            xin = [U[:, 0:2048].bitcast(F32), U[:, 2048:4096].bitcast(F32)]
            xin_b = [ub("xin0"), ub("xin1")]
            for j in range(NT):
                xi, xb = xin[j % 2], xin_b[j % 2]
                T.dma("sp", xi, xin_d[ph][j * 128:(j + 1) * 128, :], xb, writes=[xb])
                for hf in range(2):
                    ps, pb = PS()
                    for k in range(4):
                        c = hf * 4 + k
                        T.op("pe", lambda ps=ps, k=k, c=c, xi=xi: nc.tensor.transpose(ps[:, k * 128:(k + 1) * 128], xi[:, c * 128:(c + 1) * 128], ident[:]),
                             reads=[xb, CB], writes=[pb], sig=(k == 3))
                    cp(xT[:, hf * 4:hf * 4 + 4, tsl(j)], ps[:].rearrange("p (k t) -> p k t", k=4), [pb],
                       [xT_b[c][j // 4] for c in range(hf * 4, hf * 4 + 4)], eng=("act" if hf == 0 else "dve"))

            def norm(l, wh):
                shc = 0 if wh == 0 else 24
                if ph == PH_ADA and wh == 0 and l == 0:
                    ada_block(0, 0); ada_block(0, 1)
                for m in range(NM):
                    ps, pb = PS()
                    for c in range(8):
                        act(sqc[c % 2][:], xT[:, c, msl(m)], AF.Square, [xT_b[c][m]], [sqc_b[c % 2]])
                        T.op("pe", lambda c=c, ps=ps: nc.tensor.matmul(ps[:], onesb[:], sqc[c % 2][:], start=(c == 0), stop=(c == 7)),
                             reads=[sqc_b[c % 2], CB], writes=[pb], sig=True)
                    act(lnr[:], ps[:], AF.Ln, [pb, CB], [lnr_b], bias=cst[:, 0:1])
                    act(rstd[:], lnr[:], AF.Exp, [lnr_b], [rstd_b], scale=-0.5)
                    for c in range(8):
                        stt(ntm[c % 2][:], xT[:, c, msl(m)], gscT[l][:, wh, c, ci:ci + 1], rstd[:], ALU.mult, ALU.mult,
                            [xT_b[c][m], rstd_b, mod_b], [ntm_b[c % 2]])
                        act(hT[:, c, msl(m)], ntm[c % 2][:], AF.Identity, [ntm_b[c % 2], mod_b], [hT_b[c][m]],
                            bias=modcol(l, shc + c, ci))

            nb = {}

            def outproj_resid(name, l, gch):
                sl, slb = wget(name)
                w = v3(sl, 8, 1024)
                for oc in range(8):
                    for m in range(NM):
                        ps, pb = PS()
                        mmg(ps[:], [(w[:, kc, oc * 128:(oc + 1) * 128], hT[:, kc, msl(m)]) for kc in range(8)],
                            [slb] + [hT_b[kc][m] for kc in range(8)], [pb])
                        stt(xT[:, oc, msl(m)], ps[:], modcol(l, gch + oc, ci), xT[:, oc, msl(m)], ALU.mult, ALU.add,
                            [pb, mod_b], [xT_b[oc][m]])

            def ffn(l):
                aT = U[:, 0:32 * TT].rearrange("p (c t) -> p c t", c=32)
                aT_b = [[ub("aT") for m in range(2)] for c in range(32)]
                for b in range(4):
                    sl, slb = wget("w1_%d" % b)
                    w = v3(sl, 8, 1024)
                    order = [(oc, m) for m in range(NM) for oc in range(8)] if b == 0 else [(oc, m) for oc in range(8) for m in range(NM)]
                    for (oc, m) in order:
                        ch = b * 8 + oc
                        if True:
                            ps, pb = PS()
                            mmg(ps[:], [(w[:, kc, oc * 128:(oc + 1) * 128], hT[:, kc, msl(m)]) for kc in range(8)],
                                [slb] + [hT_b[kc][m] for kc in range(8)], [pb])
                            r = relu_t[(ch * NM + m) % 2]; r_b = relu_b[(ch * NM + m) % 2]
                            act(r[:], ps[:], AF.Relu, [pb], [r_b])
                            tt(aT[:, ch, msl(m)], r[:], r[:], ALU.mult, [r_b], [aT_b[ch][m]])
                    if ph == PH_ADA and l == 0 and b < 2:
                        ada_block(1, b)
                    if ph == PH_ADA and l == 1 and b == 0:
                        ada_block(1, 5)
                for b in range(4):
                    sl, slb = wget("w2_%d" % b)
                    w = v3(sl, 32, 256)
                    for o2 in range(2):
                        oc = b * 2 + o2
                        for m in range(NM):
                            ps, pb = PS()
                            mmg(ps[:], [(w[:, kc, o2 * 128:(o2 + 1) * 128], aT[:, kc, msl(m)]) for kc in range(32)],
                                [slb] + [aT_b[kc][m] for kc in range(32)], [pb])
                            stt(xT[:, oc, msl(m)], ps[:], modcol(l, 40 + oc, ci), xT[:, oc, msl(m)], ALU.mult, ALU.add,
                                [pb, mod_b], [xT_b[oc][m]])
                    if ph == PH_ADA and l == 0 and b < 3:
                        ada_block(1, 2 + b)


            stage("load%d" % ph)
            dump("x_in_%d" % ph, xT[:, :, 0:TT], [xT_b[c][m] for c in range(8) for m in range(NM)])
            norm(0, 0)
            dump("h_in_%d" % ph, hT[:, :, 0:TT], [hT_b[c][m] for c in range(8) for m in range(NM)])
            stage("norm00_%d" % ph)
            o = [0]

            def ualloc(nelem_bf16):
                a = o[0]
                o[0] += nelem_bf16
                assert o[0] <= 32768, o[0]
                return a

            a_lr = ualloc(TT); lrT = U[:, a_lr:a_lr + TT]
            a_fin = ualloc(2 * TT); finT = U[:, a_fin:a_fin + 2 * TT].rearrange("p (c t) -> p c t", c=2)
            a_xcs = ualloc(NT * 512); xcs = U[:, a_xcs:a_xcs + NT * 512].rearrange("p (j c n) -> p j c n", j=NT, c=2)
            a_og = ualloc(NT * 768); og = U[:, a_og:a_og + NT * 768].rearrange("p (j n) -> p j n", j=NT)
            a_q = ualloc(TT); qT = U[:, a_q:a_q + TT]
            a_k = ualloc(TT); kT = U[:, a_k:a_k + TT]
            a_kt = ualloc(NT * 128); ktok = U[:, a_kt:a_kt + NT * 128].rearrange("p (j n) -> p j n", j=NT)
            a_vt = ualloc(NT * 192); vtok = U[:, a_vt:a_vt + NT * 192].rearrange("p (j n) -> p j n", j=NT)
            a_sg = ualloc(NT * 192); sgtok = U[:, a_sg:a_sg + NT * 192].rearrange("p (j n) -> p j n", j=NT)
            qe = []; ke = []; sprev = []; Sst = []
            for d in range(2):
                a = ualloc(TT); qe.append(U[:, a:a + TT])
                a = ualloc(TT); ke.append(U[:, a:a + TT])
                a = ualloc(NT * 192); sprev.append(U[:, a:a + NT * 192].rearrange("p (j n) -> p j n", j=NT))
                a = ualloc(384); Sst.append(U[:, a:a + 384].bitcast(F32))
            tr = []
            for par in range(2):
                dct = {}
                for nm, ne, dtp in (("e1", 512, F32), ("sp", 256, BF16), ("ekd", 512, F32), ("kd", 256, BF16),
                                    ("eb", 512, F32), ("att", 256, BF16), ("junk", 192, BF16)):
                    a = ualloc(ne)
                    v = U[:, a:a + ne]
                    dct[nm] = v.bitcast(F32) if dtp == F32 else v
                    dct[nm + "_b"] = ub(nm)
                tr.append(dct)
            a_ss = ualloc(NT * 8); ssqa = U[:, a_ss:a_ss + NT * 8].bitcast(F32)
            a_ls = ualloc(NT * 8); lnsa = U[:, a_ls:a_ls + NT * 8].bitcast(F32)
            a_rs = ualloc(NT * 8); rsa = U[:, a_rs:a_rs + NT * 8].bitcast(F32)
            ssqa_b = ub("ssqa"); lnsa_b = ub("lnsa"); rsa_b = ub("rsa")
            lr_b = ub("lr"); fin_b = ub("fin"); xcs_b = [ub("xcs") for _ in range(NT)]; og_b = [ub("og") for _ in range(NT)]
            q_b = ub("q"); k_b = ub("k"); kt_b = ub("kt"); vt_b = ub("vt"); sg_b = ub("sg")
            qe_b = [ub("qef"), ub("qeb")]; ke_b = [ub("kef"), ub("keb")]
            sprev_b = [ub("spf"), ub("spb")]; S_b = [ub("Sf"), ub("Sb")]
            hall = lambda m: [hT_b[kc][m] for kc in range(8)]

            T.op("dve", lambda: nc.vector.memset(lrT[32:64, :], 0.0), writes=[lr_b])
            T.op("dve", lambda: nc.vector.memset(lrT[32:33, :], 1.0), writes=[lr_b])

            sl, slb = wget("evC")
            w = v3(sl, 8, 288)
            for m in range(NM):
                ps, pb = PS()
                mmg(ps[0:32, :], [(w[:, kc, 0:32], hT[:, kc, msl(m)]) for kc in range(8)], [slb] + hall(m), [pb])
                cp(lrT[0:32, msl(m)], ps[0:32, :], [pb], [lr_b], eng="act")
                for c2 in range(2):
                    ps, pb = PS()
                    mmg(ps[:], [(w[:, kc, 32 + c2 * 128:32 + (c2 + 1) * 128], hT[:, kc, msl(m)]) for kc in range(8)],
                        [slb] + hall(m), [pb])
                    cp(finT[:, c2, msl(m)], ps[:], [pb], [fin_b], eng="dve")
            for j in range(NT):
                for c2 in range(2):
                    ps, pb = PS()
                    mmg(ps[:, 0:256], [(finT[:, c2, tsl(j)], cs64[:])], [fin_b, CB], [pb])
                    cp(xcs[:, j, c2, :], ps[:, 0:256], [pb], [xcs_b[j]], eng=("act" if c2 == 0 else "dve"))

            stage("evC_%d" % ph)
            o2 = [0]

            def h2alloc(n):
                a = o2[0]; o2[0] += n
                assert o2[0] <= 6144
                return H2[:, a:a + n]

            hs = [dict(qT=qT, kT=kT, ktok=ktok, vtok=vtok, sgtok=sgtok, q=q_b, k=k_b, kt=kt_b, vt=vt_b, sg=sg_b),
                  dict(qT=h2alloc(TT), kT=h2alloc(TT),
                       ktok=h2alloc(NT * 128).rearrange("p (j n) -> p j n", j=NT),
                       vtok=h2alloc(NT * 192).rearrange("p (j n) -> p j n", j=NT),
                       sgtok=h2alloc(NT * 192).rearrange("p (j n) -> p j n", j=NT), **hs1_b)]

            def proj_groups(h):
                H = hs[h % 2]
                st = {}

                def getw():
                    if "w" not in st:
                        sl, slb = wget("evH%d" % h)
                        st["w"] = v3(sl, 8, 640); st["b"] = slb
                    return st["w"], st["b"]

                def gq(m):
                    w, slb = getw()
                    ps, pb = PS()
                    mmg(ps[:], [(w[:, kc, 0:128], hT[:, kc, msl(m)]) for kc in range(8)], [slb] + hall(m), [pb])
                    cp(H["qT"][:, msl(m)], ps[:], [pb], [H["q"]], eng="act")

                def gk(m):
                    w, slb = getw()
                    ps, pb = PS()
                    mmg(ps[:], [(w[:, kc, 128:256], hT[:, kc, msl(m)]) for kc in range(8)], [slb] + hall(m), [pb])
                    cp(H["kT"][:, msl(m)], ps[:], [pb], [H["k"]], eng="dve")

                def gt(j):
                    w, slb = getw()
                    ps, pb = PS()
                    mmg(ps[:], [(hT[:, kc, tsl(j)], w[:, kc, 128:640]) for kc in range(8)], [slb] + hall(j // 4), [pb])
                    cp(H["ktok"][:, j, :], ps[:, 0:128], [pb], [H["kt"]], eng="dve")
                    cp(H["vtok"][:, j, :], ps[:, 128:320], [pb], [H["vt"]], eng="dve")
                    cp(H["sgtok"][:, j, :], ps[:, 320:512], [pb], [H["sg"]], eng="act")

                def gsilu():
                    act(H["sgtok"], H["sgtok"], AF.Silu, [H["sg"]], [H["sg"]])

                gl = []
                for m in range(NM):
                    gl.append(lambda m=m: gq(m)); gl.append(lambda m=m: gk(m))
                for j in range(NT):
                    gl.append(lambda j=j: gt(j))
                gl.append(gsilu)
                return gl

            for g in proj_groups(0):
                g()
            for h in range(4):
                H = hs[h % 2]
                if ph == PH_ADA:
                    ada_block(0, 2 + h)
                pend = proj_groups(h + 1) if h < 3 else []
                steps = []
                for (t0, n) in seqs:
                    fw = list(range(t0, t0 + n)); bw = fw[::-1]
                    for a, b in zip(fw, bw):
                        steps.append((t0 // 2, 0, a, a == fw[0], a == fw[-1]))
                        steps.append((t0 // 2, 1, b, b == bw[0], b == bw[-1]))

                def stA(p):
                    X = tr[p % 2]
                    ps, pb = PS()
                    for i in range(2):
                        si, d, j, first, lastt = steps[2 * p + i]
                        mmg(ps[:, i * 128:(i + 1) * 128], [(lrT[0:64, tsl(j)], w2aug[0:64, d, h * 128:(h + 1) * 128])], [lr_b, CB], [pb])
                    act(X["e1"], ps[:, 0:256], AF.Exp, [pb], [X["e1_b"]], scale=-1.0)
                    act(X["sp"], X["e1"], AF.Ln, [X["e1_b"], CB], [X["sp_b"]], bias=cst[:, 1:2])

                def stB(p):
                    X = tr[p % 2]
                    ps, pb = PS()
                    for i in range(2):
                        si, d, j, first, lastt = steps[2 * p + i]
                        spi = X["sp"][:, i * 128:(i + 1) * 128]
                        mmg(ps[:, i * 128:(i + 1) * 128], [(tri[:, 2 + d, :], spi)], [X["sp_b"], CB], [pb])
                        mmg(ps[:, 256 + i * 128:256 + (i + 1) * 128], [(spi, tri[:, d, :])], [X["sp_b"], CB], [pb])
                    act(X["ekd"], ps[:, 0:256], AF.Exp, [pb], [X["ekd_b"]], scale=-1.0)
                    act(X["eb"], ps[:, 256:512], AF.Exp, [pb], [X["eb_b"]], scale=-1.0)
                    act(enbx[p % 2][:], ps[:, 256:512], AF.Exp, [pb], [enbx_b[p % 2]])

                def stC1(k):
                    si, d, j, first, lastt = steps[k]
                    X = tr[(k // 2) % 2]
                    i = k % 2
                    cs_ = slice(i * 128, (i + 1) * 128)
                    tt(X["kd"][:, cs_], H["ktok"][:, j, :], X["ekd"][:, cs_], ALU.mult, [H["kt"], X["ekd_b"]], [X["kd_b"]])
                    stt(qe[d][:, tsl(j)], H["qT"][:, tsl(j)], 128.0 ** -0.5, X["eb"][:, cs_], ALU.mult, ALU.mult,
                        [H["q"], X["eb_b"]], [qe_b[d]])
                    tt(ke[d][:, tsl(j)], H["kT"][:, tsl(j)], enbx[(k // 2) % 2][:, cs_], ALU.mult, [H["k"], enbx_b[(k // 2) % 2]], [ke_b[d]])

                def stC2(k):
                    si, d, j, first, lastt = steps[k]
                    X = tr[(k // 2) % 2]
                    i = k % 2
                    cs_ = slice(i * 128, (i + 1) * 128)
                    last = i * 128 + (127 if d == 0 else 0)
                    if first:
                        if ph == 0:
                            T.op("dve", lambda d=d: nc.vector.memset(Sst[d], 0.0), writes=[S_b[d]])
                        else:
                            T.dma("sp", Sst[d], s0_d[d][h], S_b[d], writes=[S_b[d]])
                    ps, pb = PS()
                    mmg(ps[:, 0:192], [(X["kd"][:, cs_], H["vtok"][:, j, :])], [X["kd_b"], H["vt"]], [pb])
                    cp(sprev[d][:, j, :], Sst[d], [S_b[d]], [sprev_b[d]], eng="act")
                    stt(Sst[d], Sst[d], X["eb"][:, last:last + 1], ps[:, 0:192], ALU.mult, ALU.add,
                        [S_b[d], X["eb_b"], pb], [S_b[d]])
                    if lastt and ph == 0:
                        T.dma("sp", ns_d[d][si, h], Sst[d], S_b[d], reads=[S_b[d]])

                def popg():
                    if len(pend) > 1:
                        pend.pop(0)()

                npair = len(steps) // 2
                for i in range(npair + 2):
                    if 0 <= i - 2 < npair:
                        stC1(2 * (i - 2)); stC1(2 * (i - 2) + 1)
                    if i < npair:
                        stA(i)
                    popg()
                    if 0 <= i - 1 < npair:
                        stB(i - 1)
                    popg()
                    if 0 <= i - 2 < npair:
                        stC2(2 * (i - 2)); stC2(2 * (i - 2) + 1)
                stage("h%d_p1_%d" % (h, ph))

                def p2a(j):
                    X = tr[j % 2]
                    ps, pb = PS()
                    mmg(ps[:, 0:128], [(ke[0][:, tsl(j)], qe[0][:, tsl(j)])], [ke_b[0], qe_b[0]], [pb])
                    mmg(ps[:, 128:256], [(ke[1][:, tsl(j)], qe[1][:, tsl(j)])], [ke_b[1], qe_b[1]], [pb])
                    tt(X["att"], ps[:, 0:256], mask2[:], ALU.mult, [pb, CB], [X["att_b"]])

                def p2b(j):
                    X = tr[j % 2]
                    ps, pb = PS()
                    mmg(ps[:, 0:192], [(X["att"][:, 0:128], H["vtok"][:, j, :]), (X["att"][:, 128:256], H["vtok"][:, j, :]),
                                       (qe[0][:, tsl(j)], sprev[0][:, j, :]), (qe[1][:, tsl(j)], sprev[1][:, j, :])],
                        [X["att_b"], H["vt"], qe_b[0], qe_b[1], sprev_b[0], sprev_b[1]], [pb])
                    act(X["junk"], ps[:, 0:192], AF.Square, [pb], [X["junk_b"], ssqa_b], scale=192.0 ** -0.5,
                        accum_out=ssqa[:, j * 4 + h:j * 4 + h + 1])
                    tt(og[:, j, h * 192:(h + 1) * 192], ps[:, 0:192], H["sgtok"][:, j, :], ALU.mult, [pb, H["sg"]], [og_b[j]])

                p2a(0)
                for j in range(NT):
                    if j + 1 < NT:
                        p2a(j + 1)
                    p2b(j)
                    if len(pend) > 1:
                        pend.pop(0)()
                while pend:
                    pend.pop(0)()

            act(lnsa, ssqa, AF.Ln, [ssqa_b, CB], [lnsa_b], bias=cst[:, 0:1])
            act(rsa, lnsa, AF.Exp, [lnsa_b], [rsa_b], scale=-0.5)
            og3 = U[:, a_og:a_og + NT * 768].rearrange("p (g e) -> p g e", e=192)
            tt(og3, og3, rsa.unsqueeze(2).to_broadcast([128, NT * 4, 192]), ALU.mult, [rsa_b] + og_b, og_b)
            stage("gla_%d" % ph)
            for j in range(NT):
                for c in range(6):
                    T.op("pe", lambda j=j, c=c: nc.tensor.transpose(psb[:, c * 128:(c + 1) * 128], og[:, j, c * 128:(c + 1) * 128], identb[:]),
                         reads=[og_b[j], CB], writes=[psb_b], sig=(c == 5))
                tt(hT[:, 0:6, tsl(j)], psb[:, 0:768].rearrange("p (c t) -> p c t", c=6),
                   pp[:, PP_NG:PP_NG + 6].unsqueeze(2).to_broadcast([128, 6, 128]), ALU.mult,
                   [psb_b, CB], [hT_b[c][j // 4] for c in range(6)])
            if ph == 0:
                sl, slb = wget("dft")
                dv = sl[:, 0:1024].rearrange("p (j k n) -> p j k n", j=2, k=2)
                for (t0, n) in seqs:
                    for c2 in range(2):
                        ps, pb = PS()
                        pairs = []
                        for jj in range(2):
                            pairs.append((xcs[:, t0 + jj, c2, 0:128], dv[:, jj, 0, :]))
                            pairs.append((xcs[:, t0 + jj, c2, 128:256], dv[:, jj, 1, :]))
                        mmg(ps[:, 0:256], pairs, [slb, xcs_b[t0], xcs_b[t0 + 1]], [pb])
                        cp(hT[:, 6 + c2, t0 * 128:t0 * 128 + 256], ps[:, 0:256], [pb], [hT_b[6 + c2][0]],
                           eng=("act" if c2 == 0 else "dve"))
            else:
                slc, slcb = wget("dftc")
                sls, slsb = wget("dfts", ahead=NSLOT - 2)
                dc = v3(slc, 8, 1024); ds = v3(sls, 8, 1024)
                for c2 in range(2):
                    for m in range(2):
                        ps, pb = PS()
                        pairs = []
                        for jj in range(8):
                            pairs.append((xcs[:, jj, c2, 0:128], dc[:, jj, msl(m)]))
                            pairs.append((xcs[:, jj, c2, 128:256], ds[:, jj, msl(m)]))
                        mmg(ps[:], pairs, [slcb, slsb] + xcs_b, [pb])
                        cp(hT[:, 6 + c2, msl(m)], ps[:], [pb], [hT_b[6 + c2][m]], eng=("act" if c2 == 0 else "dve"))
            stage("mix0_%d" % ph)
            dump("mix0_%d" % ph, hT[:, :, 0:TT], [hT_b[c][m] for c in range(8) for m in range(NM)])
            outproj_resid("evO", 0, 16)
            dump("x_l0mix_%d" % ph, xT[:, :, 0:TT], [xT_b[c][m] for c in range(8) for m in range(NM)])
            stage("out0_%d" % ph)
            norm(0, 1)
            ffn(0)
            stage("ffn0_%d" % ph)
            dump("x_l0_%d" % ph, xT[:, :, 0:TT], [xT_b[c][m] for c in range(8) for m in range(NM)])

            norm(1, 0)
            uT = U[:, 0:4 * TT].rearrange("p (c t) -> p c t", c=4)
            v1 = U[:, 4096:4096 + NT * 512].rearrange("p (j n) -> p j n", j=NT)
            gbT = U[:, 8192:8192 + 4 * TT].rearrange("p (c t) -> p c t", c=4)
            gcT = U[:, 12288:12288 + 4 * TT].rearrange("p (c t) -> p c t", c=4)
            zT = U[:, 24576:32768].bitcast(F32).rearrange("p (c t) -> p c t", c=4)
            u_b = ub("uT"); v1_b = ub("v1"); gb_b = ub("gb"); gc_b = ub("gc"); z_b = ub("z")
            accs = [U[:, 16384 + i * 2048:16384 + (i + 1) * 2048].bitcast(F32) for i in range(4)]
            acc_b = [ub("acc%d" % i) for i in range(4)]
            sl, slb = wget("odG")
            w = v3(sl, 8, 1024)
            for m in range(NM):
                for oc in range(8):
                    ps, pb = PS()
                    mmg(ps[:], [(w[:, kc, oc * 128:(oc + 1) * 128], hT[:, kc, msl(m)]) for kc in range(8)], [slb] + hall(m), [pb])
                    if oc < 4:
                        cp(gbT[:, oc, msl(m)], ps[:], [pb], [gb_b], eng="act")
                    else:
                        cp(gcT[:, oc - 4, msl(m)], ps[:], [pb], [gc_b], eng="dve")
            sl, slb = wget("odH")
            w = v3(sl, 8, 512)
            for oc in range(4):
                for m in range(NM):
                    ps, pb = PS()
                    mmg(ps[:], [(w[:, kc, oc * 128:(oc + 1) * 128], hT[:, kc, msl(m)]) for kc in range(8)], [slb] + hall(m), [pb])
                    tt(zT[:, oc, msl(m)], ps[:], gcT[:, oc, msl(m)], ALU.mult, [pb, gc_b], [z_b])
            RW = 256 if ph == 0 else 64
            for oc in range(4):
                acc = accs[oc][:, 0:TT]; ab = acc_b[oc]
                z = zT[:, oc, 0:TT]
                cw = lambda k: pp[:, PP_CONV + oc * 3 + k:PP_CONV + oc * 3 + k + 1]
                ts(acc, z, cw(1), None, ALU.mult, None, [z_b, CB], [ab])
                a3 = acc.rearrange("p (r w) -> p r w", w=RW)
                z3 = z.rearrange("p (r w) -> p r w", w=RW)
                stt(a3[:, :, 1:RW], z3[:, :, 0:RW - 1], cw(0), a3[:, :, 1:RW], ALU.mult, ALU.add, [z_b, CB, ab], [ab])
                stt(a3[:, :, 0:RW - 1], z3[:, :, 1:RW], cw(2), a3[:, :, 0:RW - 1], ALU.mult, ALU.add, [z_b, CB, ab], [ab])
            sl, slb = wget("odUV")
            w = v3(sl, 8, 1024)
            for oc in range(4):
                for m in range(NM):
                    ps, pb = PS()
                    mmg(ps[:], [(w[:, kc, oc * 128:(oc + 1) * 128], hT[:, kc, msl(m)]) for kc in range(8)], [slb] + hall(m), [pb])
                    act(uT[:, oc, msl(m)], ps[:], AF.Gelu_apprx_tanh, [pb], [u_b])
            for j in range(NT):
                ps, pb = PS()
                mmg(ps[:], [(hT[:, kc, tsl(j)], w[:, kc, 512:1024]) for kc in range(8)], [slb] + hall(j // 4), [pb])
                act(v1[:, j, :], ps[:], AF.Gelu_apprx_tanh, [pb], [v1_b])
            for j in range(NT):
                ps, pb = PS()
                for g in range(4):
                    T.op("pe", lambda g=g, ps=ps, j=j: nc.tensor.matmul(ps[:, g * 128:(g + 1) * 128], v1[:, j, g * 128:(g + 1) * 128], wsT[:, g, :], start=True, stop=False),
                         reads=[v1_b, CB], writes=[pb], sig=False)
                    T.op("pe", lambda g=g, ps=ps: nc.tensor.matmul(ps[:, g * 128:(g + 1) * 128], onesrow[0:1, :], gbs[0:1, g * 128:(g + 1) * 128], start=False, stop=True),
                         reads=[CB], writes=[pb], sig=(g == 3))
                tt(hT[:, 0:4, tsl(j)], ps[:].rearrange("p (g t) -> p g t", g=4), uT[:, :, tsl(j)], ALU.mult,
                   [pb, u_b], [hT_b[c][j // 4] for c in range(4)])
            for oc in range(4):
                tt(hT[:, 4 + oc, 0:TT], accs[oc][:, 0:TT], gbT[:, oc, 0:TT], ALU.mult, [acc_b[oc], gb_b], [hT_b[4 + oc][m] for m in range(NM)])
            stage("mix1_%d" % ph)
            outproj_resid("odO", 1, 16)
            dump("x_l1mix_%d" % ph, xT[:, :, 0:TT], [xT_b[c][m] for c in range(8) for m in range(NM)])
            norm(1, 1)
            ffn(1)
            stage("ffn1_%d" % ph)

            yst = [U[:, 0:2048].bitcast(F32), U[:, 2048:4096].bitcast(F32)]
            yst_b = [ub("yst0"), ub("yst1")]; nb["sq"] = ub("sq")
            for m in range(NM):
                sq = U[:, 20480:24576].rearrange("p (c t) -> p c t", c=8)
                xr = [xT_b[c][m] for c in range(8)]
                act(sq, xT[:, :, msl(m)], AF.Square, xr, [nb["sq"]])
                psr, pbr = PS()
                for jj in range(4):
                    mmg(psr[:, jj:jj + 1], [(sq[:, c, jj * 128:(jj + 1) * 128], onesb[:, 0:1]) for c in range(8)], [nb["sq"], CB], [pbr])
                act(rtok[:, 0:4], psr[:, 0:4], AF.Ln, [pbr, CB], [rtok_b], bias=cst[:, 0:1])
                act(rtok[:, 4:8], rtok[:, 0:4], AF.Exp, [rtok_b], [rtok_b], scale=-0.5)
                for jj in range(4):
                    j = m * 4 + jj
                    ys, yb = yst[j % 2], yst_b[j % 2]
                    for hf in range(2):
                        ps, pb = PS()
                        for k in range(4):
                            c = hf * 4 + k
                            T.op("pe", lambda ps=ps, k=k, c=c, j=j: nc.tensor.transpose(ps[:, k * 128:(k + 1) * 128], xT[:, c, j * 128:(j + 1) * 128], ident[:]),
                                 reads=[xT_b[c][m], CB], writes=[pb], sig=(k == 3))
                        stt(ys[:, hf * 512:(hf + 1) * 512], ps[:], rtok[:, 4 + jj:5 + jj], gfinb[:, hf * 512:(hf + 1) * 512], ALU.mult, ALU.mult,
                            [pb, rtok_b, CB], [yb])
                    T.dma("sp", y_d[ph][j * 128:(j + 1) * 128, :], ys, yb, reads=[yb])

        T.barrier(engines=("pe", "act", "dve", "sp", "pool"))
    return nc


_CACHE = {}


def _get_nc():
    if "nc" not in _CACHE:
        _CACHE["nc"] = build()
        _CACHE["consts"] = _consts()
    return _CACHE["nc"], _CACHE["consts"]


def make_in_maps(inputs, consts):
    f = lambda a: np.ascontiguousarray(np.asarray(a, dtype=np.float32))
    x_prompt = f(inputs["x_prompt"]); x_sample = f(inputs["x_sample"])
    sf = f(inputs["state_gla_fwd"]); sbw = f(inputs["state_gla_bwd"])
    c = f(inputs["c"]); c_ctx = f(inputs["c_ctx"])
    pp = np.zeros((128, PP_N), np.float32)
    for l in range(2):
        pp[:, PP_GMIX + l * 8:PP_GMIX + (l + 1) * 8] = f(inputs["norm_mix_g"])[l].reshape(8, 128).T
        pp[:, PP_GFFN + l * 8:PP_GFFN + (l + 1) * 8] = f(inputs["norm_ffn_g"])[l].reshape(8, 128).T
        pp[:, PP_ADAB + l * 48:PP_ADAB + (l + 1) * 48] = f(inputs["ada_b"])[l].reshape(48, 128).T
    pp[:, PP_CONV:PP_CONV + 12] = f(inputs["conv_w"])[0].T.reshape(4, 128, 3).transpose(1, 0, 2).reshape(128, 12)
    pp[:, PP_NG:PP_NG + 6] = np.tile(f(inputs["gla_norm_g"])[0], 4).reshape(6, 128).T
    w2aug = np.zeros((2, 64, 512), np.float32)
    w2aug[0, 0:16] = f(inputs["gla_w2_f"])[0]
    w2aug[0, 32] = f(inputs["gla_b2_f"])[0]
    w2aug[1, 16:32] = f(inputs["gla_w2_b"])[0]
    w2aug[1, 32] = f(inputs["gla_b2_b"])[0]
    shared = dict(
        pp=pp, w2aug=w2aug, gfin=f(inputs["final_norm_g"]).reshape(1, D),
        gws=f(inputs["gmlp_ws"])[0], gbs=f(inputs["gmlp_b"])[0].reshape(1, 512),
        ada_w=f(inputs["ada_w"]), ffn_w1=f(inputs["ffn_w1"]), ffn_w2=f(inputs["ffn_w2"]),
        ev_w_in=f(inputs["ev_w_in"])[0], ev_w_out=f(inputs["ev_w_out"])[0],
        od_w_in=f(inputs["od_w_in"])[0], od_w_out=f(inputs["od_w_out"])[0],
        ident=consts["ident"], tri=consts["tri"], mask2=consts["mask2"], cs64=consts["cs64"],
        dftc256=consts["dftc256"], dfts256=consts["dfts256"],
        dftc1024=consts["dftc1024"], dfts1024=consts["dfts1024"],
    )
    maps = []
    for i in range(NCORES):
        cond = np.stack([c_ctx, c[i]], axis=0)
        m = dict(shared)
        m["xp"] = x_prompt[2 * i:2 * i + 2].reshape(512, D)
        m["xs"] = x_sample[i]
        m["s0f"] = sf[i, 0]
        m["s0b"] = sbw[i, 0]
        m["condT"] = np.ascontiguousarray(cond.reshape(2, 8, 128).transpose(2, 1, 0))
        maps.append(m)
    return maps


def kernel(**inputs):
    nc, consts = _get_nc()
    maps = make_in_maps(inputs, consts)
    res = run_bass_kernel_spmd(nc, maps, core_ids=list(range(NCORES)))
    r = res.results
    y_prompt = np.concatenate([r[i]["yp"].reshape(2, 256, D) for i in range(NCORES)], axis=0)
    y_sample = np.stack([r[i]["ys"] for i in range(NCORES)], axis=0)
    nsf = np.concatenate([r[i]["nsf"].reshape(2, 1, 4, 128, 192) for i in range(NCORES)], axis=0)
    nsb = np.concatenate([r[i]["nsb"].reshape(2, 1, 4, 128, 192) for i in range(NCORES)], axis=0)
    return (y_prompt.astype(np.float32), y_sample.astype(np.float32),
            nsf.astype(np.float32), nsb.astype(np.float32))
```

```python
import contextlib
import numpy as np
import concourse.bass as bass
import concourse.mybir as mybir
from concourse.bass_utils import run_bass_kernel_spmd

F32 = mybir.dt.float32
BF16 = mybir.dt.bfloat16
AF = mybir.ActivationFunctionType
ALU = mybir.AluOpType

NCORES = 8
D = 1024
NSLOT = 3
EPS = 1e-6


class Buf:
    _n = 0

    def __init__(self, name, excl=False):
        Buf._n += 1
        self.id = Buf._n
        self.name = name
        self.excl = excl
        self.w = {}
        self.r = {}
        self.dsem = None
        self.dcnt = 0


class Tracker:
    def __init__(self, nc, es):
        self.nc = nc
        self.es = es
        self.eng = {}
        for k, h in (("pe", nc.tensor), ("act", nc.scalar), ("dve", nc.vector),
                     ("pool", nc.gpsimd), ("sp", nc.sync)):
            sem = es.enter_context(nc.semaphore("sem_" + k))
            self.eng[k] = dict(h=h, sem=sem, cnt=0, seen={}, key=k)
        self.owners = []
        self.dead = False

    def _collect(self, e, reads, writes):
        need = {}

        def add(key, sem, val):
            if key == "pe" and e["key"] == "pe":
                return
            if e["seen"].get(key, 0) >= val:
                return
            if key not in need or need[key][1] < val:
                need[key] = (sem, val)

        for b in reads:
            for key, (sem, val) in b.w.items():
                add(key, sem, val)
            if b.excl:
                for key, (sem, val) in b.r.items():
                    if key != e["key"]:
                        add(key, sem, val)
        for b in writes:
            for key, (sem, val) in b.w.items():
                add(key, sem, val)
            for key, (sem, val) in b.r.items():
                add(key, sem, val)
        return need

    def _emit(self, e, need, fn):
        items = list(need.items())
        for key, (sem, val) in items[:-1]:
            e["h"].wait_ge(sem, val)
            e["seen"][key] = val
        ins = fn()
        if items:
            key, (sem, val) = items[-1]
            ins._wait_ge(sem, val)
            e["seen"][key] = val
        return ins

    @staticmethod
    def _rec(d, key, sem, val):
        old = d.get(key)
        if old is None or old[1] < val:
            d[key] = (sem, val)

    def op(self, ek, fn, reads=(), writes=(), sig=True):
        if self.dead:
            return None
        e = self.eng[ek]
        need = self._collect(e, reads, writes)
        ins = self._emit(e, need, fn)
        if sig:
            e["cnt"] += 1
            ins.then_inc(e["sem"], 1)
            val = e["cnt"]
        else:
            val = e["cnt"] + 1
        for b in reads:
            self._rec(b.r, ek, e["sem"], val)
        for b in writes:
            self._rec(b.w, ek, e["sem"], val)
        return ins

    def dma(self, qk, out, in_, owner, reads=(), writes=()):
        if self.dead:
            return None
        e = self.eng[qk]
        if owner.dsem is None:
            owner.dsem = self.es.enter_context(self.nc.semaphore("dsem_%d" % owner.id))
            self.owners.append(owner)
        need = self._collect(e, reads, writes)
        ins = self._emit(e, need, lambda: e["h"].dma_start(out=out, in_=in_))
        owner.dcnt += 16
        ins.then_inc(owner.dsem, 16)
        key = "dma%d" % owner.id
        for b in reads:
            self._rec(b.r, key, owner.dsem, owner.dcnt)
        for b in writes:
            self._rec(b.w, key, owner.dsem, owner.dcnt)
        return ins

    def barrier(self, engines=("pe", "act", "dve", "sp")):
        if self.dead:
            return
        targets = [(k, e["sem"], e["cnt"]) for k, e in self.eng.items() if e["cnt"] > 0]
        targets += [("dma%d" % b.id, b.dsem, b.dcnt) for b in self.owners if b.dcnt > 0]
        for ek in engines:
            e = self.eng[ek]
            for key, sem, val in targets:
                if key == ek and ek == "pe":
                    continue
                if e["seen"].get(key, 0) >= val:
                    continue
                e["h"].wait_ge(sem, val)
                e["seen"][key] = val

    def wait_all(self, ek, bufs):
        e = self.eng[ek]
        need = self._collect(e, [], bufs)
        for key, (sem, val) in need.items():
            e["h"].wait_ge(sem, val)
            e["seen"][key] = val


def _consts():
    c = {}
    c["ident"] = np.eye(128, dtype=np.float32)
    s = np.arange(128)[:, None]
    t = np.arange(128)[None, :]
    tri = np.zeros((128, 4, 128), np.float32)
    tri[:, 0, :] = (s <= t) / 16.0
    tri[:, 1, :] = (s >= t) / 16.0
    tri[:, 2, :] = (s > t) / 16.0
    tri[:, 3, :] = (s < t) / 16.0
    c["tri"] = tri
    m2 = np.zeros((128, 256), np.float32)
    m2[:, 0:128] = (s <= t)
    m2[:, 128:256] = (s >= t)
    c["mask2"] = m2
    cs = np.zeros((128, 256), np.float32)
    a = np.arange(64)
    ang = 2 * np.pi * ((a[:, None] * a[None, :]) % 64) / 64.0
    for g in range(2):
        cs[g * 64:(g + 1) * 64, g * 64:(g + 1) * 64] = np.cos(ang)
        cs[g * 64:(g + 1) * 64, 128 + g * 64:128 + (g + 1) * 64] = np.sin(ang)
    c["cs64"] = cs
    for L in (256, 1024):
        tt = np.arange(L, dtype=np.int64)
        ang = 2 * np.pi * ((tt[:, None] * tt[None, :]) % L).astype(np.float64) / L
        sc = 1.0 / np.sqrt(64.0 * L)
        c["dftc%d" % L] = (np.cos(ang) * sc).astype(np.float32)
        c["dfts%d" % L] = (-np.sin(ang) * sc).astype(np.float32)
    return c


PP_GMIX, PP_GFFN, PP_ADAB, PP_CONV, PP_NG, PP_N = 0, 16, 32, 128, 140, 146


def build(debug_dump=None, stop_at=None):
    nc = bass.Bass("TRN2", target_bir_lowering=False)
    dbg = {}

    def din(name, shape):
        return nc.dram_tensor(name, list(shape), F32, kind="ExternalInput").ap()

    def dout(name, shape):
        return nc.dram_tensor(name, list(shape), F32, kind="ExternalOutput").ap()

    xin_d = [din("xp", [512, D]), din("xs", [1024, D])]
    s0_d = [din("s0f", [4, 128, 192]), din("s0b", [4, 128, 192])]
    condT_d = din("condT", [128, 8, 2])
    pp_d = din("pp", [128, PP_N])
    w2aug_d = din("w2aug", [2, 64, 512])
    gfin_d = din("gfin", [1, D])
    gws_d = din("gws", [4, 128, 128])
    gbs_d = din("gbs", [1, 512])
    ada_w = din("ada_w", [2, D, 6 * D])
    ffn_w1 = din("ffn_w1", [2, D, 4 * D])
    ffn_w2 = din("ffn_w2", [2, 4 * D, D])
    ev_w_in = din("ev_w_in", [D, 2848])
    ev_h = din("ev_h", [4, D, 640])
    ev_w_out = din("ev_w_out", [D, D])
    od_w_in = din("od_w_in", [D, 2560])
    od_w_out = din("od_w_out", [D, D])
    ident_d = din("ident", [128, 128])
    tri_d = din("tri", [128, 4, 128])
    mask2_d = din("mask2", [128, 256])
    cs64_d = din("cs64", [128, 256])
    dft256_d = [din("dftc256", [256, 256]), din("dfts256", [256, 256])]
    dft1024_d = [din("dftc1024", [1024, 1024]), din("dfts1024", [1024, 1024])]

    y_d = [dout("yp", [512, D]), dout("ys", [1024, D])]
    ns_d = [dout("nsf", [2, 4, 128, 192]), dout("nsb", [2, 4, 128, 192])]

    with contextlib.ExitStack() as es:
        T = Tracker(nc, es)

        def stage(name):
            if stop_at is not None and name == stop_at and not T.dead:
                T.barrier(engines=("pe", "act", "dve", "sp", "pool"))
                T.dead = True

        def sb(name, shape, dt=F32):
            return es.enter_context(nc.sbuf_tensor("sb_" + name, list(shape), dt))

        TMAX = 1024
        xT = sb("xT", [128, 8, TMAX])
        hT = sb("hT", [128, 8, TMAX], BF16)
        slots = [sb("slot%d" % i, [128, 8192], BF16) for i in range(NSLOT)]
        slot_b = [Buf("slot%d" % i) for i in range(NSLOT)]
        U = sb("U", [128, 32768], BF16)
        H2 = sb("H2", [128, 6144], BF16)
        enbx = [sb("enb%d" % i, [128, 256]) for i in range(2)]
        enbx_b = [Buf("enb0"), Buf("enb1")]
        hs1_b = dict(q=Buf("q1"), k=Buf("k1"), kt=Buf("kt1"), vt=Buf("vt1"), sg=Buf("sg1"))
        ident = sb("ident", [128, 128]); identb = sb("identb", [128, 128], BF16)
        tri = sb("tri", [128, 4, 128], BF16)
        mask2 = sb("mask2", [128, 256], BF16)
        cs64 = sb("cs64", [128, 256], BF16)
        onesb = sb("onesb", [128, 128], BF16)
        onesrow = sb("onesrow", [1, 128])
        cst = sb("cst", [128, 4])
        pp = sb("pp", [128, PP_N])
        w2aug = sb("w2aug", [64, 2, 512], BF16)
        gfinb = sb("gfinb", [128, D])
        gws = sb("gws", [128, 4, 128])
        wsT = sb("wsT", [128, 4, 128], BF16)
        gbs = sb("gbs", [1, 512])
        condT = sb("condT", [128, 8, 2])
        scT = sb("scT", [128, 8, 2], BF16)
        modT = [sb("modT%d" % l, [128, 48, 2]) for l in range(2)]
        gscT = [sb("gscT%d" % l, [128, 2, 8, 2]) for l in range(2)]
        rstd = sb("rstd", [128, 512]); lnr = sb("lnr", [128, 512])
        sqc = [sb("sqc%d" % i, [128, 512], BF16) for i in range(2)]
        ntm = [sb("ntm%d" % i, [128, 512]) for i in range(2)]
        sqc_b = [Buf("sqc0"), Buf("sqc1")]; ntm_b = [Buf("ntm0"), Buf("ntm1")]
        relu_t = [sb("relu%d" % i, [128, 512], BF16) for i in range(2)]
        relu_b = [Buf("relu0"), Buf("relu1")]
        rtok = sb("rtok", [128, 8]);
        CB = Buf("consts")
        U_all = []

        def ub(name):
            b = Buf(name)
            for o in U_all:
                for dct in (o.w, o.r):
                    for key, (sem, val) in dct.items():
                        Tracker._rec(b.r, key, sem, val)
            U_all.append(b)
            return b
        xT_b = [[Buf("xT%d_%d" % (c, m)) for m in range(2)] for c in range(8)]
        hT_b = [[Buf("hT%d_%d" % (c, m)) for m in range(2)] for c in range(8)]
        rstd_b = Buf("rstd"); lnr_b = Buf("lnr"); rtok_b = Buf("rtok")
        mod_b = Buf("mod")

        psf = [es.enter_context(nc.psum_tensor("psf%d" % i, [128, 512], F32)) for i in range(7)]
        psf_b = [Buf("psf%d" % i, excl=True) for i in range(7)]
        psb = es.enter_context(nc.psum_tensor("psb", [128, 1024], BF16))
        psb_b = Buf("psb", excl=True)
        ps_rr = [0]

        def PS():
            i = ps_rr[0] % 7
            ps_rr[0] += 1
            return psf[i], psf_b[i]

        def mmg(out, pairs, R, W):
            n = len(pairs)
            for i, (l, r) in enumerate(pairs):
                T.op("pe", lambda l=l, r=r, i=i: nc.tensor.matmul(out, l, r, start=(i == 0), stop=(i == n - 1)),
                     reads=R, writes=W, sig=(i == n - 1))

        def act(out, in_, func, R, W, **kw):
            T.op("act", lambda: nc.scalar.activation(out=out, in_=in_, func=func, **kw), reads=R, writes=W)

        def tt(out, a, b, op, R, W, eng="dve"):
            h = nc.vector if eng == "dve" else nc.gpsimd
            T.op(eng, lambda: h.tensor_tensor(out=out, in0=a, in1=b, op=op), reads=R, writes=W)

        def stt(out, a, scalar, b, op0, op1, R, W):
            T.op("dve", lambda: nc.vector.scalar_tensor_tensor(out=out, in0=a, scalar=scalar, in1=b, op0=op0, op1=op1),
                 reads=R, writes=W)

        def ts(out, a, s1, s2, op0, op1, R, W):
            if s2 is None:
                T.op("dve", lambda: nc.vector.tensor_scalar(out=out, in0=a, scalar1=s1, scalar2=None, op0=op0), reads=R, writes=W)
            else:
                T.op("dve", lambda: nc.vector.tensor_scalar(out=out, in0=a, scalar1=s1, scalar2=s2, op0=op0, op1=op1), reads=R, writes=W)

        def cp(out, in_, R, W, eng="dve"):
            if eng == "act":
                T.op("act", lambda: nc.scalar.copy(out=out, in_=in_), reads=R, writes=W)
            else:
                T.op("dve", lambda: nc.vector.tensor_copy(out, in_), reads=R, writes=W)

        def dump(name, ap, R):
            if debug_dump is None or name not in debug_dump:
                return
            d = dout("dbg_" + name, list(ap.shape))
            b = Buf("dbg")
            T.dma("sp" if ap.dtype == F32 else "pool", d, ap, b, reads=R)
            dbg[name] = b

        def cload(dst, src):
            T.dma("sp", dst, src, CB, writes=[CB])

        cload(ident[:], ident_d)
        cload(pp[:], pp_d)
        cload(condT[:], condT_d)
        CBP = Buf("consts_pool")
        T.dma("pool", mask2[:], mask2_d, CBP, writes=[CB])
        T.dma("pool", cs64[:], cs64_d, CBP, writes=[CB])
        T.dma("pool", tri[:], tri_d, CBP, writes=[CB])
        T.dma("pool", w2aug[:], w2aug_d.rearrange("d k n -> k d n"), CBP, writes=[CB])
        T.op("dve", lambda: nc.vector.memset(onesb[:], 1.0 / 1024), writes=[CB])
        T.op("dve", lambda: nc.vector.memset(onesrow[:], 1.0), writes=[CB])
        T.op("dve", lambda: nc.vector.memset(cst[:, 0:1], EPS), writes=[CB])
        T.op("dve", lambda: nc.vector.memset(cst[:, 1:2], 1.0), writes=[CB])
        cp(identb[:], ident[:], [CB], [CB])
        act(scT[:], condT[:], AF.Silu, [CB], [CB])
        stage("consts")
        def v3(sl, k, n):
            return sl[:, 0:k * n].rearrange("p (k n) -> p k n", k=k)

        def rows(w2d):
            return w2d.rearrange("(k p) n -> p k n", p=128)

        WSEQ = []

        def wadd(name, fn):
            WSEQ.append((name, fn))

        def wada(l, b):
            wadd("ada%d_%d" % (l, b), lambda sl, l=l, b=b: [(v3(sl, 8, 1024), rows(ada_w[l, :, b * 1024:(b + 1) * 1024]))])

        PH_ORDER = (1, 0)
        PH_ADA = PH_ORDER[0]
        for ph in PH_ORDER:
            A = (ph == PH_ADA)
            for l in range(2):
                if l == 0:
                    if A:
                        wada(0, 0); wada(0, 1)
                    wadd("evC", lambda sl: [(v3(sl, 8, 288), rows(ev_w_in[:, 2560:2848]))])
                    for h in range(4):
                        def f(sl, h=h):
                            return [(v3(sl, 8, 640), rows(ev_h[h]))]
                        wadd("evH%d" % h, f)
                        if A:
                            wada(0, 2 + h)
                    if ph == 0:
                        def f(sl):
                            v = sl[:, 0:1024].rearrange("p (j k n) -> p j k n", j=2, k=2)
                            return [(v[:, :, 0, :], rows(dft256_d[0])), (v[:, :, 1, :], rows(dft256_d[1]))]
                        wadd("dft", f)
                    else:
                        wadd("dftc", lambda sl: [(v3(sl, 8, 1024), rows(dft1024_d[0]))])
                        wadd("dfts", lambda sl: [(v3(sl, 8, 1024), rows(dft1024_d[1]))])
                    wadd("evO", lambda sl: [(v3(sl, 8, 1024), rows(ev_w_out))])
                else:
                    wadd("odG", lambda sl: [(v3(sl, 8, 1024), rows(od_w_in[:, 1024:2048]))])
                    wadd("odH", lambda sl: [(v3(sl, 8, 512), rows(od_w_in[:, 2048:2560]))])
                    wadd("odUV", lambda sl: [(v3(sl, 8, 1024), rows(od_w_in[:, 0:1024]))])
                    wadd("odO", lambda sl: [(v3(sl, 8, 1024), rows(od_w_out))])
                for b in range(4):
                    wadd("w1_%d" % b, lambda sl, l=l, b=b: [(v3(sl, 8, 1024), rows(ffn_w1[l, :, b * 1024:(b + 1) * 1024]))])
                    if A and l == 0 and b < 2:
                        wada(1, b)
                    if A and l == 1 and b == 0:
                        wada(1, 5)
                for b in range(4):
                    wadd("w2_%d" % b, lambda sl, l=l, b=b: [(v3(sl, 32, 256), rows(ffn_w2[l, :, b * 256:(b + 1) * 256]))])
                    if A and l == 0 and b < 3:
                        wada(1, 2 + b)

        wstate = dict(issued=0, nxt=0)

        def wget(name, ahead=NSLOT - 1):
            i = wstate["nxt"]
            assert WSEQ[i][0] == name, (WSEQ[i][0], name)
            while wstate["issued"] < min(i + ahead + 1, len(WSEQ)):
                j = wstate["issued"]
                s = j % NSLOT
                for (o, src) in WSEQ[j][1](slots[s]):
                    T.dma("pool", o, src, slot_b[s], writes=[slot_b[s]])
                wstate["issued"] += 1
            wstate["nxt"] += 1
            return slots[i % NSLOT], slot_b[i % NSLOT]

        def ada_block(l, b):
            sl, slb = wget("ada%d_%d" % (l, b))
            w = v3(sl, 8, 1024)
            ps, pb = PS()
            for oc in range(8):
                mmg(ps[:, oc * 2:oc * 2 + 2],
                    [(w[:, kc, oc * 128:(oc + 1) * 128], scT[:, kc, :]) for kc in range(8)],
                    [slb, CB], [pb])
            a0 = PP_ADAB + l * 48 + b * 8
            tt(modT[l][:, b * 8:(b + 1) * 8, :], ps[:, 0:16].rearrange("p (c k) -> p c k", k=2),
               pp[:, a0:a0 + 8].unsqueeze(2).to_broadcast([128, 8, 2]), ALU.add, [pb, CB], [mod_b])
            if b in (1, 4):
                wh = 0 if b == 1 else 1
                g0 = (PP_GMIX if wh == 0 else PP_GFFN) + l * 8
                stt(gscT[l][:, wh, :, :], modT[l][:, b * 8:(b + 1) * 8, :], 1.0,
                    pp[:, g0:g0 + 8].unsqueeze(2).to_broadcast([128, 8, 2]), ALU.add, ALU.mult, [mod_b, CB], [mod_b])

        stage("ada")

        def modcol(l, ch, ci):
            return modT[l][:, ch, ci:ci + 1]

        for ph in PH_ORDER:
            ci = ph
            TT = 512 if ph == 0 else 1024
            NT = TT // 128
            NM = TT // 512
            seqs = [(0, 2), (2, 2)] if ph == 0 else [(0, 8)]
            msl = lambda m: slice(m * 512, (m + 1) * 512)
            tsl = lambda j: slice(j * 128, (j + 1) * 128)

            xin = [U[:, 0:2048].bitcast(F32), U[:, 2048:4096].bitcast(F32)]
            xin_b = [ub("xin0"), ub("xin1")]
            for j in range(NT):
                xi, xb = xin[j % 2], xin_b[j % 2]
                T.dma("sp", xi, xin_d[ph][j * 128:(j + 1) * 128, :], xb, writes=[xb])
                for hf in range(2):
                    ps, pb = PS()
                    for k in range(4):
                        c = hf * 4 + k
                        T.op("pe", lambda ps=ps, k=k, c=c, xi=xi: nc.tensor.transpose(ps[:, k * 128:(k + 1) * 128], xi[:, c * 128:(c + 1) * 128], ident[:]),
                             reads=[xb, CB], writes=[pb], sig=(k == 3))
                    cp(xT[:, hf * 4:hf * 4 + 4, tsl(j)], ps[:].rearrange("p (k t) -> p k t", k=4), [pb],
                       [xT_b[c][j // 4] for c in range(hf * 4, hf * 4 + 4)], eng=("act" if hf == 0 else "dve"))

            def norm(l, wh):
                shc = 0 if wh == 0 else 24
                if ph == PH_ADA and wh == 0 and l == 0:
                    ada_block(0, 0); ada_block(0, 1)
                for m in range(NM):
                    ps, pb = PS()
                    for c in range(8):
                        act(sqc[c % 2][:], xT[:, c, msl(m)], AF.Square, [xT_b[c][m]], [sqc_b[c % 2]])
                        T.op("pe", lambda c=c, ps=ps: nc.tensor.matmul(ps[:], onesb[:], sqc[c % 2][:], start=(c == 0), stop=(c == 7)),
                             reads=[sqc_b[c % 2], CB], writes=[pb], sig=True)
                    act(lnr[:], ps[:], AF.Ln, [pb, CB], [lnr_b], bias=cst[:, 0:1])
                    act(rstd[:], lnr[:], AF.Exp, [lnr_b], [rstd_b], scale=-0.5)
                    for c in range(8):
                        stt(ntm[c % 2][:], xT[:, c, msl(m)], gscT[l][:, wh, c, ci:ci + 1], rstd[:], ALU.mult, ALU.mult,
                            [xT_b[c][m], rstd_b, mod_b], [ntm_b[c % 2]])
                        act(hT[:, c, msl(m)], ntm[c % 2][:], AF.Identity, [ntm_b[c % 2], mod_b], [hT_b[c][m]],
                            bias=modcol(l, shc + c, ci))

            nb = {}

            def outproj_resid(name, l, gch):
                sl, slb = wget(name)
                w = v3(sl, 8, 1024)
                for oc in range(8):
                    for m in range(NM):
                        ps, pb = PS()
                        mmg(ps[:], [(w[:, kc, oc * 128:(oc + 1) * 128], hT[:, kc, msl(m)]) for kc in range(8)],
                            [slb] + [hT_b[kc][m] for kc in range(8)], [pb])
                        stt(xT[:, oc, msl(m)], ps[:], modcol(l, gch + oc, ci), xT[:, oc, msl(m)], ALU.mult, ALU.add,
                            [pb, mod_b], [xT_b[oc][m]])

            def ffn(l):
                aT = U[:, 0:32 * TT].rearrange("p (c t) -> p c t", c=32)
                aT_b = [[ub("aT") for m in range(2)] for c in range(32)]
                for b in range(4):
                    sl, slb = wget("w1_%d" % b)
                    w = v3(sl, 8, 1024)
                    order = [(oc, m) for m in range(NM) for oc in range(8)] if b == 0 else [(oc, m) for oc in range(8) for m in range(NM)]
                    for (oc, m) in order:
                        ch = b * 8 + oc
                        if True:
                            ps, pb = PS()
                            mmg(ps[:], [(w[:, kc, oc * 128:(oc + 1) * 128], hT[:, kc, msl(m)]) for kc in range(8)],
                                [slb] + [hT_b[kc][m] for kc in range(8)], [pb])
                            r = relu_t[(ch * NM + m) % 2]; r_b = relu_b[(ch * NM + m) % 2]
                            act(r[:], ps[:], AF.Relu, [pb], [r_b])
                            tt(aT[:, ch, msl(m)], r[:], r[:], ALU.mult, [r_b], [aT_b[ch][m]])
                    if ph == PH_ADA and l == 0 and b < 2:
                        ada_block(1, b)
                    if ph == PH_ADA and l == 1 and b == 0:
                        ada_block(1, 5)
                for b in range(4):
                    sl, slb = wget("w2_%d" % b)
                    w = v3(sl, 32, 256)
                    for o2 in range(2):
                        oc = b * 2 + o2
                        for m in range(NM):
                            ps, pb = PS()
                            mmg(ps[:], [(w[:, kc, o2 * 128:(o2 + 1) * 128], aT[:, kc, msl(m)]) for kc in range(32)],
                                [slb] + [aT_b[kc][m] for kc in range(32)], [pb])
                            stt(xT[:, oc, msl(m)], ps[:], modcol(l, 40 + oc, ci), xT[:, oc, msl(m)], ALU.mult, ALU.add,
                                [pb, mod_b], [xT_b[oc][m]])
                    if ph == PH_ADA and l == 0 and b < 3:
                        ada_block(1, 2 + b)


            if ph == PH_ORDER[0]:
                cload(gfinb[:], gfin_d.to_broadcast([128, D]))
                cload(gws[:], gws_d.rearrange("g p q -> p g q"))
                cload(gbs[:], gbs_d)
                for g in range(4):
                    ps, pb = PS()
                    T.op("pe", lambda g=g, ps=ps: nc.tensor.transpose(ps[:, 0:128], gws[:, g, :], ident[:]), reads=[CB], writes=[pb])
                    cp(wsT[:, g, :], ps[:, 0:128], [pb], [CB])
            stage("load%d" % ph)
            dump("x_in_%d" % ph, xT[:, :, 0:TT], [xT_b[c][m] for c in range(8) for m in range(NM)])
            norm(0, 0)
            dump("h_in_%d" % ph, hT[:, :, 0:TT], [hT_b[c][m] for c in range(8) for m in range(NM)])
            stage("norm00_%d" % ph)
            o = [0]

            def ualloc(nelem_bf16):
                a = o[0]
                o[0] += nelem_bf16
                assert o[0] <= 32768, o[0]
                return a

            a_lr = ualloc(TT); lrT = U[:, a_lr:a_lr + TT]
            a_fin = ualloc(2 * TT); finT = U[:, a_fin:a_fin + 2 * TT].rearrange("p (c t) -> p c t", c=2)
            a_xcs = ualloc(NT * 512); xcs = U[:, a_xcs:a_xcs + NT * 512].rearrange("p (j c n) -> p j c n", j=NT, c=2)
            a_og = ualloc(NT * 768); og = U[:, a_og:a_og + NT * 768].rearrange("p (j n) -> p j n", j=NT)
            a_q = ualloc(TT); qT = U[:, a_q:a_q + TT]
            a_k = ualloc(TT); kT = U[:, a_k:a_k + TT]
            a_kt = ualloc(NT * 128); ktok = U[:, a_kt:a_kt + NT * 128].rearrange("p (j n) -> p j n", j=NT)
            a_vt = ualloc(NT * 192); vtok = U[:, a_vt:a_vt + NT * 192].rearrange("p (j n) -> p j n", j=NT)
            a_sg = ualloc(NT * 192); sgtok = U[:, a_sg:a_sg + NT * 192].rearrange("p (j n) -> p j n", j=NT)
            qe = []; ke = []; sprev = []; Sst = []
            for d in range(2):
                a = ualloc(TT); qe.append(U[:, a:a + TT])
                a = ualloc(TT); ke.append(U[:, a:a + TT])
                a = ualloc(NT * 192); sprev.append(U[:, a:a + NT * 192].rearrange("p (j n) -> p j n", j=NT))
                a = ualloc(384); Sst.append(U[:, a:a + 384].bitcast(F32))
            tr = []
            for par in range(2):
                dct = {}
                for nm, ne, dtp in (("e1", 512, F32), ("sp", 256, BF16), ("ekd", 512, F32), ("kd", 256, BF16),
                                    ("eb", 512, F32), ("att", 256, BF16), ("junk", 192, BF16)):
                    a = ualloc(ne)
                    v = U[:, a:a + ne]
                    dct[nm] = v.bitcast(F32) if dtp == F32 else v
                    dct[nm + "_b"] = ub(nm)
                tr.append(dct)
            a_ss = ualloc(NT * 8); ssqa = U[:, a_ss:a_ss + NT * 8].bitcast(F32)
            a_ls = ualloc(NT * 8); lnsa = U[:, a_ls:a_ls + NT * 8].bitcast(F32)
            a_rs = ualloc(NT * 8); rsa = U[:, a_rs:a_rs + NT * 8].bitcast(F32)
            ssqa_b = ub("ssqa"); lnsa_b = ub("lnsa"); rsa_b = ub("rsa")
            lr_b = ub("lr"); fin_b = ub("fin"); xcs_b = [ub("xcs") for _ in range(NT)]; og_b = [ub("og") for _ in range(NT)]
            q_b = ub("q"); k_b = ub("k"); kt_b = ub("kt"); vt_b = ub("vt"); sg_b = ub("sg")
            qe_b = [ub("qef"), ub("qeb")]; ke_b = [ub("kef"), ub("keb")]
            sprev_b = [ub("spf"), ub("spb")]; S_b = [ub("Sf"), ub("Sb")]
            hall = lambda m: [hT_b[kc][m] for kc in range(8)]

            T.op("dve", lambda: nc.vector.memset(lrT[32:64, :], 0.0), writes=[lr_b])
            T.op("dve", lambda: nc.vector.memset(lrT[32:33, :], 1.0), writes=[lr_b])

            sl, slb = wget("evC")
            w = v3(sl, 8, 288)
            for m in range(NM):
                ps, pb = PS()
                mmg(ps[0:32, :], [(w[:, kc, 0:32], hT[:, kc, msl(m)]) for kc in range(8)], [slb] + hall(m), [pb])
                cp(lrT[0:32, msl(m)], ps[0:32, :], [pb], [lr_b], eng="act")
                for c2 in range(2):
                    ps, pb = PS()
                    mmg(ps[:], [(w[:, kc, 32 + c2 * 128:32 + (c2 + 1) * 128], hT[:, kc, msl(m)]) for kc in range(8)],
                        [slb] + hall(m), [pb])
                    cp(finT[:, c2, msl(m)], ps[:], [pb], [fin_b], eng="dve")
            for j in range(NT):
                for c2 in range(2):
                    ps, pb = PS()
                    mmg(ps[:, 0:256], [(finT[:, c2, tsl(j)], cs64[:])], [fin_b, CB], [pb])
                    cp(xcs[:, j, c2, :], ps[:, 0:256], [pb], [xcs_b[j]], eng=("act" if c2 == 0 else "dve"))

            stage("evC_%d" % ph)
            o2 = [0]

            def h2alloc(n):
                a = o2[0]; o2[0] += n
                assert o2[0] <= 6144
                return H2[:, a:a + n]

            hs = [dict(qT=qT, kT=kT, ktok=ktok, vtok=vtok, sgtok=sgtok, q=q_b, k=k_b, kt=kt_b, vt=vt_b, sg=sg_b),
                  dict(qT=h2alloc(TT), kT=h2alloc(TT),
                       ktok=h2alloc(NT * 128).rearrange("p (j n) -> p j n", j=NT),
                       vtok=h2alloc(NT * 192).rearrange("p (j n) -> p j n", j=NT),
                       sgtok=h2alloc(NT * 192).rearrange("p (j n) -> p j n", j=NT), **hs1_b)]

            def proj_groups(h):
                H = hs[h % 2]
                st = {}

                def getw():
                    if "w" not in st:
                        sl, slb = wget("evH%d" % h)
                        st["w"] = v3(sl, 8, 640); st["b"] = slb
                    return st["w"], st["b"]

                def gq(m):
                    w, slb = getw()
                    ps, pb = PS()
                    mmg(ps[:], [(w[:, kc, 0:128], hT[:, kc, msl(m)]) for kc in range(8)], [slb] + hall(m), [pb])
                    cp(H["qT"][:, msl(m)], ps[:], [pb], [H["q"]], eng="act")

                def gk(m):
                    w, slb = getw()
                    ps, pb = PS()
                    mmg(ps[:], [(w[:, kc, 128:256], hT[:, kc, msl(m)]) for kc in range(8)], [slb] + hall(m), [pb])
                    cp(H["kT"][:, msl(m)], ps[:], [pb], [H["k"]], eng="dve")

                def gt(j):
                    w, slb = getw()
                    ps, pb = PS()
                    mmg(ps[:], [(hT[:, kc, tsl(j)], w[:, kc, 128:640]) for kc in range(8)], [slb] + hall(j // 4), [pb])
                    cp(H["ktok"][:, j, :], ps[:, 0:128], [pb], [H["kt"]], eng="dve")
                    cp(H["vtok"][:, j, :], ps[:, 128:320], [pb], [H["vt"]], eng="dve")
                    cp(H["sgtok"][:, j, :], ps[:, 320:512], [pb], [H["sg"]], eng="act")

                def gsilu():
                    act(H["sgtok"], H["sgtok"], AF.Silu, [H["sg"]], [H["sg"]])

                gl = []
                for m in range(NM):
                    gl.append(lambda m=m: gq(m)); gl.append(lambda m=m: gk(m))
                for j in range(NT):
                    gl.append(lambda j=j: gt(j))
                gl.append(gsilu)
                return gl

            for g in proj_groups(0):
                g()
            for h in range(4):
                H = hs[h % 2]
                if ph == PH_ADA:
                    ada_block(0, 2 + h)
                pend = proj_groups(h + 1) if h < 3 else []
                steps = []
                for (t0, n) in seqs:
                    fw = list(range(t0, t0 + n)); bw = fw[::-1]
                    for a, b in zip(fw, bw):
                        steps.append((t0 // 2, 0, a, a == fw[0], a == fw[-1]))
                        steps.append((t0 // 2, 1, b, b == bw[0], b == bw[-1]))

                def stA(p):
                    X = tr[p % 2]
                    ps, pb = PS()
                    for i in range(2):
                        si, d, j, first, lastt = steps[2 * p + i]
                        mmg(ps[:, i * 128:(i + 1) * 128], [(lrT[0:64, tsl(j)], w2aug[0:64, d, h * 128:(h + 1) * 128])], [lr_b, CB], [pb])
                    act(X["e1"], ps[:, 0:256], AF.Exp, [pb], [X["e1_b"]], scale=-1.0)
                    act(X["sp"], X["e1"], AF.Ln, [X["e1_b"], CB], [X["sp_b"]], bias=cst[:, 1:2])

                def stB(p):
                    X = tr[p % 2]
                    ps, pb = PS()
                    for i in range(2):
                        si, d, j, first, lastt = steps[2 * p + i]
                        spi = X["sp"][:, i * 128:(i + 1) * 128]
                        mmg(ps[:, i * 128:(i + 1) * 128], [(tri[:, 2 + d, :], spi)], [X["sp_b"], CB], [pb])
                        mmg(ps[:, 256 + i * 128:256 + (i + 1) * 128], [(spi, tri[:, d, :])], [X["sp_b"], CB], [pb])
                    act(X["ekd"], ps[:, 0:256], AF.Exp, [pb], [X["ekd_b"]], scale=-1.0)
                    act(X["eb"], ps[:, 256:512], AF.Exp, [pb], [X["eb_b"]], scale=-1.0)
                    act(enbx[p % 2][:], ps[:, 256:512], AF.Exp, [pb], [enbx_b[p % 2]])

                def stC1(k):
                    si, d, j, first, lastt = steps[k]
                    X = tr[(k // 2) % 2]
                    i = k % 2
                    cs_ = slice(i * 128, (i + 1) * 128)
                    tt(X["kd"][:, cs_], H["ktok"][:, j, :], X["ekd"][:, cs_], ALU.mult, [H["kt"], X["ekd_b"]], [X["kd_b"]])
                    stt(qe[d][:, tsl(j)], H["qT"][:, tsl(j)], 128.0 ** -0.5, X["eb"][:, cs_], ALU.mult, ALU.mult,
                        [H["q"], X["eb_b"]], [qe_b[d]])
                    tt(ke[d][:, tsl(j)], H["kT"][:, tsl(j)], enbx[(k // 2) % 2][:, cs_], ALU.mult, [H["k"], enbx_b[(k // 2) % 2]], [ke_b[d]])

                def stC2(k):
                    si, d, j, first, lastt = steps[k]
                    X = tr[(k // 2) % 2]
                    i = k % 2
                    cs_ = slice(i * 128, (i + 1) * 128)
                    last = i * 128 + (127 if d == 0 else 0)
                    if first:
                        if ph == 0:
                            T.op("dve", lambda d=d: nc.vector.memset(Sst[d], 0.0), writes=[S_b[d]])
                        else:
                            T.dma("sp", Sst[d], s0_d[d][h], S_b[d], writes=[S_b[d]])
                    ps, pb = PS()
                    mmg(ps[:, 0:192], [(X["kd"][:, cs_], H["vtok"][:, j, :])], [X["kd_b"], H["vt"]], [pb])
                    cp(sprev[d][:, j, :], Sst[d], [S_b[d]], [sprev_b[d]], eng="act")
                    stt(Sst[d], Sst[d], X["eb"][:, last:last + 1], ps[:, 0:192], ALU.mult, ALU.add,
                        [S_b[d], X["eb_b"], pb], [S_b[d]])
                    if lastt and ph == 0:
                        T.dma("sp", ns_d[d][si, h], Sst[d], S_b[d], reads=[S_b[d]])

                def popg():
                    if len(pend) > 1:
                        pend.pop(0)()

                npair = len(steps) // 2
                for i in range(npair + 2):
                    if 0 <= i - 2 < npair:
                        stC1(2 * (i - 2)); stC1(2 * (i - 2) + 1)
                    if i < npair:
                        stA(i)
                    popg()
                    if 0 <= i - 1 < npair:
                        stB(i - 1)
                    popg()
                    if 0 <= i - 2 < npair:
                        stC2(2 * (i - 2)); stC2(2 * (i - 2) + 1)
                stage("h%d_p1_%d" % (h, ph))

                def p2a(j):
                    X = tr[j % 2]
                    ps, pb = PS()
                    mmg(ps[:, 0:128], [(ke[0][:, tsl(j)], qe[0][:, tsl(j)])], [ke_b[0], qe_b[0]], [pb])
                    mmg(ps[:, 128:256], [(ke[1][:, tsl(j)], qe[1][:, tsl(j)])], [ke_b[1], qe_b[1]], [pb])
                    tt(X["att"], ps[:, 0:256], mask2[:], ALU.mult, [pb, CB], [X["att_b"]])

                def p2b(j):
                    X = tr[j % 2]
                    ps, pb = PS()
                    mmg(ps[:, 0:192], [(X["att"][:, 0:128], H["vtok"][:, j, :]), (X["att"][:, 128:256], H["vtok"][:, j, :]),
                                       (qe[0][:, tsl(j)], sprev[0][:, j, :]), (qe[1][:, tsl(j)], sprev[1][:, j, :])],
                        [X["att_b"], H["vt"], qe_b[0], qe_b[1], sprev_b[0], sprev_b[1]], [pb])
                    act(X["junk"], ps[:, 0:192], AF.Square, [pb], [X["junk_b"], ssqa_b], scale=192.0 ** -0.5,
                        accum_out=ssqa[:, j * 4 + h:j * 4 + h + 1])
                    tt(og[:, j, h * 192:(h + 1) * 192], ps[:, 0:192], H["sgtok"][:, j, :], ALU.mult, [pb, H["sg"]], [og_b[j]])

                p2a(0)
                for j in range(NT):
                    if j + 1 < NT:
                        p2a(j + 1)
                    p2b(j)
                    if len(pend) > 1:
                        pend.pop(0)()
                ssq3 = ssqa.rearrange("p (j h) -> p j h", h=4); lns3 = lnsa.rearrange("p (j h) -> p j h", h=4)
                rs3 = rsa.rearrange("p (j h) -> p j h", h=4)
                act(lns3[:, :, h], ssq3[:, :, h], AF.Ln, [ssqa_b, CB], [lnsa_b], bias=cst[:, 0:1])
                act(rs3[:, :, h], lns3[:, :, h], AF.Exp, [lnsa_b], [rsa_b], scale=-0.5)
                ogh = og[:, :, h * 192:(h + 1) * 192]
                tt(ogh, ogh, rs3[:, :, h:h + 1].to_broadcast([128, NT, 192]), ALU.mult, [rsa_b] + og_b, og_b)
                while pend:
                    pend.pop(0)()

            stage("gla_%d" % ph)
            for j in range(NT):
                for c in range(6):
                    T.op("pe", lambda j=j, c=c: nc.tensor.transpose(psb[:, c * 128:(c + 1) * 128], og[:, j, c * 128:(c + 1) * 128], identb[:]),
                         reads=[og_b[j], CB], writes=[psb_b], sig=(c == 5))
                tt(hT[:, 0:6, tsl(j)], psb[:, 0:768].rearrange("p (c t) -> p c t", c=6),
                   pp[:, PP_NG:PP_NG + 6].unsqueeze(2).to_broadcast([128, 6, 128]), ALU.mult,
                   [psb_b, CB], [hT_b[c][j // 4] for c in range(6)])
            if ph == 0:
                sl, slb = wget("dft")
                dv = sl[:, 0:1024].rearrange("p (j k n) -> p j k n", j=2, k=2)
                for (t0, n) in seqs:
                    for c2 in range(2):
                        ps, pb = PS()
                        pairs = []
                        for jj in range(2):
                            pairs.append((xcs[:, t0 + jj, c2, 0:128], dv[:, jj, 0, :]))
                            pairs.append((xcs[:, t0 + jj, c2, 128:256], dv[:, jj, 1, :]))
                        mmg(ps[:, 0:256], pairs, [slb, xcs_b[t0], xcs_b[t0 + 1]], [pb])
                        cp(hT[:, 6 + c2, t0 * 128:t0 * 128 + 256], ps[:, 0:256], [pb], [hT_b[6 + c2][0]],
                           eng=("act" if c2 == 0 else "dve"))
            else:
                slc, slcb = wget("dftc")
                sls, slsb = wget("dfts", ahead=NSLOT - 2)
                dc = v3(slc, 8, 1024); ds = v3(sls, 8, 1024)
                for c2 in range(2):
                    for m in range(2):
                        ps, pb = PS()
                        pairs = []
                        for jj in range(8):
                            pairs.append((xcs[:, jj, c2, 0:128], dc[:, jj, msl(m)]))
                            pairs.append((xcs[:, jj, c2, 128:256], ds[:, jj, msl(m)]))
                        mmg(ps[:], pairs, [slcb, slsb] + xcs_b, [pb])
                        cp(hT[:, 6 + c2, msl(m)], ps[:], [pb], [hT_b[6 + c2][m]], eng=("act" if c2 == 0 else "dve"))
            stage("mix0_%d" % ph)
            dump("mix0_%d" % ph, hT[:, :, 0:TT], [hT_b[c][m] for c in range(8) for m in range(NM)])
            outproj_resid("evO", 0, 16)
            dump("x_l0mix_%d" % ph, xT[:, :, 0:TT], [xT_b[c][m] for c in range(8) for m in range(NM)])
            stage("out0_%d" % ph)
            norm(0, 1)
            ffn(0)
            stage("ffn0_%d" % ph)
            dump("x_l0_%d" % ph, xT[:, :, 0:TT], [xT_b[c][m] for c in range(8) for m in range(NM)])

            norm(1, 0)
            uT = U[:, 0:4 * TT].rearrange("p (c t) -> p c t", c=4)
            v1 = U[:, 4096:4096 + NT * 512].rearrange("p (j n) -> p j n", j=NT)
            gbT = U[:, 8192:8192 + 4 * TT].rearrange("p (c t) -> p c t", c=4)
            gcT = U[:, 12288:12288 + 4 * TT].rearrange("p (c t) -> p c t", c=4)
            zT = U[:, 24576:32768].bitcast(F32).rearrange("p (c t) -> p c t", c=4)
            u_b = ub("uT"); v1_b = ub("v1"); gb_b = ub("gb"); gc_b = ub("gc"); z_b = ub("z")
            accs = [U[:, 16384 + i * 2048:16384 + (i + 1) * 2048].bitcast(F32) for i in range(4)]
            acc_b = [ub("acc%d" % i) for i in range(4)]
            sl, slb = wget("odG")
            w = v3(sl, 8, 1024)
            for m in range(NM):
                for oc in range(8):
                    ps, pb = PS()
                    mmg(ps[:], [(w[:, kc, oc * 128:(oc + 1) * 128], hT[:, kc, msl(m)]) for kc in range(8)], [slb] + hall(m), [pb])
                    if oc < 4:
                        cp(gbT[:, oc, msl(m)], ps[:], [pb], [gb_b], eng="act")
                    else:
                        cp(gcT[:, oc - 4, msl(m)], ps[:], [pb], [gc_b], eng="dve")
            sl, slb = wget("odH")
            w = v3(sl, 8, 512)
            for oc in range(4):
                for m in range(NM):
                    ps, pb = PS()
                    mmg(ps[:], [(w[:, kc, oc * 128:(oc + 1) * 128], hT[:, kc, msl(m)]) for kc in range(8)], [slb] + hall(m), [pb])
                    tt(zT[:, oc, msl(m)], ps[:], gcT[:, oc, msl(m)], ALU.mult, [pb, gc_b], [z_b])
            RW = 256 if ph == 0 else 64
            for oc in range(4):
                acc = accs[oc][:, 0:TT]; ab = acc_b[oc]
                z = zT[:, oc, 0:TT]
                cw = lambda k: pp[:, PP_CONV + oc * 3 + k:PP_CONV + oc * 3 + k + 1]
                ts(acc, z, cw(1), None, ALU.mult, None, [z_b, CB], [ab])
                a3 = acc.rearrange("p (r w) -> p r w", w=RW)
                z3 = z.rearrange("p (r w) -> p r w", w=RW)
                stt(a3[:, :, 1:RW], z3[:, :, 0:RW - 1], cw(0), a3[:, :, 1:RW], ALU.mult, ALU.add, [z_b, CB, ab], [ab])
                stt(a3[:, :, 0:RW - 1], z3[:, :, 1:RW], cw(2), a3[:, :, 0:RW - 1], ALU.mult, ALU.add, [z_b, CB, ab], [ab])
            sl, slb = wget("odUV")
            w = v3(sl, 8, 1024)
            for oc in range(4):
                for m in range(NM):
                    ps, pb = PS()
                    mmg(ps[:], [(w[:, kc, oc * 128:(oc + 1) * 128], hT[:, kc, msl(m)]) for kc in range(8)], [slb] + hall(m), [pb])
                    act(uT[:, oc, msl(m)], ps[:], AF.Gelu_apprx_tanh, [pb], [u_b])
            for j in range(NT):
                ps, pb = PS()
                mmg(ps[:], [(hT[:, kc, tsl(j)], w[:, kc, 512:1024]) for kc in range(8)], [slb] + hall(j // 4), [pb])
                act(v1[:, j, :], ps[:], AF.Gelu_apprx_tanh, [pb], [v1_b])
            for j in range(NT):
                ps, pb = PS()
                for g in range(4):
                    T.op("pe", lambda g=g, ps=ps, j=j: nc.tensor.matmul(ps[:, g * 128:(g + 1) * 128], v1[:, j, g * 128:(g + 1) * 128], wsT[:, g, :], start=True, stop=False),
                         reads=[v1_b, CB], writes=[pb], sig=False)
                    T.op("pe", lambda g=g, ps=ps: nc.tensor.matmul(ps[:, g * 128:(g + 1) * 128], onesrow[0:1, :], gbs[0:1, g * 128:(g + 1) * 128], start=False, stop=True),
                         reads=[CB], writes=[pb], sig=(g == 3))
                tt(hT[:, 0:4, tsl(j)], ps[:].rearrange("p (g t) -> p g t", g=4), uT[:, :, tsl(j)], ALU.mult,
                   [pb, u_b], [hT_b[c][j // 4] for c in range(4)])
            for oc in range(4):
                tt(hT[:, 4 + oc, 0:TT], accs[oc][:, 0:TT], gbT[:, oc, 0:TT], ALU.mult, [acc_b[oc], gb_b], [hT_b[4 + oc][m] for m in range(NM)])
            stage("mix1_%d" % ph)
            outproj_resid("odO", 1, 16)
            dump("x_l1mix_%d" % ph, xT[:, :, 0:TT], [xT_b[c][m] for c in range(8) for m in range(NM)])
            norm(1, 1)
            ffn(1)
            stage("ffn1_%d" % ph)

            yst = [U[:, 0:2048].bitcast(F32), U[:, 2048:4096].bitcast(F32)]
            yst_b = [ub("yst0"), ub("yst1")]; nb["sq"] = ub("sq")
            for m in range(NM):
                sq = U[:, 20480:24576].rearrange("p (c t) -> p c t", c=8)
                xr = [xT_b[c][m] for c in range(8)]
                act(sq, xT[:, :, msl(m)], AF.Square, xr, [nb["sq"]])
                psr, pbr = PS()
                for jj in range(4):
                    mmg(psr[:, jj:jj + 1], [(sq[:, c, jj * 128:(jj + 1) * 128], onesb[:, 0:1]) for c in range(8)], [nb["sq"], CB], [pbr])
                act(rtok[:, 0:4], psr[:, 0:4], AF.Ln, [pbr, CB], [rtok_b], bias=cst[:, 0:1])
                act(rtok[:, 4:8], rtok[:, 0:4], AF.Exp, [rtok_b], [rtok_b], scale=-0.5)
                for jj in range(4):
                    j = m * 4 + jj
                    ys, yb = yst[j % 2], yst_b[j % 2]
                    for hf in range(2):
                        ps, pb = PS()
                        for k in range(4):
                            c = hf * 4 + k
                            T.op("pe", lambda ps=ps, k=k, c=c, j=j: nc.tensor.transpose(ps[:, k * 128:(k + 1) * 128], xT[:, c, j * 128:(j + 1) * 128], ident[:]),
                                 reads=[xT_b[c][m], CB], writes=[pb], sig=(k == 3))
                        stt(ys[:, hf * 512:(hf + 1) * 512], ps[:], rtok[:, 4 + jj:5 + jj], gfinb[:, hf * 512:(hf + 1) * 512], ALU.mult, ALU.mult,
                            [pb, rtok_b, CB], [yb])
                    T.dma("sp", y_d[ph][j * 128:(j + 1) * 128, :], ys, yb, reads=[yb])

        T.barrier(engines=("pe", "act", "dve", "sp", "pool"))
    return nc


_CACHE = {}


def _get_nc():
    if "nc" not in _CACHE:
        _CACHE["nc"] = build()
        _CACHE["consts"] = _consts()
    return _CACHE["nc"], _CACHE["consts"]


def make_in_maps(inputs, consts):
    f = lambda a: np.ascontiguousarray(np.asarray(a, dtype=np.float32))
    x_prompt = f(inputs["x_prompt"]); x_sample = f(inputs["x_sample"])
    sf = f(inputs["state_gla_fwd"]); sbw = f(inputs["state_gla_bwd"])
    c = f(inputs["c"]); c_ctx = f(inputs["c_ctx"])
    pp = np.zeros((128, PP_N), np.float32)
    for l in range(2):
        pp[:, PP_GMIX + l * 8:PP_GMIX + (l + 1) * 8] = f(inputs["norm_mix_g"])[l].reshape(8, 128).T
        pp[:, PP_GFFN + l * 8:PP_GFFN + (l + 1) * 8] = f(inputs["norm_ffn_g"])[l].reshape(8, 128).T
        pp[:, PP_ADAB + l * 48:PP_ADAB + (l + 1) * 48] = f(inputs["ada_b"])[l].reshape(48, 128).T
    pp[:, PP_CONV:PP_CONV + 12] = f(inputs["conv_w"])[0].T.reshape(4, 128, 3).transpose(1, 0, 2).reshape(128, 12)
    pp[:, PP_NG:PP_NG + 6] = np.tile(f(inputs["gla_norm_g"])[0], 4).reshape(6, 128).T
    w2aug = np.zeros((2, 64, 512), np.float32)
    w2aug[0, 0:16] = f(inputs["gla_w2_f"])[0]
    w2aug[0, 32] = f(inputs["gla_b2_f"])[0]
    w2aug[1, 16:32] = f(inputs["gla_w2_b"])[0]
    w2aug[1, 32] = f(inputs["gla_b2_b"])[0]
    shared = dict(
        pp=pp, w2aug=w2aug, gfin=f(inputs["final_norm_g"]).reshape(1, D),
        gws=f(inputs["gmlp_ws"])[0], gbs=f(inputs["gmlp_b"])[0].reshape(1, 512),
        ada_w=f(inputs["ada_w"]), ffn_w1=f(inputs["ffn_w1"]), ffn_w2=f(inputs["ffn_w2"]),
        ev_w_in=f(inputs["ev_w_in"])[0], ev_w_out=f(inputs["ev_w_out"])[0],
        ev_h=np.ascontiguousarray(np.stack([np.concatenate([
            f(inputs["ev_w_in"])[0][:, h * 128:(h + 1) * 128],
            f(inputs["ev_w_in"])[0][:, 512 + h * 128:512 + (h + 1) * 128],
            f(inputs["ev_w_in"])[0][:, 1024 + h * 192:1024 + (h + 1) * 192],
            f(inputs["ev_w_in"])[0][:, 1792 + h * 192:1792 + (h + 1) * 192]], axis=1) for h in range(4)], axis=0)),
        od_w_in=f(inputs["od_w_in"])[0], od_w_out=f(inputs["od_w_out"])[0],
        ident=consts["ident"], tri=consts["tri"], mask2=consts["mask2"], cs64=consts["cs64"],
        dftc256=consts["dftc256"], dfts256=consts["dfts256"],
        dftc1024=consts["dftc1024"], dfts1024=consts["dfts1024"],
    )
    maps = []
    for i in range(NCORES):
        cond = np.stack([c_ctx, c[i]], axis=0)
        m = dict(shared)
        m["xp"] = x_prompt[2 * i:2 * i + 2].reshape(512, D)
        m["xs"] = x_sample[i]
        m["s0f"] = sf[i, 0]
        m["s0b"] = sbw[i, 0]
        m["condT"] = np.ascontiguousarray(cond.reshape(2, 8, 128).transpose(2, 1, 0))
        maps.append(m)
    return maps


def kernel(**inputs):
    nc, consts = _get_nc()
    maps = make_in_maps(inputs, consts)
    res = run_bass_kernel_spmd(nc, maps, core_ids=list(range(NCORES)))
    r = res.results
    y_prompt = np.concatenate([r[i]["yp"].reshape(2, 256, D) for i in range(NCORES)], axis=0)
    y_sample = np.stack([r[i]["ys"] for i in range(NCORES)], axis=0)
    nsf = np.concatenate([r[i]["nsf"].reshape(2, 1, 4, 128, 192) for i in range(NCORES)], axis=0)
    nsb = np.concatenate([r[i]["nsb"].reshape(2, 1, 4, 128, 192) for i in range(NCORES)], axis=0)
    return (y_prompt.astype(np.float32), y_sample.astype(np.float32),
            nsf.astype(np.float32), nsb.astype(np.float32))
```

```python
import contextlib
import numpy as np
import concourse.bass as bass
import concourse.mybir as mybir
from concourse.bass_utils import run_bass_kernel_spmd

F32 = mybir.dt.float32
BF16 = mybir.dt.bfloat16
AF = mybir.ActivationFunctionType
ALU = mybir.AluOpType

NCORES = 8
D = 1024
NSLOT = 3
EPS = 1e-6


class Buf:
    _n = 0

    def __init__(self, name, excl=False):
        Buf._n += 1
        self.id = Buf._n
        self.name = name
        self.excl = excl
        self.w = {}
        self.r = {}
        self.dsem = None
        self.dcnt = 0


class Tracker:
    def __init__(self, nc, es):
        self.nc = nc
        self.es = es
        self.eng = {}
        for k, h in (("pe", nc.tensor), ("act", nc.scalar), ("dve", nc.vector),
                     ("pool", nc.gpsimd), ("sp", nc.sync)):
            sem = es.enter_context(nc.semaphore("sem_" + k))
            self.eng[k] = dict(h=h, sem=sem, cnt=0, seen={}, key=k)
        self.owners = []
        self.dead = False

    def _collect(self, e, reads, writes):
        need = {}

        def add(key, sem, val):
            if key == "pe" and e["key"] == "pe":
                return
            if e["seen"].get(key, 0) >= val:
                return
            if key not in need or need[key][1] < val:
                need[key] = (sem, val)

        for b in reads:
            for key, (sem, val) in b.w.items():
                add(key, sem, val)
            if b.excl:
                for key, (sem, val) in b.r.items():
                    if key != e["key"]:
                        add(key, sem, val)
        for b in writes:
            for key, (sem, val) in b.w.items():
                add(key, sem, val)
            for key, (sem, val) in b.r.items():
                add(key, sem, val)
        return need

    def _emit(self, e, need, fn):
        items = list(need.items())
        for key, (sem, val) in items[:-1]:
            e["h"].wait_ge(sem, val)
            e["seen"][key] = val
        ins = fn()
        if items:
            key, (sem, val) = items[-1]
            ins._wait_ge(sem, val)
            e["seen"][key] = val
        return ins

    @staticmethod
    def _rec(d, key, sem, val):
        old = d.get(key)
        if old is None or old[1] < val:
            d[key] = (sem, val)

    def op(self, ek, fn, reads=(), writes=(), sig=True):
        if self.dead:
            return None
        e = self.eng[ek]
        need = self._collect(e, reads, writes)
        ins = self._emit(e, need, fn)
        if sig:
            e["cnt"] += 1
            ins.then_inc(e["sem"], 1)
            val = e["cnt"]
        else:
            val = e["cnt"] + 1
        for b in reads:
            self._rec(b.r, ek, e["sem"], val)
        for b in writes:
            self._rec(b.w, ek, e["sem"], val)
        return ins

    def dma(self, qk, out, in_, owner, reads=(), writes=()):
        if self.dead:
            return None
        e = self.eng[qk]
        if owner.dsem is None:
            owner.dsem = self.es.enter_context(self.nc.semaphore("dsem_%d" % owner.id))
            self.owners.append(owner)
        need = self._collect(e, reads, writes)
        ins = self._emit(e, need, lambda: e["h"].dma_start(out=out, in_=in_))
        owner.dcnt += 16
        ins.then_inc(owner.dsem, 16)
        key = "dma%d" % owner.id
        for b in reads:
            self._rec(b.r, key, owner.dsem, owner.dcnt)
        for b in writes:
            self._rec(b.w, key, owner.dsem, owner.dcnt)
        return ins

    def barrier(self, engines=("pe", "act", "dve", "sp")):
        if self.dead:
            return
        targets = [(k, e["sem"], e["cnt"]) for k, e in self.eng.items() if e["cnt"] > 0]
        targets += [("dma%d" % b.id, b.dsem, b.dcnt) for b in self.owners if b.dcnt > 0]
        for ek in engines:
            e = self.eng[ek]
            for key, sem, val in targets:
                if key == ek and ek == "pe":
                    continue
                if e["seen"].get(key, 0) >= val:
                    continue
                e["h"].wait_ge(sem, val)
                e["seen"][key] = val

    def wait_all(self, ek, bufs):
        e = self.eng[ek]
        need = self._collect(e, [], bufs)
        for key, (sem, val) in need.items():
            e["h"].wait_ge(sem, val)
            e["seen"][key] = val


def _consts():
    c = {}
    c["ident"] = np.eye(128, dtype=np.float32)
    s = np.arange(128)[:, None]
    t = np.arange(128)[None, :]
    tri = np.zeros((128, 4, 128), np.float32)
    tri[:, 0, :] = (s <= t) / 16.0
    tri[:, 1, :] = (s >= t) / 16.0
    tri[:, 2, :] = (s > t) / 16.0
    tri[:, 3, :] = (s < t) / 16.0
    c["tri"] = tri
    m2 = np.zeros((128, 256), np.float32)
    m2[:, 0:128] = (s <= t)
    m2[:, 128:256] = (s >= t)
    c["mask2"] = m2
    cs = np.zeros((128, 256), np.float32)
    a = np.arange(64)
    ang = 2 * np.pi * ((a[:, None] * a[None, :]) % 64) / 64.0
    for g in range(2):
        cs[g * 64:(g + 1) * 64, g * 64:(g + 1) * 64] = np.cos(ang)
        cs[g * 64:(g + 1) * 64, 128 + g * 64:128 + (g + 1) * 64] = np.sin(ang)
    c["cs64"] = cs
    for L in (256, 1024):
        tt = np.arange(L, dtype=np.int64)
        ang = 2 * np.pi * ((tt[:, None] * tt[None, :]) % L).astype(np.float64) / L
        sc = 1.0 / np.sqrt(64.0 * L)
        c["dftc%d" % L] = (np.cos(ang) * sc).astype(np.float32)
        c["dfts%d" % L] = (-np.sin(ang) * sc).astype(np.float32)
    return c


PP_GMIX, PP_GFFN, PP_ADAB, PP_CONV, PP_NG, PP_N = 0, 16, 32, 128, 140, 146


def build(debug_dump=None, stop_at=None):
    nc = bass.Bass("TRN2", target_bir_lowering=False)
    dbg = {}

    def din(name, shape):
        return nc.dram_tensor(name, list(shape), F32, kind="ExternalInput").ap()

    def dout(name, shape):
        return nc.dram_tensor(name, list(shape), F32, kind="ExternalOutput").ap()

    xin_d = [din("xp", [512, D]), din("xs", [1024, D])]
    s0_d = [din("s0f", [4, 128, 192]), din("s0b", [4, 128, 192])]
    condT_d = din("condT", [128, 8, 2])
    pp_d = din("pp", [128, PP_N])
    w2aug_d = din("w2aug", [2, 64, 512])
    gfin_d = din("gfin", [1, D])
    gws_d = din("gws", [4, 128, 128])
    gbs_d = din("gbs", [1, 512])
    ada_w = din("ada_w", [2, D, 6 * D])
    ffn_w1 = din("ffn_w1", [2, D, 4 * D])
    ffn_w2 = din("ffn_w2", [2, 4 * D, D])
    ev_w_in = din("ev_w_in", [D, 2848])
    ev_w_out = din("ev_w_out", [D, D])
    od_w_in = din("od_w_in", [D, 2560])
    od_w_out = din("od_w_out", [D, D])
    ident_d = din("ident", [128, 128])
    tri_d = din("tri", [128, 4, 128])
    mask2_d = din("mask2", [128, 256])
    cs64_d = din("cs64", [128, 256])
    dft256_d = [din("dftc256", [256, 256]), din("dfts256", [256, 256])]
    dft1024_d = [din("dftc1024", [1024, 1024]), din("dfts1024", [1024, 1024])]

    y_d = [dout("yp", [512, D]), dout("ys", [1024, D])]
    ns_d = [dout("nsf", [2, 4, 128, 192]), dout("nsb", [2, 4, 128, 192])]

    with contextlib.ExitStack() as es:
        T = Tracker(nc, es)

        def stage(name):
            if stop_at is not None and name == stop_at and not T.dead:
                T.barrier(engines=("pe", "act", "dve", "sp", "pool"))
                T.dead = True

        def sb(name, shape, dt=F32):
            return es.enter_context(nc.sbuf_tensor("sb_" + name, list(shape), dt))

        TMAX = 1024
        xT = sb("xT", [128, 8, TMAX])
        hT = sb("hT", [128, 8, TMAX], BF16)
        slots = [sb("slot%d" % i, [128, 8192], BF16) for i in range(NSLOT)]
        slot_b = [Buf("slot%d" % i) for i in range(NSLOT)]
        U = sb("U", [128, 32768], BF16)
        H2 = sb("H2", [128, 6144], BF16)
        enbx = [sb("enb%d" % i, [128, 256]) for i in range(2)]
        enbx_b = [Buf("enb0"), Buf("enb1")]
        hs1_b = dict(q=Buf("q1"), k=Buf("k1"), kt=Buf("kt1"), vt=Buf("vt1"), sg=Buf("sg1"))
        ident = sb("ident", [128, 128]); identb = sb("identb", [128, 128], BF16)
        tri = sb("tri", [128, 4, 128], BF16)
        mask2 = sb("mask2", [128, 256], BF16)
        cs64 = sb("cs64", [128, 256], BF16)
        onesb = sb("onesb", [128, 128], BF16)
        onesrow = sb("onesrow", [1, 128])
        cst = sb("cst", [128, 4])
        pp = sb("pp", [128, PP_N])
        w2aug = sb("w2aug", [64, 2, 512], BF16)
        gfinb = sb("gfinb", [128, D])
        gws = sb("gws", [128, 4, 128])
        wsT = sb("wsT", [128, 4, 128], BF16)
        gbs = sb("gbs", [1, 512])
        condT = sb("condT", [128, 8, 2])
        scT = sb("scT", [128, 8, 2], BF16)
        modT = [sb("modT%d" % l, [128, 48, 2]) for l in range(2)]
        gscT = [sb("gscT%d" % l, [128, 2, 8, 2]) for l in range(2)]
        rstd = sb("rstd", [128, 512]); lnr = sb("lnr", [128, 512])
        sqc = [sb("sqc%d" % i, [128, 512], BF16) for i in range(2)]
        ntm = [sb("ntm%d" % i, [128, 512]) for i in range(2)]
        sqc_b = [Buf("sqc0"), Buf("sqc1")]; ntm_b = [Buf("ntm0"), Buf("ntm1")]
        relu_t = [sb("relu%d" % i, [128, 512], BF16) for i in range(2)]
        relu_b = [Buf("relu0"), Buf("relu1")]
        rtok = sb("rtok", [128, 8]);
        CB = Buf("consts")
        U_all = []

        def ub(name):
            b = Buf(name)
            for o in U_all:
                for dct in (o.w, o.r):
                    for key, (sem, val) in dct.items():
                        Tracker._rec(b.r, key, sem, val)
            U_all.append(b)
            return b
        xT_b = [[Buf("xT%d_%d" % (c, m)) for m in range(2)] for c in range(8)]
        hT_b = [[Buf("hT%d_%d" % (c, m)) for m in range(2)] for c in range(8)]
        rstd_b = Buf("rstd"); lnr_b = Buf("lnr"); rtok_b = Buf("rtok")
        mod_b = Buf("mod")

        psf = [es.enter_context(nc.psum_tensor("psf%d" % i, [128, 512], F32)) for i in range(7)]
        psf_b = [Buf("psf%d" % i, excl=True) for i in range(7)]
        psb = es.enter_context(nc.psum_tensor("psb", [128, 1024], BF16))
        psb_b = Buf("psb", excl=True)
        ps_rr = [0]

        def PS():
            i = ps_rr[0] % 7
            ps_rr[0] += 1
            return psf[i], psf_b[i]

        def mmg(out, pairs, R, W):
            n = len(pairs)
            for i, (l, r) in enumerate(pairs):
                T.op("pe", lambda l=l, r=r, i=i: nc.tensor.matmul(out, l, r, start=(i == 0), stop=(i == n - 1)),
                     reads=R, writes=W, sig=(i == n - 1))

        def act(out, in_, func, R, W, **kw):
            T.op("act", lambda: nc.scalar.activation(out=out, in_=in_, func=func, **kw), reads=R, writes=W)

        def tt(out, a, b, op, R, W, eng="dve"):
            h = nc.vector if eng == "dve" else nc.gpsimd
            T.op(eng, lambda: h.tensor_tensor(out=out, in0=a, in1=b, op=op), reads=R, writes=W)

        def stt(out, a, scalar, b, op0, op1, R, W):
            T.op("dve", lambda: nc.vector.scalar_tensor_tensor(out=out, in0=a, scalar=scalar, in1=b, op0=op0, op1=op1),
                 reads=R, writes=W)

        def ts(out, a, s1, s2, op0, op1, R, W):
            if s2 is None:
                T.op("dve", lambda: nc.vector.tensor_scalar(out=out, in0=a, scalar1=s1, scalar2=None, op0=op0), reads=R, writes=W)
            else:
                T.op("dve", lambda: nc.vector.tensor_scalar(out=out, in0=a, scalar1=s1, scalar2=s2, op0=op0, op1=op1), reads=R, writes=W)

        def cp(out, in_, R, W, eng="dve"):
            if eng == "act":
                T.op("act", lambda: nc.scalar.copy(out=out, in_=in_), reads=R, writes=W)
            else:
                T.op("dve", lambda: nc.vector.tensor_copy(out, in_), reads=R, writes=W)

        def dump(name, ap, R):
            if debug_dump is None or name not in debug_dump:
                return
            d = dout("dbg_" + name, list(ap.shape))
            b = Buf("dbg")
            T.dma("sp" if ap.dtype == F32 else "pool", d, ap, b, reads=R)
            dbg[name] = b

        def cload(dst, src):
            T.dma("sp", dst, src, CB, writes=[CB])

        cload(ident[:], ident_d)
        cload(pp[:], pp_d)
        cload(gfinb[:], gfin_d.to_broadcast([128, D]))
        cload(gws[:], gws_d.rearrange("g p q -> p g q"))
        cload(gbs[:], gbs_d)
        cload(condT[:], condT_d)
        CBP = Buf("consts_pool")
        T.dma("pool", mask2[:], mask2_d, CBP, writes=[CB])
        T.dma("pool", cs64[:], cs64_d, CBP, writes=[CB])
        T.dma("pool", tri[:], tri_d, CBP, writes=[CB])
        T.dma("pool", w2aug[:], w2aug_d.rearrange("d k n -> k d n"), CBP, writes=[CB])
        T.op("dve", lambda: nc.vector.memset(onesb[:], 1.0 / 1024), writes=[CB])
        T.op("dve", lambda: nc.vector.memset(onesrow[:], 1.0), writes=[CB])
        T.op("dve", lambda: nc.vector.memset(cst[:, 0:1], EPS), writes=[CB])
        T.op("dve", lambda: nc.vector.memset(cst[:, 1:2], 1.0), writes=[CB])
        cp(identb[:], ident[:], [CB], [CB])
        act(scT[:], condT[:], AF.Silu, [CB], [CB])
        for g in range(4):
            ps, pb = PS()
            T.op("pe", lambda g=g, ps=ps: nc.tensor.transpose(ps[:, 0:128], gws[:, g, :], ident[:]), reads=[CB], writes=[pb])
            cp(wsT[:, g, :], ps[:, 0:128], [pb], [CB])

        stage("consts")
        def v3(sl, k, n):
            return sl[:, 0:k * n].rearrange("p (k n) -> p k n", k=k)

        def rows(w2d):
            return w2d.rearrange("(k p) n -> p k n", p=128)

        WSEQ = []

        def wadd(name, fn):
            WSEQ.append((name, fn))

        def wada(l, b):
            wadd("ada%d_%d" % (l, b), lambda sl, l=l, b=b: [(v3(sl, 8, 1024), rows(ada_w[l, :, b * 1024:(b + 1) * 1024]))])

        PH_ORDER = (1, 0)
        PH_ADA = PH_ORDER[0]
        for ph in PH_ORDER:
            A = (ph == PH_ADA)
            for l in range(2):
                if l == 0:
                    if A:
                        wada(0, 0); wada(0, 1)
                    wadd("evC", lambda sl: [(v3(sl, 8, 288), rows(ev_w_in[:, 2560:2848]))])
                    for h in range(4):
                        def f(sl, h=h):
                            v = v3(sl, 8, 640)
                            return [(v[:, :, 0:128], rows(ev_w_in[:, h * 128:(h + 1) * 128])),
                                    (v[:, :, 128:256], rows(ev_w_in[:, 512 + h * 128:512 + (h + 1) * 128])),
                                    (v[:, :, 256:448], rows(ev_w_in[:, 1024 + h * 192:1024 + (h + 1) * 192])),
                                    (v[:, :, 448:640], rows(ev_w_in[:, 1792 + h * 192:1792 + (h + 1) * 192]))]
                        wadd("evH%d" % h, f)
                        if A:
                            wada(0, 2 + h)
                    if ph == 0:
                        def f(sl):
                            v = sl[:, 0:1024].rearrange("p (j k n) -> p j k n", j=2, k=2)
                            return [(v[:, :, 0, :], rows(dft256_d[0])), (v[:, :, 1, :], rows(dft256_d[1]))]
                        wadd("dft", f)
                    else:
                        wadd("dftc", lambda sl: [(v3(sl, 8, 1024), rows(dft1024_d[0]))])
                        wadd("dfts", lambda sl: [(v3(sl, 8, 1024), rows(dft1024_d[1]))])
                    wadd("evO", lambda sl: [(v3(sl, 8, 1024), rows(ev_w_out))])
                else:
                    wadd("odG", lambda sl: [(v3(sl, 8, 1024), rows(od_w_in[:, 1024:2048]))])
                    wadd("odH", lambda sl: [(v3(sl, 8, 512), rows(od_w_in[:, 2048:2560]))])
                    wadd("odUV", lambda sl: [(v3(sl, 8, 1024), rows(od_w_in[:, 0:1024]))])
                    wadd("odO", lambda sl: [(v3(sl, 8, 1024), rows(od_w_out))])
                for b in range(4):
                    wadd("w1_%d" % b, lambda sl, l=l, b=b: [(v3(sl, 8, 1024), rows(ffn_w1[l, :, b * 1024:(b + 1) * 1024]))])
                    if A and l == 0 and b < 2:
                        wada(1, b)
                    if A and l == 1 and b == 0:
                        wada(1, 5)
                for b in range(4):
                    wadd("w2_%d" % b, lambda sl, l=l, b=b: [(v3(sl, 32, 256), rows(ffn_w2[l, :, b * 256:(b + 1) * 256]))])
                    if A and l == 0 and b < 3:
                        wada(1, 2 + b)

        wstate = dict(issued=0, nxt=0)

        def wget(name, ahead=NSLOT - 1):
            i = wstate["nxt"]
            assert WSEQ[i][0] == name, (WSEQ[i][0], name)
            while wstate["issued"] < min(i + ahead + 1, len(WSEQ)):
                j = wstate["issued"]
                s = j % NSLOT
                for (o, src) in WSEQ[j][1](slots[s]):
                    T.dma("pool", o, src, slot_b[s], writes=[slot_b[s]])
                wstate["issued"] += 1
            wstate["nxt"] += 1
            return slots[i % NSLOT], slot_b[i % NSLOT]

        def ada_block(l, b):
            sl, slb = wget("ada%d_%d" % (l, b))
            w = v3(sl, 8, 1024)
            ps, pb = PS()
            for oc in range(8):
                mmg(ps[:, oc * 2:oc * 2 + 2],
                    [(w[:, kc, oc * 128:(oc + 1) * 128], scT[:, kc, :]) for kc in range(8)],
                    [slb, CB], [pb])
            a0 = PP_ADAB + l * 48 + b * 8
            tt(modT[l][:, b * 8:(b + 1) * 8, :], ps[:, 0:16].rearrange("p (c k) -> p c k", k=2),
               pp[:, a0:a0 + 8].unsqueeze(2).to_broadcast([128, 8, 2]), ALU.add, [pb, CB], [mod_b])
            if b in (1, 4):
                wh = 0 if b == 1 else 1
                g0 = (PP_GMIX if wh == 0 else PP_GFFN) + l * 8
                stt(gscT[l][:, wh, :, :], modT[l][:, b * 8:(b + 1) * 8, :], 1.0,
                    pp[:, g0:g0 + 8].unsqueeze(2).to_broadcast([128, 8, 2]), ALU.add, ALU.mult, [mod_b, CB], [mod_b])

        stage("ada")

        def modcol(l, ch, ci):
            return modT[l][:, ch, ci:ci + 1]

        for ph in PH_ORDER:
            ci = ph
            TT = 512 if ph == 0 else 1024
            NT = TT // 128
            NM = TT // 512
            seqs = [(0, 2), (2, 2)] if ph == 0 else [(0, 8)]
            msl = lambda m: slice(m * 512, (m + 1) * 512)
            tsl = lambda j: slice(j * 128, (j + 1) * 128)

            xin = [U[:, 0:2048].bitcast(F32), U[:, 2048:4096].bitcast(F32)]
            xin_b = [ub("xin0"), ub("xin1")]
            for j in range(NT):
                xi, xb = xin[j % 2], xin_b[j % 2]
                T.dma("sp", xi, xin_d[ph][j * 128:(j + 1) * 128, :], xb, writes=[xb])
                for hf in range(2):
                    ps, pb = PS()
                    for k in range(4):
                        c = hf * 4 + k
                        T.op("pe", lambda ps=ps, k=k, c=c, xi=xi: nc.tensor.transpose(ps[:, k * 128:(k + 1) * 128], xi[:, c * 128:(c + 1) * 128], ident[:]),
                             reads=[xb, CB], writes=[pb], sig=(k == 3))
                    cp(xT[:, hf * 4:hf * 4 + 4, tsl(j)], ps[:].rearrange("p (k t) -> p k t", k=4), [pb],
                       [xT_b[c][j // 4] for c in range(hf * 4, hf * 4 + 4)], eng=("act" if hf == 0 else "dve"))

            def norm(l, wh):
                shc = 0 if wh == 0 else 24
                if ph == PH_ADA and wh == 0 and l == 0:
                    ada_block(0, 0); ada_block(0, 1)
                for m in range(NM):
                    ps, pb = PS()
                    for c in range(8):
                        act(sqc[c % 2][:], xT[:, c, msl(m)], AF.Square, [xT_b[c][m]], [sqc_b[c % 2]])
                        T.op("pe", lambda c=c, ps=ps: nc.tensor.matmul(ps[:], onesb[:], sqc[c % 2][:], start=(c == 0), stop=(c == 7)),
                             reads=[sqc_b[c % 2], CB], writes=[pb], sig=True)
                    act(lnr[:], ps[:], AF.Ln, [pb, CB], [lnr_b], bias=cst[:, 0:1])
                    act(rstd[:], lnr[:], AF.Exp, [lnr_b], [rstd_b], scale=-0.5)
                    for c in range(8):
                        stt(ntm[c % 2][:], xT[:, c, msl(m)], gscT[l][:, wh, c, ci:ci + 1], rstd[:], ALU.mult, ALU.mult,
                            [xT_b[c][m], rstd_b, mod_b], [ntm_b[c % 2]])
                        act(hT[:, c, msl(m)], ntm[c % 2][:], AF.Identity, [ntm_b[c % 2], mod_b], [hT_b[c][m]],
                            bias=modcol(l, shc + c, ci))

            nb = {}

            def outproj_resid(name, l, gch):
                sl, slb = wget(name)
                w = v3(sl, 8, 1024)
                for oc in range(8):
                    for m in range(NM):
                        ps, pb = PS()
                        mmg(ps[:], [(w[:, kc, oc * 128:(oc + 1) * 128], hT[:, kc, msl(m)]) for kc in range(8)],
                            [slb] + [hT_b[kc][m] for kc in range(8)], [pb])
                        stt(xT[:, oc, msl(m)], ps[:], modcol(l, gch + oc, ci), xT[:, oc, msl(m)], ALU.mult, ALU.add,
                            [pb, mod_b], [xT_b[oc][m]])

            def ffn(l):
                aT = U[:, 0:32 * TT].rearrange("p (c t) -> p c t", c=32)
                aT_b = [[ub("aT") for m in range(2)] for c in range(32)]
                for b in range(4):
                    sl, slb = wget("w1_%d" % b)
                    w = v3(sl, 8, 1024)
                    order = [(oc, m) for m in range(NM) for oc in range(8)] if b == 0 else [(oc, m) for oc in range(8) for m in range(NM)]
                    for (oc, m) in order:
                        ch = b * 8 + oc
                        if True:
                            ps, pb = PS()
                            mmg(ps[:], [(w[:, kc, oc * 128:(oc + 1) * 128], hT[:, kc, msl(m)]) for kc in range(8)],
                                [slb] + [hT_b[kc][m] for kc in range(8)], [pb])
                            r = relu_t[(ch * NM + m) % 2]; r_b = relu_b[(ch * NM + m) % 2]
                            act(r[:], ps[:], AF.Relu, [pb], [r_b])
                            tt(aT[:, ch, msl(m)], r[:], r[:], ALU.mult, [r_b], [aT_b[ch][m]])
                    if ph == PH_ADA and l == 0 and b < 2:
                        ada_block(1, b)
                    if ph == PH_ADA and l == 1 and b == 0:
                        ada_block(1, 5)
                for b in range(4):
                    sl, slb = wget("w2_%d" % b)
                    w = v3(sl, 32, 256)
                    for o2 in range(2):
                        oc = b * 2 + o2
                        for m in range(NM):
                            ps, pb = PS()
                            mmg(ps[:], [(w[:, kc, o2 * 128:(o2 + 1) * 128], aT[:, kc, msl(m)]) for kc in range(32)],
                                [slb] + [aT_b[kc][m] for kc in range(32)], [pb])
                            stt(xT[:, oc, msl(m)], ps[:], modcol(l, 40 + oc, ci), xT[:, oc, msl(m)], ALU.mult, ALU.add,
                                [pb, mod_b], [xT_b[oc][m]])
                    if ph == PH_ADA and l == 0 and b < 3:
                        ada_block(1, 2 + b)


            stage("load%d" % ph)
            dump("x_in_%d" % ph, xT[:, :, 0:TT], [xT_b[c][m] for c in range(8) for m in range(NM)])
            norm(0, 0)
            dump("h_in_%d" % ph, hT[:, :, 0:TT], [hT_b[c][m] for c in range(8) for m in range(NM)])
            stage("norm00_%d" % ph)
            o = [0]

            def ualloc(nelem_bf16):
                a = o[0]
                o[0] += nelem_bf16
                assert o[0] <= 32768, o[0]
                return a

            a_lr = ualloc(TT); lrT = U[:, a_lr:a_lr + TT]
            a_fin = ualloc(2 * TT); finT = U[:, a_fin:a_fin + 2 * TT].rearrange("p (c t) -> p c t", c=2)
            a_xcs = ualloc(NT * 512); xcs = U[:, a_xcs:a_xcs + NT * 512].rearrange("p (j c n) -> p j c n", j=NT, c=2)
            a_og = ualloc(NT * 768); og = U[:, a_og:a_og + NT * 768].rearrange("p (j n) -> p j n", j=NT)
            a_q = ualloc(TT); qT = U[:, a_q:a_q + TT]
            a_k = ualloc(TT); kT = U[:, a_k:a_k + TT]
            a_kt = ualloc(NT * 128); ktok = U[:, a_kt:a_kt + NT * 128].rearrange("p (j n) -> p j n", j=NT)
            a_vt = ualloc(NT * 192); vtok = U[:, a_vt:a_vt + NT * 192].rearrange("p (j n) -> p j n", j=NT)
            a_sg = ualloc(NT * 192); sgtok = U[:, a_sg:a_sg + NT * 192].rearrange("p (j n) -> p j n", j=NT)
            qe = []; ke = []; sprev = []; Sst = []
            for d in range(2):
                a = ualloc(TT); qe.append(U[:, a:a + TT])
                a = ualloc(TT); ke.append(U[:, a:a + TT])
                a = ualloc(NT * 192); sprev.append(U[:, a:a + NT * 192].rearrange("p (j n) -> p j n", j=NT))
                a = ualloc(384); Sst.append(U[:, a:a + 384].bitcast(F32))
            tr = []
            for par in range(2):
                dct = {}
                for nm, ne, dtp in (("e1", 512, F32), ("sp", 256, BF16), ("ekd", 512, F32), ("kd", 256, BF16),
                                    ("eb", 512, F32), ("att", 256, BF16), ("junk", 192, BF16)):
                    a = ualloc(ne)
                    v = U[:, a:a + ne]
                    dct[nm] = v.bitcast(F32) if dtp == F32 else v
                    dct[nm + "_b"] = ub(nm)
                tr.append(dct)
            a_ss = ualloc(NT * 8); ssqa = U[:, a_ss:a_ss + NT * 8].bitcast(F32)
            a_ls = ualloc(NT * 8); lnsa = U[:, a_ls:a_ls + NT * 8].bitcast(F32)
            a_rs = ualloc(NT * 8); rsa = U[:, a_rs:a_rs + NT * 8].bitcast(F32)
            ssqa_b = ub("ssqa"); lnsa_b = ub("lnsa"); rsa_b = ub("rsa")
            lr_b = ub("lr"); fin_b = ub("fin"); xcs_b = [ub("xcs") for _ in range(NT)]; og_b = [ub("og") for _ in range(NT)]
            q_b = ub("q"); k_b = ub("k"); kt_b = ub("kt"); vt_b = ub("vt"); sg_b = ub("sg")
            qe_b = [ub("qef"), ub("qeb")]; ke_b = [ub("kef"), ub("keb")]
            sprev_b = [ub("spf"), ub("spb")]; S_b = [ub("Sf"), ub("Sb")]
            hall = lambda m: [hT_b[kc][m] for kc in range(8)]

            T.op("dve", lambda: nc.vector.memset(lrT[32:64, :], 0.0), writes=[lr_b])
            T.op("dve", lambda: nc.vector.memset(lrT[32:33, :], 1.0), writes=[lr_b])

            sl, slb = wget("evC")
            w = v3(sl, 8, 288)
            for m in range(NM):
                ps, pb = PS()
                mmg(ps[0:32, :], [(w[:, kc, 0:32], hT[:, kc, msl(m)]) for kc in range(8)], [slb] + hall(m), [pb])
                cp(lrT[0:32, msl(m)], ps[0:32, :], [pb], [lr_b], eng="act")
                for c2 in range(2):
                    ps, pb = PS()
                    mmg(ps[:], [(w[:, kc, 32 + c2 * 128:32 + (c2 + 1) * 128], hT[:, kc, msl(m)]) for kc in range(8)],
                        [slb] + hall(m), [pb])
                    cp(finT[:, c2, msl(m)], ps[:], [pb], [fin_b], eng="dve")
            for j in range(NT):
                for c2 in range(2):
                    ps, pb = PS()
                    mmg(ps[:, 0:256], [(finT[:, c2, tsl(j)], cs64[:])], [fin_b, CB], [pb])
                    cp(xcs[:, j, c2, :], ps[:, 0:256], [pb], [xcs_b[j]], eng=("act" if c2 == 0 else "dve"))

            stage("evC_%d" % ph)
            o2 = [0]

            def h2alloc(n):
                a = o2[0]; o2[0] += n
                assert o2[0] <= 6144
                return H2[:, a:a + n]

            hs = [dict(qT=qT, kT=kT, ktok=ktok, vtok=vtok, sgtok=sgtok, q=q_b, k=k_b, kt=kt_b, vt=vt_b, sg=sg_b),
                  dict(qT=h2alloc(TT), kT=h2alloc(TT),
                       ktok=h2alloc(NT * 128).rearrange("p (j n) -> p j n", j=NT),
                       vtok=h2alloc(NT * 192).rearrange("p (j n) -> p j n", j=NT),
                       sgtok=h2alloc(NT * 192).rearrange("p (j n) -> p j n", j=NT), **hs1_b)]

            def proj_groups(h):
                H = hs[h % 2]
                st = {}

                def getw():
                    if "w" not in st:
                        sl, slb = wget("evH%d" % h)
                        st["w"] = v3(sl, 8, 640); st["b"] = slb
                    return st["w"], st["b"]

                def gq(m):
                    w, slb = getw()
                    ps, pb = PS()
                    mmg(ps[:], [(w[:, kc, 0:128], hT[:, kc, msl(m)]) for kc in range(8)], [slb] + hall(m), [pb])
                    cp(H["qT"][:, msl(m)], ps[:], [pb], [H["q"]], eng="act")

                def gk(m):
                    w, slb = getw()
                    ps, pb = PS()
                    mmg(ps[:], [(w[:, kc, 128:256], hT[:, kc, msl(m)]) for kc in range(8)], [slb] + hall(m), [pb])
                    cp(H["kT"][:, msl(m)], ps[:], [pb], [H["k"]], eng="dve")

                def gt(j):
                    w, slb = getw()
                    ps, pb = PS()
                    mmg(ps[:], [(hT[:, kc, tsl(j)], w[:, kc, 128:640]) for kc in range(8)], [slb] + hall(j // 4), [pb])
                    cp(H["ktok"][:, j, :], ps[:, 0:128], [pb], [H["kt"]], eng="dve")
                    cp(H["vtok"][:, j, :], ps[:, 128:320], [pb], [H["vt"]], eng="dve")
                    cp(H["sgtok"][:, j, :], ps[:, 320:512], [pb], [H["sg"]], eng="act")

                def gsilu():
                    act(H["sgtok"], H["sgtok"], AF.Silu, [H["sg"]], [H["sg"]])

                gl = []
                for m in range(NM):
                    gl.append(lambda m=m: gq(m)); gl.append(lambda m=m: gk(m))
                for j in range(NT):
                    gl.append(lambda j=j: gt(j))
                gl.append(gsilu)
                return gl

            for g in proj_groups(0):
                g()
            for h in range(4):
                H = hs[h % 2]
                if ph == PH_ADA:
                    ada_block(0, 2 + h)
                pend = proj_groups(h + 1) if h < 3 else []
                steps = []
                for (t0, n) in seqs:
                    fw = list(range(t0, t0 + n)); bw = fw[::-1]
                    for a, b in zip(fw, bw):
                        steps.append((t0 // 2, 0, a, a == fw[0], a == fw[-1]))
                        steps.append((t0 // 2, 1, b, b == bw[0], b == bw[-1]))

                def stA(p):
                    X = tr[p % 2]
                    ps, pb = PS()
                    for i in range(2):
                        si, d, j, first, lastt = steps[2 * p + i]
                        mmg(ps[:, i * 128:(i + 1) * 128], [(lrT[0:64, tsl(j)], w2aug[0:64, d, h * 128:(h + 1) * 128])], [lr_b, CB], [pb])
                    act(X["e1"], ps[:, 0:256], AF.Exp, [pb], [X["e1_b"]], scale=-1.0)
                    act(X["sp"], X["e1"], AF.Ln, [X["e1_b"], CB], [X["sp_b"]], bias=cst[:, 1:2])

                def stB(p):
                    X = tr[p % 2]
                    ps, pb = PS()
                    for i in range(2):
                        si, d, j, first, lastt = steps[2 * p + i]
                        spi = X["sp"][:, i * 128:(i + 1) * 128]
                        mmg(ps[:, i * 128:(i + 1) * 128], [(tri[:, 2 + d, :], spi)], [X["sp_b"], CB], [pb])
                        mmg(ps[:, 256 + i * 128:256 + (i + 1) * 128], [(spi, tri[:, d, :])], [X["sp_b"], CB], [pb])
                    act(X["ekd"], ps[:, 0:256], AF.Exp, [pb], [X["ekd_b"]], scale=-1.0)
                    act(X["eb"], ps[:, 256:512], AF.Exp, [pb], [X["eb_b"]], scale=-1.0)
                    act(enbx[p % 2][:], ps[:, 256:512], AF.Exp, [pb], [enbx_b[p % 2]])

                def stC1(k):
                    si, d, j, first, lastt = steps[k]
                    X = tr[(k // 2) % 2]
                    i = k % 2
                    cs_ = slice(i * 128, (i + 1) * 128)
                    tt(X["kd"][:, cs_], H["ktok"][:, j, :], X["ekd"][:, cs_], ALU.mult, [H["kt"], X["ekd_b"]], [X["kd_b"]])
                    stt(qe[d][:, tsl(j)], H["qT"][:, tsl(j)], 128.0 ** -0.5, X["eb"][:, cs_], ALU.mult, ALU.mult,
                        [H["q"], X["eb_b"]], [qe_b[d]])
                    tt(ke[d][:, tsl(j)], H["kT"][:, tsl(j)], enbx[(k // 2) % 2][:, cs_], ALU.mult, [H["k"], enbx_b[(k // 2) % 2]], [ke_b[d]])

                def stC2(k):
                    si, d, j, first, lastt = steps[k]
                    X = tr[(k // 2) % 2]
                    i = k % 2
                    cs_ = slice(i * 128, (i + 1) * 128)
                    last = i * 128 + (127 if d == 0 else 0)
                    if first:
                        if ph == 0:
                            T.op("dve", lambda d=d: nc.vector.memset(Sst[d], 0.0), writes=[S_b[d]])
                        else:
                            T.dma("sp", Sst[d], s0_d[d][h], S_b[d], writes=[S_b[d]])
                    ps, pb = PS()
                    mmg(ps[:, 0:192], [(X["kd"][:, cs_], H["vtok"][:, j, :])], [X["kd_b"], H["vt"]], [pb])
                    cp(sprev[d][:, j, :], Sst[d], [S_b[d]], [sprev_b[d]], eng="act")
                    stt(Sst[d], Sst[d], X["eb"][:, last:last + 1], ps[:, 0:192], ALU.mult, ALU.add,
                        [S_b[d], X["eb_b"], pb], [S_b[d]])
                    if lastt and ph == 0:
                        T.dma("sp", ns_d[d][si, h], Sst[d], S_b[d], reads=[S_b[d]])

                def popg():
                    if len(pend) > 1:
                        pend.pop(0)()

                npair = len(steps) // 2
                nit = npair + 2
                for i in range(nit):
                    npop = -(-(len(pend) - 1) // (nit - i)) if len(pend) > 1 else 0
                    if 0 <= i - 2 < npair:
                        stC1(2 * (i - 2)); stC1(2 * (i - 2) + 1)
                    if i < npair:
                        stA(i)
                    if npop >= 1:
                        popg()
                    if 0 <= i - 1 < npair:
                        stB(i - 1)
                    for _ in range(max(0, npop - 1)):
                        popg()
                    if 0 <= i - 2 < npair:
                        stC2(2 * (i - 2)); stC2(2 * (i - 2) + 1)
                stage("h%d_p1_%d" % (h, ph))

                def p2a(j):
                    X = tr[j % 2]
                    ps, pb = PS()
                    mmg(ps[:, 0:128], [(ke[0][:, tsl(j)], qe[0][:, tsl(j)])], [ke_b[0], qe_b[0]], [pb])
                    mmg(ps[:, 128:256], [(ke[1][:, tsl(j)], qe[1][:, tsl(j)])], [ke_b[1], qe_b[1]], [pb])
                    tt(X["att"], ps[:, 0:256], mask2[:], ALU.mult, [pb, CB], [X["att_b"]])

                def p2b(j):
                    X = tr[j % 2]
                    ps, pb = PS()
                    mmg(ps[:, 0:192], [(X["att"][:, 0:128], H["vtok"][:, j, :]), (X["att"][:, 128:256], H["vtok"][:, j, :]),
                                       (qe[0][:, tsl(j)], sprev[0][:, j, :]), (qe[1][:, tsl(j)], sprev[1][:, j, :])],
                        [X["att_b"], H["vt"], qe_b[0], qe_b[1], sprev_b[0], sprev_b[1]], [pb])
                    act(X["junk"], ps[:, 0:192], AF.Square, [pb], [X["junk_b"], ssqa_b], scale=192.0 ** -0.5,
                        accum_out=ssqa[:, j * 4 + h:j * 4 + h + 1])
                    tt(og[:, j, h * 192:(h + 1) * 192], ps[:, 0:192], H["sgtok"][:, j, :], ALU.mult, [pb, H["sg"]], [og_b[j]])

                p2a(0)
                for j in range(NT):
                    if j + 1 < NT:
                        p2a(j + 1)
                    p2b(j)
                    if len(pend) > 1:
                        pend.pop(0)()
                while pend:
                    pend.pop(0)()

            act(lnsa, ssqa, AF.Ln, [ssqa_b, CB], [lnsa_b], bias=cst[:, 0:1])
            act(rsa, lnsa, AF.Exp, [lnsa_b], [rsa_b], scale=-0.5)
            og3 = U[:, a_og:a_og + NT * 768].rearrange("p (g e) -> p g e", e=192)
            tt(og3, og3, rsa.unsqueeze(2).to_broadcast([128, NT * 4, 192]), ALU.mult, [rsa_b] + og_b, og_b)
            stage("gla_%d" % ph)
            for j in range(NT):
                for c in range(6):
                    T.op("pe", lambda j=j, c=c: nc.tensor.transpose(psb[:, c * 128:(c + 1) * 128], og[:, j, c * 128:(c + 1) * 128], identb[:]),
                         reads=[og_b[j], CB], writes=[psb_b], sig=(c == 5))
                tt(hT[:, 0:6, tsl(j)], psb[:, 0:768].rearrange("p (c t) -> p c t", c=6),
                   pp[:, PP_NG:PP_NG + 6].unsqueeze(2).to_broadcast([128, 6, 128]), ALU.mult,
                   [psb_b, CB], [hT_b[c][j // 4] for c in range(6)])
            if ph == 0:
                sl, slb = wget("dft")
                dv = sl[:, 0:1024].rearrange("p (j k n) -> p j k n", j=2, k=2)
                for (t0, n) in seqs:
                    for c2 in range(2):
                        ps, pb = PS()
                        pairs = []
                        for jj in range(2):
                            pairs.append((xcs[:, t0 + jj, c2, 0:128], dv[:, jj, 0, :]))
                            pairs.append((xcs[:, t0 + jj, c2, 128:256], dv[:, jj, 1, :]))
                        mmg(ps[:, 0:256], pairs, [slb, xcs_b[t0], xcs_b[t0 + 1]], [pb])
                        cp(hT[:, 6 + c2, t0 * 128:t0 * 128 + 256], ps[:, 0:256], [pb], [hT_b[6 + c2][0]],
                           eng=("act" if c2 == 0 else "dve"))
            else:
                slc, slcb = wget("dftc")
                sls, slsb = wget("dfts", ahead=NSLOT - 2)
                dc = v3(slc, 8, 1024); ds = v3(sls, 8, 1024)
                for c2 in range(2):
                    for m in range(2):
                        ps, pb = PS()
                        pairs = []
                        for jj in range(8):
                            pairs.append((xcs[:, jj, c2, 0:128], dc[:, jj, msl(m)]))
                            pairs.append((xcs[:, jj, c2, 128:256], ds[:, jj, msl(m)]))
                        mmg(ps[:], pairs, [slcb, slsb] + xcs_b, [pb])
                        cp(hT[:, 6 + c2, msl(m)], ps[:], [pb], [hT_b[6 + c2][m]], eng=("act" if c2 == 0 else "dve"))
            stage("mix0_%d" % ph)
            dump("mix0_%d" % ph, hT[:, :, 0:TT], [hT_b[c][m] for c in range(8) for m in range(NM)])
            outproj_resid("evO", 0, 16)
            dump("x_l0mix_%d" % ph, xT[:, :, 0:TT], [xT_b[c][m] for c in range(8) for m in range(NM)])
            stage("out0_%d" % ph)
            norm(0, 1)
            ffn(0)
            stage("ffn0_%d" % ph)
            dump("x_l0_%d" % ph, xT[:, :, 0:TT], [xT_b[c][m] for c in range(8) for m in range(NM)])

            norm(1, 0)
            uT = U[:, 0:4 * TT].rearrange("p (c t) -> p c t", c=4)
            v1 = U[:, 4096:4096 + NT * 512].rearrange("p (j n) -> p j n", j=NT)
            gbT = U[:, 8192:8192 + 4 * TT].rearrange("p (c t) -> p c t", c=4)
            gcT = U[:, 12288:12288 + 4 * TT].rearrange("p (c t) -> p c t", c=4)
            zT = U[:, 24576:32768].bitcast(F32).rearrange("p (c t) -> p c t", c=4)
            u_b = ub("uT"); v1_b = ub("v1"); gb_b = ub("gb"); gc_b = ub("gc"); z_b = ub("z")
            accs = [U[:, 16384 + i * 2048:16384 + (i + 1) * 2048].bitcast(F32) for i in range(4)]
            acc_b = [ub("acc%d" % i) for i in range(4)]
            sl, slb = wget("odG")
            w = v3(sl, 8, 1024)
            for m in range(NM):
                for oc in range(8):
                    ps, pb = PS()
                    mmg(ps[:], [(w[:, kc, oc * 128:(oc + 1) * 128], hT[:, kc, msl(m)]) for kc in range(8)], [slb] + hall(m), [pb])
                    if oc < 4:
                        cp(gbT[:, oc, msl(m)], ps[:], [pb], [gb_b], eng="act")
                    else:
                        cp(gcT[:, oc - 4, msl(m)], ps[:], [pb], [gc_b], eng="dve")
            sl, slb = wget("odH")
            w = v3(sl, 8, 512)
            for oc in range(4):
                for m in range(NM):
                    ps, pb = PS()
                    mmg(ps[:], [(w[:, kc, oc * 128:(oc + 1) * 128], hT[:, kc, msl(m)]) for kc in range(8)], [slb] + hall(m), [pb])
                    tt(zT[:, oc, msl(m)], ps[:], gcT[:, oc, msl(m)], ALU.mult, [pb, gc_b], [z_b])
            RW = 256 if ph == 0 else 64
            for oc in range(4):
                acc = accs[oc][:, 0:TT]; ab = acc_b[oc]
                z = zT[:, oc, 0:TT]
                cw = lambda k: pp[:, PP_CONV + oc * 3 + k:PP_CONV + oc * 3 + k + 1]
                ts(acc, z, cw(1), None, ALU.mult, None, [z_b, CB], [ab])
                a3 = acc.rearrange("p (r w) -> p r w", w=RW)
                z3 = z.rearrange("p (r w) -> p r w", w=RW)
                stt(a3[:, :, 1:RW], z3[:, :, 0:RW - 1], cw(0), a3[:, :, 1:RW], ALU.mult, ALU.add, [z_b, CB, ab], [ab])
                stt(a3[:, :, 0:RW - 1], z3[:, :, 1:RW], cw(2), a3[:, :, 0:RW - 1], ALU.mult, ALU.add, [z_b, CB, ab], [ab])
            sl, slb = wget("odUV")
            w = v3(sl, 8, 1024)
            for oc in range(4):
                for m in range(NM):
                    ps, pb = PS()
                    mmg(ps[:], [(w[:, kc, oc * 128:(oc + 1) * 128], hT[:, kc, msl(m)]) for kc in range(8)], [slb] + hall(m), [pb])
                    act(uT[:, oc, msl(m)], ps[:], AF.Gelu_apprx_tanh, [pb], [u_b])
            for j in range(NT):
                ps, pb = PS()
                mmg(ps[:], [(hT[:, kc, tsl(j)], w[:, kc, 512:1024]) for kc in range(8)], [slb] + hall(j // 4), [pb])
                act(v1[:, j, :], ps[:], AF.Gelu_apprx_tanh, [pb], [v1_b])
            for j in range(NT):
                ps, pb = PS()
                for g in range(4):
                    T.op("pe", lambda g=g, ps=ps, j=j: nc.tensor.matmul(ps[:, g * 128:(g + 1) * 128], v1[:, j, g * 128:(g + 1) * 128], wsT[:, g, :], start=True, stop=False),
                         reads=[v1_b, CB], writes=[pb], sig=False)
                    T.op("pe", lambda g=g, ps=ps: nc.tensor.matmul(ps[:, g * 128:(g + 1) * 128], onesrow[0:1, :], gbs[0:1, g * 128:(g + 1) * 128], start=False, stop=True),
                         reads=[CB], writes=[pb], sig=(g == 3))
                tt(hT[:, 0:4, tsl(j)], ps[:].rearrange("p (g t) -> p g t", g=4), uT[:, :, tsl(j)], ALU.mult,
                   [pb, u_b], [hT_b[c][j // 4] for c in range(4)])
            for oc in range(4):
                tt(hT[:, 4 + oc, 0:TT], accs[oc][:, 0:TT], gbT[:, oc, 0:TT], ALU.mult, [acc_b[oc], gb_b], [hT_b[4 + oc][m] for m in range(NM)])
            stage("mix1_%d" % ph)
            outproj_resid("odO", 1, 16)
            dump("x_l1mix_%d" % ph, xT[:, :, 0:TT], [xT_b[c][m] for c in range(8) for m in range(NM)])
            norm(1, 1)
            ffn(1)
            stage("ffn1_%d" % ph)

            yst = [U[:, 0:2048].bitcast(F32), U[:, 2048:4096].bitcast(F32)]
            yst_b = [ub("yst0"), ub("yst1")]; nb["sq"] = ub("sq")
            for m in range(NM):
                sq = U[:, 20480:24576].rearrange("p (c t) -> p c t", c=8)
                xr = [xT_b[c][m] for c in range(8)]
                act(sq, xT[:, :, msl(m)], AF.Square, xr, [nb["sq"]])
                psr, pbr = PS()
                for jj in range(4):
                    mmg(psr[:, jj:jj + 1], [(sq[:, c, jj * 128:(jj + 1) * 128], onesb[:, 0:1]) for c in range(8)], [nb["sq"], CB], [pbr])
                act(rtok[:, 0:4], psr[:, 0:4], AF.Ln, [pbr, CB], [rtok_b], bias=cst[:, 0:1])
                act(rtok[:, 4:8], rtok[:, 0:4], AF.Exp, [rtok_b], [rtok_b], scale=-0.5)
                for jj in range(4):
                    j = m * 4 + jj
                    ys, yb = yst[j % 2], yst_b[j % 2]
                    for hf in range(2):
                        ps, pb = PS()
                        for k in range(4):
                            c = hf * 4 + k
                            T.op("pe", lambda ps=ps, k=k, c=c, j=j: nc.tensor.transpose(ps[:, k * 128:(k + 1) * 128], xT[:, c, j * 128:(j + 1) * 128], ident[:]),
                                 reads=[xT_b[c][m], CB], writes=[pb], sig=(k == 3))
                        stt(ys[:, hf * 512:(hf + 1) * 512], ps[:], rtok[:, 4 + jj:5 + jj], gfinb[:, hf * 512:(hf + 1) * 512], ALU.mult, ALU.mult,
                            [pb, rtok_b, CB], [yb])
                    T.dma("sp", y_d[ph][j * 128:(j + 1) * 128, :], ys, yb, reads=[yb])

        T.barrier(engines=("pe", "act", "dve", "sp", "pool"))
    return nc


_CACHE = {}


def _get_nc():
    if "nc" not in _CACHE:
        _CACHE["nc"] = build()
        _CACHE["consts"] = _consts()
    return _CACHE["nc"], _CACHE["consts"]


def make_in_maps(inputs, consts):
    f = lambda a: np.ascontiguousarray(np.asarray(a, dtype=np.float32))
    x_prompt = f(inputs["x_prompt"]); x_sample = f(inputs["x_sample"])
    sf = f(inputs["state_gla_fwd"]); sbw = f(inputs["state_gla_bwd"])
    c = f(inputs["c"]); c_ctx = f(inputs["c_ctx"])
    pp = np.zeros((128, PP_N), np.float32)
    for l in range(2):
        pp[:, PP_GMIX + l * 8:PP_GMIX + (l + 1) * 8] = f(inputs["norm_mix_g"])[l].reshape(8, 128).T
        pp[:, PP_GFFN + l * 8:PP_GFFN + (l + 1) * 8] = f(inputs["norm_ffn_g"])[l].reshape(8, 128).T
        pp[:, PP_ADAB + l * 48:PP_ADAB + (l + 1) * 48] = f(inputs["ada_b"])[l].reshape(48, 128).T
    pp[:, PP_CONV:PP_CONV + 12] = f(inputs["conv_w"])[0].T.reshape(4, 128, 3).transpose(1, 0, 2).reshape(128, 12)
    pp[:, PP_NG:PP_NG + 6] = np.tile(f(inputs["gla_norm_g"])[0], 4).reshape(6, 128).T
    w2aug = np.zeros((2, 64, 512), np.float32)
    w2aug[0, 0:16] = f(inputs["gla_w2_f"])[0]
    w2aug[0, 32] = f(inputs["gla_b2_f"])[0]
    w2aug[1, 16:32] = f(inputs["gla_w2_b"])[0]
    w2aug[1, 32] = f(inputs["gla_b2_b"])[0]
    shared = dict(
        pp=pp, w2aug=w2aug, gfin=f(inputs["final_norm_g"]).reshape(1, D),
        gws=f(inputs["gmlp_ws"])[0], gbs=f(inputs["gmlp_b"])[0].reshape(1, 512),
        ada_w=f(inputs["ada_w"]), ffn_w1=f(inputs["ffn_w1"]), ffn_w2=f(inputs["ffn_w2"]),
        ev_w_in=f(inputs["ev_w_in"])[0], ev_w_out=f(inputs["ev_w_out"])[0],
        od_w_in=f(inputs["od_w_in"])[0], od_w_out=f(inputs["od_w_out"])[0],
        ident=consts["ident"], tri=consts["tri"], mask2=consts["mask2"], cs64=consts["cs64"],
        dftc256=consts["dftc256"], dfts256=consts["dfts256"],
        dftc1024=consts["dftc1024"], dfts1024=consts["dfts1024"],
    )
    maps = []
    for i in range(NCORES):
        cond = np.stack([c_ctx, c[i]], axis=0)
        m = dict(shared)
        m["xp"] = x_prompt[2 * i:2 * i + 2].reshape(512, D)
        m["xs"] = x_sample[i]
        m["s0f"] = sf[i, 0]
        m["s0b"] = sbw[i, 0]
        m["condT"] = np.ascontiguousarray(cond.reshape(2, 8, 128).transpose(2, 1, 0))
        maps.append(m)
    return maps


def kernel(**inputs):
    nc, consts = _get_nc()
    maps = make_in_maps(inputs, consts)
    res = run_bass_kernel_spmd(nc, maps, core_ids=list(range(NCORES)))
    r = res.results
    y_prompt = np.concatenate([r[i]["yp"].reshape(2, 256, D) for i in range(NCORES)], axis=0)
    y_sample = np.stack([r[i]["ys"] for i in range(NCORES)], axis=0)
    nsf = np.concatenate([r[i]["nsf"].reshape(2, 1, 4, 128, 192) for i in range(NCORES)], axis=0)
    nsb = np.concatenate([r[i]["nsb"].reshape(2, 1, 4, 128, 192) for i in range(NCORES)], axis=0)
    return (y_prompt.astype(np.float32), y_sample.astype(np.float32),
            nsf.astype(np.float32), nsb.astype(np.float32))
```

```python
import contextlib
import numpy as np
import concourse.bass as bass
import concourse.mybir as mybir
from concourse.bass_utils import run_bass_kernel_spmd

F32 = mybir.dt.float32
BF16 = mybir.dt.bfloat16
AF = mybir.ActivationFunctionType
ALU = mybir.AluOpType

NCORES = 8
D = 1024
NSLOT = 3
EPS = 1e-6


class Buf:
    _n = 0

    def __init__(self, name, excl=False):
        Buf._n += 1
        self.id = Buf._n
        self.name = name
        self.excl = excl
        self.w = {}
        self.r = {}
        self.dsem = None
        self.dcnt = 0


class Tracker:
    def __init__(self, nc, es):
        self.nc = nc
        self.es = es
        self.eng = {}
        for k, h in (("pe", nc.tensor), ("act", nc.scalar), ("dve", nc.vector),
                     ("pool", nc.gpsimd), ("sp", nc.sync)):
            sem = es.enter_context(nc.semaphore("sem_" + k))
            self.eng[k] = dict(h=h, sem=sem, cnt=0, seen={}, key=k)
        self.owners = []
        self.dead = False

    def _collect(self, e, reads, writes):
        need = {}

        def add(key, sem, val):
            if key == "pe" and e["key"] == "pe":
                return
            if e["seen"].get(key, 0) >= val:
                return
            if key not in need or need[key][1] < val:
                need[key] = (sem, val)

        for b in reads:
            for key, (sem, val) in b.w.items():
                add(key, sem, val)
            if b.excl:
                for key, (sem, val) in b.r.items():
                    if key != e["key"]:
                        add(key, sem, val)
        for b in writes:
            for key, (sem, val) in b.w.items():
                add(key, sem, val)
            for key, (sem, val) in b.r.items():
                add(key, sem, val)
        return need

    def _emit(self, e, need, fn):
        items = list(need.items())
        for key, (sem, val) in items[:-1]:
            e["h"].wait_ge(sem, val)
            e["seen"][key] = val
        ins = fn()
        if items:
            key, (sem, val) = items[-1]
            ins._wait_ge(sem, val)
            e["seen"][key] = val
        return ins

    @staticmethod
    def _rec(d, key, sem, val):
        old = d.get(key)
        if old is None or old[1] < val:
            d[key] = (sem, val)

    def op(self, ek, fn, reads=(), writes=(), sig=True):
        if self.dead:
            return None
        e = self.eng[ek]
        need = self._collect(e, reads, writes)
        ins = self._emit(e, need, fn)
        if sig:
            e["cnt"] += 1
            ins.then_inc(e["sem"], 1)
            val = e["cnt"]
        else:
            val = e["cnt"] + 1
        for b in reads:
            self._rec(b.r, ek, e["sem"], val)
        for b in writes:
            self._rec(b.w, ek, e["sem"], val)
        return ins

    def dma(self, qk, out, in_, owner, reads=(), writes=()):
        if self.dead:
            return None
        e = self.eng[qk]
        if owner.dsem is None:
            owner.dsem = self.es.enter_context(self.nc.semaphore("dsem_%d" % owner.id))
            self.owners.append(owner)
        need = self._collect(e, reads, writes)
        ins = self._emit(e, need, lambda: e["h"].dma_start(out=out, in_=in_))
        owner.dcnt += 16
        ins.then_inc(owner.dsem, 16)
        key = "dma%d" % owner.id
        for b in reads:
            self._rec(b.r, key, owner.dsem, owner.dcnt)
        for b in writes:
            self._rec(b.w, key, owner.dsem, owner.dcnt)
        return ins

    def barrier(self, engines=("pe", "act", "dve", "sp")):
        if self.dead:
            return
        targets = [(k, e["sem"], e["cnt"]) for k, e in self.eng.items() if e["cnt"] > 0]
        targets += [("dma%d" % b.id, b.dsem, b.dcnt) for b in self.owners if b.dcnt > 0]
        for ek in engines:
            e = self.eng[ek]
            for key, sem, val in targets:
                if key == ek and ek == "pe":
                    continue
                if e["seen"].get(key, 0) >= val:
                    continue
                e["h"].wait_ge(sem, val)
                e["seen"][key] = val

    def wait_all(self, ek, bufs):
        e = self.eng[ek]
        need = self._collect(e, [], bufs)
        for key, (sem, val) in need.items():
            e["h"].wait_ge(sem, val)
            e["seen"][key] = val


def _consts():
    c = {}
    c["ident"] = np.eye(128, dtype=np.float32)
    s = np.arange(128)[:, None]
    t = np.arange(128)[None, :]
    tri = np.zeros((128, 4, 128), np.float32)
    tri[:, 0, :] = (s <= t) / 16.0
    tri[:, 1, :] = (s >= t) / 16.0
    tri[:, 2, :] = (s > t) / 16.0
    tri[:, 3, :] = (s < t) / 16.0
    c["tri"] = tri
    m2 = np.zeros((128, 256), np.float32)
    m2[:, 0:128] = (s <= t)
    m2[:, 128:256] = (s >= t)
    c["mask2"] = m2
    cs = np.zeros((128, 256), np.float32)
    a = np.arange(64)
    ang = 2 * np.pi * ((a[:, None] * a[None, :]) % 64) / 64.0
    for g in range(2):
        cs[g * 64:(g + 1) * 64, g * 64:(g + 1) * 64] = np.cos(ang)
        cs[g * 64:(g + 1) * 64, 128 + g * 64:128 + (g + 1) * 64] = np.sin(ang)
    c["cs64"] = cs
    for L in (256, 1024):
        tt = np.arange(L, dtype=np.int64)
        ang = 2 * np.pi * ((tt[:, None] * tt[None, :]) % L).astype(np.float64) / L
        sc = 1.0 / np.sqrt(64.0 * L)
        c["dftc%d" % L] = (np.cos(ang) * sc).astype(np.float32)
        c["dfts%d" % L] = (-np.sin(ang) * sc).astype(np.float32)
    return c


PP_GMIX, PP_GFFN, PP_ADAB, PP_CONV, PP_NG, PP_N = 0, 16, 32, 128, 140, 146


def build(debug_dump=None, stop_at=None):
    nc = bass.Bass("TRN2", target_bir_lowering=False)
    dbg = {}

    def din(name, shape):
        return nc.dram_tensor(name, list(shape), F32, kind="ExternalInput").ap()

    def dout(name, shape):
        return nc.dram_tensor(name, list(shape), F32, kind="ExternalOutput").ap()

    xin_d = [din("xp", [512, D]), din("xs", [1024, D])]
    s0_d = [din("s0f", [4, 128, 192]), din("s0b", [4, 128, 192])]
    condT_d = din("condT", [128, 8, 2])
    pp_d = din("pp", [128, PP_N])
    w2aug_d = din("w2aug", [2, 64, 512])
    gfin_d = din("gfin", [1, D])
    gws_d = din("gws", [4, 128, 128])
    gbs_d = din("gbs", [1, 512])
    ada_w = din("ada_w", [2, D, 6 * D])
    ffn_w1 = din("ffn_w1", [2, D, 4 * D])
    ffn_w2 = din("ffn_w2", [2, 4 * D, D])
    ev_w_in = din("ev_w_in", [D, 2848])
    ev_w_out = din("ev_w_out", [D, D])
    od_w_in = din("od_w_in", [D, 2560])
    od_w_out = din("od_w_out", [D, D])
    ident_d = din("ident", [128, 128])
    tri_d = din("tri", [128, 4, 128])
    mask2_d = din("mask2", [128, 256])
    cs64_d = din("cs64", [128, 256])
    dft256_d = [din("dftc256", [256, 256]), din("dfts256", [256, 256])]
    dft1024_d = [din("dftc1024", [1024, 1024]), din("dfts1024", [1024, 1024])]

    y_d = [dout("yp", [512, D]), dout("ys", [1024, D])]
    ns_d = [dout("nsf", [2, 4, 128, 192]), dout("nsb", [2, 4, 128, 192])]

    with contextlib.ExitStack() as es:
        T = Tracker(nc, es)

        def stage(name):
            if stop_at is not None and name == stop_at and not T.dead:
                T.barrier(engines=("pe", "act", "dve", "sp", "pool"))
                T.dead = True

        def sb(name, shape, dt=F32):
            return es.enter_context(nc.sbuf_tensor("sb_" + name, list(shape), dt))

        TMAX = 1024
        xT = sb("xT", [128, 8, TMAX])
        hT = sb("hT", [128, 8, TMAX], BF16)
        slots = [sb("slot%d" % i, [128, 8192], BF16) for i in range(NSLOT)]
        slot_b = [Buf("slot%d" % i) for i in range(NSLOT)]
        U = sb("U", [128, 32768], BF16)
        H2 = sb("H2", [128, 6144], BF16)
        enbx = [sb("enb%d" % i, [128, 256]) for i in range(2)]
        enbx_b = [Buf("enb0"), Buf("enb1")]
        hs1_b = dict(q=Buf("q1"), k=Buf("k1"), kt=Buf("kt1"), vt=Buf("vt1"), sg=Buf("sg1"))
        ident = sb("ident", [128, 128]); identb = sb("identb", [128, 128], BF16)
        tri = sb("tri", [128, 4, 128], BF16)
        mask2 = sb("mask2", [128, 256], BF16)
        cs64 = sb("cs64", [128, 256], BF16)
        onesb = sb("onesb", [128, 128], BF16)
        onesrow = sb("onesrow", [1, 128])
        cst = sb("cst", [128, 4])
        pp = sb("pp", [128, PP_N])
        w2aug = sb("w2aug", [64, 2, 512], BF16)
        gfinb = sb("gfinb", [128, D])
        gws = sb("gws", [128, 4, 128])
        wsT = sb("wsT", [128, 4, 128], BF16)
        gbs = sb("gbs", [1, 512])
        gbs_hi = sb("gbs_hi", [1, 512], BF16); gbs_lo = sb("gbs_lo", [1, 512], BF16); gbs_t = sb("gbs_t", [1, 512])
        onesrow_b = sb("onesrow_b", [1, 128], BF16)
        condT = sb("condT", [128, 8, 2])
        scT = sb("scT", [128, 8, 2], BF16)
        modT = [sb("modT%d" % l, [128, 48, 2]) for l in range(2)]
        gscT = [sb("gscT%d" % l, [128, 2, 8, 2]) for l in range(2)]
        rstd = sb("rstd", [128, 512]); lnr = sb("lnr", [128, 512])
        sqc = [sb("sqc%d" % i, [128, 512], BF16) for i in range(2)]
        ntm = [sb("ntm%d" % i, [128, 512]) for i in range(2)]
        sqc_b = [Buf("sqc0"), Buf("sqc1")]; ntm_b = [Buf("ntm0"), Buf("ntm1")]
        relu_t = [sb("relu%d" % i, [128, 512], BF16) for i in range(2)]
        relu_b = [Buf("relu0"), Buf("relu1")]
        rtok = sb("rtok", [128, 8]);
        CB = Buf("consts")
        U_all = []

        def ub(name):
            b = Buf(name)
            for o in U_all:
                for dct in (o.w, o.r):
                    for key, (sem, val) in dct.items():
                        Tracker._rec(b.r, key, sem, val)
            U_all.append(b)
            return b
        xT_b = [[Buf("xT%d_%d" % (c, m)) for m in range(2)] for c in range(8)]
        hT_b = [[Buf("hT%d_%d" % (c, m)) for m in range(2)] for c in range(8)]
        rstd_b = Buf("rstd"); lnr_b = Buf("lnr"); rtok_b = Buf("rtok")
        mod_b = Buf("mod")

        psf = [es.enter_context(nc.psum_tensor("psf%d" % i, [128, 512], F32)) for i in range(7)]
        psf_b = [Buf("psf%d" % i, excl=True) for i in range(7)]
        psb = es.enter_context(nc.psum_tensor("psb", [128, 1024], BF16))
        psb_b = Buf("psb", excl=True)
        ps_rr = [0]

        def PS():
            i = ps_rr[0] % 7
            ps_rr[0] += 1
            return psf[i], psf_b[i]

        def mmg(out, pairs, R, W):
            n = len(pairs)
            for i, (l, r) in enumerate(pairs):
                T.op("pe", lambda l=l, r=r, i=i: nc.tensor.matmul(out, l, r, start=(i == 0), stop=(i == n - 1)),
                     reads=R, writes=W, sig=(i == n - 1))

        def mm_kc_outer(specs):
            n = len(specs[0][2])
            for kc in range(n):
                for (out, pb, prs) in specs:
                    l, r, R = prs[kc]
                    T.op("pe", lambda out=out, l=l, r=r, kc=kc: nc.tensor.matmul(out, l, r, start=(kc == 0), stop=(kc == n - 1)),
                         reads=R, writes=[pb], sig=(kc == n - 1))

        def act(out, in_, func, R, W, **kw):
            T.op("act", lambda: nc.scalar.activation(out=out, in_=in_, func=func, **kw), reads=R, writes=W)

        def tt(out, a, b, op, R, W, eng="dve"):
            h = nc.vector if eng == "dve" else nc.gpsimd
            T.op(eng, lambda: h.tensor_tensor(out=out, in0=a, in1=b, op=op), reads=R, writes=W)

        def stt(out, a, scalar, b, op0, op1, R, W):
            T.op("dve", lambda: nc.vector.scalar_tensor_tensor(out=out, in0=a, scalar=scalar, in1=b, op0=op0, op1=op1),
                 reads=R, writes=W)

        def ts(out, a, s1, s2, op0, op1, R, W):
            if s2 is None:
                T.op("dve", lambda: nc.vector.tensor_scalar(out=out, in0=a, scalar1=s1, scalar2=None, op0=op0), reads=R, writes=W)
            else:
                T.op("dve", lambda: nc.vector.tensor_scalar(out=out, in0=a, scalar1=s1, scalar2=s2, op0=op0, op1=op1), reads=R, writes=W)

        def cp(out, in_, R, W, eng="dve"):
            if eng == "act":
                T.op("act", lambda: nc.scalar.copy(out=out, in_=in_), reads=R, writes=W)
            else:
                T.op("dve", lambda: nc.vector.tensor_copy(out, in_), reads=R, writes=W)

        def dump(name, ap, R):
            if debug_dump is None or name not in debug_dump:
                return
            d = dout("dbg_" + name, list(ap.shape))
            b = Buf("dbg")
            T.dma("sp" if ap.dtype == F32 else "pool", d, ap, b, reads=R)
            dbg[name] = b

        def cload(dst, src):
            T.dma("sp", dst, src, CB, writes=[CB])

        cload(ident[:], ident_d)
        cload(condT[:], condT_d)
        cload(pp[:], pp_d)
        CBP = Buf("consts_pool")
        T.dma("pool", tri[:], tri_d, CBP, writes=[CB])
        T.dma("pool", w2aug[:], w2aug_d.rearrange("d k n -> k d n"), CBP, writes=[CB])
        T.dma("pool", mask2[:], mask2_d, CBP, writes=[CB])
        T.dma("pool", cs64[:], cs64_d, CBP, writes=[CB])
        T.op("dve", lambda: nc.vector.memset(onesb[:], 1.0 / 1024), writes=[CB])
        T.op("dve", lambda: nc.vector.memset(onesrow[:], 1.0), writes=[CB])
        T.op("dve", lambda: nc.vector.memset(cst[:, 0:1], EPS), writes=[CB])
        T.op("dve", lambda: nc.vector.memset(cst[:, 1:2], 1.0), writes=[CB])
        cp(identb[:], ident[:], [CB], [CB])
        act(scT[:], condT[:], AF.Silu, [CB], [CB])

        def late_consts():
            CB2 = Buf("consts_late")
            T.dma("act", gfinb[:], gfin_d.to_broadcast([128, D]), CB2, writes=[CB])
            T.dma("act", gws[:], gws_d.rearrange("g p q -> p g q"), CB2, writes=[CB])
            T.dma("act", gbs[:], gbs_d, CB2, writes=[CB])
            T.op("dve", lambda: nc.vector.memset(onesrow_b[:], 1.0), writes=[CB])
            cp(gbs_hi[:], gbs[:], [CB], [CB])
            tt(gbs_t[:], gbs[:], gbs_hi[:], ALU.subtract, [CB], [CB])
            cp(gbs_lo[:], gbs_t[:], [CB], [CB])
            for g in range(4):
                ps, pb = PS()
                T.op("pe", lambda g=g, ps=ps: nc.tensor.transpose(ps[:, 0:128], gws[:, g, :], ident[:]), reads=[CB], writes=[pb])
                cp(wsT[:, g, :], ps[:, 0:128], [pb], [CB])

        stage("consts")
        def v3(sl, k, n):
            return sl[:, 0:k * n].rearrange("p (k n) -> p k n", k=k)

        def rows(w2d):
            return w2d.rearrange("(k p) n -> p k n", p=128)

        WSEQ = []

        def wadd(name, fn):
            WSEQ.append((name, fn))

        def wada(l, b):
            wadd("ada%d_%d" % (l, b), lambda sl, l=l, b=b: [(v3(sl, 8, 1024), rows(ada_w[l, :, b * 1024:(b + 1) * 1024]))])

        PH_ORDER = (1, 0)
        PH_ADA = PH_ORDER[0]
        for ph in PH_ORDER:
            A = (ph == PH_ADA)
            for l in range(2):
                if l == 0:
                    if A:
                        wada(0, 0); wada(0, 1)
                    wadd("evC", lambda sl: [(v3(sl, 8, 288), rows(ev_w_in[:, 2560:2848]))])
                    for h in range(4):
                        def f(sl, h=h):
                            v = v3(sl, 8, 640)
                            return [(v[:, :, 0:128], rows(ev_w_in[:, h * 128:(h + 1) * 128])),
                                    (v[:, :, 128:256], rows(ev_w_in[:, 512 + h * 128:512 + (h + 1) * 128])),
                                    (v[:, :, 256:448], rows(ev_w_in[:, 1024 + h * 192:1024 + (h + 1) * 192])),
                                    (v[:, :, 448:640], rows(ev_w_in[:, 1792 + h * 192:1792 + (h + 1) * 192]))]
                        wadd("evH%d" % h, f)
                        if A:
                            wada(0, 2 + h)
                    if ph == 0:
                        def f(sl):
                            v = sl[:, 0:1024].rearrange("p (j k n) -> p j k n", j=2, k=2)
                            return [(v[:, :, 0, :], rows(dft256_d[0])), (v[:, :, 1, :], rows(dft256_d[1]))]
                        wadd("dft", f)
                    else:
                        wadd("dftc", lambda sl: [(v3(sl, 8, 1024), rows(dft1024_d[0]))])
                        wadd("dfts", lambda sl: [(v3(sl, 8, 1024), rows(dft1024_d[1]))])
                    wadd("evO", lambda sl: [(v3(sl, 8, 1024), rows(ev_w_out))])
                else:
                    wadd("odG", lambda sl: [(v3(sl, 8, 1024), rows(od_w_in[:, 1024:2048]))])
                    wadd("odH", lambda sl: [(v3(sl, 8, 512), rows(od_w_in[:, 2048:2560]))])
                    wadd("odUV", lambda sl: [(v3(sl, 8, 1024), rows(od_w_in[:, 0:1024]))])
                    wadd("odO", lambda sl: [(v3(sl, 8, 1024), rows(od_w_out))])
                for b in range(4):
                    wadd("w1_%d" % b, lambda sl, l=l, b=b: [(v3(sl, 8, 1024), rows(ffn_w1[l, :, b * 1024:(b + 1) * 1024]))])
                    if A and l == 0 and b < 2:
                        wada(1, b)
                    if A and l == 1 and b == 0:
                        wada(1, 5)
                for b in range(4):
                    wadd("w2_%d" % b, lambda sl, l=l, b=b: [(v3(sl, 32, 256), rows(ffn_w2[l, :, b * 256:(b + 1) * 256]))])
                    if A and l == 0 and b < 3:
                        wada(1, 2 + b)

        wstate = dict(issued=0, nxt=0)

        def wget(name, ahead=NSLOT - 1):
            i = wstate["nxt"]
            assert WSEQ[i][0] == name, (WSEQ[i][0], name)
            while wstate["issued"] < min(i + ahead + 1, len(WSEQ)):
                j = wstate["issued"]
                s = j % NSLOT
                for (o, src) in WSEQ[j][1](slots[s]):
                    T.dma("pool", o, src, slot_b[s], writes=[slot_b[s]])
                wstate["issued"] += 1
            wstate["nxt"] += 1
            return slots[i % NSLOT], slot_b[i % NSLOT]

        def ada_block(l, b):
            sl, slb = wget("ada%d_%d" % (l, b))
            w = v3(sl, 8, 1024)
            ps, pb = PS()
            for oc in range(8):
                mmg(ps[:, oc * 2:oc * 2 + 2],
                    [(w[:, kc, oc * 128:(oc + 1) * 128], scT[:, kc, :]) for kc in range(8)],
                    [slb, CB], [pb])
            a0 = PP_ADAB + l * 48 + b * 8
            tt(modT[l][:, b * 8:(b + 1) * 8, :], ps[:, 0:16].rearrange("p (c k) -> p c k", k=2),
               pp[:, a0:a0 + 8].unsqueeze(2).to_broadcast([128, 8, 2]), ALU.add, [pb, CB], [mod_b])
            if b in (1, 4):
                wh = 0 if b == 1 else 1
                g0 = (PP_GMIX if wh == 0 else PP_GFFN) + l * 8
                stt(gscT[l][:, wh, :, :], modT[l][:, b * 8:(b + 1) * 8, :], 1.0,
                    pp[:, g0:g0 + 8].unsqueeze(2).to_broadcast([128, 8, 2]), ALU.add, ALU.mult, [mod_b, CB], [mod_b])

        stage("ada")

        def modcol(l, ch, ci):
            return modT[l][:, ch, ci:ci + 1]

        for ph in PH_ORDER:
            ci = ph
            TT = 512 if ph == 0 else 1024
            NT = TT // 128
            NM = TT // 512
            seqs = [(0, 2), (2, 2)] if ph == 0 else [(0, 8)]
            msl = lambda m: slice(m * 512, (m + 1) * 512)
            tsl = lambda j: slice(j * 128, (j + 1) * 128)

            xin = [U[:, 0:2048].bitcast(F32), U[:, 2048:4096].bitcast(F32)]
            xin_b = [ub("xin0"), ub("xin1")]
            for j in range(NT):
                xi, xb = xin[j % 2], xin_b[j % 2]
                T.dma("sp", xi, xin_d[ph][j * 128:(j + 1) * 128, :], xb, writes=[xb])
                for hf in range(2):
                    ps, pb = PS()
                    for k in range(4):
                        c = hf * 4 + k
                        T.op("pe", lambda ps=ps, k=k, c=c, xi=xi: nc.tensor.transpose(ps[:, k * 128:(k + 1) * 128], xi[:, c * 128:(c + 1) * 128], ident[:]),
                             reads=[xb, CB], writes=[pb], sig=(k == 3))
                    cp(xT[:, hf * 4:hf * 4 + 4, tsl(j)], ps[:].rearrange("p (k t) -> p k t", k=4), [pb],
                       [xT_b[c][j // 4] for c in range(hf * 4, hf * 4 + 4)], eng=("act" if hf == 0 else "dve"))

            def norm(l, wh):
                shc = 0 if wh == 0 else 24
                if ph == PH_ADA and wh == 0 and l == 0:
                    ada_block(0, 0); ada_block(0, 1)
                for m in range(NM):
                    ps, pb = PS()
                    for c in range(8):
                        act(sqc[c % 2][:], xT[:, c, msl(m)], AF.Square, [xT_b[c][m]], [sqc_b[c % 2]])
                        T.op("pe", lambda c=c, ps=ps: nc.tensor.matmul(ps[:], onesb[:], sqc[c % 2][:], start=(c == 0), stop=(c == 7)),
                             reads=[sqc_b[c % 2], CB], writes=[pb], sig=True)
                    act(lnr[:], ps[:], AF.Ln, [pb, CB], [lnr_b], bias=cst[:, 0:1])
                    act(rstd[:], lnr[:], AF.Exp, [lnr_b], [rstd_b], scale=-0.5)
                    for c in range(8):
                        stt(ntm[c % 2][:], xT[:, c, msl(m)], gscT[l][:, wh, c, ci:ci + 1], rstd[:], ALU.mult, ALU.mult,
                            [xT_b[c][m], rstd_b, mod_b], [ntm_b[c % 2]])
                        act(hT[:, c, msl(m)], ntm[c % 2][:], AF.Identity, [ntm_b[c % 2], mod_b], [hT_b[c][m]],
                            bias=modcol(l, shc + c, ci))

            nb = {}

            def outproj_resid(name, l, gch):
                sl, slb = wget(name)
                w = v3(sl, 8, 1024)
                for oc in range(8):
                    for m in range(NM):
                        ps, pb = PS()
                        mmg(ps[:], [(w[:, kc, oc * 128:(oc + 1) * 128], hT[:, kc, msl(m)]) for kc in range(8)],
                            [slb] + [hT_b[kc][m] for kc in range(8)], [pb])
                        stt(xT[:, oc, msl(m)], ps[:], modcol(l, gch + oc, ci), xT[:, oc, msl(m)], ALU.mult, ALU.add,
                            [pb, mod_b], [xT_b[oc][m]])

            def ffn(l):
                aT = U[:, 0:32 * TT].rearrange("p (c t) -> p c t", c=32)
                aT_b = [[ub("aT") for m in range(2)] for c in range(32)]
                for b in range(4):
                    sl, slb = wget("w1_%d" % b)
                    w = v3(sl, 8, 1024)
                    order = [(oc, m) for m in range(NM) for oc in range(8)] if b == 0 else [(oc, m) for oc in range(8) for m in range(NM)]
                    pre = {}
                    if b == 0:
                        specs = []
                        for (oc, m) in order[:4]:
                            ps, pb = PS()
                            pre[(oc, m)] = (ps, pb)
                            specs.append((ps[:], pb, [(w[:, kc, oc * 128:(oc + 1) * 128], hT[:, kc, msl(m)], [slb, hT_b[kc][m]]) for kc in range(8)]))
                        mm_kc_outer(specs)
                    for (oc, m) in order:
                        ch = b * 8 + oc
                        if (oc, m) in pre:
                            ps, pb = pre[(oc, m)]
                        else:
                            ps, pb = PS()
                            mmg(ps[:], [(w[:, kc, oc * 128:(oc + 1) * 128], hT[:, kc, msl(m)]) for kc in range(8)],
                                [slb] + [hT_b[kc][m] for kc in range(8)], [pb])
                        r = relu_t[(ch * NM + m) % 2]; r_b = relu_b[(ch * NM + m) % 2]
                        act(r[:], ps[:], AF.Relu, [pb], [r_b])
                        tt(aT[:, ch, msl(m)], r[:], r[:], ALU.mult, [r_b], [aT_b[ch][m]])
                    if ph == PH_ADA and l == 0 and b < 2:
                        ada_block(1, b)
                    if ph == PH_ADA and l == 1 and b == 0:
                        ada_block(1, 5)
                for b in range(4):
                    sl, slb = wget("w2_%d" % b)
                    w = v3(sl, 32, 256)
                    for o2 in range(2):
                        oc = b * 2 + o2
                        for m in range(NM):
                            ps, pb = PS()
                            mmg(ps[:], [(w[:, kc, o2 * 128:(o2 + 1) * 128], aT[:, kc, msl(m)]) for kc in range(32)],
                                [slb] + [aT_b[kc][m] for kc in range(32)], [pb])
                            stt(xT[:, oc, msl(m)], ps[:], modcol(l, 40 + oc, ci), xT[:, oc, msl(m)], ALU.mult, ALU.add,
                                [pb, mod_b], [xT_b[oc][m]])
                    if ph == PH_ADA and l == 0 and b < 3:
                        ada_block(1, 2 + b)


            if ph == PH_ORDER[0]:
                late_consts()
            stage("load%d" % ph)
            dump("x_in_%d" % ph, xT[:, :, 0:TT], [xT_b[c][m] for c in range(8) for m in range(NM)])
            norm(0, 0)
            dump("h_in_%d" % ph, hT[:, :, 0:TT], [hT_b[c][m] for c in range(8) for m in range(NM)])
            stage("norm00_%d" % ph)
            o = [0]

            def ualloc(nelem_bf16):
                a = o[0]
                o[0] += nelem_bf16
                assert o[0] <= 32768, o[0]
                return a

            a_lr = ualloc(TT); lrT = U[:, a_lr:a_lr + TT]
            a_fin = ualloc(2 * TT); finT = U[:, a_fin:a_fin + 2 * TT].rearrange("p (c t) -> p c t", c=2)
            a_xcs = ualloc(NT * 512); xcs = U[:, a_xcs:a_xcs + NT * 512].rearrange("p (j c n) -> p j c n", j=NT, c=2)
            a_og = ualloc(NT * 768); og = U[:, a_og:a_og + NT * 768].rearrange("p (j n) -> p j n", j=NT)
            a_q = ualloc(TT); qT = U[:, a_q:a_q + TT]
            a_k = ualloc(TT); kT = U[:, a_k:a_k + TT]
            a_kt = ualloc(NT * 128); ktok = U[:, a_kt:a_kt + NT * 128].rearrange("p (j n) -> p j n", j=NT)
            a_vt = ualloc(NT * 192); vtok = U[:, a_vt:a_vt + NT * 192].rearrange("p (j n) -> p j n", j=NT)
            a_sg = ualloc(NT * 192); sgtok = U[:, a_sg:a_sg + NT * 192].rearrange("p (j n) -> p j n", j=NT)
            qe = []; ke = []; sprev = []; Sst = []
            for d in range(2):
                a = ualloc(TT); qe.append(U[:, a:a + TT])
                a = ualloc(TT); ke.append(U[:, a:a + TT])
                a = ualloc(NT * 192); sprev.append(U[:, a:a + NT * 192].rearrange("p (j n) -> p j n", j=NT))
                a = ualloc(384); Sst.append(U[:, a:a + 384].bitcast(F32))
            tr = []
            for par in range(2):
                dct = {}
                for nm, ne, dtp in (("e1", 512, F32), ("sp", 256, BF16), ("ekd", 512, F32), ("kd", 256, BF16),
                                    ("eb", 512, F32), ("att", 256, BF16), ("junk", 192, BF16)):
                    a = ualloc(ne)
                    v = U[:, a:a + ne]
                    dct[nm] = v.bitcast(F32) if dtp == F32 else v
                    dct[nm + "_b"] = ub(nm)
                tr.append(dct)
            a_ss = ualloc(NT * 8); ssqa = U[:, a_ss:a_ss + NT * 8].bitcast(F32)
            a_ls = ualloc(NT * 8); lnsa = U[:, a_ls:a_ls + NT * 8].bitcast(F32)
            a_rs = ualloc(NT * 8); rsa = U[:, a_rs:a_rs + NT * 8].bitcast(F32)
            ssqa_b = ub("ssqa"); lnsa_b = ub("lnsa"); rsa_b = ub("rsa")
            lr_b = ub("lr"); fin_b = ub("fin"); xcs_b = [ub("xcs") for _ in range(NT)]; og_b = [ub("og") for _ in range(NT)]
            q_b = ub("q"); k_b = ub("k"); kt_b = ub("kt"); vt_b = ub("vt"); sg_b = ub("sg")
            qe_b = [ub("qef"), ub("qeb")]; ke_b = [ub("kef"), ub("keb")]
            sprev_b = [ub("spf"), ub("spb")]; S_b = [ub("Sf"), ub("Sb")]
            hall = lambda m: [hT_b[kc][m] for kc in range(8)]

            T.op("dve", lambda: nc.vector.memset(lrT[32:64, :], 0.0), writes=[lr_b])
            T.op("dve", lambda: nc.vector.memset(lrT[32:33, :], 1.0), writes=[lr_b])

            sl, slb = wget("evC")
            w = v3(sl, 8, 288)
            for m in range(NM):
                psl, pbl = PS()
                psf2 = [PS(), PS()]
                if m == 0:
                    specs = [(psl[0:32, :], pbl, [(w[:, kc, 0:32], hT[:, kc, msl(0)], [slb, hT_b[kc][0]]) for kc in range(8)])]
                    for c2 in range(2):
                        specs.append((psf2[c2][0][:], psf2[c2][1],
                                      [(w[:, kc, 32 + c2 * 128:32 + (c2 + 1) * 128], hT[:, kc, msl(0)], [slb, hT_b[kc][0]]) for kc in range(8)]))
                    mm_kc_outer(specs)
                else:
                    mmg(psl[0:32, :], [(w[:, kc, 0:32], hT[:, kc, msl(m)]) for kc in range(8)], [slb] + hall(m), [pbl])
                    for c2 in range(2):
                        mmg(psf2[c2][0][:], [(w[:, kc, 32 + c2 * 128:32 + (c2 + 1) * 128], hT[:, kc, msl(m)]) for kc in range(8)],
                            [slb] + hall(m), [psf2[c2][1]])
                cp(lrT[0:32, msl(m)], psl[0:32, :], [pbl], [lr_b], eng="act")
                for c2 in range(2):
                    cp(finT[:, c2, msl(m)], psf2[c2][0][:], [psf2[c2][1]], [fin_b], eng="dve")
            for j in range(NT):
                for c2 in range(2):
                    ps, pb = PS()
                    mmg(ps[:, 0:256], [(finT[:, c2, tsl(j)], cs64[:])], [fin_b, CB], [pb])
                    cp(xcs[:, j, c2, :], ps[:, 0:256], [pb], [xcs_b[j]], eng=("act" if c2 == 0 else "dve"))

            stage("evC_%d" % ph)
            o2 = [0]

            def h2alloc(n):
                a = o2[0]; o2[0] += n
                assert o2[0] <= 6144
                return H2[:, a:a + n]

            hs = [dict(qT=qT, kT=kT, ktok=ktok, vtok=vtok, sgtok=sgtok, q=q_b, k=k_b, kt=kt_b, vt=vt_b, sg=sg_b),
                  dict(qT=h2alloc(TT), kT=h2alloc(TT),
                       ktok=h2alloc(NT * 128).rearrange("p (j n) -> p j n", j=NT),
                       vtok=h2alloc(NT * 192).rearrange("p (j n) -> p j n", j=NT),
                       sgtok=h2alloc(NT * 192).rearrange("p (j n) -> p j n", j=NT), **hs1_b)]

            def proj_groups(h):
                H = hs[h % 2]
                st = {}

                def getw():
                    if "w" not in st:
                        sl, slb = wget("evH%d" % h)
                        st["w"] = v3(sl, 8, 640); st["b"] = slb
                    return st["w"], st["b"]

                def gq(m):
                    w, slb = getw()
                    ps, pb = PS()
                    mmg(ps[:], [(w[:, kc, 0:128], hT[:, kc, msl(m)]) for kc in range(8)], [slb] + hall(m), [pb])
                    cp(H["qT"][:, msl(m)], ps[:], [pb], [H["q"]], eng="act")

                def gk(m):
                    w, slb = getw()
                    ps, pb = PS()
                    mmg(ps[:], [(w[:, kc, 128:256], hT[:, kc, msl(m)]) for kc in range(8)], [slb] + hall(m), [pb])
                    cp(H["kT"][:, msl(m)], ps[:], [pb], [H["k"]], eng="dve")

                def gt(j):
                    w, slb = getw()
                    ps, pb = PS()
                    mmg(ps[:], [(hT[:, kc, tsl(j)], w[:, kc, 128:640]) for kc in range(8)], [slb] + hall(j // 4), [pb])
                    cp(H["ktok"][:, j, :], ps[:, 0:128], [pb], [H["kt"]], eng="dve")
                    cp(H["vtok"][:, j, :], ps[:, 128:320], [pb], [H["vt"]], eng="dve")
                    cp(H["sgtok"][:, j, :], ps[:, 320:512], [pb], [H["sg"]], eng="act")

                def gsilu():
                    act(H["sgtok"], H["sgtok"], AF.Silu, [H["sg"]], [H["sg"]])

                gl = []
                for m in range(NM):
                    gl.append(lambda m=m: gq(m)); gl.append(lambda m=m: gk(m))
                for j in range(NT):
                    gl.append(lambda j=j: gt(j))
                gl.append(gsilu)
                return gl

            for g in proj_groups(0):
                g()
            for h in range(4):
                H = hs[h % 2]
                if ph == PH_ADA:
                    ada_block(0, 2 + h)
                pend = proj_groups(h + 1) if h < 3 else []
                steps = []
                for (t0, n) in seqs:
                    fw = list(range(t0, t0 + n)); bw = fw[::-1]
                    for a, b in zip(fw, bw):
                        steps.append((t0 // 2, 0, a, a == fw[0], a == fw[-1]))
                        steps.append((t0 // 2, 1, b, b == bw[0], b == bw[-1]))

                def stA(p):
                    X = tr[p % 2]
                    ps, pb = PS()
                    for i in range(2):
                        si, d, j, first, lastt = steps[2 * p + i]
                        mmg(ps[:, i * 128:(i + 1) * 128], [(lrT[0:64, tsl(j)], w2aug[0:64, d, h * 128:(h + 1) * 128])], [lr_b, CB], [pb])
                    act(X["e1"], ps[:, 0:256], AF.Exp, [pb], [X["e1_b"]], scale=-1.0)
                    act(X["sp"], X["e1"], AF.Ln, [X["e1_b"], CB], [X["sp_b"]], bias=cst[:, 1:2])

                def stB(p):
                    X = tr[p % 2]
                    ps, pb = PS()
                    for i in range(2):
                        si, d, j, first, lastt = steps[2 * p + i]
                        spi = X["sp"][:, i * 128:(i + 1) * 128]
                        mmg(ps[:, i * 128:(i + 1) * 128], [(tri[:, 2 + d, :], spi)], [X["sp_b"], CB], [pb])
                        mmg(ps[:, 256 + i * 128:256 + (i + 1) * 128], [(spi, tri[:, d, :])], [X["sp_b"], CB], [pb])
                    act(X["ekd"], ps[:, 0:256], AF.Exp, [pb], [X["ekd_b"]], scale=-1.0)
                    act(X["eb"], ps[:, 256:512], AF.Exp, [pb], [X["eb_b"]], scale=-1.0)
                    act(enbx[p % 2][:], ps[:, 256:512], AF.Exp, [pb], [enbx_b[p % 2]])

                def stC1(k):
                    si, d, j, first, lastt = steps[k]
                    X = tr[(k // 2) % 2]
                    i = k % 2
                    cs_ = slice(i * 128, (i + 1) * 128)
                    tt(X["kd"][:, cs_], H["ktok"][:, j, :], X["ekd"][:, cs_], ALU.mult, [H["kt"], X["ekd_b"]], [X["kd_b"]])
                    stt(qe[d][:, tsl(j)], H["qT"][:, tsl(j)], 128.0 ** -0.5, X["eb"][:, cs_], ALU.mult, ALU.mult,
                        [H["q"], X["eb_b"]], [qe_b[d]])
                    tt(ke[d][:, tsl(j)], H["kT"][:, tsl(j)], enbx[(k // 2) % 2][:, cs_], ALU.mult, [H["k"], enbx_b[(k // 2) % 2]], [ke_b[d]])

                def stC2(k):
                    si, d, j, first, lastt = steps[k]
                    X = tr[(k // 2) % 2]
                    i = k % 2
                    cs_ = slice(i * 128, (i + 1) * 128)
                    last = i * 128 + (127 if d == 0 else 0)
                    if first:
                        if ph == 0:
                            T.op("dve", lambda d=d: nc.vector.memset(Sst[d], 0.0), writes=[S_b[d]])
                        else:
                            T.dma("sp", Sst[d], s0_d[d][h], S_b[d], writes=[S_b[d]])
                    ps, pb = PS()
                    mmg(ps[:, 0:192], [(X["kd"][:, cs_], H["vtok"][:, j, :])], [X["kd_b"], H["vt"]], [pb])
                    cp(sprev[d][:, j, :], Sst[d], [S_b[d]], [sprev_b[d]], eng="act")
                    stt(Sst[d], Sst[d], X["eb"][:, last:last + 1], ps[:, 0:192], ALU.mult, ALU.add,
                        [S_b[d], X["eb_b"], pb], [S_b[d]])
                    if lastt and ph == 0:
                        T.dma("sp", ns_d[d][si, h], Sst[d], S_b[d], reads=[S_b[d]])

                def popg():
                    if len(pend) > 1:
                        pend.pop(0)()

                npair = len(steps) // 2
                for i in range(npair + 2):
                    if 0 <= i - 2 < npair:
                        stC1(2 * (i - 2)); stC1(2 * (i - 2) + 1)
                    if i < npair:
                        stA(i)
                    popg()
                    if 0 <= i - 1 < npair:
                        stB(i - 1)
                    popg()
                    if 0 <= i - 2 < npair:
                        stC2(2 * (i - 2)); stC2(2 * (i - 2) + 1)
                stage("h%d_p1_%d" % (h, ph))

                def p2a(j):
                    X = tr[j % 2]
                    ps, pb = PS()
                    mmg(ps[:, 0:128], [(ke[0][:, tsl(j)], qe[0][:, tsl(j)])], [ke_b[0], qe_b[0]], [pb])
                    mmg(ps[:, 128:256], [(ke[1][:, tsl(j)], qe[1][:, tsl(j)])], [ke_b[1], qe_b[1]], [pb])
                    tt(X["att"], ps[:, 0:256], mask2[:], ALU.mult, [pb, CB], [X["att_b"]])

                def p2b(j):
                    X = tr[j % 2]
                    ps, pb = PS()
                    mmg(ps[:, 0:192], [(X["att"][:, 0:128], H["vtok"][:, j, :]), (X["att"][:, 128:256], H["vtok"][:, j, :]),
                                       (qe[0][:, tsl(j)], sprev[0][:, j, :]), (qe[1][:, tsl(j)], sprev[1][:, j, :])],
                        [X["att_b"], H["vt"], qe_b[0], qe_b[1], sprev_b[0], sprev_b[1]], [pb])
                    act(X["junk"], ps[:, 0:192], AF.Square, [pb], [X["junk_b"], ssqa_b], scale=192.0 ** -0.5,
                        accum_out=ssqa[:, j * 4 + h:j * 4 + h + 1])
                    tt(og[:, j, h * 192:(h + 1) * 192], ps[:, 0:192], H["sgtok"][:, j, :], ALU.mult, [pb, H["sg"]], [og_b[j]])

                p2a(0)
                for j in range(NT):
                    if j + 1 < NT:
                        p2a(j + 1)
                    p2b(j)
                    if len(pend) > 1:
                        pend.pop(0)()
                while pend:
                    pend.pop(0)()

            act(lnsa, ssqa, AF.Ln, [ssqa_b, CB], [lnsa_b], bias=cst[:, 0:1])
            act(rsa, lnsa, AF.Exp, [lnsa_b], [rsa_b], scale=-0.5)
            og3 = U[:, a_og:a_og + NT * 768].rearrange("p (g e) -> p g e", e=192)
            tt(og3, og3, rsa.unsqueeze(2).to_broadcast([128, NT * 4, 192]), ALU.mult, [rsa_b] + og_b, og_b)
            stage("gla_%d" % ph)
            for j in range(NT):
                for c in range(6):
                    T.op("pe", lambda j=j, c=c: nc.tensor.transpose(psb[:, c * 128:(c + 1) * 128], og[:, j, c * 128:(c + 1) * 128], identb[:]),
                         reads=[og_b[j], CB], writes=[psb_b], sig=(c == 5))
                tt(hT[:, 0:6, tsl(j)], psb[:, 0:768].rearrange("p (c t) -> p c t", c=6),
                   pp[:, PP_NG:PP_NG + 6].unsqueeze(2).to_broadcast([128, 6, 128]), ALU.mult,
                   [psb_b, CB], [hT_b[c][j // 4] for c in range(6)])
            if ph == 0:
                sl, slb = wget("dft")
                dv = sl[:, 0:1024].rearrange("p (j k n) -> p j k n", j=2, k=2)
                for (t0, n) in seqs:
                    for c2 in range(2):
                        ps, pb = PS()
                        pairs = []
                        for jj in range(2):
                            pairs.append((xcs[:, t0 + jj, c2, 0:128], dv[:, jj, 0, :]))
                            pairs.append((xcs[:, t0 + jj, c2, 128:256], dv[:, jj, 1, :]))
                        mmg(ps[:, 0:256], pairs, [slb, xcs_b[t0], xcs_b[t0 + 1]], [pb])
                        cp(hT[:, 6 + c2, t0 * 128:t0 * 128 + 256], ps[:, 0:256], [pb], [hT_b[6 + c2][0]],
                           eng=("act" if c2 == 0 else "dve"))
            else:
                slc, slcb = wget("dftc")
                sls, slsb = wget("dfts", ahead=NSLOT - 2)
                dc = v3(slc, 8, 1024); ds = v3(sls, 8, 1024)
                for c2 in range(2):
                    for m in range(2):
                        ps, pb = PS()
                        pairs = []
                        for jj in range(8):
                            pairs.append((xcs[:, jj, c2, 0:128], dc[:, jj, msl(m)]))
                            pairs.append((xcs[:, jj, c2, 128:256], ds[:, jj, msl(m)]))
                        mmg(ps[:], pairs, [slcb, slsb] + xcs_b, [pb])
                        cp(hT[:, 6 + c2, msl(m)], ps[:], [pb], [hT_b[6 + c2][m]], eng=("act" if c2 == 0 else "dve"))
            stage("mix0_%d" % ph)
            dump("mix0_%d" % ph, hT[:, :, 0:TT], [hT_b[c][m] for c in range(8) for m in range(NM)])
            outproj_resid("evO", 0, 16)
            dump("x_l0mix_%d" % ph, xT[:, :, 0:TT], [xT_b[c][m] for c in range(8) for m in range(NM)])
            stage("out0_%d" % ph)
            norm(0, 1)
            ffn(0)
            stage("ffn0_%d" % ph)
            dump("x_l0_%d" % ph, xT[:, :, 0:TT], [xT_b[c][m] for c in range(8) for m in range(NM)])

            norm(1, 0)
            uT = U[:, 0:4 * TT].rearrange("p (c t) -> p c t", c=4)
            v1 = U[:, 4096:4096 + NT * 512].rearrange("p (j n) -> p j n", j=NT)
            gbT = U[:, 8192:8192 + 4 * TT].rearrange("p (c t) -> p c t", c=4)
            gcT = U[:, 12288:12288 + 4 * TT].rearrange("p (c t) -> p c t", c=4)
            zT = U[:, 24576:32768].bitcast(F32).rearrange("p (c t) -> p c t", c=4)
            u_b = ub("uT"); v1_b = ub("v1"); gb_b = ub("gb"); gc_b = ub("gc"); z_b = ub("z")
            accs = [U[:, 16384 + i * 2048:16384 + (i + 1) * 2048].bitcast(F32) for i in range(4)]
            acc_b = [ub("acc%d" % i) for i in range(4)]
            sl, slb = wget("odG")
            w = v3(sl, 8, 1024)
            pre = {}
            specs = []
            for oc in range(4):
                ps, pb = PS()
                pre[(oc, 0)] = (ps, pb)
                specs.append((ps[:], pb, [(w[:, kc, oc * 128:(oc + 1) * 128], hT[:, kc, msl(0)], [slb, hT_b[kc][0]]) for kc in range(8)]))
            mm_kc_outer(specs)
            for m in range(NM):
                for oc in range(8):
                    if (oc, m) in pre:
                        ps, pb = pre[(oc, m)]
                    else:
                        ps, pb = PS()
                        mmg(ps[:], [(w[:, kc, oc * 128:(oc + 1) * 128], hT[:, kc, msl(m)]) for kc in range(8)], [slb] + hall(m), [pb])
                    if oc < 4:
                        cp(gbT[:, oc, msl(m)], ps[:], [pb], [gb_b], eng="act")
                    else:
                        cp(gcT[:, oc - 4, msl(m)], ps[:], [pb], [gc_b], eng="dve")
            sl, slb = wget("odH")
            w = v3(sl, 8, 512)
            for oc in range(4):
                for m in range(NM):
                    ps, pb = PS()
                    mmg(ps[:], [(w[:, kc, oc * 128:(oc + 1) * 128], hT[:, kc, msl(m)]) for kc in range(8)], [slb] + hall(m), [pb])
                    tt(zT[:, oc, msl(m)], ps[:], gcT[:, oc, msl(m)], ALU.mult, [pb, gc_b], [z_b])
            RW = 256 if ph == 0 else 64
            for oc in range(4):
                acc = accs[oc][:, 0:TT]; ab = acc_b[oc]
                z = zT[:, oc, 0:TT]
                cw = lambda k: pp[:, PP_CONV + oc * 3 + k:PP_CONV + oc * 3 + k + 1]
                ts(acc, z, cw(1), None, ALU.mult, None, [z_b, CB], [ab])
                a3 = acc.rearrange("p (r w) -> p r w", w=RW)
                z3 = z.rearrange("p (r w) -> p r w", w=RW)
                stt(a3[:, :, 1:RW], z3[:, :, 0:RW - 1], cw(0), a3[:, :, 1:RW], ALU.mult, ALU.add, [z_b, CB, ab], [ab])
                stt(a3[:, :, 0:RW - 1], z3[:, :, 1:RW], cw(2), a3[:, :, 0:RW - 1], ALU.mult, ALU.add, [z_b, CB, ab], [ab])
            sl, slb = wget("odUV")
            w = v3(sl, 8, 1024)
            for oc in range(4):
                for m in range(NM):
                    ps, pb = PS()
                    mmg(ps[:], [(w[:, kc, oc * 128:(oc + 1) * 128], hT[:, kc, msl(m)]) for kc in range(8)], [slb] + hall(m), [pb])
                    act(uT[:, oc, msl(m)], ps[:], AF.Gelu_apprx_tanh, [pb], [u_b])
            for j in range(NT):
                ps, pb = PS()
                mmg(ps[:], [(hT[:, kc, tsl(j)], w[:, kc, 512:1024]) for kc in range(8)], [slb] + hall(j // 4), [pb])
                act(v1[:, j, :], ps[:], AF.Gelu_apprx_tanh, [pb], [v1_b])
            for j in range(NT):
                ps, pb = PS()
                for g in range(4):
                    T.op("pe", lambda g=g, ps=ps, j=j: nc.tensor.matmul(ps[:, g * 128:(g + 1) * 128], v1[:, j, g * 128:(g + 1) * 128], wsT[:, g, :], start=True, stop=False),
                         reads=[v1_b, CB], writes=[pb], sig=False)
                    T.op("pe", lambda g=g, ps=ps: nc.tensor.matmul(ps[:, g * 128:(g + 1) * 128], onesrow_b[0:1, :], gbs_hi[0:1, g * 128:(g + 1) * 128], start=False, stop=False),
                         reads=[CB], writes=[pb], sig=False)
                    T.op("pe", lambda g=g, ps=ps: nc.tensor.matmul(ps[:, g * 128:(g + 1) * 128], onesrow_b[0:1, :], gbs_lo[0:1, g * 128:(g + 1) * 128], start=False, stop=True),
                         reads=[CB], writes=[pb], sig=(g == 3))
                tt(hT[:, 0:4, tsl(j)], ps[:].rearrange("p (g t) -> p g t", g=4), uT[:, :, tsl(j)], ALU.mult,
                   [pb, u_b], [hT_b[c][j // 4] for c in range(4)])
            for oc in range(4):
                tt(hT[:, 4 + oc, 0:TT], accs[oc][:, 0:TT], gbT[:, oc, 0:TT], ALU.mult, [acc_b[oc], gb_b], [hT_b[4 + oc][m] for m in range(NM)])
            stage("mix1_%d" % ph)
            outproj_resid("odO", 1, 16)
            dump("x_l1mix_%d" % ph, xT[:, :, 0:TT], [xT_b[c][m] for c in range(8) for m in range(NM)])
            norm(1, 1)
            ffn(1)
            stage("ffn1_%d" % ph)

            yst = [U[:, 0:2048].bitcast(F32), U[:, 2048:4096].bitcast(F32)]
            yst_b = [ub("yst0"), ub("yst1")]; nb["sq"] = ub("sq")
            for m in range(NM):
                sq = U[:, 20480:24576].rearrange("p (c t) -> p c t", c=8)
                xr = [xT_b[c][m] for c in range(8)]
                act(sq, xT[:, :, msl(m)], AF.Square, xr, [nb["sq"]])
                psr, pbr = PS()
                for jj in range(4):
                    mmg(psr[:, jj:jj + 1], [(sq[:, c, jj * 128:(jj + 1) * 128], onesb[:, 0:1]) for c in range(8)], [nb["sq"], CB], [pbr])
                act(rtok[:, 0:4], psr[:, 0:4], AF.Ln, [pbr, CB], [rtok_b], bias=cst[:, 0:1])
                act(rtok[:, 4:8], rtok[:, 0:4], AF.Exp, [rtok_b], [rtok_b], scale=-0.5)
                for jj in range(4):
                    j = m * 4 + jj
                    ys, yb = yst[j % 2], yst_b[j % 2]
                    for hf in range(2):
                        ps, pb = PS()
                        for k in range(4):
                            c = hf * 4 + k
                            T.op("pe", lambda ps=ps, k=k, c=c, j=j: nc.tensor.transpose(ps[:, k * 128:(k + 1) * 128], xT[:, c, j * 128:(j + 1) * 128], ident[:]),
                                 reads=[xT_b[c][m], CB], writes=[pb], sig=(k == 3))
                        stt(ys[:, hf * 512:(hf + 1) * 512], ps[:], rtok[:, 4 + jj:5 + jj], gfinb[:, hf * 512:(hf + 1) * 512], ALU.mult, ALU.mult,
                            [pb, rtok_b, CB], [yb])
                    T.dma("sp", y_d[ph][j * 128:(j + 1) * 128, :], ys, yb, reads=[yb])

        T.barrier(engines=("pe", "act", "dve", "sp", "pool"))
    return nc


_CACHE = {}


def _get_nc():
    if "nc" not in _CACHE:
        _CACHE["nc"] = build()
        _CACHE["consts"] = _consts()
    return _CACHE["nc"], _CACHE["consts"]


def make_in_maps(inputs, consts):
    f = lambda a: np.ascontiguousarray(np.asarray(a, dtype=np.float32))
    x_prompt = f(inputs["x_prompt"]); x_sample = f(inputs["x_sample"])
    sf = f(inputs["state_gla_fwd"]); sbw = f(inputs["state_gla_bwd"])
    c = f(inputs["c"]); c_ctx = f(inputs["c_ctx"])
    pp = np.zeros((128, PP_N), np.float32)
    for l in range(2):
        pp[:, PP_GMIX + l * 8:PP_GMIX + (l + 1) * 8] = f(inputs["norm_mix_g"])[l].reshape(8, 128).T
        pp[:, PP_GFFN + l * 8:PP_GFFN + (l + 1) * 8] = f(inputs["norm_ffn_g"])[l].reshape(8, 128).T
        pp[:, PP_ADAB + l * 48:PP_ADAB + (l + 1) * 48] = f(inputs["ada_b"])[l].reshape(48, 128).T
    pp[:, PP_CONV:PP_CONV + 12] = f(inputs["conv_w"])[0].T.reshape(4, 128, 3).transpose(1, 0, 2).reshape(128, 12)
    pp[:, PP_NG:PP_NG + 6] = np.tile(f(inputs["gla_norm_g"])[0], 4).reshape(6, 128).T
    w2aug = np.zeros((2, 64, 512), np.float32)
    w2aug[0, 0:16] = f(inputs["gla_w2_f"])[0]
    w2aug[0, 32] = f(inputs["gla_b2_f"])[0]
    w2aug[1, 16:32] = f(inputs["gla_w2_b"])[0]
    w2aug[1, 32] = f(inputs["gla_b2_b"])[0]
    shared = dict(
        pp=pp, w2aug=w2aug, gfin=f(inputs["final_norm_g"]).reshape(1, D),
        gws=f(inputs["gmlp_ws"])[0], gbs=f(inputs["gmlp_b"])[0].reshape(1, 512),
        ada_w=f(inputs["ada_w"]), ffn_w1=f(inputs["ffn_w1"]), ffn_w2=f(inputs["ffn_w2"]),
        ev_w_in=f(inputs["ev_w_in"])[0], ev_w_out=f(inputs["ev_w_out"])[0],
        od_w_in=f(inputs["od_w_in"])[0], od_w_out=f(inputs["od_w_out"])[0],
        ident=consts["ident"], tri=consts["tri"], mask2=consts["mask2"], cs64=consts["cs64"],
        dftc256=consts["dftc256"], dfts256=consts["dfts256"],
        dftc1024=consts["dftc1024"], dfts1024=consts["dfts1024"],
    )
    maps = []
    for i in range(NCORES):
        cond = np.stack([c_ctx, c[i]], axis=0)
        m = dict(shared)
        m["xp"] = x_prompt[2 * i:2 * i + 2].reshape(512, D)
        m["xs"] = x_sample[i]
        m["s0f"] = sf[i, 0]
        m["s0b"] = sbw[i, 0]
        m["condT"] = np.ascontiguousarray(cond.reshape(2, 8, 128).transpose(2, 1, 0))
        maps.append(m)
    return maps


def kernel(**inputs):
    nc, consts = _get_nc()
    maps = make_in_maps(inputs, consts)
    res = run_bass_kernel_spmd(nc, maps, core_ids=list(range(NCORES)))
    r = res.results
    y_prompt = np.concatenate([r[i]["yp"].reshape(2, 256, D) for i in range(NCORES)], axis=0)
    y_sample = np.stack([r[i]["ys"] for i in range(NCORES)], axis=0)
    nsf = np.concatenate([r[i]["nsf"].reshape(2, 1, 4, 128, 192) for i in range(NCORES)], axis=0)
    nsb = np.concatenate([r[i]["nsb"].reshape(2, 1, 4, 128, 192) for i in range(NCORES)], axis=0)
    return (y_prompt.astype(np.float32), y_sample.astype(np.float32),
            nsf.astype(np.float32), nsb.astype(np.float32))
```

```python
import contextlib
import numpy as np
import concourse.bass as bass
import concourse.mybir as mybir
from concourse.bass_utils import run_bass_kernel_spmd

F32 = mybir.dt.float32
BF16 = mybir.dt.bfloat16
AF = mybir.ActivationFunctionType
ALU = mybir.AluOpType

NCORES = 8
D = 1024
NSLOT = 3
EPS = 1e-6


class Buf:
    _n = 0

    def __init__(self, name, excl=False):
        Buf._n += 1
        self.id = Buf._n
        self.name = name
        self.excl = excl
        self.w = {}
        self.r = {}
        self.dsem = None
        self.dcnt = 0


class Tracker:
    def __init__(self, nc, es):
        self.nc = nc
        self.es = es
        self.eng = {}
        for k, h in (("pe", nc.tensor), ("act", nc.scalar), ("dve", nc.vector),
                     ("pool", nc.gpsimd), ("sp", nc.sync)):
            sem = es.enter_context(nc.semaphore("sem_" + k))
            self.eng[k] = dict(h=h, sem=sem, cnt=0, seen={}, key=k)
        self.owners = []
        self.dead = False

    def _collect(self, e, reads, writes):
        need = {}

        def add(key, sem, val):
            if key == "pe" and e["key"] == "pe":
                return
            if e["seen"].get(key, 0) >= val:
                return
            if key not in need or need[key][1] < val:
                need[key] = (sem, val)

        for b in reads:
            for key, (sem, val) in b.w.items():
                add(key, sem, val)
            if b.excl:
                for key, (sem, val) in b.r.items():
                    if key != e["key"]:
                        add(key, sem, val)
        for b in writes:
            for key, (sem, val) in b.w.items():
                add(key, sem, val)
            for key, (sem, val) in b.r.items():
                add(key, sem, val)
        return need

    def _emit(self, e, need, fn):
        items = list(need.items())
        for key, (sem, val) in items[:-1]:
            e["h"].wait_ge(sem, val)
            e["seen"][key] = val
        ins = fn()
        if items:
            key, (sem, val) = items[-1]
            ins._wait_ge(sem, val)
            e["seen"][key] = val
        return ins

    @staticmethod
    def _rec(d, key, sem, val):
        old = d.get(key)
        if old is None or old[1] < val:
            d[key] = (sem, val)

    def op(self, ek, fn, reads=(), writes=(), sig=True):
        if self.dead:
            return None
        e = self.eng[ek]
        need = self._collect(e, reads, writes)
        ins = self._emit(e, need, fn)
        if sig:
            e["cnt"] += 1
            ins.then_inc(e["sem"], 1)
            val = e["cnt"]
        else:
            val = e["cnt"] + 1
        for b in reads:
            self._rec(b.r, ek, e["sem"], val)
        for b in writes:
            self._rec(b.w, ek, e["sem"], val)
        return ins

    def dma(self, qk, out, in_, owner, reads=(), writes=()):
        if self.dead:
            return None
        e = self.eng[qk]
        if owner.dsem is None:
            owner.dsem = self.es.enter_context(self.nc.semaphore("dsem_%d" % owner.id))
            self.owners.append(owner)
        need = self._collect(e, reads, writes)
        ins = self._emit(e, need, lambda: e["h"].dma_start(out=out, in_=in_))
        owner.dcnt += 16
        ins.then_inc(owner.dsem, 16)
        key = "dma%d" % owner.id
        for b in reads:
            self._rec(b.r, key, owner.dsem, owner.dcnt)
        for b in writes:
            self._rec(b.w, key, owner.dsem, owner.dcnt)
        return ins

    def barrier(self, engines=("pe", "act", "dve", "sp")):
        if self.dead:
            return
        targets = [(k, e["sem"], e["cnt"]) for k, e in self.eng.items() if e["cnt"] > 0]
        targets += [("dma%d" % b.id, b.dsem, b.dcnt) for b in self.owners if b.dcnt > 0]
        for ek in engines:
            e = self.eng[ek]
            for key, sem, val in targets:
                if key == ek and ek == "pe":
                    continue
                if e["seen"].get(key, 0) >= val:
                    continue
                e["h"].wait_ge(sem, val)
                e["seen"][key] = val

    def wait_all(self, ek, bufs):
        e = self.eng[ek]
        need = self._collect(e, [], bufs)
        for key, (sem, val) in need.items():
            e["h"].wait_ge(sem, val)
            e["seen"][key] = val


def _consts():
    c = {}
    c["ident"] = np.eye(128, dtype=np.float32)
    s = np.arange(128)[:, None]
    t = np.arange(128)[None, :]
    tri = np.zeros((128, 4, 128), np.float32)
    tri[:, 0, :] = (s <= t) / 16.0
    tri[:, 1, :] = (s >= t) / 16.0
    tri[:, 2, :] = (s > t) / 16.0
    tri[:, 3, :] = (s < t) / 16.0
    c["tri"] = tri
    m2 = np.zeros((128, 256), np.float32)
    m2[:, 0:128] = (s <= t)
    m2[:, 128:256] = (s >= t)
    c["mask2"] = m2
    cs = np.zeros((128, 256), np.float32)
    a = np.arange(64)
    ang = 2 * np.pi * ((a[:, None] * a[None, :]) % 64) / 64.0
    for g in range(2):
        cs[g * 64:(g + 1) * 64, g * 64:(g + 1) * 64] = np.cos(ang)
        cs[g * 64:(g + 1) * 64, 128 + g * 64:128 + (g + 1) * 64] = np.sin(ang)
    c["cs64"] = cs
    for L in (256, 1024):
        tt = np.arange(L, dtype=np.int64)
        ang = 2 * np.pi * ((tt[:, None] * tt[None, :]) % L).astype(np.float64) / L
        sc = 1.0 / np.sqrt(64.0 * L)
        c["dftc%d" % L] = (np.cos(ang) * sc).astype(np.float32)
        c["dfts%d" % L] = (-np.sin(ang) * sc).astype(np.float32)
    return c


PP_GMIX, PP_GFFN, PP_ADAB, PP_CONV, PP_NG, PP_N = 0, 16, 32, 128, 140, 146


def build(debug_dump=None, stop_at=None):
    nc = bass.Bass("TRN2", target_bir_lowering=False)
    dbg = {}

    def din(name, shape):
        return nc.dram_tensor(name, list(shape), F32, kind="ExternalInput").ap()

    def dout(name, shape):
        return nc.dram_tensor(name, list(shape), F32, kind="ExternalOutput").ap()

    xin_d = [din("xp", [512, D]), din("xs", [1024, D])]
    s0_d = [din("s0f", [4, 128, 192]), din("s0b", [4, 128, 192])]
    condT_d = din("condT", [128, 8, 2])
    pp_d = din("pp", [128, PP_N])
    w2aug_d = din("w2aug", [2, 64, 512])
    gfin_d = din("gfin", [1, D])
    gws_d = din("gws", [4, 128, 128])
    gbs_d = din("gbs", [1, 512])
    ada_w = din("ada_w", [2, D, 6 * D])
    ffn_w1 = din("ffn_w1", [2, D, 4 * D])
    ffn_w2 = din("ffn_w2", [2, 4 * D, D])
    ev_w_in = din("ev_w_in", [D, 2848])
    ev_w_out = din("ev_w_out", [D, D])
    od_w_in = din("od_w_in", [D, 2560])
    od_w_out = din("od_w_out", [D, D])
    ident_d = din("ident", [128, 128])
    tri_d = din("tri", [128, 4, 128])
    mask2_d = din("mask2", [128, 256])
    cs64_d = din("cs64", [128, 256])
    dft256_d = [din("dftc256", [256, 256]), din("dfts256", [256, 256])]
    dft1024_d = [din("dftc1024", [1024, 1024]), din("dfts1024", [1024, 1024])]

    y_d = [dout("yp", [512, D]), dout("ys", [1024, D])]
    ns_d = [dout("nsf", [2, 4, 128, 192]), dout("nsb", [2, 4, 128, 192])]

    with contextlib.ExitStack() as es:
        T = Tracker(nc, es)

        def stage(name):
            if stop_at is not None and name == stop_at and not T.dead:
                T.barrier(engines=("pe", "act", "dve", "sp", "pool"))
                T.dead = True

        def sb(name, shape, dt=F32):
            return es.enter_context(nc.sbuf_tensor("sb_" + name, list(shape), dt))

        TMAX = 1024
        xT = sb("xT", [128, 8, TMAX])
        hT = sb("hT", [128, 8, TMAX], BF16)
        slots = [sb("slot%d" % i, [128, 8192], BF16) for i in range(NSLOT)]
        slot_b = [Buf("slot%d" % i) for i in range(NSLOT)]
        U = sb("U", [128, 32768], BF16)
        H2 = sb("H2", [128, 6144], BF16)
        enbx = [sb("enb%d" % i, [128, 256]) for i in range(2)]
        enbx_b = [Buf("enb0"), Buf("enb1")]
        hs1_b = dict(q=Buf("q1"), k=Buf("k1"), kt=Buf("kt1"), vt=Buf("vt1"), sg=Buf("sg1"))
        ident = sb("ident", [128, 128]); identb = sb("identb", [128, 128], BF16)
        tri = sb("tri", [128, 4, 128], BF16)
        mask2 = sb("mask2", [128, 256], BF16)
        cs64 = sb("cs64", [128, 256], BF16)
        onesb = sb("onesb", [128, 128], BF16)
        onesrow = sb("onesrow", [1, 128])
        cst = sb("cst", [128, 4])
        pp = sb("pp", [128, PP_N])
        w2aug = sb("w2aug", [64, 2, 512], BF16)
        gfinb = sb("gfinb", [128, D])
        gws = sb("gws", [128, 4, 128])
        wsT = sb("wsT", [128, 4, 128], BF16)
        gbs = sb("gbs", [1, 512])
        gbs_hi = sb("gbs_hi", [1, 512], BF16); gbs_lo = sb("gbs_lo", [1, 512], BF16); gbs_t = sb("gbs_t", [1, 512])
        onesrow_b = sb("onesrow_b", [1, 128], BF16)
        condT = sb("condT", [128, 8, 2])
        scT = sb("scT", [128, 8, 2], BF16)
        modT = [sb("modT%d" % l, [128, 48, 2]) for l in range(2)]
        gscT = [sb("gscT%d" % l, [128, 2, 8, 2]) for l in range(2)]
        rstd = sb("rstd", [128, 512]); lnr = sb("lnr", [128, 512])
        sqc = [sb("sqc%d" % i, [128, 512], BF16) for i in range(2)]
        ntm = [sb("ntm%d" % i, [128, 512]) for i in range(2)]
        sqc_b = [Buf("sqc0"), Buf("sqc1")]; ntm_b = [Buf("ntm0"), Buf("ntm1")]
        relu_t = [sb("relu%d" % i, [128, 512], BF16) for i in range(2)]
        relu_b = [Buf("relu0"), Buf("relu1")]
        rtok = sb("rtok", [128, 8]);
        CB = Buf("consts")
        U_all = []

        def ub(name):
            b = Buf(name)
            for o in U_all:
                for dct in (o.w, o.r):
                    for key, (sem, val) in dct.items():
                        Tracker._rec(b.r, key, sem, val)
            U_all.append(b)
            return b
        xT_b = [[Buf("xT%d_%d" % (c, m)) for m in range(2)] for c in range(8)]
        hT_b = [[Buf("hT%d_%d" % (c, m)) for m in range(2)] for c in range(8)]
        rstd_b = Buf("rstd"); lnr_b = Buf("lnr"); rtok_b = Buf("rtok")
        mod_b = Buf("mod")

        psf = [es.enter_context(nc.psum_tensor("psf%d" % i, [128, 512], F32)) for i in range(7)]
        psf_b = [Buf("psf%d" % i, excl=True) for i in range(7)]
        psb = es.enter_context(nc.psum_tensor("psb", [128, 1024], BF16))
        psb_b = Buf("psb", excl=True)
        ps_rr = [0]

        def PS():
            i = ps_rr[0] % 7
            ps_rr[0] += 1
            return psf[i], psf_b[i]

        def mmg(out, pairs, R, W):
            n = len(pairs)
            for i, (l, r) in enumerate(pairs):
                T.op("pe", lambda l=l, r=r, i=i: nc.tensor.matmul(out, l, r, start=(i == 0), stop=(i == n - 1)),
                     reads=R, writes=W, sig=(i == n - 1))

        def mm_kc_outer(specs):
            n = len(specs[0][2])
            for kc in range(n):
                for (out, pb, prs) in specs:
                    l, r, R = prs[kc]
                    T.op("pe", lambda out=out, l=l, r=r, kc=kc: nc.tensor.matmul(out, l, r, start=(kc == 0), stop=(kc == n - 1)),
                         reads=R, writes=[pb], sig=(kc == n - 1))

        def act(out, in_, func, R, W, **kw):
            T.op("act", lambda: nc.scalar.activation(out=out, in_=in_, func=func, **kw), reads=R, writes=W)

        def tt(out, a, b, op, R, W, eng="dve"):
            h = nc.vector if eng == "dve" else nc.gpsimd
            T.op(eng, lambda: h.tensor_tensor(out=out, in0=a, in1=b, op=op), reads=R, writes=W)

        def stt(out, a, scalar, b, op0, op1, R, W):
            T.op("dve", lambda: nc.vector.scalar_tensor_tensor(out=out, in0=a, scalar=scalar, in1=b, op0=op0, op1=op1),
                 reads=R, writes=W)

        def ts(out, a, s1, s2, op0, op1, R, W):
            if s2 is None:
                T.op("dve", lambda: nc.vector.tensor_scalar(out=out, in0=a, scalar1=s1, scalar2=None, op0=op0), reads=R, writes=W)
            else:
                T.op("dve", lambda: nc.vector.tensor_scalar(out=out, in0=a, scalar1=s1, scalar2=s2, op0=op0, op1=op1), reads=R, writes=W)

        def cp(out, in_, R, W, eng="dve"):
            if eng == "act":
                T.op("act", lambda: nc.scalar.copy(out=out, in_=in_), reads=R, writes=W)
            else:
                T.op("dve", lambda: nc.vector.tensor_copy(out, in_), reads=R, writes=W)

        def dump(name, ap, R):
            if debug_dump is None or name not in debug_dump:
                return
            d = dout("dbg_" + name, list(ap.shape))
            b = Buf("dbg")
            T.dma("sp" if ap.dtype == F32 else "pool", d, ap, b, reads=R)
            dbg[name] = b

        def cload(dst, src):
            T.dma("sp", dst, src, CB, writes=[CB])

        cload(ident[:], ident_d)
        cload(pp[:], pp_d)
        cload(gfinb[:], gfin_d.to_broadcast([128, D]))
        cload(gws[:], gws_d.rearrange("g p q -> p g q"))
        cload(gbs[:], gbs_d)
        T.op("dve", lambda: nc.vector.memset(onesrow_b[:], 1.0), writes=[CB])
        cp(gbs_hi[:], gbs[:], [CB], [CB])
        tt(gbs_t[:], gbs[:], gbs_hi[:], ALU.subtract, [CB], [CB])
        cp(gbs_lo[:], gbs_t[:], [CB], [CB])
        cload(condT[:], condT_d)
        CBP = Buf("consts_pool")
        T.dma("pool", mask2[:], mask2_d, CBP, writes=[CB])
        T.dma("pool", cs64[:], cs64_d, CBP, writes=[CB])
        T.dma("pool", tri[:], tri_d, CBP, writes=[CB])
        T.dma("pool", w2aug[:], w2aug_d.rearrange("d k n -> k d n"), CBP, writes=[CB])
        T.op("dve", lambda: nc.vector.memset(onesb[:], 1.0 / 1024), writes=[CB])
        T.op("dve", lambda: nc.vector.memset(onesrow[:], 1.0), writes=[CB])
        T.op("dve", lambda: nc.vector.memset(cst[:, 0:1], EPS), writes=[CB])
        T.op("dve", lambda: nc.vector.memset(cst[:, 1:2], 1.0), writes=[CB])
        cp(identb[:], ident[:], [CB], [CB])
        act(scT[:], condT[:], AF.Silu, [CB], [CB])
        for g in range(4):
            ps, pb = PS()
            T.op("pe", lambda g=g, ps=ps: nc.tensor.transpose(ps[:, 0:128], gws[:, g, :], ident[:]), reads=[CB], writes=[pb])
            cp(wsT[:, g, :], ps[:, 0:128], [pb], [CB])

        stage("consts")
        def v3(sl, k, n):
            return sl[:, 0:k * n].rearrange("p (k n) -> p k n", k=k)

        def rows(w2d):
            return w2d.rearrange("(k p) n -> p k n", p=128)

        WSEQ = []

        def wadd(name, fn):
            WSEQ.append((name, fn))

        def wada(l, b):
            wadd("ada%d_%d" % (l, b), lambda sl, l=l, b=b: [(v3(sl, 8, 1024), rows(ada_w[l, :, b * 1024:(b + 1) * 1024]))])

        PH_ORDER = (1, 0)
        PH_ADA = PH_ORDER[0]
        for ph in PH_ORDER:
            A = (ph == PH_ADA)
            for l in range(2):
                if l == 0:
                    if A:
                        wada(0, 0); wada(0, 1)
                    wadd("evC", lambda sl: [(v3(sl, 8, 288), rows(ev_w_in[:, 2560:2848]))])
                    for h in range(4):
                        def f(sl, h=h):
                            v = v3(sl, 8, 640)
                            return [(v[:, :, 0:128], rows(ev_w_in[:, h * 128:(h + 1) * 128])),
                                    (v[:, :, 128:256], rows(ev_w_in[:, 512 + h * 128:512 + (h + 1) * 128])),
                                    (v[:, :, 256:448], rows(ev_w_in[:, 1024 + h * 192:1024 + (h + 1) * 192])),
                                    (v[:, :, 448:640], rows(ev_w_in[:, 1792 + h * 192:1792 + (h + 1) * 192]))]
                        wadd("evH%d" % h, f)
                        if A:
                            wada(0, 2 + h)
                    if ph == 0:
                        def f(sl):
                            v = sl[:, 0:1024].rearrange("p (j k n) -> p j k n", j=2, k=2)
                            return [(v[:, :, 0, :], rows(dft256_d[0])), (v[:, :, 1, :], rows(dft256_d[1]))]
                        wadd("dft", f)
                    else:
                        wadd("dftc", lambda sl: [(v3(sl, 8, 1024), rows(dft1024_d[0]))])
                        wadd("dfts", lambda sl: [(v3(sl, 8, 1024), rows(dft1024_d[1]))])
                    wadd("evO", lambda sl: [(v3(sl, 8, 1024), rows(ev_w_out))])
                else:
                    wadd("odG", lambda sl: [(v3(sl, 8, 1024), rows(od_w_in[:, 1024:2048]))])
                    wadd("odH", lambda sl: [(v3(sl, 8, 512), rows(od_w_in[:, 2048:2560]))])
                    wadd("odUV", lambda sl: [(v3(sl, 8, 1024), rows(od_w_in[:, 0:1024]))])
                    wadd("odO", lambda sl: [(v3(sl, 8, 1024), rows(od_w_out))])
                for b in range(4):
                    wadd("w1_%d" % b, lambda sl, l=l, b=b: [(v3(sl, 8, 1024), rows(ffn_w1[l, :, b * 1024:(b + 1) * 1024]))])
                    if A and l == 0 and b < 2:
                        wada(1, b)
                    if A and l == 1 and b == 0:
                        wada(1, 5)
                for b in range(4):
                    wadd("w2_%d" % b, lambda sl, l=l, b=b: [(v3(sl, 32, 256), rows(ffn_w2[l, :, b * 256:(b + 1) * 256]))])
                    if A and l == 0 and b < 3:
                        wada(1, 2 + b)

        wstate = dict(issued=0, nxt=0)

        def wget(name, ahead=NSLOT - 1):
            i = wstate["nxt"]
            assert WSEQ[i][0] == name, (WSEQ[i][0], name)
            while wstate["issued"] < min(i + ahead + 1, len(WSEQ)):
                j = wstate["issued"]
                s = j % NSLOT
                for (o, src) in WSEQ[j][1](slots[s]):
                    T.dma("pool", o, src, slot_b[s], writes=[slot_b[s]])
                wstate["issued"] += 1
            wstate["nxt"] += 1
            return slots[i % NSLOT], slot_b[i % NSLOT]

        def ada_block(l, b):
            sl, slb = wget("ada%d_%d" % (l, b))
            w = v3(sl, 8, 1024)
            ps, pb = PS()
            for oc in range(8):
                mmg(ps[:, oc * 2:oc * 2 + 2],
                    [(w[:, kc, oc * 128:(oc + 1) * 128], scT[:, kc, :]) for kc in range(8)],
                    [slb, CB], [pb])
            a0 = PP_ADAB + l * 48 + b * 8
            tt(modT[l][:, b * 8:(b + 1) * 8, :], ps[:, 0:16].rearrange("p (c k) -> p c k", k=2),
               pp[:, a0:a0 + 8].unsqueeze(2).to_broadcast([128, 8, 2]), ALU.add, [pb, CB], [mod_b])
            if b in (1, 4):
                wh = 0 if b == 1 else 1
                g0 = (PP_GMIX if wh == 0 else PP_GFFN) + l * 8
                stt(gscT[l][:, wh, :, :], modT[l][:, b * 8:(b + 1) * 8, :], 1.0,
                    pp[:, g0:g0 + 8].unsqueeze(2).to_broadcast([128, 8, 2]), ALU.add, ALU.mult, [mod_b, CB], [mod_b])

        stage("ada")

        def modcol(l, ch, ci):
            return modT[l][:, ch, ci:ci + 1]

        for ph in PH_ORDER:
            ci = ph
            TT = 512 if ph == 0 else 1024
            NT = TT // 128
            NM = TT // 512
            seqs = [(0, 2), (2, 2)] if ph == 0 else [(0, 8)]
            msl = lambda m: slice(m * 512, (m + 1) * 512)
            tsl = lambda j: slice(j * 128, (j + 1) * 128)

            xin = [U[:, 0:2048].bitcast(F32), U[:, 2048:4096].bitcast(F32)]
            xin_b = [ub("xin0"), ub("xin1")]
            for j in range(NT):
                xi, xb = xin[j % 2], xin_b[j % 2]
                T.dma("sp", xi, xin_d[ph][j * 128:(j + 1) * 128, :], xb, writes=[xb])
                for hf in range(2):
                    ps, pb = PS()
                    for k in range(4):
                        c = hf * 4 + k
                        T.op("pe", lambda ps=ps, k=k, c=c, xi=xi: nc.tensor.transpose(ps[:, k * 128:(k + 1) * 128], xi[:, c * 128:(c + 1) * 128], ident[:]),
                             reads=[xb, CB], writes=[pb], sig=(k == 3))
                    cp(xT[:, hf * 4:hf * 4 + 4, tsl(j)], ps[:].rearrange("p (k t) -> p k t", k=4), [pb],
                       [xT_b[c][j // 4] for c in range(hf * 4, hf * 4 + 4)], eng=("act" if hf == 0 else "dve"))

            def norm(l, wh):
                shc = 0 if wh == 0 else 24
                if ph == PH_ADA and wh == 0 and l == 0:
                    ada_block(0, 0); ada_block(0, 1)
                for m in range(NM):
                    ps, pb = PS()
                    for c in range(8):
                        act(sqc[c % 2][:], xT[:, c, msl(m)], AF.Square, [xT_b[c][m]], [sqc_b[c % 2]])
                        T.op("pe", lambda c=c, ps=ps: nc.tensor.matmul(ps[:], onesb[:], sqc[c % 2][:], start=(c == 0), stop=(c == 7)),
                             reads=[sqc_b[c % 2], CB], writes=[pb], sig=True)
                    act(lnr[:], ps[:], AF.Ln, [pb, CB], [lnr_b], bias=cst[:, 0:1])
                    act(rstd[:], lnr[:], AF.Exp, [lnr_b], [rstd_b], scale=-0.5)
                    for c in range(8):
                        stt(ntm[c % 2][:], xT[:, c, msl(m)], gscT[l][:, wh, c, ci:ci + 1], rstd[:], ALU.mult, ALU.mult,
                            [xT_b[c][m], rstd_b, mod_b], [ntm_b[c % 2]])
                        act(hT[:, c, msl(m)], ntm[c % 2][:], AF.Identity, [ntm_b[c % 2], mod_b], [hT_b[c][m]],
                            bias=modcol(l, shc + c, ci))

            nb = {}

            def outproj_resid(name, l, gch):
                sl, slb = wget(name)
                w = v3(sl, 8, 1024)
                for oc in range(8):
                    for m in range(NM):
                        ps, pb = PS()
                        mmg(ps[:], [(w[:, kc, oc * 128:(oc + 1) * 128], hT[:, kc, msl(m)]) for kc in range(8)],
                            [slb] + [hT_b[kc][m] for kc in range(8)], [pb])
                        stt(xT[:, oc, msl(m)], ps[:], modcol(l, gch + oc, ci), xT[:, oc, msl(m)], ALU.mult, ALU.add,
                            [pb, mod_b], [xT_b[oc][m]])

            def ffn(l):
                aT = U[:, 0:32 * TT].rearrange("p (c t) -> p c t", c=32)
                aT_b = [[ub("aT") for m in range(2)] for c in range(32)]
                for b in range(4):
                    sl, slb = wget("w1_%d" % b)
                    w = v3(sl, 8, 1024)
                    order = [(oc, m) for m in range(NM) for oc in range(8)] if b == 0 else [(oc, m) for oc in range(8) for m in range(NM)]
                    pre = {}
                    if b == 0:
                        specs = []
                        for (oc, m) in order[:4]:
                            ps, pb = PS()
                            pre[(oc, m)] = (ps, pb)
                            specs.append((ps[:], pb, [(w[:, kc, oc * 128:(oc + 1) * 128], hT[:, kc, msl(m)], [slb, hT_b[kc][m]]) for kc in range(8)]))
                        mm_kc_outer(specs)
                    for (oc, m) in order:
                        ch = b * 8 + oc
                        if (oc, m) in pre:
                            ps, pb = pre[(oc, m)]
                        else:
                            ps, pb = PS()
                            mmg(ps[:], [(w[:, kc, oc * 128:(oc + 1) * 128], hT[:, kc, msl(m)]) for kc in range(8)],
                                [slb] + [hT_b[kc][m] for kc in range(8)], [pb])
                        r = relu_t[(ch * NM + m) % 2]; r_b = relu_b[(ch * NM + m) % 2]
                        act(r[:], ps[:], AF.Relu, [pb], [r_b])
                        tt(aT[:, ch, msl(m)], r[:], r[:], ALU.mult, [r_b], [aT_b[ch][m]])
                    if ph == PH_ADA and l == 0 and b < 2:
                        ada_block(1, b)
                    if ph == PH_ADA and l == 1 and b == 0:
                        ada_block(1, 5)
                for b in range(4):
                    sl, slb = wget("w2_%d" % b)
                    w = v3(sl, 32, 256)
                    for o2 in range(2):
                        oc = b * 2 + o2
                        for m in range(NM):
                            ps, pb = PS()
                            mmg(ps[:], [(w[:, kc, o2 * 128:(o2 + 1) * 128], aT[:, kc, msl(m)]) for kc in range(32)],
                                [slb] + [aT_b[kc][m] for kc in range(32)], [pb])
                            stt(xT[:, oc, msl(m)], ps[:], modcol(l, 40 + oc, ci), xT[:, oc, msl(m)], ALU.mult, ALU.add,
                                [pb, mod_b], [xT_b[oc][m]])
                    if ph == PH_ADA and l == 0 and b < 3:
                        ada_block(1, 2 + b)


            stage("load%d" % ph)
            dump("x_in_%d" % ph, xT[:, :, 0:TT], [xT_b[c][m] for c in range(8) for m in range(NM)])
            norm(0, 0)
            dump("h_in_%d" % ph, hT[:, :, 0:TT], [hT_b[c][m] for c in range(8) for m in range(NM)])
            stage("norm00_%d" % ph)
            o = [0]

            def ualloc(nelem_bf16):
                a = o[0]
                o[0] += nelem_bf16
                assert o[0] <= 32768, o[0]
                return a

            a_lr = ualloc(TT); lrT = U[:, a_lr:a_lr + TT]
            a_fin = ualloc(2 * TT); finT = U[:, a_fin:a_fin + 2 * TT].rearrange("p (c t) -> p c t", c=2)
            a_xcs = ualloc(NT * 512); xcs = U[:, a_xcs:a_xcs + NT * 512].rearrange("p (j c n) -> p j c n", j=NT, c=2)
            a_og = ualloc(NT * 768); og = U[:, a_og:a_og + NT * 768].rearrange("p (j n) -> p j n", j=NT)
            a_q = ualloc(TT); qT = U[:, a_q:a_q + TT]
            a_k = ualloc(TT); kT = U[:, a_k:a_k + TT]
            a_kt = ualloc(NT * 128); ktok = U[:, a_kt:a_kt + NT * 128].rearrange("p (j n) -> p j n", j=NT)
            a_vt = ualloc(NT * 192); vtok = U[:, a_vt:a_vt + NT * 192].rearrange("p (j n) -> p j n", j=NT)
            a_sg = ualloc(NT * 192); sgtok = U[:, a_sg:a_sg + NT * 192].rearrange("p (j n) -> p j n", j=NT)
            qe = []; ke = []; sprev = []; Sst = []
            for d in range(2):
                a = ualloc(TT); qe.append(U[:, a:a + TT])
                a = ualloc(TT); ke.append(U[:, a:a + TT])
                a = ualloc(NT * 192); sprev.append(U[:, a:a + NT * 192].rearrange("p (j n) -> p j n", j=NT))
                Sd = []
                for k2 in range(2 if ph == 0 else 1):
                    a = ualloc(384); Sd.append(U[:, a:a + 384].bitcast(F32))
                Sst.append(Sd)
            tr = []
            for par in range(2):
                dct = {}
                for nm, ne, dtp in (("e1", 512, F32), ("sp", 256, BF16), ("ekd", 512, F32), ("kd", 256, BF16),
                                    ("eb", 512, F32), ("att", 256, BF16), ("junk", 192, BF16)):
                    a = ualloc(ne)
                    v = U[:, a:a + ne]
                    dct[nm] = v.bitcast(F32) if dtp == F32 else v
                    dct[nm + "_b"] = ub(nm)
                tr.append(dct)
            a_ss = ualloc(NT * 8); ssqa = U[:, a_ss:a_ss + NT * 8].bitcast(F32)
            a_ls = ualloc(NT * 8); lnsa = U[:, a_ls:a_ls + NT * 8].bitcast(F32)
            a_rs = ualloc(NT * 8); rsa = U[:, a_rs:a_rs + NT * 8].bitcast(F32)
            ssqa_b = ub("ssqa"); lnsa_b = ub("lnsa"); rsa_b = ub("rsa")
            lr_b = ub("lr"); fin_b = ub("fin"); xcs_b = [ub("xcs") for _ in range(NT)]; og_b = [ub("og") for _ in range(NT)]
            q_b = ub("q"); k_b = ub("k"); kt_b = ub("kt"); vt_b = ub("vt"); sg_b = ub("sg")
            qe_b = [ub("qef"), ub("qeb")]; ke_b = [ub("kef"), ub("keb")]
            sprev_b = [ub("spf"), ub("spb")]; S_b = [[ub("Sf0"), ub("Sf1")], [ub("Sb0"), ub("Sb1")]]
            hall = lambda m: [hT_b[kc][m] for kc in range(8)]

            T.op("dve", lambda: nc.vector.memset(lrT[32:64, :], 0.0), writes=[lr_b])
            T.op("dve", lambda: nc.vector.memset(lrT[32:33, :], 1.0), writes=[lr_b])

            sl, slb = wget("evC")
            w = v3(sl, 8, 288)
            for m in range(NM):
                psl, pbl = PS()
                psf2 = [PS(), PS()]
                if m == 0:
                    specs = [(psl[0:32, :], pbl, [(w[:, kc, 0:32], hT[:, kc, msl(0)], [slb, hT_b[kc][0]]) for kc in range(8)])]
                    for c2 in range(2):
                        specs.append((psf2[c2][0][:], psf2[c2][1],
                                      [(w[:, kc, 32 + c2 * 128:32 + (c2 + 1) * 128], hT[:, kc, msl(0)], [slb, hT_b[kc][0]]) for kc in range(8)]))
                    mm_kc_outer(specs)
                else:
                    mmg(psl[0:32, :], [(w[:, kc, 0:32], hT[:, kc, msl(m)]) for kc in range(8)], [slb] + hall(m), [pbl])
                    for c2 in range(2):
                        mmg(psf2[c2][0][:], [(w[:, kc, 32 + c2 * 128:32 + (c2 + 1) * 128], hT[:, kc, msl(m)]) for kc in range(8)],
                            [slb] + hall(m), [psf2[c2][1]])
                cp(lrT[0:32, msl(m)], psl[0:32, :], [pbl], [lr_b], eng="act")
                for c2 in range(2):
                    cp(finT[:, c2, msl(m)], psf2[c2][0][:], [psf2[c2][1]], [fin_b], eng="dve")
            for j in range(NT):
                for c2 in range(2):
                    ps, pb = PS()
                    mmg(ps[:, 0:256], [(finT[:, c2, tsl(j)], cs64[:])], [fin_b, CB], [pb])
                    cp(xcs[:, j, c2, :], ps[:, 0:256], [pb], [xcs_b[j]], eng=("act" if c2 == 0 else "dve"))

            stage("evC_%d" % ph)
            o2 = [0]

            def h2alloc(n):
                a = o2[0]; o2[0] += n
                assert o2[0] <= 6144
                return H2[:, a:a + n]

            hs = [dict(qT=qT, kT=kT, ktok=ktok, vtok=vtok, sgtok=sgtok, q=q_b, k=k_b, kt=kt_b, vt=vt_b, sg=sg_b),
                  dict(qT=h2alloc(TT), kT=h2alloc(TT),
                       ktok=h2alloc(NT * 128).rearrange("p (j n) -> p j n", j=NT),
                       vtok=h2alloc(NT * 192).rearrange("p (j n) -> p j n", j=NT),
                       sgtok=h2alloc(NT * 192).rearrange("p (j n) -> p j n", j=NT), **hs1_b)]

            def proj_groups(h):
                H = hs[h % 2]
                st = {}

                def getw():
                    if "w" not in st:
                        sl, slb = wget("evH%d" % h)
                        st["w"] = v3(sl, 8, 640); st["b"] = slb
                    return st["w"], st["b"]

                def gq(m):
                    w, slb = getw()
                    ps, pb = PS()
                    mmg(ps[:], [(w[:, kc, 0:128], hT[:, kc, msl(m)]) for kc in range(8)], [slb] + hall(m), [pb])
                    cp(H["qT"][:, msl(m)], ps[:], [pb], [H["q"]], eng="act")

                def gk(m):
                    w, slb = getw()
                    ps, pb = PS()
                    mmg(ps[:], [(w[:, kc, 128:256], hT[:, kc, msl(m)]) for kc in range(8)], [slb] + hall(m), [pb])
                    cp(H["kT"][:, msl(m)], ps[:], [pb], [H["k"]], eng="dve")

                def gt(j):
                    w, slb = getw()
                    ps, pb = PS()
                    mmg(ps[:], [(hT[:, kc, tsl(j)], w[:, kc, 128:640]) for kc in range(8)], [slb] + hall(j // 4), [pb])
                    cp(H["ktok"][:, j, :], ps[:, 0:128], [pb], [H["kt"]], eng="dve")
                    cp(H["vtok"][:, j, :], ps[:, 128:320], [pb], [H["vt"]], eng="dve")
                    cp(H["sgtok"][:, j, :], ps[:, 320:512], [pb], [H["sg"]], eng="act")

                def gsilu():
                    act(H["sgtok"], H["sgtok"], AF.Silu, [H["sg"]], [H["sg"]])

                gl = []
                for m in range(NM):
                    gl.append(lambda m=m: gq(m)); gl.append(lambda m=m: gk(m))
                for j in range(NT):
                    gl.append(lambda j=j: gt(j))
                gl.append(gsilu)
                return gl

            for g in proj_groups(0):
                g()
            for h in range(4):
                H = hs[h % 2]
                if ph == PH_ADA:
                    ada_block(0, 2 + h)
                pend = proj_groups(h + 1) if h < 3 else []
                steps = []
                for (t0, n) in seqs:
                    fw = list(range(t0, t0 + n)); bw = fw[::-1]
                    for a, b in zip(fw, bw):
                        steps.append((t0 // 2, 0, a, a == fw[0], a == fw[-1]))
                        steps.append((t0 // 2, 1, b, b == bw[0], b == bw[-1]))

                def stA(p):
                    X = tr[p % 2]
                    ps, pb = PS()
                    for i in range(2):
                        si, d, j, first, lastt = steps[2 * p + i]
                        mmg(ps[:, i * 128:(i + 1) * 128], [(lrT[0:64, tsl(j)], w2aug[0:64, d, h * 128:(h + 1) * 128])], [lr_b, CB], [pb])
                    act(X["e1"], ps[:, 0:256], AF.Exp, [pb], [X["e1_b"]], scale=-1.0)
                    act(X["sp"], X["e1"], AF.Ln, [X["e1_b"], CB], [X["sp_b"]], bias=cst[:, 1:2])

                def stB(p):
                    X = tr[p % 2]
                    ps, pb = PS()
                    for i in range(2):
                        si, d, j, first, lastt = steps[2 * p + i]
                        spi = X["sp"][:, i * 128:(i + 1) * 128]
                        mmg(ps[:, i * 128:(i + 1) * 128], [(tri[:, 2 + d, :], spi)], [X["sp_b"], CB], [pb])
                        mmg(ps[:, 256 + i * 128:256 + (i + 1) * 128], [(spi, tri[:, d, :])], [X["sp_b"], CB], [pb])
                    act(X["ekd"], ps[:, 0:256], AF.Exp, [pb], [X["ekd_b"]], scale=-1.0)
                    act(X["eb"], ps[:, 256:512], AF.Exp, [pb], [X["eb_b"]], scale=-1.0)
                    act(enbx[p % 2][:], ps[:, 256:512], AF.Exp, [pb], [enbx_b[p % 2]])

                def stC1(k):
                    si, d, j, first, lastt = steps[k]
                    X = tr[(k // 2) % 2]
                    i = k % 2
                    cs_ = slice(i * 128, (i + 1) * 128)
                    tt(X["kd"][:, cs_], H["ktok"][:, j, :], X["ekd"][:, cs_], ALU.mult, [H["kt"], X["ekd_b"]], [X["kd_b"]])
                    stt(qe[d][:, tsl(j)], H["qT"][:, tsl(j)], 128.0 ** -0.5, X["eb"][:, cs_], ALU.mult, ALU.mult,
                        [H["q"], X["eb_b"]], [qe_b[d]])
                    tt(ke[d][:, tsl(j)], H["kT"][:, tsl(j)], enbx[(k // 2) % 2][:, cs_], ALU.mult, [H["k"], enbx_b[(k // 2) % 2]], [ke_b[d]])

                def stC2(k):
                    si, d, j, first, lastt = steps[k]
                    X = tr[(k // 2) % 2]
                    i = k % 2
                    cs_ = slice(i * 128, (i + 1) * 128)
                    last = i * 128 + (127 if d == 0 else 0)
                    k2 = si if ph == 0 else 0
                    S_ = Sst[d][k2]; Sb_ = S_b[d][k2]
                    if first and ph == 0:
                        T.op("dve", lambda S_=S_: nc.vector.memset(S_, 0.0), writes=[Sb_])
                    ps, pb = PS()
                    mmg(ps[:, 0:192], [(X["kd"][:, cs_], H["vtok"][:, j, :])], [X["kd_b"], H["vt"]], [pb])
                    cp(sprev[d][:, j, :], S_, [Sb_], [sprev_b[d]], eng="act")
                    stt(S_, S_, X["eb"][:, last:last + 1], ps[:, 0:192], ALU.mult, ALU.add,
                        [Sb_, X["eb_b"], pb], [Sb_])
                    if lastt and ph == 0:
                        T.dma("sp", ns_d[d][si, h], S_, Sb_, reads=[Sb_])

                if ph == 1:
                    for d in range(2):
                        T.dma("sp", Sst[d][0], s0_d[d][h], S_b[d][0], writes=[S_b[d][0]])

                def popg():
                    if len(pend) > 1:
                        pend.pop(0)()

                npair = len(steps) // 2
                for i in range(npair + 2):
                    if 0 <= i - 2 < npair:
                        stC1(2 * (i - 2)); stC1(2 * (i - 2) + 1)
                    if i < npair:
                        stA(i)
                    popg()
                    if 0 <= i - 1 < npair:
                        stB(i - 1)
                    popg()
                    if 0 <= i - 2 < npair:
                        stC2(2 * (i - 2)); stC2(2 * (i - 2) + 1)
                stage("h%d_p1_%d" % (h, ph))

                def p2a(j):
                    X = tr[j % 2]
                    ps, pb = PS()
                    mmg(ps[:, 0:128], [(ke[0][:, tsl(j)], qe[0][:, tsl(j)])], [ke_b[0], qe_b[0]], [pb])
                    mmg(ps[:, 128:256], [(ke[1][:, tsl(j)], qe[1][:, tsl(j)])], [ke_b[1], qe_b[1]], [pb])
                    tt(X["att"], ps[:, 0:256], mask2[:], ALU.mult, [pb, CB], [X["att_b"]])

                def p2b(j):
                    X = tr[j % 2]
                    ps, pb = PS()
                    mmg(ps[:, 0:192], [(X["att"][:, 0:128], H["vtok"][:, j, :]), (X["att"][:, 128:256], H["vtok"][:, j, :]),
                                       (qe[0][:, tsl(j)], sprev[0][:, j, :]), (qe[1][:, tsl(j)], sprev[1][:, j, :])],
                        [X["att_b"], H["vt"], qe_b[0], qe_b[1], sprev_b[0], sprev_b[1]], [pb])
                    act(X["junk"], ps[:, 0:192], AF.Square, [pb], [X["junk_b"], ssqa_b], scale=192.0 ** -0.5,
                        accum_out=ssqa[:, j * 4 + h:j * 4 + h + 1])
                    tt(og[:, j, h * 192:(h + 1) * 192], ps[:, 0:192], H["sgtok"][:, j, :], ALU.mult, [pb, H["sg"]], [og_b[j]])

                p2a(0)
                for j in range(NT):
                    if j + 1 < NT:
                        p2a(j + 1)
                    p2b(j)
                    if len(pend) > 1:
                        pend.pop(0)()
                while pend:
                    pend.pop(0)()

            act(lnsa, ssqa, AF.Ln, [ssqa_b, CB], [lnsa_b], bias=cst[:, 0:1])
            act(rsa, lnsa, AF.Exp, [lnsa_b], [rsa_b], scale=-0.5)
            og3 = U[:, a_og:a_og + NT * 768].rearrange("p (g e) -> p g e", e=192)
            tt(og3, og3, rsa.unsqueeze(2).to_broadcast([128, NT * 4, 192]), ALU.mult, [rsa_b] + og_b, og_b)
            stage("gla_%d" % ph)
            for j in range(NT):
                for c in range(6):
                    T.op("pe", lambda j=j, c=c: nc.tensor.transpose(psb[:, c * 128:(c + 1) * 128], og[:, j, c * 128:(c + 1) * 128], identb[:]),
                         reads=[og_b[j], CB], writes=[psb_b], sig=(c == 5))
                tt(hT[:, 0:6, tsl(j)], psb[:, 0:768].rearrange("p (c t) -> p c t", c=6),
                   pp[:, PP_NG:PP_NG + 6].unsqueeze(2).to_broadcast([128, 6, 128]), ALU.mult,
                   [psb_b, CB], [hT_b[c][j // 4] for c in range(6)])
            if ph == 0:
                sl, slb = wget("dft")
                dv = sl[:, 0:1024].rearrange("p (j k n) -> p j k n", j=2, k=2)
                for (t0, n) in seqs:
                    for c2 in range(2):
                        ps, pb = PS()
                        pairs = []
                        for jj in range(2):
                            pairs.append((xcs[:, t0 + jj, c2, 0:128], dv[:, jj, 0, :]))
                            pairs.append((xcs[:, t0 + jj, c2, 128:256], dv[:, jj, 1, :]))
                        mmg(ps[:, 0:256], pairs, [slb, xcs_b[t0], xcs_b[t0 + 1]], [pb])
                        cp(hT[:, 6 + c2, t0 * 128:t0 * 128 + 256], ps[:, 0:256], [pb], [hT_b[6 + c2][0]],
                           eng=("act" if c2 == 0 else "dve"))
            else:
                slc, slcb = wget("dftc")
                sls, slsb = wget("dfts", ahead=NSLOT - 2)
                dc = v3(slc, 8, 1024); ds = v3(sls, 8, 1024)
                for c2 in range(2):
                    for m in range(2):
                        ps, pb = PS()
                        pairs = []
                        for jj in range(8):
                            pairs.append((xcs[:, jj, c2, 0:128], dc[:, jj, msl(m)]))
                            pairs.append((xcs[:, jj, c2, 128:256], ds[:, jj, msl(m)]))
                        mmg(ps[:], pairs, [slcb, slsb] + xcs_b, [pb])
                        cp(hT[:, 6 + c2, msl(m)], ps[:], [pb], [hT_b[6 + c2][m]], eng=("act" if c2 == 0 else "dve"))
            stage("mix0_%d" % ph)
            dump("mix0_%d" % ph, hT[:, :, 0:TT], [hT_b[c][m] for c in range(8) for m in range(NM)])
            outproj_resid("evO", 0, 16)
            dump("x_l0mix_%d" % ph, xT[:, :, 0:TT], [xT_b[c][m] for c in range(8) for m in range(NM)])
            stage("out0_%d" % ph)
            norm(0, 1)
            ffn(0)
            stage("ffn0_%d" % ph)
            dump("x_l0_%d" % ph, xT[:, :, 0:TT], [xT_b[c][m] for c in range(8) for m in range(NM)])

            norm(1, 0)
            uT = U[:, 0:4 * TT].rearrange("p (c t) -> p c t", c=4)
            v1 = U[:, 4096:4096 + NT * 512].rearrange("p (j n) -> p j n", j=NT)
            gbT = U[:, 8192:8192 + 4 * TT].rearrange("p (c t) -> p c t", c=4)
            gcT = U[:, 12288:12288 + 4 * TT].rearrange("p (c t) -> p c t", c=4)
            zT = U[:, 24576:32768].bitcast(F32).rearrange("p (c t) -> p c t", c=4)
            u_b = ub("uT"); v1_b = ub("v1"); gb_b = ub("gb"); gc_b = ub("gc"); z_b = ub("z")
            accs = [U[:, 16384 + i * 2048:16384 + (i + 1) * 2048].bitcast(F32) for i in range(4)]
            acc_b = [ub("acc%d" % i) for i in range(4)]
            sl, slb = wget("odG")
            w = v3(sl, 8, 1024)
            pre = {}
            specs = []
            for oc in range(4):
                ps, pb = PS()
                pre[(oc, 0)] = (ps, pb)
                specs.append((ps[:], pb, [(w[:, kc, oc * 128:(oc + 1) * 128], hT[:, kc, msl(0)], [slb, hT_b[kc][0]]) for kc in range(8)]))
            mm_kc_outer(specs)
            for m in range(NM):
                for oc in range(8):
                    if (oc, m) in pre:
                        ps, pb = pre[(oc, m)]
                    else:
                        ps, pb = PS()
                        mmg(ps[:], [(w[:, kc, oc * 128:(oc + 1) * 128], hT[:, kc, msl(m)]) for kc in range(8)], [slb] + hall(m), [pb])
                    if oc < 4:
                        cp(gbT[:, oc, msl(m)], ps[:], [pb], [gb_b], eng="act")
                    else:
                        cp(gcT[:, oc - 4, msl(m)], ps[:], [pb], [gc_b], eng="dve")
            sl, slb = wget("odH")
            w = v3(sl, 8, 512)
            for oc in range(4):
                for m in range(NM):
                    ps, pb = PS()
                    mmg(ps[:], [(w[:, kc, oc * 128:(oc + 1) * 128], hT[:, kc, msl(m)]) for kc in range(8)], [slb] + hall(m), [pb])
                    tt(zT[:, oc, msl(m)], ps[:], gcT[:, oc, msl(m)], ALU.mult, [pb, gc_b], [z_b])
            RW = 256 if ph == 0 else 64
            for oc in range(4):
                acc = accs[oc][:, 0:TT]; ab = acc_b[oc]
                z = zT[:, oc, 0:TT]
                cw = lambda k: pp[:, PP_CONV + oc * 3 + k:PP_CONV + oc * 3 + k + 1]
                ts(acc, z, cw(1), None, ALU.mult, None, [z_b, CB], [ab])
                a3 = acc.rearrange("p (r w) -> p r w", w=RW)
                z3 = z.rearrange("p (r w) -> p r w", w=RW)
                stt(a3[:, :, 1:RW], z3[:, :, 0:RW - 1], cw(0), a3[:, :, 1:RW], ALU.mult, ALU.add, [z_b, CB, ab], [ab])
                stt(a3[:, :, 0:RW - 1], z3[:, :, 1:RW], cw(2), a3[:, :, 0:RW - 1], ALU.mult, ALU.add, [z_b, CB, ab], [ab])
            sl, slb = wget("odUV")
            w = v3(sl, 8, 1024)
            for oc in range(4):
                for m in range(NM):
                    ps, pb = PS()
                    mmg(ps[:], [(w[:, kc, oc * 128:(oc + 1) * 128], hT[:, kc, msl(m)]) for kc in range(8)], [slb] + hall(m), [pb])
                    act(uT[:, oc, msl(m)], ps[:], AF.Gelu_apprx_tanh, [pb], [u_b])
            for j in range(NT):
                ps, pb = PS()
                mmg(ps[:], [(hT[:, kc, tsl(j)], w[:, kc, 512:1024]) for kc in range(8)], [slb] + hall(j // 4), [pb])
                act(v1[:, j, :], ps[:], AF.Gelu_apprx_tanh, [pb], [v1_b])
            for j in range(NT):
                ps, pb = PS()
                for g in range(4):
                    T.op("pe", lambda g=g, ps=ps, j=j: nc.tensor.matmul(ps[:, g * 128:(g + 1) * 128], v1[:, j, g * 128:(g + 1) * 128], wsT[:, g, :], start=True, stop=False),
                         reads=[v1_b, CB], writes=[pb], sig=False)
                    T.op("pe", lambda g=g, ps=ps: nc.tensor.matmul(ps[:, g * 128:(g + 1) * 128], onesrow_b[0:1, :], gbs_hi[0:1, g * 128:(g + 1) * 128], start=False, stop=False),
                         reads=[CB], writes=[pb], sig=False)
                    T.op("pe", lambda g=g, ps=ps: nc.tensor.matmul(ps[:, g * 128:(g + 1) * 128], onesrow_b[0:1, :], gbs_lo[0:1, g * 128:(g + 1) * 128], start=False, stop=True),
                         reads=[CB], writes=[pb], sig=(g == 3))
                tt(hT[:, 0:4, tsl(j)], ps[:].rearrange("p (g t) -> p g t", g=4), uT[:, :, tsl(j)], ALU.mult,
                   [pb, u_b], [hT_b[c][j // 4] for c in range(4)])
            for oc in range(4):
                tt(hT[:, 4 + oc, 0:TT], accs[oc][:, 0:TT], gbT[:, oc, 0:TT], ALU.mult, [acc_b[oc], gb_b], [hT_b[4 + oc][m] for m in range(NM)])
            stage("mix1_%d" % ph)
            outproj_resid("odO", 1, 16)
            dump("x_l1mix_%d" % ph, xT[:, :, 0:TT], [xT_b[c][m] for c in range(8) for m in range(NM)])
            norm(1, 1)
            ffn(1)
            stage("ffn1_%d" % ph)

            yst = [U[:, 0:2048].bitcast(F32), U[:, 2048:4096].bitcast(F32)]
            yst_b = [ub("yst0"), ub("yst1")]; nb["sq"] = ub("sq")
            for m in range(NM):
                sq = U[:, 20480:24576].rearrange("p (c t) -> p c t", c=8)
                xr = [xT_b[c][m] for c in range(8)]
                act(sq, xT[:, :, msl(m)], AF.Square, xr, [nb["sq"]])
                psr, pbr = PS()
                for jj in range(4):
                    mmg(psr[:, jj:jj + 1], [(sq[:, c, jj * 128:(jj + 1) * 128], onesb[:, 0:1]) for c in range(8)], [nb["sq"], CB], [pbr])
                act(rtok[:, 0:4], psr[:, 0:4], AF.Ln, [pbr, CB], [rtok_b], bias=cst[:, 0:1])
                act(rtok[:, 4:8], rtok[:, 0:4], AF.Exp, [rtok_b], [rtok_b], scale=-0.5)
                for jj in range(4):
                    j = m * 4 + jj
                    ys, yb = yst[j % 2], yst_b[j % 2]
                    for hf in range(2):
                        ps, pb = PS()
                        for k in range(4):
                            c = hf * 4 + k
                            T.op("pe", lambda ps=ps, k=k, c=c, j=j: nc.tensor.transpose(ps[:, k * 128:(k + 1) * 128], xT[:, c, j * 128:(j + 1) * 128], ident[:]),
                                 reads=[xT_b[c][m], CB], writes=[pb], sig=(k == 3))
                        stt(ys[:, hf * 512:(hf + 1) * 512], ps[:], rtok[:, 4 + jj:5 + jj], gfinb[:, hf * 512:(hf + 1) * 512], ALU.mult, ALU.mult,
                            [pb, rtok_b, CB], [yb])
                    T.dma("sp", y_d[ph][j * 128:(j + 1) * 128, :], ys, yb, reads=[yb])

        T.barrier(engines=("pe", "act", "dve", "sp", "pool"))
    return nc


_CACHE = {}


def _get_nc():
    if "nc" not in _CACHE:
        _CACHE["nc"] = build()
        _CACHE["consts"] = _consts()
    return _CACHE["nc"], _CACHE["consts"]


def make_in_maps(inputs, consts):
    f = lambda a: np.ascontiguousarray(np.asarray(a, dtype=np.float32))
    x_prompt = f(inputs["x_prompt"]); x_sample = f(inputs["x_sample"])
    sf = f(inputs["state_gla_fwd"]); sbw = f(inputs["state_gla_bwd"])
    c = f(inputs["c"]); c_ctx = f(inputs["c_ctx"])
    pp = np.zeros((128, PP_N), np.float32)
    for l in range(2):
        pp[:, PP_GMIX + l * 8:PP_GMIX + (l + 1) * 8] = f(inputs["norm_mix_g"])[l].reshape(8, 128).T
        pp[:, PP_GFFN + l * 8:PP_GFFN + (l + 1) * 8] = f(inputs["norm_ffn_g"])[l].reshape(8, 128).T
        pp[:, PP_ADAB + l * 48:PP_ADAB + (l + 1) * 48] = f(inputs["ada_b"])[l].reshape(48, 128).T
    pp[:, PP_CONV:PP_CONV + 12] = f(inputs["conv_w"])[0].T.reshape(4, 128, 3).transpose(1, 0, 2).reshape(128, 12)
    pp[:, PP_NG:PP_NG + 6] = np.tile(f(inputs["gla_norm_g"])[0], 4).reshape(6, 128).T
    w2aug = np.zeros((2, 64, 512), np.float32)
    w2aug[0, 0:16] = f(inputs["gla_w2_f"])[0]
    w2aug[0, 32] = f(inputs["gla_b2_f"])[0]
    w2aug[1, 16:32] = f(inputs["gla_w2_b"])[0]
    w2aug[1, 32] = f(inputs["gla_b2_b"])[0]
    shared = dict(
        pp=pp, w2aug=w2aug, gfin=f(inputs["final_norm_g"]).reshape(1, D),
        gws=f(inputs["gmlp_ws"])[0], gbs=f(inputs["gmlp_b"])[0].reshape(1, 512),
        ada_w=f(inputs["ada_w"]), ffn_w1=f(inputs["ffn_w1"]), ffn_w2=f(inputs["ffn_w2"]),
        ev_w_in=f(inputs["ev_w_in"])[0], ev_w_out=f(inputs["ev_w_out"])[0],
        od_w_in=f(inputs["od_w_in"])[0], od_w_out=f(inputs["od_w_out"])[0],
        ident=consts["ident"], tri=consts["tri"], mask2=consts["mask2"], cs64=consts["cs64"],
        dftc256=consts["dftc256"], dfts256=consts["dfts256"],
        dftc1024=consts["dftc1024"], dfts1024=consts["dfts1024"],
    )
    maps = []
    for i in range(NCORES):
        cond = np.stack([c_ctx, c[i]], axis=0)
        m = dict(shared)
        m["xp"] = x_prompt[2 * i:2 * i + 2].reshape(512, D)
        m["xs"] = x_sample[i]
        m["s0f"] = sf[i, 0]
        m["s0b"] = sbw[i, 0]
        m["condT"] = np.ascontiguousarray(cond.reshape(2, 8, 128).transpose(2, 1, 0))
        maps.append(m)
    return maps


def kernel(**inputs):
    nc, consts = _get_nc()
    maps = make_in_maps(inputs, consts)
    res = run_bass_kernel_spmd(nc, maps, core_ids=list(range(NCORES)))
    r = res.results
    y_prompt = np.concatenate([r[i]["yp"].reshape(2, 256, D) for i in range(NCORES)], axis=0)
    y_sample = np.stack([r[i]["ys"] for i in range(NCORES)], axis=0)
    nsf = np.concatenate([r[i]["nsf"].reshape(2, 1, 4, 128, 192) for i in range(NCORES)], axis=0)
    nsb = np.concatenate([r[i]["nsb"].reshape(2, 1, 4, 128, 192) for i in range(NCORES)], axis=0)
    return (y_prompt.astype(np.float32), y_sample.astype(np.float32),
            nsf.astype(np.float32), nsb.astype(np.float32))
```

```python
import contextlib
import numpy as np
import concourse.bass as bass
import concourse.mybir as mybir
from concourse.bass_utils import run_bass_kernel_spmd

F32 = mybir.dt.float32
BF16 = mybir.dt.bfloat16
AF = mybir.ActivationFunctionType
ALU = mybir.AluOpType

NCORES = 8
D = 1024
NSLOT = 3
EPS = 1e-6


class Buf:
    _n = 0

    def __init__(self, name, excl=False):
        Buf._n += 1
        self.id = Buf._n
        self.name = name
        self.excl = excl
        self.w = {}
        self.r = {}
        self.dsem = None
        self.dcnt = 0


class Tracker:
    def __init__(self, nc, es):
        self.nc = nc
        self.es = es
        self.eng = {}
        for k, h in (("pe", nc.tensor), ("act", nc.scalar), ("dve", nc.vector),
                     ("pool", nc.gpsimd), ("sp", nc.sync)):
            sem = es.enter_context(nc.semaphore("sem_" + k))
            self.eng[k] = dict(h=h, sem=sem, cnt=0, seen={}, key=k)
        self.owners = []
        self.dead = False

    def _collect(self, e, reads, writes):
        need = {}

        def add(key, sem, val):
            if key == "pe" and e["key"] == "pe":
                return
            if e["seen"].get(key, 0) >= val:
                return
            if key not in need or need[key][1] < val:
                need[key] = (sem, val)

        for b in reads:
            for key, (sem, val) in b.w.items():
                add(key, sem, val)
            if b.excl:
                for key, (sem, val) in b.r.items():
                    if key != e["key"]:
                        add(key, sem, val)
        for b in writes:
            for key, (sem, val) in b.w.items():
                add(key, sem, val)
            for key, (sem, val) in b.r.items():
                add(key, sem, val)
        return need

    def _emit(self, e, need, fn):
        items = list(need.items())
        for key, (sem, val) in items[:-1]:
            e["h"].wait_ge(sem, val)
            e["seen"][key] = val
        ins = fn()
        if items:
            key, (sem, val) = items[-1]
            ins._wait_ge(sem, val)
            e["seen"][key] = val
        return ins

    @staticmethod
    def _rec(d, key, sem, val):
        old = d.get(key)
        if old is None or old[1] < val:
            d[key] = (sem, val)

    def op(self, ek, fn, reads=(), writes=(), sig=True):
        if self.dead:
            return None
        e = self.eng[ek]
        need = self._collect(e, reads, writes)
        ins = self._emit(e, need, fn)
        if sig:
            e["cnt"] += 1
            ins.then_inc(e["sem"], 1)
            val = e["cnt"]
        else:
            val = e["cnt"] + 1
        for b in reads:
            self._rec(b.r, ek, e["sem"], val)
        for b in writes:
            self._rec(b.w, ek, e["sem"], val)
        return ins

    def dma(self, qk, out, in_, owner, reads=(), writes=()):
        if self.dead:
            return None
        e = self.eng[qk]
        if owner.dsem is None:
            owner.dsem = self.es.enter_context(self.nc.semaphore("dsem_%d" % owner.id))
            self.owners.append(owner)
        need = self._collect(e, reads, writes)
        ins = self._emit(e, need, lambda: e["h"].dma_start(out=out, in_=in_))
        owner.dcnt += 16
        ins.then_inc(owner.dsem, 16)
        key = "dma%d" % owner.id
        for b in reads:
            self._rec(b.r, key, owner.dsem, owner.dcnt)
        for b in writes:
            self._rec(b.w, key, owner.dsem, owner.dcnt)
        return ins

    def barrier(self, engines=("pe", "act", "dve", "sp")):
        if self.dead:
            return
        targets = [(k, e["sem"], e["cnt"]) for k, e in self.eng.items() if e["cnt"] > 0]
        targets += [("dma%d" % b.id, b.dsem, b.dcnt) for b in self.owners if b.dcnt > 0]
        for ek in engines:
            e = self.eng[ek]
            for key, sem, val in targets:
                if key == ek and ek == "pe":
                    continue
                if e["seen"].get(key, 0) >= val:
                    continue
                e["h"].wait_ge(sem, val)
                e["seen"][key] = val

    def wait_all(self, ek, bufs):
        e = self.eng[ek]
        need = self._collect(e, [], bufs)
        for key, (sem, val) in need.items():
            e["h"].wait_ge(sem, val)
            e["seen"][key] = val


def _consts():
    c = {}
    c["ident"] = np.eye(128, dtype=np.float32)
    s = np.arange(128)[:, None]
    t = np.arange(128)[None, :]
    tri = np.zeros((128, 4, 128), np.float32)
    tri[:, 0, :] = (s <= t) / 16.0
    tri[:, 1, :] = (s >= t) / 16.0
    tri[:, 2, :] = (s > t) / 16.0
    tri[:, 3, :] = (s < t) / 16.0
    c["tri"] = tri
    m2 = np.zeros((128, 256), np.float32)
    m2[:, 0:128] = (s <= t)
    m2[:, 128:256] = (s >= t)
    c["mask2"] = m2
    cs = np.zeros((128, 256), np.float32)
    a = np.arange(64)
    ang = 2 * np.pi * ((a[:, None] * a[None, :]) % 64) / 64.0
    for g in range(2):
        cs[g * 64:(g + 1) * 64, g * 64:(g + 1) * 64] = np.cos(ang)
        cs[g * 64:(g + 1) * 64, 128 + g * 64:128 + (g + 1) * 64] = np.sin(ang)
    c["cs64"] = cs
    for L in (256, 1024):
        tt = np.arange(L, dtype=np.int64)
        ang = 2 * np.pi * ((tt[:, None] * tt[None, :]) % L).astype(np.float64) / L
        sc = 1.0 / np.sqrt(64.0 * L)
        c["dftc%d" % L] = (np.cos(ang) * sc).astype(np.float32)
        c["dfts%d" % L] = (-np.sin(ang) * sc).astype(np.float32)
    return c


PP_GMIX, PP_GFFN, PP_ADAB, PP_CONV, PP_NG, PP_N = 0, 16, 32, 128, 140, 146


def build(debug_dump=None, stop_at=None):
    nc = bass.Bass("TRN2", target_bir_lowering=False)
    dbg = {}

    def din(name, shape):
        return nc.dram_tensor(name, list(shape), F32, kind="ExternalInput").ap()

    def dout(name, shape):
        return nc.dram_tensor(name, list(shape), F32, kind="ExternalOutput").ap()

    xin_d = [din("xp", [512, D]), din("xs", [1024, D])]
    s0_d = [din("s0f", [4, 128, 192]), din("s0b", [4, 128, 192])]
    condT_d = din("condT", [128, 8, 2])
    pp_d = din("pp", [128, PP_N])
    w2aug_d = din("w2aug", [2, 64, 512])
    gfin_d = din("gfin", [1, D])
    gws_d = din("gws", [4, 128, 128])
    gbs_d = din("gbs", [1, 512])
    ada_w = din("ada_w", [2, D, 6 * D])
    ffn_w1 = din("ffn_w1", [2, D, 4 * D])
    ffn_w2 = din("ffn_w2", [2, 4 * D, D])
    ev_w_in = din("ev_w_in", [D, 2848])
    ev_w_out = din("ev_w_out", [D, D])
    od_w_in = din("od_w_in", [D, 2560])
    od_w_out = din("od_w_out", [D, D])
    ident_d = din("ident", [128, 128])
    tri_d = din("tri", [128, 4, 128])
    mask2_d = din("mask2", [128, 256])
    cs64_d = din("cs64", [128, 256])
    dft256_d = [din("dftc256", [256, 256]), din("dfts256", [256, 256])]
    dft1024_d = [din("dftc1024", [1024, 1024]), din("dfts1024", [1024, 1024])]

    y_d = [dout("yp", [512, D]), dout("ys", [1024, D])]
    ns_d = [dout("nsf", [2, 4, 128, 192]), dout("nsb", [2, 4, 128, 192])]

    with contextlib.ExitStack() as es:
        T = Tracker(nc, es)

        def stage(name):
            if stop_at is not None and name == stop_at and not T.dead:
                T.barrier(engines=("pe", "act", "dve", "sp", "pool"))
                T.dead = True

        def sb(name, shape, dt=F32):
            return es.enter_context(nc.sbuf_tensor("sb_" + name, list(shape), dt))

        TMAX = 1024
        xT = sb("xT", [128, 8, TMAX])
        hT = sb("hT", [128, 8, TMAX], BF16)
        slots = [sb("slot%d" % i, [128, 8192], BF16) for i in range(NSLOT)]
        slot_b = [Buf("slot%d" % i) for i in range(NSLOT)]
        U = sb("U", [128, 32768], BF16)
        H2 = sb("H2", [128, 6144], BF16)
        enbx = [sb("enb%d" % i, [128, 256]) for i in range(2)]
        enbx_b = [Buf("enb0"), Buf("enb1")]
        hs1_b = dict(q=Buf("q1"), k=Buf("k1"), kt=Buf("kt1"), vt=Buf("vt1"), sg=Buf("sg1"))
        ident = sb("ident", [128, 128]); identb = sb("identb", [128, 128], BF16)
        tri = sb("tri", [128, 4, 128], BF16)
        mask2 = sb("mask2", [128, 256], BF16)
        cs64 = sb("cs64", [128, 256], BF16)
        onesb = sb("onesb", [128, 128], BF16)
        onesrow = sb("onesrow", [1, 128])
        cst = sb("cst", [128, 4])
        pp = sb("pp", [128, PP_N])
        w2aug = sb("w2aug", [64, 2, 512], BF16)
        gfinb = sb("gfinb", [128, D])
        gws = sb("gws", [128, 4, 128])
        wsT = sb("wsT", [128, 4, 128], BF16)
        gbs = sb("gbs", [1, 512])
        gbs_hi = sb("gbs_hi", [1, 512], BF16); gbs_lo = sb("gbs_lo", [1, 512], BF16); gbs_t = sb("gbs_t", [1, 512])
        onesrow_b = sb("onesrow_b", [1, 128], BF16)
        condT = sb("condT", [128, 8, 2])
        scT = sb("scT", [128, 8, 2], BF16)
        modT = [sb("modT%d" % l, [128, 48, 2]) for l in range(2)]
        gscT = [sb("gscT%d" % l, [128, 2, 8, 2]) for l in range(2)]
        rstd = sb("rstd", [128, 512]); lnr = sb("lnr", [128, 512])
        sqc = [sb("sqc%d" % i, [128, 512], BF16) for i in range(2)]
        ntm = [sb("ntm%d" % i, [128, 512]) for i in range(2)]
        sqc_b = [Buf("sqc0"), Buf("sqc1")]; ntm_b = [Buf("ntm0"), Buf("ntm1")]
        relu_t = [sb("relu%d" % i, [128, 512], BF16) for i in range(2)]
        relu_b = [Buf("relu0"), Buf("relu1")]
        rtok = sb("rtok", [128, 8]);
        CB = Buf("consts")
        U_all = []

        def ub(name):
            b = Buf(name)
            for o in U_all:
                for dct in (o.w, o.r):
                    for key, (sem, val) in dct.items():
                        Tracker._rec(b.r, key, sem, val)
            U_all.append(b)
            return b
        xT_b = [[Buf("xT%d_%d" % (c, m)) for m in range(2)] for c in range(8)]
        hT_b = [[Buf("hT%d_%d" % (c, m)) for m in range(2)] for c in range(8)]
        rstd_b = Buf("rstd"); lnr_b = Buf("lnr"); rtok_b = Buf("rtok")
        mod_b = Buf("mod")

        psf = [es.enter_context(nc.psum_tensor("psf%d" % i, [128, 512], F32)) for i in range(7)]
        psf_b = [Buf("psf%d" % i, excl=True) for i in range(7)]
        psb = es.enter_context(nc.psum_tensor("psb", [128, 1024], BF16))
        psb_b = Buf("psb", excl=True)
        ps_rr = [0]

        def PS():
            i = ps_rr[0] % 7
            ps_rr[0] += 1
            return psf[i], psf_b[i]

        def mmg(out, pairs, R, W):
            n = len(pairs)
            for i, (l, r) in enumerate(pairs):
                T.op("pe", lambda l=l, r=r, i=i: nc.tensor.matmul(out, l, r, start=(i == 0), stop=(i == n - 1)),
                     reads=R, writes=W, sig=(i == n - 1))

        def mm_kc_outer(specs):
            n = len(specs[0][2])
            for kc in range(n):
                for (out, pb, prs) in specs:
                    l, r, R = prs[kc]
                    T.op("pe", lambda out=out, l=l, r=r, kc=kc: nc.tensor.matmul(out, l, r, start=(kc == 0), stop=(kc == n - 1)),
                         reads=R, writes=[pb], sig=(kc == n - 1))

        def act(out, in_, func, R, W, **kw):
            T.op("act", lambda: nc.scalar.activation(out=out, in_=in_, func=func, **kw), reads=R, writes=W)

        def tt(out, a, b, op, R, W, eng="dve"):
            h = nc.vector if eng == "dve" else nc.gpsimd
            T.op(eng, lambda: h.tensor_tensor(out=out, in0=a, in1=b, op=op), reads=R, writes=W)

        def stt(out, a, scalar, b, op0, op1, R, W):
            T.op("dve", lambda: nc.vector.scalar_tensor_tensor(out=out, in0=a, scalar=scalar, in1=b, op0=op0, op1=op1),
                 reads=R, writes=W)

        def ts(out, a, s1, s2, op0, op1, R, W):
            if s2 is None:
                T.op("dve", lambda: nc.vector.tensor_scalar(out=out, in0=a, scalar1=s1, scalar2=None, op0=op0), reads=R, writes=W)
            else:
                T.op("dve", lambda: nc.vector.tensor_scalar(out=out, in0=a, scalar1=s1, scalar2=s2, op0=op0, op1=op1), reads=R, writes=W)

        def cp(out, in_, R, W, eng="dve"):
            if eng == "act":
                T.op("act", lambda: nc.scalar.copy(out=out, in_=in_), reads=R, writes=W)
            else:
                T.op("dve", lambda: nc.vector.tensor_copy(out, in_), reads=R, writes=W)

        def dump(name, ap, R):
            if debug_dump is None or name not in debug_dump:
                return
            d = dout("dbg_" + name, list(ap.shape))
            b = Buf("dbg")
            T.dma("sp" if ap.dtype == F32 else "pool", d, ap, b, reads=R)
            dbg[name] = b

        def cload(dst, src):
            T.dma("sp", dst, src, CB, writes=[CB])

        cload(ident[:], ident_d)
        cload(pp[:], pp_d)
        cload(gfinb[:], gfin_d.to_broadcast([128, D]))
        cload(gws[:], gws_d.rearrange("g p q -> p g q"))
        cload(gbs[:], gbs_d)
        T.op("dve", lambda: nc.vector.memset(onesrow_b[:], 1.0), writes=[CB])
        cp(gbs_hi[:], gbs[:], [CB], [CB])
        tt(gbs_t[:], gbs[:], gbs_hi[:], ALU.subtract, [CB], [CB])
        cp(gbs_lo[:], gbs_t[:], [CB], [CB])
        cload(condT[:], condT_d)
        CBP = Buf("consts_pool")
        T.dma("pool", mask2[:], mask2_d, CBP, writes=[CB])
        T.dma("pool", cs64[:], cs64_d, CBP, writes=[CB])
        T.dma("pool", tri[:], tri_d, CBP, writes=[CB])
        T.dma("pool", w2aug[:], w2aug_d.rearrange("d k n -> k d n"), CBP, writes=[CB])
        T.op("dve", lambda: nc.vector.memset(onesb[:], 1.0 / 1024), writes=[CB])
        T.op("dve", lambda: nc.vector.memset(onesrow[:], 1.0), writes=[CB])
        T.op("dve", lambda: nc.vector.memset(cst[:, 0:1], EPS), writes=[CB])
        T.op("dve", lambda: nc.vector.memset(cst[:, 1:2], 1.0), writes=[CB])
        cp(identb[:], ident[:], [CB], [CB])
        act(scT[:], condT[:], AF.Silu, [CB], [CB])
        for g in range(4):
            ps, pb = PS()
            T.op("pe", lambda g=g, ps=ps: nc.tensor.transpose(ps[:, 0:128], gws[:, g, :], ident[:]), reads=[CB], writes=[pb])
            cp(wsT[:, g, :], ps[:, 0:128], [pb], [CB])

        stage("consts")
        def v3(sl, k, n):
            return sl[:, 0:k * n].rearrange("p (k n) -> p k n", k=k)

        def rows(w2d):
            return w2d.rearrange("(k p) n -> p k n", p=128)

        WSEQ = []

        def wadd(name, fn):
            WSEQ.append((name, fn))

        def wada(l, b):
            wadd("ada%d_%d" % (l, b), lambda sl, l=l, b=b: [(v3(sl, 8, 1024), rows(ada_w[l, :, b * 1024:(b + 1) * 1024]))])

        PH_ORDER = (1, 0)
        PH_ADA = PH_ORDER[0]
        for ph in PH_ORDER:
            A = (ph == PH_ADA)
            for l in range(2):
                if l == 0:
                    if A:
                        wada(0, 0); wada(0, 1)
                    wadd("evC", lambda sl: [(v3(sl, 8, 288), rows(ev_w_in[:, 2560:2848]))])
                    for h in range(4):
                        def f(sl, h=h):
                            v = v3(sl, 8, 640)
                            return [(v[:, :, 0:128], rows(ev_w_in[:, h * 128:(h + 1) * 128])),
                                    (v[:, :, 128:256], rows(ev_w_in[:, 512 + h * 128:512 + (h + 1) * 128])),
                                    (v[:, :, 256:448], rows(ev_w_in[:, 1024 + h * 192:1024 + (h + 1) * 192])),
                                    (v[:, :, 448:640], rows(ev_w_in[:, 1792 + h * 192:1792 + (h + 1) * 192]))]
                        wadd("evH%d" % h, f)
                        if A:
                            wada(0, 2 + h)
                    if ph == 0:
                        def f(sl):
                            v = sl[:, 0:1024].rearrange("p (j k n) -> p j k n", j=2, k=2)
                            return [(v[:, :, 0, :], rows(dft256_d[0])), (v[:, :, 1, :], rows(dft256_d[1]))]
                        wadd("dft", f)
                    else:
                        wadd("dftc", lambda sl: [(v3(sl, 8, 1024), rows(dft1024_d[0]))])
                        wadd("dfts", lambda sl: [(v3(sl, 8, 1024), rows(dft1024_d[1]))])
                    wadd("evO", lambda sl: [(v3(sl, 8, 1024), rows(ev_w_out))])
                else:
                    wadd("odG", lambda sl: [(v3(sl, 8, 1024), rows(od_w_in[:, 1024:2048]))])
                    wadd("odH", lambda sl: [(v3(sl, 8, 512), rows(od_w_in[:, 2048:2560]))])
                    wadd("odUV", lambda sl: [(v3(sl, 8, 1024), rows(od_w_in[:, 0:1024]))])
                    wadd("odO", lambda sl: [(v3(sl, 8, 1024), rows(od_w_out))])
                for b in range(4):
                    wadd("w1_%d" % b, lambda sl, l=l, b=b: [(v3(sl, 8, 1024), rows(ffn_w1[l, :, b * 1024:(b + 1) * 1024]))])
                    if A and l == 0 and b < 2:
                        wada(1, b)
                    if A and l == 1 and b == 0:
                        wada(1, 5)
                for b in range(4):
                    wadd("w2_%d" % b, lambda sl, l=l, b=b: [(v3(sl, 32, 256), rows(ffn_w2[l, :, b * 256:(b + 1) * 256]))])
                    if A and l == 0 and b < 3:
                        wada(1, 2 + b)

        wstate = dict(issued=0, nxt=0)

        def wget(name, ahead=NSLOT - 1):
            i = wstate["nxt"]
            assert WSEQ[i][0] == name, (WSEQ[i][0], name)
            while wstate["issued"] < min(i + ahead + 1, len(WSEQ)):
                j = wstate["issued"]
                s = j % NSLOT
                for (o, src) in WSEQ[j][1](slots[s]):
                    T.dma("pool", o, src, slot_b[s], writes=[slot_b[s]])
                wstate["issued"] += 1
            wstate["nxt"] += 1
            return slots[i % NSLOT], slot_b[i % NSLOT]

        def ada_block(l, b):
            sl, slb = wget("ada%d_%d" % (l, b))
            w = v3(sl, 8, 1024)
            ps, pb = PS()
            for oc in range(8):
                mmg(ps[:, oc * 2:oc * 2 + 2],
                    [(w[:, kc, oc * 128:(oc + 1) * 128], scT[:, kc, :]) for kc in range(8)],
                    [slb, CB], [pb])
            a0 = PP_ADAB + l * 48 + b * 8
            tt(modT[l][:, b * 8:(b + 1) * 8, :], ps[:, 0:16].rearrange("p (c k) -> p c k", k=2),
               pp[:, a0:a0 + 8].unsqueeze(2).to_broadcast([128, 8, 2]), ALU.add, [pb, CB], [mod_b])
            if b in (1, 4):
                wh = 0 if b == 1 else 1
                g0 = (PP_GMIX if wh == 0 else PP_GFFN) + l * 8
                stt(gscT[l][:, wh, :, :], modT[l][:, b * 8:(b + 1) * 8, :], 1.0,
                    pp[:, g0:g0 + 8].unsqueeze(2).to_broadcast([128, 8, 2]), ALU.add, ALU.mult, [mod_b, CB], [mod_b])

        stage("ada")

        def modcol(l, ch, ci):
            return modT[l][:, ch, ci:ci + 1]

        for ph in PH_ORDER:
            ci = ph
            TT = 512 if ph == 0 else 1024
            NT = TT // 128
            NM = TT // 512
            seqs = [(0, 2), (2, 2)] if ph == 0 else [(0, 8)]
            msl = lambda m: slice(m * 512, (m + 1) * 512)
            tsl = lambda j: slice(j * 128, (j + 1) * 128)

            xin = [U[:, 0:2048].bitcast(F32), U[:, 2048:4096].bitcast(F32)]
            xin_b = [ub("xin0"), ub("xin1")]
            for j in range(NT):
                xi, xb = xin[j % 2], xin_b[j % 2]
                T.dma("sp", xi, xin_d[ph][j * 128:(j + 1) * 128, :], xb, writes=[xb])
                for hf in range(2):
                    ps, pb = PS()
                    for k in range(4):
                        c = hf * 4 + k
                        T.op("pe", lambda ps=ps, k=k, c=c, xi=xi: nc.tensor.transpose(ps[:, k * 128:(k + 1) * 128], xi[:, c * 128:(c + 1) * 128], ident[:]),
                             reads=[xb, CB], writes=[pb], sig=(k == 3))
                    cp(xT[:, hf * 4:hf * 4 + 4, tsl(j)], ps[:].rearrange("p (k t) -> p k t", k=4), [pb],
                       [xT_b[c][j // 4] for c in range(hf * 4, hf * 4 + 4)], eng=("act" if hf == 0 else "dve"))

            def norm(l, wh):
                shc = 0 if wh == 0 else 24
                if ph == PH_ADA and wh == 0 and l == 0:
                    ada_block(0, 0); ada_block(0, 1)
                for m in range(NM):
                    ps, pb = PS()
                    for c in range(8):
                        act(sqc[c % 2][:], xT[:, c, msl(m)], AF.Square, [xT_b[c][m]], [sqc_b[c % 2]])
                        T.op("pe", lambda c=c, ps=ps: nc.tensor.matmul(ps[:], onesb[:], sqc[c % 2][:], start=(c == 0), stop=(c == 7)),
                             reads=[sqc_b[c % 2], CB], writes=[pb], sig=True)
                    act(lnr[:], ps[:], AF.Ln, [pb, CB], [lnr_b], bias=cst[:, 0:1])
                    act(rstd[:], lnr[:], AF.Exp, [lnr_b], [rstd_b], scale=-0.5)
                    for c in range(8):
                        stt(ntm[c % 2][:], xT[:, c, msl(m)], gscT[l][:, wh, c, ci:ci + 1], rstd[:], ALU.mult, ALU.mult,
                            [xT_b[c][m], rstd_b, mod_b], [ntm_b[c % 2]])
                        act(hT[:, c, msl(m)], ntm[c % 2][:], AF.Identity, [ntm_b[c % 2], mod_b], [hT_b[c][m]],
                            bias=modcol(l, shc + c, ci))

            nb = {}

            def outproj_resid(name, l, gch):
                sl, slb = wget(name)
                w = v3(sl, 8, 1024)
                for oc in range(8):
                    for m in range(NM):
                        ps, pb = PS()
                        mmg(ps[:], [(w[:, kc, oc * 128:(oc + 1) * 128], hT[:, kc, msl(m)]) for kc in range(8)],
                            [slb] + [hT_b[kc][m] for kc in range(8)], [pb])
                        stt(xT[:, oc, msl(m)], ps[:], modcol(l, gch + oc, ci), xT[:, oc, msl(m)], ALU.mult, ALU.add,
                            [pb, mod_b], [xT_b[oc][m]])

            def ffn(l):
                aT = U[:, 0:32 * TT].rearrange("p (c t) -> p c t", c=32)
                aT_b = [[ub("aT") for m in range(2)] for c in range(32)]
                for b in range(4):
                    sl, slb = wget("w1_%d" % b)
                    w = v3(sl, 8, 1024)
                    order = [(oc, m) for m in range(NM) for oc in range(8)] if b == 0 else [(oc, m) for oc in range(8) for m in range(NM)]
                    pre = {}
                    if b == 0:
                        specs = []
                        for (oc, m) in order[:4]:
                            ps, pb = PS()
                            pre[(oc, m)] = (ps, pb)
                            specs.append((ps[:], pb, [(w[:, kc, oc * 128:(oc + 1) * 128], hT[:, kc, msl(m)], [slb, hT_b[kc][m]]) for kc in range(8)]))
                        mm_kc_outer(specs)
                    for (oc, m) in order:
                        ch = b * 8 + oc
                        if (oc, m) in pre:
                            ps, pb = pre[(oc, m)]
                        else:
                            ps, pb = PS()
                            mmg(ps[:], [(w[:, kc, oc * 128:(oc + 1) * 128], hT[:, kc, msl(m)]) for kc in range(8)],
                                [slb] + [hT_b[kc][m] for kc in range(8)], [pb])
                        r = relu_t[(ch * NM + m) % 2]; r_b = relu_b[(ch * NM + m) % 2]
                        act(r[:], ps[:], AF.Relu, [pb], [r_b])
                        tt(aT[:, ch, msl(m)], r[:], r[:], ALU.mult, [r_b], [aT_b[ch][m]])
                    if ph == PH_ADA and l == 0 and b < 2:
                        ada_block(1, b)
                    if ph == PH_ADA and l == 1 and b == 0:
                        ada_block(1, 5)
                for b in range(4):
                    sl, slb = wget("w2_%d" % b)
                    w = v3(sl, 32, 256)
                    for o2 in range(2):
                        oc = b * 2 + o2
                        for m in range(NM):
                            ps, pb = PS()
                            mmg(ps[:], [(w[:, kc, o2 * 128:(o2 + 1) * 128], aT[:, kc, msl(m)]) for kc in range(32)],
                                [slb] + [aT_b[kc][m] for kc in range(32)], [pb])
                            stt(xT[:, oc, msl(m)], ps[:], modcol(l, 40 + oc, ci), xT[:, oc, msl(m)], ALU.mult, ALU.add,
                                [pb, mod_b], [xT_b[oc][m]])
                    if ph == PH_ADA and l == 0 and b < 3:
                        ada_block(1, 2 + b)


            stage("load%d" % ph)
            dump("x_in_%d" % ph, xT[:, :, 0:TT], [xT_b[c][m] for c in range(8) for m in range(NM)])
            norm(0, 0)
            dump("h_in_%d" % ph, hT[:, :, 0:TT], [hT_b[c][m] for c in range(8) for m in range(NM)])
            stage("norm00_%d" % ph)
            o = [0]

            def ualloc(nelem_bf16):
                a = o[0]
                o[0] += nelem_bf16
                assert o[0] <= 32768, o[0]
                return a

            a_lr = ualloc(TT); lrT = U[:, a_lr:a_lr + TT]
            a_fin = ualloc(2 * TT); finT = U[:, a_fin:a_fin + 2 * TT].rearrange("p (c t) -> p c t", c=2)
            a_xcs = ualloc(NT * 512); xcs = U[:, a_xcs:a_xcs + NT * 512].rearrange("p (j c n) -> p j c n", j=NT, c=2)
            a_og = ualloc(NT * 768); og = U[:, a_og:a_og + NT * 768].rearrange("p (j n) -> p j n", j=NT)
            a_q = ualloc(TT); qT = U[:, a_q:a_q + TT]
            a_k = ualloc(TT); kT = U[:, a_k:a_k + TT]
            a_kt = ualloc(NT * 128); ktok = U[:, a_kt:a_kt + NT * 128].rearrange("p (j n) -> p j n", j=NT)
            a_vt = ualloc(NT * 192); vtok = U[:, a_vt:a_vt + NT * 192].rearrange("p (j n) -> p j n", j=NT)
            a_sg = ualloc(NT * 192); sgtok = U[:, a_sg:a_sg + NT * 192].rearrange("p (j n) -> p j n", j=NT)
            qe = []; ke = []; sprev = []; Sst = []
            for d in range(2):
                a = ualloc(TT); qe.append(U[:, a:a + TT])
                a = ualloc(TT); ke.append(U[:, a:a + TT])
                a = ualloc(NT * 192); sprev.append(U[:, a:a + NT * 192].rearrange("p (j n) -> p j n", j=NT))
                Sd = []
                for k2 in range(2 if ph == 0 else 1):
                    a = ualloc(384); Sd.append(U[:, a:a + 384].bitcast(F32))
                Sst.append(Sd)
            tr = []
            for par in range(2):
                dct = {}
                for nm, ne, dtp in (("e1", 512, F32), ("sp", 256, BF16), ("ekd", 512, F32), ("kd", 256, BF16),
                                    ("eb", 512, F32), ("att", 256, BF16), ("junk", 192, BF16)):
                    a = ualloc(ne)
                    v = U[:, a:a + ne]
                    dct[nm] = v.bitcast(F32) if dtp == F32 else v
                    dct[nm + "_b"] = ub(nm)
                tr.append(dct)
            a_ss = ualloc(NT * 8); ssqa = U[:, a_ss:a_ss + NT * 8].bitcast(F32)
            a_ls = ualloc(NT * 8); lnsa = U[:, a_ls:a_ls + NT * 8].bitcast(F32)
            a_rs = ualloc(NT * 8); rsa = U[:, a_rs:a_rs + NT * 8].bitcast(F32)
            ssqa_b = ub("ssqa"); lnsa_b = ub("lnsa"); rsa_b = ub("rsa")
            lr_b = ub("lr"); fin_b = ub("fin"); xcs_b = [ub("xcs") for _ in range(NT)]; og_b = [ub("og") for _ in range(NT)]
            q_b = ub("q"); k_b = ub("k"); kt_b = ub("kt"); vt_b = ub("vt"); sg_b = ub("sg")
            qe_b = [ub("qef"), ub("qeb")]; ke_b = [ub("kef"), ub("keb")]
            sprev_b = [ub("spf"), ub("spb")]; S_b = [[ub("Sf0"), ub("Sf1")], [ub("Sb0"), ub("Sb1")]]
            hall = lambda m: [hT_b[kc][m] for kc in range(8)]

            T.op("dve", lambda: nc.vector.memset(lrT[32:64, :], 0.0), writes=[lr_b])
            T.op("dve", lambda: nc.vector.memset(lrT[32:33, :], 1.0), writes=[lr_b])

            sl, slb = wget("evC")
            w = v3(sl, 8, 288)
            for m in range(NM):
                psl, pbl = PS()
                psf2 = [PS(), PS()]
                if m == 0:
                    specs = [(psl[0:32, :], pbl, [(w[:, kc, 0:32], hT[:, kc, msl(0)], [slb, hT_b[kc][0]]) for kc in range(8)])]
                    for c2 in range(2):
                        specs.append((psf2[c2][0][:], psf2[c2][1],
                                      [(w[:, kc, 32 + c2 * 128:32 + (c2 + 1) * 128], hT[:, kc, msl(0)], [slb, hT_b[kc][0]]) for kc in range(8)]))
                    mm_kc_outer(specs)
                else:
                    mmg(psl[0:32, :], [(w[:, kc, 0:32], hT[:, kc, msl(m)]) for kc in range(8)], [slb] + hall(m), [pbl])
                    for c2 in range(2):
                        mmg(psf2[c2][0][:], [(w[:, kc, 32 + c2 * 128:32 + (c2 + 1) * 128], hT[:, kc, msl(m)]) for kc in range(8)],
                            [slb] + hall(m), [psf2[c2][1]])
                cp(lrT[0:32, msl(m)], psl[0:32, :], [pbl], [lr_b], eng="act")
                for c2 in range(2):
                    cp(finT[:, c2, msl(m)], psf2[c2][0][:], [psf2[c2][1]], [fin_b], eng="dve")
            for j in range(NT):
                for c2 in range(2):
                    ps, pb = PS()
                    mmg(ps[:, 0:256], [(finT[:, c2, tsl(j)], cs64[:])], [fin_b, CB], [pb])
                    cp(xcs[:, j, c2, :], ps[:, 0:256], [pb], [xcs_b[j]], eng=("act" if c2 == 0 else "dve"))

            stage("evC_%d" % ph)
            o2 = [0]

            def h2alloc(n):
                a = o2[0]; o2[0] += n
                assert o2[0] <= 6144
                return H2[:, a:a + n]

            hs = [dict(qT=qT, kT=kT, ktok=ktok, vtok=vtok, sgtok=sgtok, q=q_b, k=k_b, kt=kt_b, vt=vt_b, sg=sg_b),
                  dict(qT=h2alloc(TT), kT=h2alloc(TT),
                       ktok=h2alloc(NT * 128).rearrange("p (j n) -> p j n", j=NT),
                       vtok=h2alloc(NT * 192).rearrange("p (j n) -> p j n", j=NT),
                       sgtok=h2alloc(NT * 192).rearrange("p (j n) -> p j n", j=NT), **hs1_b)]

            def proj_groups(h):
                H = hs[h % 2]
                st = {}

                def getw():
                    if "w" not in st:
                        sl, slb = wget("evH%d" % h)
                        st["w"] = v3(sl, 8, 640); st["b"] = slb
                    return st["w"], st["b"]

                def gq(m):
                    w, slb = getw()
                    ps, pb = PS()
                    mmg(ps[:], [(w[:, kc, 0:128], hT[:, kc, msl(m)]) for kc in range(8)], [slb] + hall(m), [pb])
                    cp(H["qT"][:, msl(m)], ps[:], [pb], [H["q"]], eng="act")

                def gk(m):
                    w, slb = getw()
                    ps, pb = PS()
                    mmg(ps[:], [(w[:, kc, 128:256], hT[:, kc, msl(m)]) for kc in range(8)], [slb] + hall(m), [pb])
                    cp(H["kT"][:, msl(m)], ps[:], [pb], [H["k"]], eng="dve")

                def gt(j):
                    w, slb = getw()
                    ps, pb = PS()
                    mmg(ps[:], [(hT[:, kc, tsl(j)], w[:, kc, 128:640]) for kc in range(8)], [slb] + hall(j // 4), [pb])
                    cp(H["ktok"][:, j, :], ps[:, 0:128], [pb], [H["kt"]], eng="dve")
                    cp(H["vtok"][:, j, :], ps[:, 128:320], [pb], [H["vt"]], eng="dve")
                    cp(H["sgtok"][:, j, :], ps[:, 320:512], [pb], [H["sg"]], eng="act")

                def gsilu():
                    act(H["sgtok"], H["sgtok"], AF.Silu, [H["sg"]], [H["sg"]])

                gl = []
                for m in range(NM):
                    gl.append(lambda m=m: gq(m)); gl.append(lambda m=m: gk(m))
                for j in range(NT):
                    gl.append(lambda j=j: gt(j))
                gl.append(gsilu)
                return gl

            def dft_groups():
                st = {}
                gl = []
                if ph == 0:
                    def getd():
                        if "dv" not in st:
                            sl, slb = wget("dft")
                            st["dv"] = sl[:, 0:1024].rearrange("p (j k n) -> p j k n", j=2, k=2); st["b"] = slb
                        return st["dv"], st["b"]

                    def g0(t0, c2):
                        dv, slb = getd()
                        ps, pb = PS()
                        pairs = []
                        for jj in range(2):
                            pairs.append((xcs[:, t0 + jj, c2, 0:128], dv[:, jj, 0, :]))
                            pairs.append((xcs[:, t0 + jj, c2, 128:256], dv[:, jj, 1, :]))
                        mmg(ps[:, 0:256], pairs, [slb, xcs_b[t0], xcs_b[t0 + 1]], [pb])
                        cp(hT[:, 6 + c2, t0 * 128:t0 * 128 + 256], ps[:, 0:256], [pb], [hT_b[6 + c2][0]],
                           eng=("act" if c2 == 0 else "dve"))
                    for (t0, n) in seqs:
                        for c2 in range(2):
                            gl.append(lambda t0=t0, c2=c2: g0(t0, c2))
                else:
                    def getd():
                        if "dc" not in st:
                            slc, slcb = wget("dftc")
                            sls, slsb = wget("dfts", ahead=NSLOT - 2)
                            st["dc"] = v3(slc, 8, 1024); st["ds"] = v3(sls, 8, 1024); st["b"] = [slcb, slsb]
                        return st["dc"], st["ds"], st["b"]

                    def g1(c2, m):
                        dc, ds, bb = getd()
                        ps, pb = PS()
                        pairs = []
                        for jj in range(8):
                            pairs.append((xcs[:, jj, c2, 0:128], dc[:, jj, msl(m)]))
                            pairs.append((xcs[:, jj, c2, 128:256], ds[:, jj, msl(m)]))
                        mmg(ps[:], pairs, bb + xcs_b, [pb])
                        cp(hT[:, 6 + c2, msl(m)], ps[:], [pb], [hT_b[6 + c2][m]], eng=("act" if c2 == 0 else "dve"))
                    for c2 in range(2):
                        for m in range(2):
                            gl.append(lambda c2=c2, m=m: g1(c2, m))
                gl.append(lambda: None)
                return gl

            for g in proj_groups(0):
                g()
            for h in range(4):
                H = hs[h % 2]
                if ph == PH_ADA:
                    ada_block(0, 2 + h)
                pend = proj_groups(h + 1) if h < 3 else dft_groups()
                steps = []
                for (t0, n) in seqs:
                    fw = list(range(t0, t0 + n)); bw = fw[::-1]
                    for a, b in zip(fw, bw):
                        steps.append((t0 // 2, 0, a, a == fw[0], a == fw[-1]))
                        steps.append((t0 // 2, 1, b, b == bw[0], b == bw[-1]))

                def stA(p):
                    X = tr[p % 2]
                    ps, pb = PS()
                    for i in range(2):
                        si, d, j, first, lastt = steps[2 * p + i]
                        mmg(ps[:, i * 128:(i + 1) * 128], [(lrT[0:64, tsl(j)], w2aug[0:64, d, h * 128:(h + 1) * 128])], [lr_b, CB], [pb])
                    act(X["e1"], ps[:, 0:256], AF.Exp, [pb], [X["e1_b"]], scale=-1.0)
                    act(X["sp"], X["e1"], AF.Ln, [X["e1_b"], CB], [X["sp_b"]], bias=cst[:, 1:2])

                def stB(p):
                    X = tr[p % 2]
                    ps, pb = PS()
                    for i in range(2):
                        si, d, j, first, lastt = steps[2 * p + i]
                        spi = X["sp"][:, i * 128:(i + 1) * 128]
                        mmg(ps[:, i * 128:(i + 1) * 128], [(tri[:, 2 + d, :], spi)], [X["sp_b"], CB], [pb])
                        mmg(ps[:, 256 + i * 128:256 + (i + 1) * 128], [(spi, tri[:, d, :])], [X["sp_b"], CB], [pb])
                    act(X["ekd"], ps[:, 0:256], AF.Exp, [pb], [X["ekd_b"]], scale=-1.0)
                    act(X["eb"], ps[:, 256:512], AF.Exp, [pb], [X["eb_b"]], scale=-1.0)
                    act(enbx[p % 2][:], ps[:, 256:512], AF.Exp, [pb], [enbx_b[p % 2]])

                def stC1(k):
                    si, d, j, first, lastt = steps[k]
                    X = tr[(k // 2) % 2]
                    i = k % 2
                    cs_ = slice(i * 128, (i + 1) * 128)
                    tt(X["kd"][:, cs_], H["ktok"][:, j, :], X["ekd"][:, cs_], ALU.mult, [H["kt"], X["ekd_b"]], [X["kd_b"]])
                    stt(qe[d][:, tsl(j)], H["qT"][:, tsl(j)], 128.0 ** -0.5, X["eb"][:, cs_], ALU.mult, ALU.mult,
                        [H["q"], X["eb_b"]], [qe_b[d]])
                    tt(ke[d][:, tsl(j)], H["kT"][:, tsl(j)], enbx[(k // 2) % 2][:, cs_], ALU.mult, [H["k"], enbx_b[(k // 2) % 2]], [ke_b[d]])

                def stC2(k):
                    si, d, j, first, lastt = steps[k]
                    X = tr[(k // 2) % 2]
                    i = k % 2
                    cs_ = slice(i * 128, (i + 1) * 128)
                    last = i * 128 + (127 if d == 0 else 0)
                    k2 = si if ph == 0 else 0
                    S_ = Sst[d][k2]; Sb_ = S_b[d][k2]
                    if first and ph == 0:
                        T.op("dve", lambda S_=S_: nc.vector.memset(S_, 0.0), writes=[Sb_])
                    ps, pb = PS()
                    mmg(ps[:, 0:192], [(X["kd"][:, cs_], H["vtok"][:, j, :])], [X["kd_b"], H["vt"]], [pb])
                    cp(sprev[d][:, j, :], S_, [Sb_], [sprev_b[d]], eng="act")
                    stt(S_, S_, X["eb"][:, last:last + 1], ps[:, 0:192], ALU.mult, ALU.add,
                        [Sb_, X["eb_b"], pb], [Sb_])
                    if lastt and ph == 0:
                        T.dma("sp", ns_d[d][si, h], S_, Sb_, reads=[Sb_])

                if ph == 1:
                    for d in range(2):
                        T.dma("sp", Sst[d][0], s0_d[d][h], S_b[d][0], writes=[S_b[d][0]])

                def popg():
                    if len(pend) > 1:
                        pend.pop(0)()

                npair = len(steps) // 2
                for i in range(npair + 2):
                    if 0 <= i - 2 < npair:
                        stC1(2 * (i - 2)); stC1(2 * (i - 2) + 1)
                    if i < npair:
                        stA(i)
                    popg()
                    if 0 <= i - 1 < npair:
                        stB(i - 1)
                    popg()
                    if 0 <= i - 2 < npair:
                        stC2(2 * (i - 2)); stC2(2 * (i - 2) + 1)
                stage("h%d_p1_%d" % (h, ph))

                def p2a(j):
                    X = tr[j % 2]
                    ps, pb = PS()
                    mmg(ps[:, 0:128], [(ke[0][:, tsl(j)], qe[0][:, tsl(j)])], [ke_b[0], qe_b[0]], [pb])
                    mmg(ps[:, 128:256], [(ke[1][:, tsl(j)], qe[1][:, tsl(j)])], [ke_b[1], qe_b[1]], [pb])
                    tt(X["att"], ps[:, 0:256], mask2[:], ALU.mult, [pb, CB], [X["att_b"]])

                def p2b(j):
                    X = tr[j % 2]
                    ps, pb = PS()
                    mmg(ps[:, 0:192], [(X["att"][:, 0:128], H["vtok"][:, j, :]), (X["att"][:, 128:256], H["vtok"][:, j, :]),
                                       (qe[0][:, tsl(j)], sprev[0][:, j, :]), (qe[1][:, tsl(j)], sprev[1][:, j, :])],
                        [X["att_b"], H["vt"], qe_b[0], qe_b[1], sprev_b[0], sprev_b[1]], [pb])
                    act(X["junk"], ps[:, 0:192], AF.Square, [pb], [X["junk_b"], ssqa_b], scale=192.0 ** -0.5,
                        accum_out=ssqa[:, j * 4 + h:j * 4 + h + 1])
                    tt(og[:, j, h * 192:(h + 1) * 192], ps[:, 0:192], H["sgtok"][:, j, :], ALU.mult, [pb, H["sg"]], [og_b[j]])

                p2a(0)
                for j in range(NT):
                    if j + 1 < NT:
                        p2a(j + 1)
                    p2b(j)
                    if len(pend) > 1:
                        pend.pop(0)()
                while pend:
                    pend.pop(0)()

            act(lnsa, ssqa, AF.Ln, [ssqa_b, CB], [lnsa_b], bias=cst[:, 0:1])
            act(rsa, lnsa, AF.Exp, [lnsa_b], [rsa_b], scale=-0.5)
            og3 = U[:, a_og:a_og + NT * 768].rearrange("p (g e) -> p g e", e=192)
            tt(og3, og3, rsa.unsqueeze(2).to_broadcast([128, NT * 4, 192]), ALU.mult, [rsa_b] + og_b, og_b)
            stage("gla_%d" % ph)
            for j in range(NT):
                for c in range(6):
                    T.op("pe", lambda j=j, c=c: nc.tensor.transpose(psb[:, c * 128:(c + 1) * 128], og[:, j, c * 128:(c + 1) * 128], identb[:]),
                         reads=[og_b[j], CB], writes=[psb_b], sig=(c == 5))
                tt(hT[:, 0:6, tsl(j)], psb[:, 0:768].rearrange("p (c t) -> p c t", c=6),
                   pp[:, PP_NG:PP_NG + 6].unsqueeze(2).to_broadcast([128, 6, 128]), ALU.mult,
                   [psb_b, CB], [hT_b[c][j // 4] for c in range(6)])
            stage("mix0_%d" % ph)
            dump("mix0_%d" % ph, hT[:, :, 0:TT], [hT_b[c][m] for c in range(8) for m in range(NM)])
            outproj_resid("evO", 0, 16)
            dump("x_l0mix_%d" % ph, xT[:, :, 0:TT], [xT_b[c][m] for c in range(8) for m in range(NM)])
            stage("out0_%d" % ph)
            norm(0, 1)
            ffn(0)
            stage("ffn0_%d" % ph)
            dump("x_l0_%d" % ph, xT[:, :, 0:TT], [xT_b[c][m] for c in range(8) for m in range(NM)])

            norm(1, 0)
            uT = U[:, 0:4 * TT].rearrange("p (c t) -> p c t", c=4)
            v1 = U[:, 4096:4096 + NT * 512].rearrange("p (j n) -> p j n", j=NT)
            gbT = U[:, 8192:8192 + 4 * TT].rearrange("p (c t) -> p c t", c=4)
            gcT = U[:, 12288:12288 + 4 * TT].rearrange("p (c t) -> p c t", c=4)
            zT = U[:, 24576:32768].bitcast(F32).rearrange("p (c t) -> p c t", c=4)
            u_b = ub("uT"); v1_b = ub("v1"); gb_b = ub("gb"); gc_b = ub("gc"); z_b = ub("z")
            accs = [U[:, 16384 + i * 2048:16384 + (i + 1) * 2048].bitcast(F32) for i in range(4)]
            acc_b = [ub("acc%d" % i) for i in range(4)]
            sl, slb = wget("odG")
            w = v3(sl, 8, 1024)
            pre = {}
            specs = []
            for oc in range(4):
                ps, pb = PS()
                pre[(oc, 0)] = (ps, pb)
                specs.append((ps[:], pb, [(w[:, kc, oc * 128:(oc + 1) * 128], hT[:, kc, msl(0)], [slb, hT_b[kc][0]]) for kc in range(8)]))
            mm_kc_outer(specs)
            for m in range(NM):
                for oc in range(8):
                    if (oc, m) in pre:
                        ps, pb = pre[(oc, m)]
                    else:
                        ps, pb = PS()
                        mmg(ps[:], [(w[:, kc, oc * 128:(oc + 1) * 128], hT[:, kc, msl(m)]) for kc in range(8)], [slb] + hall(m), [pb])
                    if oc < 4:
                        cp(gbT[:, oc, msl(m)], ps[:], [pb], [gb_b], eng="act")
                    else:
                        cp(gcT[:, oc - 4, msl(m)], ps[:], [pb], [gc_b], eng="dve")
            sl, slb = wget("odH")
            w = v3(sl, 8, 512)
            for oc in range(4):
                for m in range(NM):
                    ps, pb = PS()
                    mmg(ps[:], [(w[:, kc, oc * 128:(oc + 1) * 128], hT[:, kc, msl(m)]) for kc in range(8)], [slb] + hall(m), [pb])
                    tt(zT[:, oc, msl(m)], ps[:], gcT[:, oc, msl(m)], ALU.mult, [pb, gc_b], [z_b])
            RW = 256 if ph == 0 else 64
            for oc in range(4):
                acc = accs[oc][:, 0:TT]; ab = acc_b[oc]
                z = zT[:, oc, 0:TT]
                cw = lambda k: pp[:, PP_CONV + oc * 3 + k:PP_CONV + oc * 3 + k + 1]
                ts(acc, z, cw(1), None, ALU.mult, None, [z_b, CB], [ab])
                a3 = acc.rearrange("p (r w) -> p r w", w=RW)
                z3 = z.rearrange("p (r w) -> p r w", w=RW)
                stt(a3[:, :, 1:RW], z3[:, :, 0:RW - 1], cw(0), a3[:, :, 1:RW], ALU.mult, ALU.add, [z_b, CB, ab], [ab])
                stt(a3[:, :, 0:RW - 1], z3[:, :, 1:RW], cw(2), a3[:, :, 0:RW - 1], ALU.mult, ALU.add, [z_b, CB, ab], [ab])
            sl, slb = wget("odUV")
            w = v3(sl, 8, 1024)
            for oc in range(4):
                for m in range(NM):
                    ps, pb = PS()
                    mmg(ps[:], [(w[:, kc, oc * 128:(oc + 1) * 128], hT[:, kc, msl(m)]) for kc in range(8)], [slb] + hall(m), [pb])
                    act(uT[:, oc, msl(m)], ps[:], AF.Gelu_apprx_tanh, [pb], [u_b])
            for j in range(NT):
                ps, pb = PS()
                mmg(ps[:], [(hT[:, kc, tsl(j)], w[:, kc, 512:1024]) for kc in range(8)], [slb] + hall(j // 4), [pb])
                act(v1[:, j, :], ps[:], AF.Gelu_apprx_tanh, [pb], [v1_b])
            for j in range(NT):
                ps, pb = PS()
                for g in range(4):
                    T.op("pe", lambda g=g, ps=ps, j=j: nc.tensor.matmul(ps[:, g * 128:(g + 1) * 128], v1[:, j, g * 128:(g + 1) * 128], wsT[:, g, :], start=True, stop=False),
                         reads=[v1_b, CB], writes=[pb], sig=False)
                    T.op("pe", lambda g=g, ps=ps: nc.tensor.matmul(ps[:, g * 128:(g + 1) * 128], onesrow_b[0:1, :], gbs_hi[0:1, g * 128:(g + 1) * 128], start=False, stop=False),
                         reads=[CB], writes=[pb], sig=False)
                    T.op("pe", lambda g=g, ps=ps: nc.tensor.matmul(ps[:, g * 128:(g + 1) * 128], onesrow_b[0:1, :], gbs_lo[0:1, g * 128:(g + 1) * 128], start=False, stop=True),
                         reads=[CB], writes=[pb], sig=(g == 3))
                tt(hT[:, 0:4, tsl(j)], ps[:].rearrange("p (g t) -> p g t", g=4), uT[:, :, tsl(j)], ALU.mult,
                   [pb, u_b], [hT_b[c][j // 4] for c in range(4)])
            for oc in range(4):
                tt(hT[:, 4 + oc, 0:TT], accs[oc][:, 0:TT], gbT[:, oc, 0:TT], ALU.mult, [acc_b[oc], gb_b], [hT_b[4 + oc][m] for m in range(NM)])
            stage("mix1_%d" % ph)
            outproj_resid("odO", 1, 16)
            dump("x_l1mix_%d" % ph, xT[:, :, 0:TT], [xT_b[c][m] for c in range(8) for m in range(NM)])
            norm(1, 1)
            ffn(1)
            stage("ffn1_%d" % ph)

            yst = [U[:, 0:2048].bitcast(F32), U[:, 2048:4096].bitcast(F32)]
            yst_b = [ub("yst0"), ub("yst1")]; nb["sq"] = ub("sq")
            for m in range(NM):
                sq = U[:, 20480:24576].rearrange("p (c t) -> p c t", c=8)
                xr = [xT_b[c][m] for c in range(8)]
                act(sq, xT[:, :, msl(m)], AF.Square, xr, [nb["sq"]])
                psr, pbr = PS()
                for jj in range(4):
                    mmg(psr[:, jj:jj + 1], [(sq[:, c, jj * 128:(jj + 1) * 128], onesb[:, 0:1]) for c in range(8)], [nb["sq"], CB], [pbr])
                act(rtok[:, 0:4], psr[:, 0:4], AF.Ln, [pbr, CB], [rtok_b], bias=cst[:, 0:1])
                act(rtok[:, 4:8], rtok[:, 0:4], AF.Exp, [rtok_b], [rtok_b], scale=-0.5)
                for jj in range(4):
                    j = m * 4 + jj
                    ys, yb = yst[j % 2], yst_b[j % 2]
                    for hf in range(2):
                        ps, pb = PS()
                        for k in range(4):
                            c = hf * 4 + k
                            T.op("pe", lambda ps=ps, k=k, c=c, j=j: nc.tensor.transpose(ps[:, k * 128:(k + 1) * 128], xT[:, c, j * 128:(j + 1) * 128], ident[:]),
                                 reads=[xT_b[c][m], CB], writes=[pb], sig=(k == 3))
                        stt(ys[:, hf * 512:(hf + 1) * 512], ps[:], rtok[:, 4 + jj:5 + jj], gfinb[:, hf * 512:(hf + 1) * 512], ALU.mult, ALU.mult,
                            [pb, rtok_b, CB], [yb])
                    T.dma("sp", y_d[ph][j * 128:(j + 1) * 128, :], ys, yb, reads=[yb])

        T.barrier(engines=("pe", "act", "dve", "sp", "pool"))
    return nc


_CACHE = {}


def _get_nc():
    if "nc" not in _CACHE:
        _CACHE["nc"] = build()
        _CACHE["consts"] = _consts()
    return _CACHE["nc"], _CACHE["consts"]


def make_in_maps(inputs, consts):
    f = lambda a: np.ascontiguousarray(np.asarray(a, dtype=np.float32))
    x_prompt = f(inputs["x_prompt"]); x_sample = f(inputs["x_sample"])
    sf = f(inputs["state_gla_fwd"]); sbw = f(inputs["state_gla_bwd"])
    c = f(inputs["c"]); c_ctx = f(inputs["c_ctx"])
    pp = np.zeros((128, PP_N), np.float32)
    for l in range(2):
        pp[:, PP_GMIX + l * 8:PP_GMIX + (l + 1) * 8] = f(inputs["norm_mix_g"])[l].reshape(8, 128).T
        pp[:, PP_GFFN + l * 8:PP_GFFN + (l + 1) * 8] = f(inputs["norm_ffn_g"])[l].reshape(8, 128).T
        pp[:, PP_ADAB + l * 48:PP_ADAB + (l + 1) * 48] = f(inputs["ada_b"])[l].reshape(48, 128).T
    pp[:, PP_CONV:PP_CONV + 12] = f(inputs["conv_w"])[0].T.reshape(4, 128, 3).transpose(1, 0, 2).reshape(128, 12)
    pp[:, PP_NG:PP_NG + 6] = np.tile(f(inputs["gla_norm_g"])[0], 4).reshape(6, 128).T
    w2aug = np.zeros((2, 64, 512), np.float32)
    w2aug[0, 0:16] = f(inputs["gla_w2_f"])[0]
    w2aug[0, 32] = f(inputs["gla_b2_f"])[0]
    w2aug[1, 16:32] = f(inputs["gla_w2_b"])[0]
    w2aug[1, 32] = f(inputs["gla_b2_b"])[0]
    shared = dict(
        pp=pp, w2aug=w2aug, gfin=f(inputs["final_norm_g"]).reshape(1, D),
        gws=f(inputs["gmlp_ws"])[0], gbs=f(inputs["gmlp_b"])[0].reshape(1, 512),
        ada_w=f(inputs["ada_w"]), ffn_w1=f(inputs["ffn_w1"]), ffn_w2=f(inputs["ffn_w2"]),
        ev_w_in=f(inputs["ev_w_in"])[0], ev_w_out=f(inputs["ev_w_out"])[0],
        od_w_in=f(inputs["od_w_in"])[0], od_w_out=f(inputs["od_w_out"])[0],
        ident=consts["ident"], tri=consts["tri"], mask2=consts["mask2"], cs64=consts["cs64"],
        dftc256=consts["dftc256"], dfts256=consts["dfts256"],
        dftc1024=consts["dftc1024"], dfts1024=consts["dfts1024"],
    )
    maps = []
    for i in range(NCORES):
        cond = np.stack([c_ctx, c[i]], axis=0)
        m = dict(shared)
        m["xp"] = x_prompt[2 * i:2 * i + 2].reshape(512, D)
        m["xs"] = x_sample[i]
        m["s0f"] = sf[i, 0]
        m["s0b"] = sbw[i, 0]
        m["condT"] = np.ascontiguousarray(cond.reshape(2, 8, 128).transpose(2, 1, 0))
        maps.append(m)
    return maps


def kernel(**inputs):
    nc, consts = _get_nc()
    maps = make_in_maps(inputs, consts)
    res = run_bass_kernel_spmd(nc, maps, core_ids=list(range(NCORES)))
    r = res.results
    y_prompt = np.concatenate([r[i]["yp"].reshape(2, 256, D) for i in range(NCORES)], axis=0)
    y_sample = np.stack([r[i]["ys"] for i in range(NCORES)], axis=0)
    nsf = np.concatenate([r[i]["nsf"].reshape(2, 1, 4, 128, 192) for i in range(NCORES)], axis=0)
    nsb = np.concatenate([r[i]["nsb"].reshape(2, 1, 4, 128, 192) for i in range(NCORES)], axis=0)
    return (y_prompt.astype(np.float32), y_sample.astype(np.float32),
            nsf.astype(np.float32), nsb.astype(np.float32))
```

```python
import contextlib
import numpy as np
import concourse.bass as bass
import concourse.mybir as mybir
from concourse.bass_utils import run_bass_kernel_spmd

F32 = mybir.dt.float32
BF16 = mybir.dt.bfloat16
AF = mybir.ActivationFunctionType
ALU = mybir.AluOpType

NCORES = 8
D = 1024
NSLOT = 3
EPS = 1e-6


class Buf:
    _n = 0

    def __init__(self, name, excl=False):
        Buf._n += 1
        self.id = Buf._n
        self.name = name
        self.excl = excl
        self.w = {}
        self.r = {}
        self.dsem = None
        self.dcnt = 0


class Tracker:
    def __init__(self, nc, es):
        self.nc = nc
        self.es = es
        self.eng = {}
        for k, h in (("pe", nc.tensor), ("act", nc.scalar), ("dve", nc.vector),
                     ("pool", nc.gpsimd), ("sp", nc.sync)):
            sem = es.enter_context(nc.semaphore("sem_" + k))
            self.eng[k] = dict(h=h, sem=sem, cnt=0, seen={}, key=k)
        self.owners = []
        self.dead = False

    def _collect(self, e, reads, writes):
        need = {}

        def add(key, sem, val):
            if key == "pe" and e["key"] == "pe":
                return
            if e["seen"].get(key, 0) >= val:
                return
            if key not in need or need[key][1] < val:
                need[key] = (sem, val)

        for b in reads:
            for key, (sem, val) in b.w.items():
                add(key, sem, val)
            if b.excl:
                for key, (sem, val) in b.r.items():
                    if key != e["key"]:
                        add(key, sem, val)
        for b in writes:
            for key, (sem, val) in b.w.items():
                add(key, sem, val)
            for key, (sem, val) in b.r.items():
                add(key, sem, val)
        return need

    def _emit(self, e, need, fn):
        items = list(need.items())
        for key, (sem, val) in items[:-1]:
            e["h"].wait_ge(sem, val)
            e["seen"][key] = val
        ins = fn()
        if items:
            key, (sem, val) = items[-1]
            ins._wait_ge(sem, val)
            e["seen"][key] = val
        return ins

    @staticmethod
    def _rec(d, key, sem, val):
        old = d.get(key)
        if old is None or old[1] < val:
            d[key] = (sem, val)

    def op(self, ek, fn, reads=(), writes=(), sig=True):
        if self.dead:
            return None
        e = self.eng[ek]
        need = self._collect(e, reads, writes)
        ins = self._emit(e, need, fn)
        if sig:
            e["cnt"] += 1
            ins.then_inc(e["sem"], 1)
            val = e["cnt"]
        else:
            val = e["cnt"] + 1
        for b in reads:
            self._rec(b.r, ek, e["sem"], val)
        for b in writes:
            self._rec(b.w, ek, e["sem"], val)
        return ins

    def dma(self, qk, out, in_, owner, reads=(), writes=()):
        if self.dead:
            return None
        e = self.eng[qk]
        if owner.dsem is None:
            owner.dsem = self.es.enter_context(self.nc.semaphore("dsem_%d" % owner.id))
            self.owners.append(owner)
        need = self._collect(e, reads, writes)
        ins = self._emit(e, need, lambda: e["h"].dma_start(out=out, in_=in_))
        owner.dcnt += 16
        ins.then_inc(owner.dsem, 16)
        key = "dma%d" % owner.id
        for b in reads:
            self._rec(b.r, key, owner.dsem, owner.dcnt)
        for b in writes:
            self._rec(b.w, key, owner.dsem, owner.dcnt)
        return ins

    def barrier(self, engines=("pe", "act", "dve", "sp")):
        if self.dead:
            return
        targets = [(k, e["sem"], e["cnt"]) for k, e in self.eng.items() if e["cnt"] > 0]
        targets += [("dma%d" % b.id, b.dsem, b.dcnt) for b in self.owners if b.dcnt > 0]
        for ek in engines:
            e = self.eng[ek]
            for key, sem, val in targets:
                if key == ek and ek == "pe":
                    continue
                if e["seen"].get(key, 0) >= val:
                    continue
                e["h"].wait_ge(sem, val)
                e["seen"][key] = val

    def wait_all(self, ek, bufs):
        e = self.eng[ek]
        need = self._collect(e, [], bufs)
        for key, (sem, val) in need.items():
            e["h"].wait_ge(sem, val)
            e["seen"][key] = val


def _consts():
    c = {}
    c["ident"] = np.eye(128, dtype=np.float32)
    s = np.arange(128)[:, None]
    t = np.arange(128)[None, :]
    tri = np.zeros((128, 4, 128), np.float32)
    tri[:, 0, :] = (s <= t) / 16.0
    tri[:, 1, :] = (s >= t) / 16.0
    tri[:, 2, :] = (s > t) / 16.0
    tri[:, 3, :] = (s < t) / 16.0
    c["tri"] = tri
    m2 = np.zeros((128, 256), np.float32)
    m2[:, 0:128] = (s <= t)
    m2[:, 128:256] = (s >= t)
    c["mask2"] = m2
    cs = np.zeros((128, 256), np.float32)
    a = np.arange(64)
    ang = 2 * np.pi * ((a[:, None] * a[None, :]) % 64) / 64.0
    for g in range(2):
        cs[g * 64:(g + 1) * 64, g * 64:(g + 1) * 64] = np.cos(ang)
        cs[g * 64:(g + 1) * 64, 128 + g * 64:128 + (g + 1) * 64] = np.sin(ang)
    c["cs64"] = cs
    for L in (256, 1024):
        tt = np.arange(L, dtype=np.int64)
        ang = 2 * np.pi * ((tt[:, None] * tt[None, :]) % L).astype(np.float64) / L
        sc = 1.0 / np.sqrt(64.0 * L)
        c["dftc%d" % L] = (np.cos(ang) * sc).astype(np.float32)
        c["dfts%d" % L] = (-np.sin(ang) * sc).astype(np.float32)
    return c


PP_GMIX, PP_GFFN, PP_ADAB, PP_CONV, PP_NG, PP_N = 0, 16, 32, 128, 140, 146


def build(debug_dump=None, stop_at=None):
    nc = bass.Bass("TRN2", target_bir_lowering=False)
    dbg = {}

    def din(name, shape):
        return nc.dram_tensor(name, list(shape), F32, kind="ExternalInput").ap()

    def dout(name, shape):
        return nc.dram_tensor(name, list(shape), F32, kind="ExternalOutput").ap()

    xin_d = [din("xp", [512, D]), din("xs", [1024, D])]
    s0_d = [din("s0f", [4, 128, 192]), din("s0b", [4, 128, 192])]
    condT_d = din("condT", [128, 8, 2])
    pp_d = din("pp", [128, PP_N])
    w2aug_d = din("w2aug", [2, 64, 512])
    gfin_d = din("gfin", [1, D])
    gws_d = din("gws", [4, 128, 128])
    gbs_d = din("gbs", [1, 512])
    ada_w = din("ada_w", [2, D, 6 * D])
    ffn_w1 = din("ffn_w1", [2, D, 4 * D])
    ffn_w2 = din("ffn_w2", [2, 4 * D, D])
    ev_w_in = din("ev_w_in", [D, 2848])
    ev_w_out = din("ev_w_out", [D, D])
    od_w_in = din("od_w_in", [D, 2560])
    od_w_out = din("od_w_out", [D, D])
    ident_d = din("ident", [128, 128])
    tri_d = din("tri", [128, 4, 128])
    mask2_d = din("mask2", [128, 256])
    cs64_d = din("cs64", [128, 256])
    dft256_d = [din("dftc256", [256, 256]), din("dfts256", [256, 256])]
    dft1024_d = [din("dftc1024", [1024, 1024]), din("dfts1024", [1024, 1024])]

    y_d = [dout("yp", [512, D]), dout("ys", [1024, D])]
    ns_d = [dout("nsf", [2, 4, 128, 192]), dout("nsb", [2, 4, 128, 192])]

    with contextlib.ExitStack() as es:
        T = Tracker(nc, es)

        def stage(name):
            if stop_at is not None and name == stop_at and not T.dead:
                T.barrier(engines=("pe", "act", "dve", "sp", "pool"))
                T.dead = True

        def sb(name, shape, dt=F32):
            return es.enter_context(nc.sbuf_tensor("sb_" + name, list(shape), dt))

        TMAX = 1024
        xT = sb("xT", [128, 8, TMAX])
        hT = sb("hT", [128, 8, TMAX], BF16)
        slots = [sb("slot%d" % i, [128, 8192], BF16) for i in range(NSLOT)]
        slot_b = [Buf("slot%d" % i) for i in range(NSLOT)]
        U = sb("U", [128, 32768], BF16)
        H2 = sb("H2", [128, 6144], BF16)
        enbx = [sb("enb%d" % i, [128, 256]) for i in range(2)]
        enbx_b = [Buf("enb0"), Buf("enb1")]
        hs1_b = dict(q=Buf("q1"), k=Buf("k1"), kt=Buf("kt1"), vt=Buf("vt1"), sg=Buf("sg1"))
        ident = sb("ident", [128, 128]); identb = sb("identb", [128, 128], BF16)
        tri = sb("tri", [128, 4, 128], BF16)
        mask2 = sb("mask2", [128, 256], BF16)
        cs64 = sb("cs64", [128, 256], BF16)
        onesb = sb("onesb", [128, 128], BF16)
        onesrow = sb("onesrow", [1, 128])
        cst = sb("cst", [128, 4])
        pp = sb("pp", [128, PP_N])
        w2aug = sb("w2aug", [64, 2, 512], BF16)
        gfinb = sb("gfinb", [128, D])
        gws = sb("gws", [128, 4, 128])
        wsT = sb("wsT", [128, 4, 128], BF16)
        gbs = sb("gbs", [1, 512])
        gbs_hi = sb("gbs_hi", [1, 512], BF16); gbs_lo = sb("gbs_lo", [1, 512], BF16); gbs_t = sb("gbs_t", [1, 512])
        onesrow_b = sb("onesrow_b", [1, 128], BF16)
        condT = sb("condT", [128, 8, 2])
        scT = sb("scT", [128, 8, 2], BF16)
        modT = [sb("modT%d" % l, [128, 48, 2]) for l in range(2)]
        gscT = [sb("gscT%d" % l, [128, 2, 8, 2]) for l in range(2)]
        rstd = sb("rstd", [128, 512]); lnr = sb("lnr", [128, 512])
        sqc = [sb("sqc%d" % i, [128, 512], BF16) for i in range(2)]
        ntm = [sb("ntm%d" % i, [128, 512]) for i in range(2)]
        sqc_b = [Buf("sqc0"), Buf("sqc1")]; ntm_b = [Buf("ntm0"), Buf("ntm1")]
        relu_t = [sb("relu%d" % i, [128, 512], BF16) for i in range(2)]
        relu_b = [Buf("relu0"), Buf("relu1")]
        rtok = sb("rtok", [128, 8]);
        CB = Buf("consts")
        U_all = []

        def ub(name):
            b = Buf(name)
            for o in U_all:
                for dct in (o.w, o.r):
                    for key, (sem, val) in dct.items():
                        Tracker._rec(b.r, key, sem, val)
            U_all.append(b)
            return b
        xT_b = [[Buf("xT%d_%d" % (c, m)) for m in range(2)] for c in range(8)]
        hT_b = [[Buf("hT%d_%d" % (c, m)) for m in range(2)] for c in range(8)]
        rstd_b = Buf("rstd"); lnr_b = Buf("lnr"); rtok_b = Buf("rtok")
        mod_b = Buf("mod")

        psf = [es.enter_context(nc.psum_tensor("psf%d" % i, [128, 512], F32)) for i in range(7)]
        psf_b = [Buf("psf%d" % i, excl=True) for i in range(7)]
        psb = es.enter_context(nc.psum_tensor("psb", [128, 1024], BF16))
        psb_b = Buf("psb", excl=True)
        ps_rr = [0]

        def PS():
            i = ps_rr[0] % 7
            ps_rr[0] += 1
            return psf[i], psf_b[i]

        def mmg(out, pairs, R, W):
            n = len(pairs)
            for i, (l, r) in enumerate(pairs):
                T.op("pe", lambda l=l, r=r, i=i: nc.tensor.matmul(out, l, r, start=(i == 0), stop=(i == n - 1)),
                     reads=R, writes=W, sig=(i == n - 1))

        def mm_kc_outer(specs):
            n = len(specs[0][2])
            for kc in range(n):
                for (out, pb, prs) in specs:
                    l, r, R = prs[kc]
                    T.op("pe", lambda out=out, l=l, r=r, kc=kc: nc.tensor.matmul(out, l, r, start=(kc == 0), stop=(kc == n - 1)),
                         reads=R, writes=[pb], sig=(kc == n - 1))

        def act(out, in_, func, R, W, **kw):
            T.op("act", lambda: nc.scalar.activation(out=out, in_=in_, func=func, **kw), reads=R, writes=W)

        def tt(out, a, b, op, R, W, eng="dve"):
            h = nc.vector if eng == "dve" else nc.gpsimd
            T.op(eng, lambda: h.tensor_tensor(out=out, in0=a, in1=b, op=op), reads=R, writes=W)

        def stt(out, a, scalar, b, op0, op1, R, W):
            T.op("dve", lambda: nc.vector.scalar_tensor_tensor(out=out, in0=a, scalar=scalar, in1=b, op0=op0, op1=op1),
                 reads=R, writes=W)

        def ts(out, a, s1, s2, op0, op1, R, W):
            if s2 is None:
                T.op("dve", lambda: nc.vector.tensor_scalar(out=out, in0=a, scalar1=s1, scalar2=None, op0=op0), reads=R, writes=W)
            else:
                T.op("dve", lambda: nc.vector.tensor_scalar(out=out, in0=a, scalar1=s1, scalar2=s2, op0=op0, op1=op1), reads=R, writes=W)

        def cp(out, in_, R, W, eng="dve"):
            if eng == "act":
                T.op("act", lambda: nc.scalar.copy(out=out, in_=in_), reads=R, writes=W)
            else:
                T.op("dve", lambda: nc.vector.tensor_copy(out, in_), reads=R, writes=W)

        def dump(name, ap, R):
            if debug_dump is None or name not in debug_dump:
                return
            d = dout("dbg_" + name, list(ap.shape))
            b = Buf("dbg")
            T.dma("sp" if ap.dtype == F32 else "pool", d, ap, b, reads=R)
            dbg[name] = b

        def cload(dst, src):
            T.dma("sp", dst, src, CB, writes=[CB])

        cload(ident[:], ident_d)
        cload(pp[:], pp_d)
        cload(gfinb[:], gfin_d.to_broadcast([128, D]))
        cload(gws[:], gws_d.rearrange("g p q -> p g q"))
        cload(gbs[:], gbs_d)
        T.op("dve", lambda: nc.vector.memset(onesrow_b[:], 1.0), writes=[CB])
        cp(gbs_hi[:], gbs[:], [CB], [CB])
        tt(gbs_t[:], gbs[:], gbs_hi[:], ALU.subtract, [CB], [CB])
        cp(gbs_lo[:], gbs_t[:], [CB], [CB])
        cload(condT[:], condT_d)
        CBP = Buf("consts_pool")
        T.dma("pool", mask2[:], mask2_d, CBP, writes=[CB])
        T.dma("pool", cs64[:], cs64_d, CBP, writes=[CB])
        T.dma("pool", tri[:], tri_d, CBP, writes=[CB])
        T.dma("pool", w2aug[:], w2aug_d.rearrange("d k n -> k d n"), CBP, writes=[CB])
        T.op("dve", lambda: nc.vector.memset(onesb[:], 1.0 / 1024), writes=[CB])
        T.op("dve", lambda: nc.vector.memset(onesrow[:], 1.0), writes=[CB])
        T.op("dve", lambda: nc.vector.memset(cst[:, 0:1], EPS), writes=[CB])
        T.op("dve", lambda: nc.vector.memset(cst[:, 1:2], 1.0), writes=[CB])
        cp(identb[:], ident[:], [CB], [CB])
        act(scT[:], condT[:], AF.Silu, [CB], [CB])
        for g in range(4):
            ps, pb = PS()
            T.op("pe", lambda g=g, ps=ps: nc.tensor.transpose(ps[:, 0:128], gws[:, g, :], ident[:]), reads=[CB], writes=[pb])
            cp(wsT[:, g, :], ps[:, 0:128], [pb], [CB])

        stage("consts")
        def v3(sl, k, n):
            return sl[:, 0:k * n].rearrange("p (k n) -> p k n", k=k)

        def rows(w2d):
            return w2d.rearrange("(k p) n -> p k n", p=128)

        WSEQ = []

        def wadd(name, fn):
            WSEQ.append((name, fn))

        def wada(l, b):
            wadd("ada%d_%d" % (l, b), lambda sl, l=l, b=b: [(v3(sl, 8, 1024), rows(ada_w[l, :, b * 1024:(b + 1) * 1024]))])

        PH_ORDER = (1, 0)
        PH_ADA = PH_ORDER[0]
        for ph in PH_ORDER:
            A = (ph == PH_ADA)
            for l in range(2):
                if l == 0:
                    if A:
                        wada(0, 0); wada(0, 1)
                    wadd("evC", lambda sl: [(v3(sl, 8, 288), rows(ev_w_in[:, 2560:2848]))])
                    for h in range(4):
                        def f(sl, h=h):
                            v = v3(sl, 8, 640)
                            return [(v[:, :, 0:128], rows(ev_w_in[:, h * 128:(h + 1) * 128])),
                                    (v[:, :, 128:256], rows(ev_w_in[:, 512 + h * 128:512 + (h + 1) * 128])),
                                    (v[:, :, 256:448], rows(ev_w_in[:, 1024 + h * 192:1024 + (h + 1) * 192])),
                                    (v[:, :, 448:640], rows(ev_w_in[:, 1792 + h * 192:1792 + (h + 1) * 192]))]
                        wadd("evH%d" % h, f)
                        if A:
                            wada(0, 2 + h)
                    if ph == 0:
                        def f(sl):
                            v = sl[:, 0:1024].rearrange("p (j k n) -> p j k n", j=2, k=2)
                            return [(v[:, :, 0, :], rows(dft256_d[0])), (v[:, :, 1, :], rows(dft256_d[1]))]
                        wadd("dft", f)
                    else:
                        wadd("dftc", lambda sl: [(v3(sl, 8, 1024), rows(dft1024_d[0]))])
                        wadd("dfts", lambda sl: [(v3(sl, 8, 1024), rows(dft1024_d[1]))])
                    wadd("evO", lambda sl: [(v3(sl, 8, 1024), rows(ev_w_out))])
                else:
                    wadd("odG", lambda sl: [(v3(sl, 8, 1024), rows(od_w_in[:, 1024:2048]))])
                    wadd("odH", lambda sl: [(v3(sl, 8, 512), rows(od_w_in[:, 2048:2560]))])
                    wadd("odUV", lambda sl: [(v3(sl, 8, 1024), rows(od_w_in[:, 0:1024]))])
                    wadd("odO", lambda sl: [(v3(sl, 8, 1024), rows(od_w_out))])
                for b in range(4):
                    wadd("w1_%d" % b, lambda sl, l=l, b=b: [(v3(sl, 8, 1024), rows(ffn_w1[l, :, b * 1024:(b + 1) * 1024]))])
                    if A and l == 0 and b < 2:
                        wada(1, b)
                    if A and l == 1 and b == 0:
                        wada(1, 5)
                for b in range(4):
                    wadd("w2_%d" % b, lambda sl, l=l, b=b: [(v3(sl, 32, 256), rows(ffn_w2[l, :, b * 256:(b + 1) * 256]))])
                    if A and l == 0 and b < 3:
                        wada(1, 2 + b)

        wstate = dict(issued=0, nxt=0)

        def wget(name, ahead=NSLOT - 1):
            i = wstate["nxt"]
            assert WSEQ[i][0] == name, (WSEQ[i][0], name)
            while wstate["issued"] < min(i + ahead + 1, len(WSEQ)):
                j = wstate["issued"]
                s = j % NSLOT
                for (o, src) in WSEQ[j][1](slots[s]):
                    T.dma("pool", o, src, slot_b[s], writes=[slot_b[s]])
                wstate["issued"] += 1
            wstate["nxt"] += 1
            return slots[i % NSLOT], slot_b[i % NSLOT]

        def ada_block(l, b):
            sl, slb = wget("ada%d_%d" % (l, b))
            w = v3(sl, 8, 1024)
            ps, pb = PS()
            for oc in range(8):
                mmg(ps[:, oc * 2:oc * 2 + 2],
                    [(w[:, kc, oc * 128:(oc + 1) * 128], scT[:, kc, :]) for kc in range(8)],
                    [slb, CB], [pb])
            a0 = PP_ADAB + l * 48 + b * 8
            tt(modT[l][:, b * 8:(b + 1) * 8, :], ps[:, 0:16].rearrange("p (c k) -> p c k", k=2),
               pp[:, a0:a0 + 8].unsqueeze(2).to_broadcast([128, 8, 2]), ALU.add, [pb, CB], [mod_b])
            if b in (1, 4):
                wh = 0 if b == 1 else 1
                g0 = (PP_GMIX if wh == 0 else PP_GFFN) + l * 8
                stt(gscT[l][:, wh, :, :], modT[l][:, b * 8:(b + 1) * 8, :], 1.0,
                    pp[:, g0:g0 + 8].unsqueeze(2).to_broadcast([128, 8, 2]), ALU.add, ALU.mult, [mod_b, CB], [mod_b])

        stage("ada")

        def modcol(l, ch, ci):
            return modT[l][:, ch, ci:ci + 1]

        for ph in PH_ORDER:
            ci = ph
            TT = 512 if ph == 0 else 1024
            NT = TT // 128
            NM = TT // 512
            seqs = [(0, 2), (2, 2)] if ph == 0 else [(0, 8)]
            msl = lambda m: slice(m * 512, (m + 1) * 512)
            tsl = lambda j: slice(j * 128, (j + 1) * 128)

            xin = [U[:, 0:2048].bitcast(F32), U[:, 2048:4096].bitcast(F32)]
            xin_b = [ub("xin0"), ub("xin1")]
            for j in range(NT):
                xi, xb = xin[j % 2], xin_b[j % 2]
                T.dma("sp", xi, xin_d[ph][j * 128:(j + 1) * 128, :], xb, writes=[xb])
                for hf in range(2):
                    ps, pb = PS()
                    for k in range(4):
                        c = hf * 4 + k
                        T.op("pe", lambda ps=ps, k=k, c=c, xi=xi: nc.tensor.transpose(ps[:, k * 128:(k + 1) * 128], xi[:, c * 128:(c + 1) * 128], ident[:]),
                             reads=[xb, CB], writes=[pb], sig=(k == 3))
                    cp(xT[:, hf * 4:hf * 4 + 4, tsl(j)], ps[:].rearrange("p (k t) -> p k t", k=4), [pb],
                       [xT_b[c][j // 4] for c in range(hf * 4, hf * 4 + 4)], eng=("act" if hf == 0 else "dve"))

            def norm(l, wh):
                shc = 0 if wh == 0 else 24
                if ph == PH_ADA and wh == 0 and l == 0:
                    ada_block(0, 0); ada_block(0, 1)
                for m in range(NM):
                    ps, pb = PS()
                    for c in range(8):
                        act(sqc[c % 2][:], xT[:, c, msl(m)], AF.Square, [xT_b[c][m]], [sqc_b[c % 2]])
                        T.op("pe", lambda c=c, ps=ps: nc.tensor.matmul(ps[:], onesb[:], sqc[c % 2][:], start=(c == 0), stop=(c == 7)),
                             reads=[sqc_b[c % 2], CB], writes=[pb], sig=True)
                    act(lnr[:], ps[:], AF.Ln, [pb, CB], [lnr_b], bias=cst[:, 0:1])
                    act(rstd[:], lnr[:], AF.Exp, [lnr_b], [rstd_b], scale=-0.5)
                    for c in range(8):
                        stt(ntm[c % 2][:], xT[:, c, msl(m)], gscT[l][:, wh, c, ci:ci + 1], rstd[:], ALU.mult, ALU.mult,
                            [xT_b[c][m], rstd_b, mod_b], [ntm_b[c % 2]])
                        act(hT[:, c, msl(m)], ntm[c % 2][:], AF.Identity, [ntm_b[c % 2], mod_b], [hT_b[c][m]],
                            bias=modcol(l, shc + c, ci))

            nb = {}

            def outproj_resid(name, l, gch):
                sl, slb = wget(name)
                w = v3(sl, 8, 1024)
                for oc in range(8):
                    for m in range(NM):
                        ps, pb = PS()
                        mmg(ps[:], [(w[:, kc, oc * 128:(oc + 1) * 128], hT[:, kc, msl(m)]) for kc in range(8)],
                            [slb] + [hT_b[kc][m] for kc in range(8)], [pb])
                        stt(xT[:, oc, msl(m)], ps[:], modcol(l, gch + oc, ci), xT[:, oc, msl(m)], ALU.mult, ALU.add,
                            [pb, mod_b], [xT_b[oc][m]])

            def ffn(l):
                aT = U[:, 0:32 * TT].rearrange("p (c t) -> p c t", c=32)
                aT_b = [[ub("aT") for m in range(2)] for c in range(32)]
                for b in range(4):
                    sl, slb = wget("w1_%d" % b)
                    w = v3(sl, 8, 1024)
                    order = [(oc, m) for m in range(NM) for oc in range(8)] if b == 0 else [(oc, m) for oc in range(8) for m in range(NM)]
                    pre = {}
                    if b == 0:
                        specs = []
                        for (oc, m) in order[:4]:
                            ps, pb = PS()
                            pre[(oc, m)] = (ps, pb)
                            specs.append((ps[:], pb, [(w[:, kc, oc * 128:(oc + 1) * 128], hT[:, kc, msl(m)], [slb, hT_b[kc][m]]) for kc in range(8)]))
                        mm_kc_outer(specs)
                    for (oc, m) in order:
                        ch = b * 8 + oc
                        if (oc, m) in pre:
                            ps, pb = pre[(oc, m)]
                        else:
                            ps, pb = PS()
                            mmg(ps[:], [(w[:, kc, oc * 128:(oc + 1) * 128], hT[:, kc, msl(m)]) for kc in range(8)],
                                [slb] + [hT_b[kc][m] for kc in range(8)], [pb])
                        r = relu_t[(ch * NM + m) % 2]; r_b = relu_b[(ch * NM + m) % 2]
                        act(r[:], ps[:], AF.Relu, [pb], [r_b])
                        tt(aT[:, ch, msl(m)], r[:], r[:], ALU.mult, [r_b], [aT_b[ch][m]])
                    if ph == PH_ADA and l == 0 and b < 2:
                        ada_block(1, b)
                    if ph == PH_ADA and l == 1 and b == 0:
                        ada_block(1, 5)
                for b in range(4):
                    sl, slb = wget("w2_%d" % b)
                    w = v3(sl, 32, 256)
                    for o2 in range(2):
                        oc = b * 2 + o2
                        for m in range(NM):
                            ps, pb = PS()
                            mmg(ps[:], [(w[:, kc, o2 * 128:(o2 + 1) * 128], aT[:, kc, msl(m)]) for kc in range(32)],
                                [slb] + [aT_b[kc][m] for kc in range(32)], [pb])
                            stt(xT[:, oc, msl(m)], ps[:], modcol(l, 40 + oc, ci), xT[:, oc, msl(m)], ALU.mult, ALU.add,
                                [pb, mod_b], [xT_b[oc][m]])
                    if ph == PH_ADA and l == 0 and b < 3:
                        ada_block(1, 2 + b)


            stage("load%d" % ph)
            dump("x_in_%d" % ph, xT[:, :, 0:TT], [xT_b[c][m] for c in range(8) for m in range(NM)])
            norm(0, 0)
            dump("h_in_%d" % ph, hT[:, :, 0:TT], [hT_b[c][m] for c in range(8) for m in range(NM)])
            stage("norm00_%d" % ph)
            o = [0]

            def ualloc(nelem_bf16):
                a = o[0]
                o[0] += nelem_bf16
                assert o[0] <= 32768, o[0]
                return a

            a_lr = ualloc(TT); lrT = U[:, a_lr:a_lr + TT]
            a_fin = ualloc(2 * TT); finT = U[:, a_fin:a_fin + 2 * TT].rearrange("p (c t) -> p c t", c=2)
            a_xcs = ualloc(NT * 512); xcs = U[:, a_xcs:a_xcs + NT * 512].rearrange("p (j c n) -> p j c n", j=NT, c=2)
            a_og = ualloc(NT * 768); og = U[:, a_og:a_og + NT * 768].rearrange("p (j n) -> p j n", j=NT)
            a_q = ualloc(TT); qT = U[:, a_q:a_q + TT]
            a_k = ualloc(TT); kT = U[:, a_k:a_k + TT]
            a_kvg = ualloc(NT * 512); kvg0 = U[:, a_kvg:a_kvg + NT * 512].rearrange("p (j n) -> p j n", j=NT)
            ktok = kvg0[:, :, 0:128]; vtok = kvg0[:, :, 128:320]; sgtok = kvg0[:, :, 320:512]
            qe = []; ke = []; sprev = []; Sst = []
            for d in range(2):
                a = ualloc(TT); qe.append(U[:, a:a + TT])
                a = ualloc(TT); ke.append(U[:, a:a + TT])
                a = ualloc(NT * 192); sprev.append(U[:, a:a + NT * 192].rearrange("p (j n) -> p j n", j=NT))
                Sd = []
                for k2 in range(2 if ph == 0 else 1):
                    a = ualloc(384); Sd.append(U[:, a:a + 384].bitcast(F32))
                Sst.append(Sd)
            tr = []
            for par in range(2):
                dct = {}
                for nm, ne, dtp in (("e1", 512, F32), ("sp", 256, BF16), ("ekd", 512, F32), ("kd", 256, BF16),
                                    ("eb", 512, F32), ("att", 256, BF16), ("junk", 192, BF16)):
                    a = ualloc(ne)
                    v = U[:, a:a + ne]
                    dct[nm] = v.bitcast(F32) if dtp == F32 else v
                    dct[nm + "_b"] = ub(nm)
                tr.append(dct)
            a_ss = ualloc(NT * 8); ssqa = U[:, a_ss:a_ss + NT * 8].bitcast(F32)
            a_ls = ualloc(NT * 8); lnsa = U[:, a_ls:a_ls + NT * 8].bitcast(F32)
            a_rs = ualloc(NT * 8); rsa = U[:, a_rs:a_rs + NT * 8].bitcast(F32)
            ssqa_b = ub("ssqa"); lnsa_b = ub("lnsa"); rsa_b = ub("rsa")
            lr_b = ub("lr"); fin_b = ub("fin"); xcs_b = [ub("xcs") for _ in range(NT)]; og_b = [ub("og") for _ in range(NT)]
            q_b = ub("q"); k_b = ub("k"); kt_b = ub("kt"); vt_b = ub("vt"); sg_b = ub("sg")
            qe_b = [ub("qef"), ub("qeb")]; ke_b = [ub("kef"), ub("keb")]
            sprev_b = [ub("spf"), ub("spb")]; S_b = [[ub("Sf0"), ub("Sf1")], [ub("Sb0"), ub("Sb1")]]
            hall = lambda m: [hT_b[kc][m] for kc in range(8)]

            T.op("dve", lambda: nc.vector.memset(lrT[32:64, :], 0.0), writes=[lr_b])
            T.op("dve", lambda: nc.vector.memset(lrT[32:33, :], 1.0), writes=[lr_b])

            sl, slb = wget("evC")
            w = v3(sl, 8, 288)
            for m in range(NM):
                psl, pbl = PS()
                psf2 = [PS(), PS()]
                if m == 0:
                    specs = [(psl[0:32, :], pbl, [(w[:, kc, 0:32], hT[:, kc, msl(0)], [slb, hT_b[kc][0]]) for kc in range(8)])]
                    for c2 in range(2):
                        specs.append((psf2[c2][0][:], psf2[c2][1],
                                      [(w[:, kc, 32 + c2 * 128:32 + (c2 + 1) * 128], hT[:, kc, msl(0)], [slb, hT_b[kc][0]]) for kc in range(8)]))
                    mm_kc_outer(specs)
                else:
                    mmg(psl[0:32, :], [(w[:, kc, 0:32], hT[:, kc, msl(m)]) for kc in range(8)], [slb] + hall(m), [pbl])
                    for c2 in range(2):
                        mmg(psf2[c2][0][:], [(w[:, kc, 32 + c2 * 128:32 + (c2 + 1) * 128], hT[:, kc, msl(m)]) for kc in range(8)],
                            [slb] + hall(m), [psf2[c2][1]])
                cp(lrT[0:32, msl(m)], psl[0:32, :], [pbl], [lr_b], eng="act")
                for c2 in range(2):
                    cp(finT[:, c2, msl(m)], psf2[c2][0][:], [psf2[c2][1]], [fin_b], eng="dve")
            for j in range(NT):
                for c2 in range(2):
                    ps, pb = PS()
                    mmg(ps[:, 0:256], [(finT[:, c2, tsl(j)], cs64[:])], [fin_b, CB], [pb])
                    cp(xcs[:, j, c2, :], ps[:, 0:256], [pb], [xcs_b[j]], eng=("act" if c2 == 0 else "dve"))

            stage("evC_%d" % ph)
            o2 = [0]

            def h2alloc(n):
                a = o2[0]; o2[0] += n
                assert o2[0] <= 6144
                return H2[:, a:a + n]

            q1_ = h2alloc(TT); k1_ = h2alloc(TT)
            kvg1 = h2alloc(NT * 512).rearrange("p (j n) -> p j n", j=NT)
            hs = [dict(qT=qT, kT=kT, kvg=kvg0, ktok=ktok, vtok=vtok, sgtok=sgtok, q=q_b, k=k_b, kt=kt_b, vt=vt_b, sg=sg_b),
                  dict(qT=q1_, kT=k1_, kvg=kvg1, ktok=kvg1[:, :, 0:128], vtok=kvg1[:, :, 128:320], sgtok=kvg1[:, :, 320:512], **hs1_b)]

            def proj_groups(h):
                H = hs[h % 2]
                st = {}

                def getw():
                    if "w" not in st:
                        sl, slb = wget("evH%d" % h)
                        st["w"] = v3(sl, 8, 640); st["b"] = slb
                    return st["w"], st["b"]

                def gq(m):
                    w, slb = getw()
                    ps, pb = PS()
                    mmg(ps[:], [(w[:, kc, 0:128], hT[:, kc, msl(m)]) for kc in range(8)], [slb] + hall(m), [pb])
                    cp(H["qT"][:, msl(m)], ps[:], [pb], [H["q"]], eng="act")

                def gk(m):
                    w, slb = getw()
                    ps, pb = PS()
                    mmg(ps[:], [(w[:, kc, 128:256], hT[:, kc, msl(m)]) for kc in range(8)], [slb] + hall(m), [pb])
                    cp(H["kT"][:, msl(m)], ps[:], [pb], [H["k"]], eng="dve")

                def gt(j):
                    w, slb = getw()
                    ps, pb = PS()
                    mmg(ps[:], [(hT[:, kc, tsl(j)], w[:, kc, 128:640]) for kc in range(8)], [slb] + hall(j // 4), [pb])
                    cp(H["kvg"][:, j, :], ps[:], [pb], [H["kt"], H["vt"], H["sg"]], eng=("dve" if j % 2 == 0 else "act"))

                def gsilu():
                    act(H["sgtok"], H["sgtok"], AF.Silu, [H["sg"]], [H["sg"]])

                gl = []
                for m in range(NM):
                    gl.append(lambda m=m: gq(m)); gl.append(lambda m=m: gk(m))
                for j in range(NT):
                    gl.append(lambda j=j: gt(j))
                gl.append(gsilu)
                return gl

            def dft_groups():
                st = {}
                gl = []
                if ph == 0:
                    def getd():
                        if "dv" not in st:
                            sl, slb = wget("dft")
                            st["dv"] = sl[:, 0:1024].rearrange("p (j k n) -> p j k n", j=2, k=2); st["b"] = slb
                        return st["dv"], st["b"]

                    def g0(t0, c2):
                        dv, slb = getd()
                        ps, pb = PS()
                        pairs = []
                        for jj in range(2):
                            pairs.append((xcs[:, t0 + jj, c2, 0:128], dv[:, jj, 0, :]))
                            pairs.append((xcs[:, t0 + jj, c2, 128:256], dv[:, jj, 1, :]))
                        mmg(ps[:, 0:256], pairs, [slb, xcs_b[t0], xcs_b[t0 + 1]], [pb])
                        cp(hT[:, 6 + c2, t0 * 128:t0 * 128 + 256], ps[:, 0:256], [pb], [hT_b[6 + c2][0]],
                           eng=("act" if c2 == 0 else "dve"))
                    for (t0, n) in seqs:
                        for c2 in range(2):
                            gl.append(lambda t0=t0, c2=c2: g0(t0, c2))
                else:
                    def getd():
                        if "dc" not in st:
                            slc, slcb = wget("dftc")
                            sls, slsb = wget("dfts", ahead=NSLOT - 2)
                            st["dc"] = v3(slc, 8, 1024); st["ds"] = v3(sls, 8, 1024); st["b"] = [slcb, slsb]
                        return st["dc"], st["ds"], st["b"]

                    def g1(c2, m):
                        dc, ds, bb = getd()
                        ps, pb = PS()
                        pairs = []
                        for jj in range(8):
                            pairs.append((xcs[:, jj, c2, 0:128], dc[:, jj, msl(m)]))
                            pairs.append((xcs[:, jj, c2, 128:256], ds[:, jj, msl(m)]))
                        mmg(ps[:], pairs, bb + xcs_b, [pb])
                        cp(hT[:, 6 + c2, msl(m)], ps[:], [pb], [hT_b[6 + c2][m]], eng=("act" if c2 == 0 else "dve"))
                    for c2 in range(2):
                        for m in range(2):
                            gl.append(lambda c2=c2, m=m: g1(c2, m))
                gl.append(lambda: None)
                return gl

            for g in proj_groups(0):
                g()
            for h in range(4):
                H = hs[h % 2]
                if ph == PH_ADA:
                    ada_block(0, 2 + h)
                pend = proj_groups(h + 1) if h < 3 else dft_groups()
                steps = []
                for (t0, n) in seqs:
                    fw = list(range(t0, t0 + n)); bw = fw[::-1]
                    for a, b in zip(fw, bw):
                        steps.append((t0 // 2, 0, a, a == fw[0], a == fw[-1]))
                        steps.append((t0 // 2, 1, b, b == bw[0], b == bw[-1]))

                def stA(p):
                    X = tr[p % 2]
                    ps, pb = PS()
                    for i in range(2):
                        si, d, j, first, lastt = steps[2 * p + i]
                        mmg(ps[:, i * 128:(i + 1) * 128], [(lrT[0:64, tsl(j)], w2aug[0:64, d, h * 128:(h + 1) * 128])], [lr_b, CB], [pb])
                    act(X["e1"], ps[:, 0:256], AF.Exp, [pb], [X["e1_b"]], scale=-1.0)
                    act(X["sp"], X["e1"], AF.Ln, [X["e1_b"], CB], [X["sp_b"]], bias=cst[:, 1:2])

                def stB(p):
                    X = tr[p % 2]
                    ps, pb = PS()
                    for i in range(2):
                        si, d, j, first, lastt = steps[2 * p + i]
                        spi = X["sp"][:, i * 128:(i + 1) * 128]
                        mmg(ps[:, i * 128:(i + 1) * 128], [(tri[:, 2 + d, :], spi)], [X["sp_b"], CB], [pb])
                        mmg(ps[:, 256 + i * 128:256 + (i + 1) * 128], [(spi, tri[:, d, :])], [X["sp_b"], CB], [pb])
                    act(X["ekd"], ps[:, 0:256], AF.Exp, [pb], [X["ekd_b"]], scale=-1.0)
                    act(X["eb"], ps[:, 256:512], AF.Exp, [pb], [X["eb_b"]], scale=-1.0)
                    act(enbx[p % 2][:], ps[:, 256:512], AF.Exp, [pb], [enbx_b[p % 2]])

                def stC1(k):
                    si, d, j, first, lastt = steps[k]
                    X = tr[(k // 2) % 2]
                    i = k % 2
                    cs_ = slice(i * 128, (i + 1) * 128)
                    tt(X["kd"][:, cs_], H["ktok"][:, j, :], X["ekd"][:, cs_], ALU.mult, [H["kt"], X["ekd_b"]], [X["kd_b"]])
                    stt(qe[d][:, tsl(j)], H["qT"][:, tsl(j)], 128.0 ** -0.5, X["eb"][:, cs_], ALU.mult, ALU.mult,
                        [H["q"], X["eb_b"]], [qe_b[d]])
                    tt(ke[d][:, tsl(j)], H["kT"][:, tsl(j)], enbx[(k // 2) % 2][:, cs_], ALU.mult, [H["k"], enbx_b[(k // 2) % 2]], [ke_b[d]])

                def stC2(k):
                    si, d, j, first, lastt = steps[k]
                    X = tr[(k // 2) % 2]
                    i = k % 2
                    cs_ = slice(i * 128, (i + 1) * 128)
                    last = i * 128 + (127 if d == 0 else 0)
                    k2 = si if ph == 0 else 0
                    S_ = Sst[d][k2]; Sb_ = S_b[d][k2]
                    if first and ph == 0:
                        T.op("dve", lambda S_=S_: nc.vector.memset(S_, 0.0), writes=[Sb_])
                    ps, pb = PS()
                    mmg(ps[:, 0:192], [(X["kd"][:, cs_], H["vtok"][:, j, :])], [X["kd_b"], H["vt"]], [pb])
                    cp(sprev[d][:, j, :], S_, [Sb_], [sprev_b[d]], eng="act")
                    stt(S_, S_, X["eb"][:, last:last + 1], ps[:, 0:192], ALU.mult, ALU.add,
                        [Sb_, X["eb_b"], pb], [Sb_])
                    if lastt and ph == 0:
                        T.dma("sp", ns_d[d][si, h], S_, Sb_, reads=[Sb_])

                if ph == 1:
                    for d in range(2):
                        T.dma("sp", Sst[d][0], s0_d[d][h], S_b[d][0], writes=[S_b[d][0]])

                def popg():
                    if len(pend) > 1:
                        pend.pop(0)()

                npair = len(steps) // 2
                for i in range(npair + 2):
                    if 0 <= i - 2 < npair:
                        stC1(2 * (i - 2)); stC1(2 * (i - 2) + 1)
                    if i < npair:
                        stA(i)
                    popg()
                    if 0 <= i - 1 < npair:
                        stB(i - 1)
                    popg()
                    if 0 <= i - 2 < npair:
                        stC2(2 * (i - 2)); stC2(2 * (i - 2) + 1)
                stage("h%d_p1_%d" % (h, ph))

                def p2a(j):
                    X = tr[j % 2]
                    ps, pb = PS()
                    mmg(ps[:, 0:128], [(ke[0][:, tsl(j)], qe[0][:, tsl(j)])], [ke_b[0], qe_b[0]], [pb])
                    mmg(ps[:, 128:256], [(ke[1][:, tsl(j)], qe[1][:, tsl(j)])], [ke_b[1], qe_b[1]], [pb])
                    tt(X["att"], ps[:, 0:256], mask2[:], ALU.mult, [pb, CB], [X["att_b"]])

                def p2b(j):
                    X = tr[j % 2]
                    ps, pb = PS()
                    mmg(ps[:, 0:192], [(X["att"][:, 0:128], H["vtok"][:, j, :]), (X["att"][:, 128:256], H["vtok"][:, j, :]),
                                       (qe[0][:, tsl(j)], sprev[0][:, j, :]), (qe[1][:, tsl(j)], sprev[1][:, j, :])],
                        [X["att_b"], H["vt"], qe_b[0], qe_b[1], sprev_b[0], sprev_b[1]], [pb])
                    act(X["junk"], ps[:, 0:192], AF.Square, [pb], [X["junk_b"], ssqa_b], scale=192.0 ** -0.5,
                        accum_out=ssqa[:, j * 4 + h:j * 4 + h + 1])
                    tt(og[:, j, h * 192:(h + 1) * 192], ps[:, 0:192], H["sgtok"][:, j, :], ALU.mult, [pb, H["sg"]], [og_b[j]])

                p2a(0)
                for j in range(NT):
                    if j + 1 < NT:
                        p2a(j + 1)
                    p2b(j)
                    if len(pend) > 1:
                        pend.pop(0)()
                while pend:
                    pend.pop(0)()

            act(lnsa, ssqa, AF.Ln, [ssqa_b, CB], [lnsa_b], bias=cst[:, 0:1])
            act(rsa, lnsa, AF.Exp, [lnsa_b], [rsa_b], scale=-0.5)
            og3 = U[:, a_og:a_og + NT * 768].rearrange("p (g e) -> p g e", e=192)
            tt(og3, og3, rsa.unsqueeze(2).to_broadcast([128, NT * 4, 192]), ALU.mult, [rsa_b] + og_b, og_b)
            stage("gla_%d" % ph)
            for j in range(NT):
                for c in range(6):
                    T.op("pe", lambda j=j, c=c: nc.tensor.transpose(psb[:, c * 128:(c + 1) * 128], og[:, j, c * 128:(c + 1) * 128], identb[:]),
                         reads=[og_b[j], CB], writes=[psb_b], sig=(c == 5))
                tt(hT[:, 0:6, tsl(j)], psb[:, 0:768].rearrange("p (c t) -> p c t", c=6),
                   pp[:, PP_NG:PP_NG + 6].unsqueeze(2).to_broadcast([128, 6, 128]), ALU.mult,
                   [psb_b, CB], [hT_b[c][j // 4] for c in range(6)])
            stage("mix0_%d" % ph)
            dump("mix0_%d" % ph, hT[:, :, 0:TT], [hT_b[c][m] for c in range(8) for m in range(NM)])
            outproj_resid("evO", 0, 16)
            dump("x_l0mix_%d" % ph, xT[:, :, 0:TT], [xT_b[c][m] for c in range(8) for m in range(NM)])
            stage("out0_%d" % ph)
            norm(0, 1)
            ffn(0)
            stage("ffn0_%d" % ph)
            dump("x_l0_%d" % ph, xT[:, :, 0:TT], [xT_b[c][m] for c in range(8) for m in range(NM)])

            norm(1, 0)
            uT = U[:, 0:4 * TT].rearrange("p (c t) -> p c t", c=4)
            v1 = U[:, 4096:4096 + NT * 512].rearrange("p (j n) -> p j n", j=NT)
            gbT = U[:, 8192:8192 + 4 * TT].rearrange("p (c t) -> p c t", c=4)
            gcT = U[:, 12288:12288 + 4 * TT].rearrange("p (c t) -> p c t", c=4)
            zT = U[:, 24576:32768].bitcast(F32).rearrange("p (c t) -> p c t", c=4)
            u_b = ub("uT"); v1_b = ub("v1"); gb_b = ub("gb"); gc_b = ub("gc"); z_b = ub("z")
            accs = [U[:, 16384 + i * 2048:16384 + (i + 1) * 2048].bitcast(F32) for i in range(4)]
            acc_b = [ub("acc%d" % i) for i in range(4)]
            sl, slb = wget("odG")
            w = v3(sl, 8, 1024)
            pre = {}
            specs = []
            for oc in range(4):
                ps, pb = PS()
                pre[(oc, 0)] = (ps, pb)
                specs.append((ps[:], pb, [(w[:, kc, oc * 128:(oc + 1) * 128], hT[:, kc, msl(0)], [slb, hT_b[kc][0]]) for kc in range(8)]))
            mm_kc_outer(specs)
            for m in range(NM):
                for oc in range(8):
                    if (oc, m) in pre:
                        ps, pb = pre[(oc, m)]
                    else:
                        ps, pb = PS()
                        mmg(ps[:], [(w[:, kc, oc * 128:(oc + 1) * 128], hT[:, kc, msl(m)]) for kc in range(8)], [slb] + hall(m), [pb])
                    if oc < 4:
                        cp(gbT[:, oc, msl(m)], ps[:], [pb], [gb_b], eng="act")
                    else:
                        cp(gcT[:, oc - 4, msl(m)], ps[:], [pb], [gc_b], eng="dve")
            sl, slb = wget("odH")
            w = v3(sl, 8, 512)
            for oc in range(4):
                for m in range(NM):
                    ps, pb = PS()
                    mmg(ps[:], [(w[:, kc, oc * 128:(oc + 1) * 128], hT[:, kc, msl(m)]) for kc in range(8)], [slb] + hall(m), [pb])
                    tt(zT[:, oc, msl(m)], ps[:], gcT[:, oc, msl(m)], ALU.mult, [pb, gc_b], [z_b])
            RW = 256 if ph == 0 else 64
            for oc in range(4):
                acc = accs[oc][:, 0:TT]; ab = acc_b[oc]
                z = zT[:, oc, 0:TT]
                cw = lambda k: pp[:, PP_CONV + oc * 3 + k:PP_CONV + oc * 3 + k + 1]
                ts(acc, z, cw(1), None, ALU.mult, None, [z_b, CB], [ab])
                a3 = acc.rearrange("p (r w) -> p r w", w=RW)
                z3 = z.rearrange("p (r w) -> p r w", w=RW)
                stt(a3[:, :, 1:RW], z3[:, :, 0:RW - 1], cw(0), a3[:, :, 1:RW], ALU.mult, ALU.add, [z_b, CB, ab], [ab])
                stt(a3[:, :, 0:RW - 1], z3[:, :, 1:RW], cw(2), a3[:, :, 0:RW - 1], ALU.mult, ALU.add, [z_b, CB, ab], [ab])
            sl, slb = wget("odUV")
            w = v3(sl, 8, 1024)
            for oc in range(4):
                for m in range(NM):
                    ps, pb = PS()
                    mmg(ps[:], [(w[:, kc, oc * 128:(oc + 1) * 128], hT[:, kc, msl(m)]) for kc in range(8)], [slb] + hall(m), [pb])
                    act(uT[:, oc, msl(m)], ps[:], AF.Gelu_apprx_tanh, [pb], [u_b])
            for j in range(NT):
                ps, pb = PS()
                mmg(ps[:], [(hT[:, kc, tsl(j)], w[:, kc, 512:1024]) for kc in range(8)], [slb] + hall(j // 4), [pb])
                act(v1[:, j, :], ps[:], AF.Gelu_apprx_tanh, [pb], [v1_b])
            for j in range(NT):
                ps, pb = PS()
                for g in range(4):
                    T.op("pe", lambda g=g, ps=ps, j=j: nc.tensor.matmul(ps[:, g * 128:(g + 1) * 128], v1[:, j, g * 128:(g + 1) * 128], wsT[:, g, :], start=True, stop=False),
                         reads=[v1_b, CB], writes=[pb], sig=False)
                    T.op("pe", lambda g=g, ps=ps: nc.tensor.matmul(ps[:, g * 128:(g + 1) * 128], onesrow_b[0:1, :], gbs_hi[0:1, g * 128:(g + 1) * 128], start=False, stop=False),
                         reads=[CB], writes=[pb], sig=False)
                    T.op("pe", lambda g=g, ps=ps: nc.tensor.matmul(ps[:, g * 128:(g + 1) * 128], onesrow_b[0:1, :], gbs_lo[0:1, g * 128:(g + 1) * 128], start=False, stop=True),
                         reads=[CB], writes=[pb], sig=(g == 3))
                tt(hT[:, 0:4, tsl(j)], ps[:].rearrange("p (g t) -> p g t", g=4), uT[:, :, tsl(j)], ALU.mult,
                   [pb, u_b], [hT_b[c][j // 4] for c in range(4)])
            for oc in range(4):
                tt(hT[:, 4 + oc, 0:TT], accs[oc][:, 0:TT], gbT[:, oc, 0:TT], ALU.mult, [acc_b[oc], gb_b], [hT_b[4 + oc][m] for m in range(NM)])
            stage("mix1_%d" % ph)
            outproj_resid("odO", 1, 16)
            dump("x_l1mix_%d" % ph, xT[:, :, 0:TT], [xT_b[c][m] for c in range(8) for m in range(NM)])
            norm(1, 1)
            ffn(1)
            stage("ffn1_%d" % ph)

            yst = [U[:, 0:2048].bitcast(F32), U[:, 2048:4096].bitcast(F32)]
            yst_b = [ub("yst0"), ub("yst1")]; nb["sq"] = ub("sq")
            for m in range(NM):
                sq = U[:, 20480:24576].rearrange("p (c t) -> p c t", c=8)
                xr = [xT_b[c][m] for c in range(8)]
                act(sq, xT[:, :, msl(m)], AF.Square, xr, [nb["sq"]])
                psr, pbr = PS()
                for jj in range(4):
                    mmg(psr[:, jj:jj + 1], [(sq[:, c, jj * 128:(jj + 1) * 128], onesb[:, 0:1]) for c in range(8)], [nb["sq"], CB], [pbr])
                act(rtok[:, 0:4], psr[:, 0:4], AF.Ln, [pbr, CB], [rtok_b], bias=cst[:, 0:1])
                act(rtok[:, 4:8], rtok[:, 0:4], AF.Exp, [rtok_b], [rtok_b], scale=-0.5)
                for jj in range(4):
                    j = m * 4 + jj
                    ys, yb = yst[j % 2], yst_b[j % 2]
                    for hf in range(2):
                        ps, pb = PS()
                        for k in range(4):
                            c = hf * 4 + k
                            T.op("pe", lambda ps=ps, k=k, c=c, j=j: nc.tensor.transpose(ps[:, k * 128:(k + 1) * 128], xT[:, c, j * 128:(j + 1) * 128], ident[:]),
                                 reads=[xT_b[c][m], CB], writes=[pb], sig=(k == 3))
                        stt(ys[:, hf * 512:(hf + 1) * 512], ps[:], rtok[:, 4 + jj:5 + jj], gfinb[:, hf * 512:(hf + 1) * 512], ALU.mult, ALU.mult,
                            [pb, rtok_b, CB], [yb])
                    T.dma("sp", y_d[ph][j * 128:(j + 1) * 128, :], ys, yb, reads=[yb])

        T.barrier(engines=("pe", "act", "dve", "sp", "pool"))
    return nc


_CACHE = {}


def _get_nc():
    if "nc" not in _CACHE:
        _CACHE["nc"] = build()
        _CACHE["consts"] = _consts()
    return _CACHE["nc"], _CACHE["consts"]


def make_in_maps(inputs, consts):
    f = lambda a: np.ascontiguousarray(np.asarray(a, dtype=np.float32))
    x_prompt = f(inputs["x_prompt"]); x_sample = f(inputs["x_sample"])
    sf = f(inputs["state_gla_fwd"]); sbw = f(inputs["state_gla_bwd"])
    c = f(inputs["c"]); c_ctx = f(inputs["c_ctx"])
    pp = np.zeros((128, PP_N), np.float32)
    for l in range(2):
        pp[:, PP_GMIX + l * 8:PP_GMIX + (l + 1) * 8] = f(inputs["norm_mix_g"])[l].reshape(8, 128).T
        pp[:, PP_GFFN + l * 8:PP_GFFN + (l + 1) * 8] = f(inputs["norm_ffn_g"])[l].reshape(8, 128).T
        pp[:, PP_ADAB + l * 48:PP_ADAB + (l + 1) * 48] = f(inputs["ada_b"])[l].reshape(48, 128).T
    pp[:, PP_CONV:PP_CONV + 12] = f(inputs["conv_w"])[0].T.reshape(4, 128, 3).transpose(1, 0, 2).reshape(128, 12)
    pp[:, PP_NG:PP_NG + 6] = np.tile(f(inputs["gla_norm_g"])[0], 4).reshape(6, 128).T
    w2aug = np.zeros((2, 64, 512), np.float32)
    w2aug[0, 0:16] = f(inputs["gla_w2_f"])[0]
    w2aug[0, 32] = f(inputs["gla_b2_f"])[0]
    w2aug[1, 16:32] = f(inputs["gla_w2_b"])[0]
    w2aug[1, 32] = f(inputs["gla_b2_b"])[0]
    shared = dict(
        pp=pp, w2aug=w2aug, gfin=f(inputs["final_norm_g"]).reshape(1, D),
        gws=f(inputs["gmlp_ws"])[0], gbs=f(inputs["gmlp_b"])[0].reshape(1, 512),
        ada_w=f(inputs["ada_w"]), ffn_w1=f(inputs["ffn_w1"]), ffn_w2=f(inputs["ffn_w2"]),
        ev_w_in=f(inputs["ev_w_in"])[0], ev_w_out=f(inputs["ev_w_out"])[0],
        od_w_in=f(inputs["od_w_in"])[0], od_w_out=f(inputs["od_w_out"])[0],
        ident=consts["ident"], tri=consts["tri"], mask2=consts["mask2"], cs64=consts["cs64"],
        dftc256=consts["dftc256"], dfts256=consts["dfts256"],
        dftc1024=consts["dftc1024"], dfts1024=consts["dfts1024"],
    )
    maps = []
    for i in range(NCORES):
        cond = np.stack([c_ctx, c[i]], axis=0)
        m = dict(shared)
        m["xp"] = x_prompt[2 * i:2 * i + 2].reshape(512, D)
        m["xs"] = x_sample[i]
        m["s0f"] = sf[i, 0]
        m["s0b"] = sbw[i, 0]
        m["condT"] = np.ascontiguousarray(cond.reshape(2, 8, 128).transpose(2, 1, 0))
        maps.append(m)
    return maps


def kernel(**inputs):
    nc, consts = _get_nc()
    maps = make_in_maps(inputs, consts)
    res = run_bass_kernel_spmd(nc, maps, core_ids=list(range(NCORES)))
    r = res.results
    y_prompt = np.concatenate([r[i]["yp"].reshape(2, 256, D) for i in range(NCORES)], axis=0)
    y_sample = np.stack([r[i]["ys"] for i in range(NCORES)], axis=0)
    nsf = np.concatenate([r[i]["nsf"].reshape(2, 1, 4, 128, 192) for i in range(NCORES)], axis=0)
    nsb = np.concatenate([r[i]["nsb"].reshape(2, 1, 4, 128, 192) for i in range(NCORES)], axis=0)
    return (y_prompt.astype(np.float32), y_sample.astype(np.float32),
            nsf.astype(np.float32), nsb.astype(np.float32))
```
